# Optimizing a Trainium2 kernel written in Bass

```python
import math
import jax, jax.numpy as jnp
from jax import lax
import numpy as np

D_MODEL = 2048
BATCH = 2
SEQ = 16384
DEPTH = 1
DEC_BATCH = 16
DEC_SEQ = 64
PAST_LEN = 2048

CHUNK = 64
Q_BLOCK = 128
EPS = 1e-6
FORGET_BIAS_INIT = 2.0
FOX_HEAD_DIM = 64
FOX_DIM = D_MODEL // 2
FOX_HEADS = FOX_DIM // FOX_HEAD_DIM
SSD_HEAD_DIM = 64
SSD_DIM = D_MODEL - FOX_DIM
SSD_HEADS = SSD_DIM // SSD_HEAD_DIM
SSD_GROUPS = 2
SSD_HPG = SSD_HEADS // SSD_GROUPS
SSD_STATE = 128
CONV_W = 4
CONV_DIM = SSD_DIM + 2 * SSD_GROUPS * SSD_STATE
MIX_DIM = FOX_DIM + SSD_DIM
IN_DIM = 3 * FOX_DIM + FOX_HEADS + SSD_DIM + CONV_DIM + SSD_HEADS
D_FF = ((8 * D_MODEL + 3 * 256 - 1) // (3 * 256)) * 256

kernel_name = "fox_ssd_parallel_hybrid_stream_step"


def rmsnorm(x, g):
    xf = x.astype(jnp.float32)
    y = xf * lax.rsqrt(jnp.mean(xf * xf, axis=-1, keepdims=True) + EPS)
    return (y * g).astype(x.dtype)


def gated_rmsnorm(y, z, g):
    b, T, _ = y.shape
    u = (y.astype(jnp.float32) * jax.nn.silu(z.astype(jnp.float32)))
    u = u.reshape(b, T, SSD_GROUPS, SSD_DIM // SSD_GROUPS)
    u = u * lax.rsqrt(jnp.mean(u * u, axis=-1, keepdims=True) + EPS)
    return u.reshape(b, T, SSD_DIM) * g


def causal_dwconv(xpad, w, bias):
    T = xpad.shape[1] - (CONV_W - 1)
    out = bias
    for j in range(CONV_W):
        out = out + xpad[:, j:j + T] * w[j]
    return out


def fox_block(q, k, v, fq, fk, qpos, kpos):
    s = jnp.einsum("bthd,bshd->bhts", q, k).astype(jnp.float32) * (FOX_HEAD_DIM ** -0.5)
    bias = jnp.swapaxes(fq, 1, 2)[:, :, :, None] - jnp.swapaxes(fk, 1, 2)[:, :, None, :]
    causal = kpos[None, :] <= qpos[:, None]
    s = jnp.where(causal, s + bias, -jnp.inf)
    p = jax.nn.softmax(s, axis=-1)
    return jnp.einsum("bhts,bshd->bthd", p.astype(v.dtype), v)


def fox_attention(q, k, v, fq, fk, qpos, kpos):
    b, T = q.shape[0], q.shape[1]
    if T <= Q_BLOCK:
        return fox_block(q, k, v, fq, fk, qpos, kpos)
    nb = T // Q_BLOCK

    def blocks(a):
        return jnp.moveaxis(a.reshape((b, nb, Q_BLOCK) + a.shape[2:]), 1, 0)

    out = lax.map(lambda xs: fox_block(xs[0], k, v, xs[1], fk, xs[2], kpos),
                  (blocks(q), blocks(fq), qpos.reshape(nb, Q_BLOCK)))
    return jnp.moveaxis(out, 0, 1).reshape(b, T, FOX_HEADS, FOX_HEAD_DIM)


def ssd_scan(xs, dt, A, Bm, Cm, h0):
    b, T = xs.shape[0], xs.shape[1]
    l = T if T <= CHUNK else CHUNK
    nc = T // l
    x = xs.reshape(b, nc, l, SSD_GROUPS, SSD_HPG, SSD_HEAD_DIM)
    dtc = dt.reshape(b, nc, l, SSD_GROUPS, SSD_HPG)
    Bc = Bm.reshape(b, nc, l, SSD_GROUPS, SSD_STATE)
    Cc = Cm.reshape(b, nc, l, SSD_GROUPS, SSD_STATE)
    cum = jnp.cumsum(dtc * A.reshape(SSD_GROUPS, SSD_HPG), axis=2)
    causal = jnp.tril(jnp.ones((l, l), dtype=bool))[:, :, None, None]
    seg = cum[:, :, :, None] - cum[:, :, None, :]
    decay = jnp.exp(jnp.where(causal, seg, -jnp.inf))
    cb = jnp.einsum("bctgn,bcsgn->bctsg", Cc, Bc)
    m = cb[..., None] * decay * dtc[:, :, None]
    y_diag = jnp.einsum("bctsgh,bcsghp->bctghp", m, x)
    last = cum[:, :, -1]
    xw = x * (jnp.exp(last[:, :, None] - cum) * dtc)[..., None]
    states = jnp.einsum("bcsgn,bcsghp->bcghpn", Bc, xw)

    def step(h, inp):
        st, dec = inp
        return dec[..., None, None] * h + st, h

    h_init = h0.astype(jnp.float32).reshape(b, SSD_GROUPS, SSD_HPG, SSD_HEAD_DIM, SSD_STATE)
    h_final, h_prev = lax.scan(step, h_init, (jnp.moveaxis(states, 1, 0), jnp.moveaxis(jnp.exp(last), 1, 0)))
    h_prev = jnp.moveaxis(h_prev, 0, 1)
    y_off = jnp.einsum("bctgn,bcghpn->bctghp", Cc, h_prev) * jnp.exp(cum)[..., None]
    y = (y_diag + y_off).reshape(b, T, SSD_HEADS, SSD_HEAD_DIM)
    return y, h_final.reshape(b, SSD_HEADS, SSD_HEAD_DIM, SSD_STATE)


def hybrid_layer(x, conv_prev, ssm_prev, k_prev, v_prev, logf_prev,
                 g_mix, w_in, f_bias, g_q, g_k, w_conv, b_conv, dt_bias, a_log, d_skip, g_ssd,
                 w_out, g_ffn, w_gate, w_up, w_down):
    b, T, _ = x.shape
    past = k_prev.shape[1]
    h = rmsnorm(x, g_mix)
    proj = h @ w_in
    offs = [FOX_DIM, 2 * FOX_DIM, 3 * FOX_DIM, 3 * FOX_DIM + FOX_HEADS,
            3 * FOX_DIM + FOX_HEADS + SSD_DIM, 3 * FOX_DIM + FOX_HEADS + SSD_DIM + CONV_DIM]
    q, k, v, f_raw, z, xbc, dt_raw = jnp.split(proj, offs, axis=-1)
    q = rmsnorm(q.reshape(b, T, FOX_HEADS, FOX_HEAD_DIM), g_q)
    k = rmsnorm(k.reshape(b, T, FOX_HEADS, FOX_HEAD_DIM), g_k)
    v = v.reshape(b, T, FOX_HEADS, FOX_HEAD_DIM)
    logf = jax.nn.log_sigmoid(f_raw.astype(jnp.float32) + f_bias)
    k_all = jnp.concatenate([k_prev.astype(k.dtype), k], axis=1)
    v_all = jnp.concatenate([v_prev.astype(v.dtype), v], axis=1)
    logf_all = jnp.concatenate([logf_prev.astype(jnp.float32), logf], axis=1)
    f_all = jnp.cumsum(logf_all, axis=1)
    qpos = past + jnp.arange(T, dtype=jnp.int32)
    kpos = jnp.arange(past + T, dtype=jnp.int32)
    attn = fox_attention(q, k_all, v_all, f_all[:, past:], f_all, qpos, kpos)
    xbc_all = jnp.concatenate([conv_prev.astype(xbc.dtype), xbc], axis=1)
    u = jax.nn.silu(causal_dwconv(xbc_all, w_conv, b_conv))
    xs, Bm, Cm = jnp.split(u, [SSD_DIM, SSD_DIM + SSD_GROUPS * SSD_STATE], axis=-1)
    xs = xs.reshape(b, T, SSD_HEADS, SSD_HEAD_DIM)
    Bm = Bm.reshape(b, T, SSD_GROUPS, SSD_STATE)
    Cm = Cm.reshape(b, T, SSD_GROUPS, SSD_STATE)
    dt = jax.nn.softplus(dt_raw.astype(jnp.float32) + dt_bias)
    A = -jnp.exp(a_log.astype(jnp.float32))
    y, h_last = ssd_scan(xs, dt, A, Bm, Cm, ssm_prev)
    y = y + d_skip[:, None] * xs
    y = gated_rmsnorm(y.reshape(b, T, SSD_DIM), z, g_ssd)
    mix = jnp.concatenate([attn.reshape(b, T, FOX_DIM).astype(x.dtype), y.astype(x.dtype)], axis=-1) @ w_out
    x = x + mix
    hf = rmsnorm(x, g_ffn)
    x = x + ((jax.nn.silu(hf @ w_gate) * (hf @ w_up)) @ w_down).astype(x.dtype)
    return x, xbc_all[:, -(CONV_W - 1):], h_last, k, v, logf


def setup_inputs(seed: int = 0) -> dict:
    key = jax.random.key(seed)
    ks = jax.random.split(key, 24)
    f32 = jnp.float32

    def nrm(k, shape, scale):
        return jax.random.normal(k, shape, f32) * scale

    x_prompt = nrm(ks[0], (BATCH, SEQ, D_MODEL), 1.0)
    x_sample = nrm(ks[1], (DEC_BATCH, DEC_SEQ, D_MODEL), 1.0)
    cache_conv = nrm(ks[2], (DEPTH, DEC_BATCH, CONV_W - 1, CONV_DIM), 1.0)
    state_ssm = nrm(ks[3], (DEPTH, DEC_BATCH, SSD_HEADS, SSD_HEAD_DIM, SSD_STATE), 0.1)
    cache_fox_k = nrm(ks[4], (DEPTH, DEC_BATCH, PAST_LEN, FOX_HEADS, FOX_HEAD_DIM), 1.0)
    cache_fox_v = nrm(ks[5], (DEPTH, DEC_BATCH, PAST_LEN, FOX_HEADS, FOX_HEAD_DIM), 1.0)
    cache_fox_logf = jax.nn.log_sigmoid(FORGET_BIAS_INIT + nrm(ks[6], (DEPTH, DEC_BATCH, PAST_LEN, FOX_HEADS), 1.0))
    g_mix = 1.0 + nrm(ks[7], (DEPTH, D_MODEL), 0.02)
    w_in = nrm(ks[8], (DEPTH, D_MODEL, IN_DIM), D_MODEL ** -0.5)
    f_bias = FORGET_BIAS_INIT + nrm(ks[9], (DEPTH, FOX_HEADS), 0.1)
    g_q = 1.0 + nrm(ks[10], (DEPTH, FOX_HEAD_DIM), 0.02)
    g_k = 1.0 + nrm(ks[11], (DEPTH, FOX_HEAD_DIM), 0.02)
    w_conv = nrm(ks[12], (DEPTH, CONV_W, CONV_DIM), CONV_W ** -0.5)
    b_conv = nrm(ks[13], (DEPTH, CONV_DIM), 0.02)
    dt0 = jnp.exp(jax.random.uniform(ks[14], (DEPTH, SSD_HEADS), f32, math.log(1e-3), math.log(1e-1)))
    dt_bias = dt0 + jnp.log(-jnp.expm1(-dt0))
    a_log = jnp.log(jax.random.uniform(ks[15], (DEPTH, SSD_HEADS), f32, 1.0, 16.0))
    d_skip = 1.0 + nrm(ks[16], (DEPTH, SSD_HEADS), 0.02)
    g_ssd = 1.0 + nrm(ks[17], (DEPTH, SSD_DIM), 0.02)
    w_out = nrm(ks[18], (DEPTH, MIX_DIM, D_MODEL), MIX_DIM ** -0.5)
    g_ffn = 1.0 + nrm(ks[19], (DEPTH, D_MODEL), 0.02)
    w_gate = nrm(ks[20], (DEPTH, D_MODEL, D_FF), D_MODEL ** -0.5)
    w_up = nrm(ks[21], (DEPTH, D_MODEL, D_FF), D_MODEL ** -0.5)
    w_down = nrm(ks[22], (DEPTH, D_FF, D_MODEL), D_FF ** -0.5)
    return {"x_prompt": x_prompt, "x_sample": x_sample,
            "cache_conv": cache_conv, "state_ssm": state_ssm,
            "cache_fox_k": cache_fox_k, "cache_fox_v": cache_fox_v, "cache_fox_logf": cache_fox_logf,
            "g_mix": g_mix, "w_in": w_in, "f_bias": f_bias, "g_q": g_q, "g_k": g_k,
            "w_conv": w_conv, "b_conv": b_conv, "dt_bias": dt_bias, "a_log": a_log,
            "d_skip": d_skip, "g_ssd": g_ssd, "w_out": w_out, "g_ffn": g_ffn,
            "w_gate": w_gate, "w_up": w_up, "w_down": w_down}


def reference(x_prompt, x_sample, cache_conv, state_ssm, cache_fox_k, cache_fox_v, cache_fox_logf,
              g_mix, w_in, f_bias, g_q, g_k, w_conv, b_conv, dt_bias, a_log, d_skip, g_ssd,
              w_out, g_ffn, w_gate, w_up, w_down):
    yp, ys = x_prompt, x_sample
    bp = x_prompt.shape[0]
    p_conv, p_ssm, p_k, p_v, p_f = [], [], [], [], []
    s_conv, s_ssm, s_k, s_v, s_f = [], [], [], [], []
    for i in range(DEPTH):
        wts = (g_mix[i], w_in[i], f_bias[i], g_q[i], g_k[i], w_conv[i], b_conv[i], dt_bias[i],
               a_log[i], d_skip[i], g_ssd[i], w_out[i], g_ffn[i], w_gate[i], w_up[i], w_down[i])
        yp, c, h, k, v, f = hybrid_layer(
            yp,
            jnp.zeros((bp, CONV_W - 1, CONV_DIM), yp.dtype),
            jnp.zeros((bp, SSD_HEADS, SSD_HEAD_DIM, SSD_STATE), jnp.float32),
            jnp.zeros((bp, 0, FOX_HEADS, FOX_HEAD_DIM), yp.dtype),
            jnp.zeros((bp, 0, FOX_HEADS, FOX_HEAD_DIM), yp.dtype),
            jnp.zeros((bp, 0, FOX_HEADS), jnp.float32),
            *wts)
        p_conv.append(c); p_ssm.append(h); p_k.append(k); p_v.append(v); p_f.append(f)
        ys, c, h, k, v, f = hybrid_layer(
            ys, cache_conv[i], state_ssm[i], cache_fox_k[i], cache_fox_v[i], cache_fox_logf[i], *wts)
        s_conv.append(c); s_ssm.append(h); s_k.append(k); s_v.append(v); s_f.append(f)
    return (yp, ys,
            jnp.stack(p_conv), jnp.stack(p_ssm), jnp.stack(p_k), jnp.stack(p_v), jnp.stack(p_f),
            jnp.stack(s_conv), jnp.stack(s_ssm), jnp.stack(s_k), jnp.stack(s_v), jnp.stack(s_f))
```

```python
import numpy as np
import ml_dtypes
import concourse.bass as bass
import concourse.mybir as mybir
from concourse.bass_utils import run_bass_kernel_spmd

F32 = mybir.dt.float32
BF16 = mybir.dt.bfloat16
ALU = mybir.AluOpType
AF = mybir.ActivationFunctionType
ET = mybir.EngineType

D = 2048
EPS = 1e-6
HG = 4
NCOL = 1544
PAST = 2048
DFF = 5632


class Tracker:
    CE = ("pe", "act", "dve", "pool", "sp")

    def __init__(self, nc, nslot=8):
        self.nc = nc
        self.nslot = nslot
        self.ops = {e: [] for e in ("pe", "act", "dve", "pool", "sp")}
        self.cnt = {e: 0 for e in self.CE}
        self.ndma = {"sp": 0, "pool": 0}
        self.slotval = {}
        self.res = {}
        self.waited = {e: {} for e in self.ops}

    def _need(self, eng, tok, waits):
        if tok is None:
            return
        kind, key, val = tok
        if kind == "c" and key == eng and eng == "pe":
            return
        k = (kind, key)
        if self.waited[eng].get(k, 0) >= val:
            return
        waits[k] = max(waits.get(k, 0), val)

    def op(self, eng, fn, reads=(), writes=(), dma=False):
        waits = {}
        for r in reads:
            st = self.res.get(r)
            if st:
                self._need(eng, st["w"], waits)
        for w in writes:
            st = self.res.get(w)
            if st:
                self._need(eng, st["w"], waits)
                for t in st["r"]:
                    self._need(eng, t, waits)
        if dma:
            slot = self.ndma[eng] % self.nslot
            self.ndma[eng] += 1
            key = (eng, slot)
            prev = self.slotval.get(key, 0)
            if prev:
                self._need(eng, ("d", key, prev), waits)
            tok = ["d", key, prev]
        else:
            self.cnt[eng] += 1
            tok = ("c", eng, self.cnt[eng])
        for k, v in waits.items():
            self.waited[eng][k] = v
        rec = {"fn": fn, "waits": dict(waits), "tok": tok, "dma": dma, "nd": 0}
        if dma:
            n = getattr(fn, "ndma", 1)
            rec["nd"] = n
            self.slotval[key] = prev + 16 * n
            tok[2] = prev + 16 * n
            tok = tuple(tok)
            rec["tok"] = tok
        self.ops[eng].append(rec)
        for r in reads:
            self.res.setdefault(r, {"w": None, "r": []})["r"].append(tok)
        for w in writes:
            self.res[w] = {"w": tok, "r": []}
        return tok

    def barrier(self):
        allr = list(self.res.keys())
        self.op("sp", lambda e: e.nop(), reads=allr, writes=allr + ["__bar"])
        for en in ("pe", "act", "dve", "pool"):
            self.op(en, lambda e: e.nop(), reads=["__bar"])

    def flush(self, sems, dsems):
        nc = self.nc
        ops, self.ops = self.ops, {e: [] for e in self.ops}

        def run(ename, e):
            for rec in ops[ename]:
                for (kind, key), val in rec["waits"].items():
                    if kind == "c":
                        e.wait_ge(sems[key], val)
                    else:
                        e.wait_ge(dsems[key], val)
                if rec["dma"]:
                    sem = dsems[rec["tok"][1]]
                    cnt = [0]

                    def dma(out, in_, _sem=sem, _cnt=cnt, **kw):
                        _cnt[0] += 1
                        return e.dma_start(out=out, in_=in_, **kw).then_inc(_sem, 16)
                    rec["fn"](e, dma)
                    assert cnt[0] == rec["nd"], (cnt[0], rec["nd"])
                else:
                    ins = rec["fn"](e)
                    ins.then_inc(sems[ename], 1)
        with nc.Block() as block:
            @block.sync
            def _(e):
                run("sp", e)

            @block.tensor
            def _(e):
                run("pe", e)

            @block.scalar
            def _(e):
                run("act", e)

            @block.vector
            def _(e):
                run("dve", e)

            @block.gpsimd
            def _(e):
                run("pool", e)


def _nd(n):
    def deco(f):
        f.ndma = n
        return f
    return deco


import os
from contextlib import ExitStack


def build(T, NSQ=8, TS=64, past=PAST):
    assert T % 1024 == 0 and NSQ * TS == 512 and TS == 64 and past % 512 == 0
    STAGE = int(os.environ.get("STAGE", "99"))
    nc = bass.Bass("TRN2", target_bir_lowering=False)
    TT = T + NSQ * TS
    NG = TT // 512
    NGP = T // 512
    NCH = T // 1024
    TKS = past + 128
    NBS = TKS // 128
    NT2 = NCH * 256 + 128

    def din(name, shape, dt=F32):
        return nc.dram_tensor(name, list(shape), dt, kind="ExternalInput").ap()

    def dout(name, shape, dt=F32):
        return nc.dram_tensor(name, list(shape), dt, kind="ExternalOutput").ap()

    def dscr(name, shape, dt):
        return nc.dram_tensor(name, list(shape), dt, kind="Internal").ap()

    x_all = din("x_all", [TT, D])
    w_in = din("w_in", [D, NCOL])
    gmix = din("gmix", [1, D])
    cols = din("cols", [128, 16])
    wconv = din("wconv", [128, 4, 5])
    row4 = din("row4", [1, 8])
    ident_f = din("ident_f", [128, 128])
    ident_b = din("ident_b", [128, 128], BF16)
    bd_ones = din("bd_ones", [128, 128], BF16)
    triu_f = din("triu_f", [128, 128])
    triu_b = din("triu_b", [128, 128], BF16)
    ones_f = din("ones_f", [128, 128])
    ones_b = din("ones_b", [128, 128], BF16)
    c_k = din("c_k", [NSQ, past, 256])
    c_v = din("c_v", [NSQ, past, 256])
    c_lf = din("c_lf", [NSQ, past, 4])
    c_convT = din("c_convT", [512, NSQ, 3])
    c_ssm = din("c_ssm", [NSQ, 256, 128])
    x2 = din("x2", [NT2, D])
    sel = din("sel", [128, 8, 256], BF16)
    sel_s = din("sel_s", [128, 4, 128], BF16)
    gffn = din("gffn", [1, D])
    gssd = din("gssd", [128, 8])
    w_out = din("w_out", [D, D])
    w_gate = din("w_gate", [D, DFF])
    w_up = din("w_up", [D, DFF])
    w_down = din("w_down", [DFF, D])
    k_out = dout("k_out", [TT, 256])
    v_out = dout("v_out", [TT, 256])
    lf_out = dout("lf_out", [TT, 4])
    conv_p = dout("conv_p", [512, 3])
    conv_s = dout("conv_s", [512, NSQ, 3])
    ssm_p = dout("ssm_p", [256, 128])
    ssm_s = dout("ssm_s", [NSQ, 256, 128])
    y2 = dout("y2", [NT2, D])

    w_in_b = dscr("w_in_b", [D, NCOL], BF16)
    QT = dscr("QT", [70, HG, TT], BF16)
    KTn = dscr("KTn", [70, HG, TT], BF16)
    KTs = dscr("KTs", [NSQ, 70, HG, TKS], BF16)
    Vb = dscr("Vb", [TT, 256], BF16)
    Vs = dscr("Vs", [NSQ, TKS, 256], BF16)
    FD = dscr("FD", [TT, 8], F32)
    ZS = dscr("ZS", [TT, 256], F32)
    UT = dscr("UT", [256, TT], BF16)
    Utok = dscr("Utok", [TT, 384], BF16)
    E = dscr("E", [TT, 512], BF16)
    Gq = [dscr(f"G{q}", [4 * 1024, 512], BF16) for q in range(NCH)]
    Gs = dscr("Gs", [4 * 512, 512], BF16)
    wo_b = dscr("wo_b", [D, D], BF16)
    wg_b = dscr("wg_b", [D, DFF], BF16)
    wu_b = dscr("wu_b", [D, DFF], BF16)
    wd_b = dscr("wd_b", [DFF, D], BF16)

    tk = Tracker(nc)
    O = tk.op
    uid = [0]

    def nxt():
        uid[0] += 1
        return uid[0]

    sstack = ExitStack()
    sems = {e: sstack.enter_context(nc.semaphore(f"s_{e}")) for e in Tracker.CE}
    dsems = {}
    for q in ("sp", "pool"):
        for s in range(tk.nslot):
            dsems[(q, s)] = sstack.enter_context(nc.semaphore(f"d_{q}{s}"))

    ges = ExitStack()

    def gsb(name, shape, dt=F32):
        return ges.enter_context(nc.sbuf_tensor(name, list(shape), dt))

    c_cols = gsb("c_cols", [128, 16])
    c_idf = gsb("c_idf", [128, 128])
    c_idb = gsb("c_idb", [128, 128], BF16)
    c_bd = gsb("c_bd", [128, 128], BF16)
    c_trf = gsb("c_trf", [128, 128])
    c_trb = gsb("c_trb", [128, 128], BF16)
    c_onf = gsb("c_onf", [128, 128])
    c_onb = gsb("c_onb", [128, 128], BF16)
    c_fd = gsb("c_fd", [128, 4])
    c_wc = gsb("c_wc", [128, 4, 5])
    c_r4 = gsb("c_r4", [128, 8])

    @_nd(10)
    def ld_const(e, dma):
        dma(c_cols[:, :], cols[:, :])
        dma(c_idf[:, :], ident_f[:, :])
        dma(c_idb[:, :], ident_b[:, :])
        dma(c_bd[:, :], bd_ones[:, :])
        dma(c_trf[:, :], triu_f[:, :])
        dma(c_trb[:, :], triu_b[:, :])
        dma(c_onf[:, :], ones_f[:, :])
        dma(c_onb[:, :], ones_b[:, :])
        dma(c_wc[:, :, :], wconv[:, :, :])
        dma(c_r4[:, :], row4[0:1, :].partition_broadcast(128))
    O("sp", ld_const, writes=["const"], dma=True)
    O("dve", lambda e: e.tensor_tensor(out=c_fd[0:8, 0:1], in0=c_cols[0:8, 3:4], in1=c_cols[0:8, 2:3], op=ALU.mult),
      reads=["const"], writes=["cfd0"])
    O("dve", lambda e: e.tensor_scalar(out=c_fd[:, 1:2], in0=c_cols[:, 1:2], scalar1=8.0, scalar2=None, op0=ALU.mult),
      reads=["const"], writes=["cfd1"])
    O("act", lambda e: e.activation(out=c_r4[:, 0:4], in_=c_r4[:, 0:4], func=AF.Exp), reads=["const"], writes=["cr4"])
    O("dve", lambda e: e.tensor_scalar(out=c_r4[:, 0:4], in0=c_r4[:, 0:4], scalar1=-1.0, scalar2=None, op0=ALU.mult),
      reads=["cr4"], writes=["cr4"])

    @_nd(1)
    def cast_w(e, dma):
        dma(w_in_b[:, :], w_in[:, :])
    O("pool", cast_w, writes=["w_in_b"], dma=True)

    def cast_big(dst, src, rows, name):
        step = 512
        for r in range(0, rows, step):
            @_nd(1)
            def f(e, dma, r=r):
                dma(dst[r:r + step, :], src[r:r + step, :])
            O("pool", f, writes=[name + f"_{r}"], dma=True)
    if STAGE >= 8:
        cast_big(wo_b, w_out, D, "wo_b")
        cast_big(wg_b, w_gate, D, "wg_b")
        cast_big(wu_b, w_up, D, "wu_b")
        cast_big(wd_b, w_down, DFF, "wd_b")

    def phase1():
        es = ExitStack()

        def sb(name, shape, dt=F32):
            return es.enter_context(nc.sbuf_tensor(f'ph0_' + name, list(shape), dt))

        def ps(name, shape, dt=F32):
            return es.enter_context(nc.psum_tensor(f'ph0_' + name, list(shape), dt))

        wb = sb("wb", [128, 16, NCOL], BF16)
        c_gmix = sb("c_gmix", [128, D])

        @_nd(2)
        def ld_w(e, dma):
            dma(wb[:, :, :], w_in_b.rearrange("(kc p) n -> p kc n", p=128))
            dma(c_gmix[:, :], gmix[0:1, :].partition_broadcast(128))
        O("sp", ld_w, reads=["w_in_b"], writes=["wb"], dma=True)

        hT = [sb(f"hT{i}", [128, 16, 512], BF16) for i in range(2)]
        mmc = [0]
        rc_ = {}
        xt = [sb(f"xt{i}", [128, D]) for i in range(2)]
        hb = [sb(f"hb{i}", [128, D], BF16) for i in range(2)]
        junk = sb("junk", [128, D])
        ss = [sb(f"ss{i}", [128, 2]) for i in range(2)]
        psT = [ps(f"psT{i}", [128, 4, 128], BF16) for i in range(2)]
        psA = [ps(f"psA{i}", [128, 512]) for i in range(3)]
        psB = ps("psB", [128, 512])
        psF = [ps(f"psF{i}", [128, 4, 128]) for i in range(2)]
        sq = [sb(f"sq{i}", [128, 512], BF16) for i in range(2)]
        rs = [sb(f"rs{i}", [128, 512]) for i in range(2)]
        qn = [sb(f"qn{i}", [128, 512], BF16) for i in range(2)]
        kf = [sb(f"kf{i}", [128, 512]) for i in range(3)]
        kraw = [sb(f"kraw{i}", [128, 512]) for i in range(2)]
        tokf = [sb(f"tokf{i}", [128, 4, 128]) for i in range(2)]
        tokb = [sb(f"tokb{i}", [128, 4, 128], BF16) for i in range(2)]
        fd = sb("fd", [8, 512])
        fdt = sb("fdt", [128, 4, 8])
        xpad = sb("xpad", [128, 4, 536])
        acc = [sb(f"acc{i}", [128, 512]) for i in range(3)]
        ub = [sb(f"ub{i}", [128, 512], BF16) for i in range(3)]
        O("dve", lambda e: e.memset(xpad[:, :, :], 0.0), writes=[f"xpad{ci}" for ci in range(4)])

        def part_a(tg):
            t0 = tg * 512
            hTb = hT[tg % 2]
            for ti in range(4):
                i2 = nxt() % 2
                r0 = t0 + ti * 128

                @_nd(1)
                def ldx(e, dma, i2=i2, r0=r0):
                    dma(xt[i2][:, :], x_all[r0:r0 + 128, :])
                O("sp", ldx, writes=[f"xt{i2}"], dma=True)
                O("act", lambda e, i2=i2: e.activation(out=junk[:, :], in_=xt[i2][:, :], func=AF.Square,
                                                      accum_out=ss[i2][:, 0:1]),
                  reads=[f"xt{i2}"], writes=["junk", f"ss{i2}a"])
                O("act", lambda e, i2=i2: e.activation(out=ss[i2][:, 1:2], in_=ss[i2][:, 0:1], func=AF.Sqrt,
                                                      bias=EPS, scale=1.0 / D),
                  reads=[f"ss{i2}a"], writes=[f"ss{i2}b"])
                O("dve", lambda e, i2=i2: e.reciprocal(out=ss[i2][:, 1:2], in_=ss[i2][:, 1:2]),
                  reads=[f"ss{i2}b"], writes=[f"ss{i2}b"])
                O("dve", lambda e, i2=i2: e.scalar_tensor_tensor(out=hb[i2][:, :], in0=xt[i2][:, :], scalar=ss[i2][:, 1:2],
                                                                 in1=c_gmix[:, :], op0=ALU.mult, op1=ALU.mult),
                  reads=[f"xt{i2}", f"ss{i2}b", "wb"], writes=[f"hb{i2}"])
                for k4 in range(4):
                    p2 = nxt() % 2

                    def tr(e, i2=i2, k4=k4, p2=p2):
                        for j in range(4):
                            kc = k4 * 4 + j
                            ins = e.transpose(out=psT[p2][:, j, :], in_=hb[i2][:, kc * 128:(kc + 1) * 128],
                                              identity=c_idb[:, :])
                        return ins
                    O("pe", tr, reads=[f"hb{i2}", "const"], writes=[f"psT{p2}"])
                    if k4 % 2:
                        O("act", lambda e, k4=k4, p2=p2, ti=ti, hTb=hTb: e.copy(
                            out=hTb[:, k4 * 4:(k4 + 1) * 4, ti * 128:(ti + 1) * 128], in_=psT[p2][:, :, :]),
                          reads=[f"psT{p2}"], writes=[f"hT{tg % 2}_{ti}_{k4}"])
                    else:
                        O("dve", lambda e, k4=k4, p2=p2, ti=ti, hTb=hTb: e.tensor_copy(
                            out=hTb[:, k4 * 4:(k4 + 1) * 4, ti * 128:(ti + 1) * 128], in_=psT[p2][:, :, :]),
                          reads=[f"psT{p2}"], writes=[f"hT{tg % 2}_{ti}_{k4}"])

        part_a(0)
        for tg in range(NG):
            t0 = tg * 512
            is_s = tg >= NGP
            hTb = hT[tg % 2]
            hT_res = [f"hT{tg % 2}_{ti}_{k4}" for ti in range(4) for k4 in range(4)]

            def mm_chunk(c0, m, pst, hTb=hTb):
                def f(e):
                    for kc in range(16):
                        ins = e.matmul(pst[0:m, :], lhsT=wb[:, kc, c0:c0 + m], rhs=hTb[:, kc, :],
                                       start=(kc == 0), stop=(kc == 15))
                    return ins
                return f

            def tr4_f32(src, f2):
                def f(e):
                    for j in range(4):
                        ins = e.transpose(out=psF[f2][:, j, :], in_=src[:, j * 128:(j + 1) * 128], identity=c_idf[:, :])
                    return ins
                return f

            chunks = []

            def rot(name):
                rc_[name] = rc_.get(name, 0) + 1
                return rc_[name] % 2

            def rot3(name):
                rc_[name] = rc_.get(name, 0) + 1
                return rc_[name] % 3

            def qk1(a3, cc, tg=tg, t0=t0):
                isq = cc < 2
                hp = cc % 2
                a2 = rot("sq")
                O("act", lambda e, a2=a2, a3=a3: e.activation(out=sq[a2][:, :], in_=psA[a3][:, :], func=AF.Square),
                  reads=[f"psA{a3}"], writes=[f"sq{a2}"])
                r2 = rot("kraw")
                O("act", lambda e, r2=r2, a3=a3: e.copy(out=kraw[r2][:, :], in_=psA[a3][:, :]),
                  reads=[f"psA{a3}"], writes=[f"kraw{r2}"])
                O("pe", lambda e, a2=a2: e.matmul(psB[:, :], lhsT=c_bd[:, :], rhs=sq[a2][:, :], start=True, stop=True),
                  reads=[f"sq{a2}", "const"], writes=["psB"])
                O("act", lambda e, a2=a2: e.activation(out=rs[a2][:, :], in_=psB[:, :], func=AF.Sqrt,
                                                      bias=64.0 * EPS, scale=1.0),
                  reads=["psB"], writes=[f"rs{a2}"])
                O("dve", lambda e, a2=a2: e.reciprocal(out=rs[a2][:, :], in_=rs[a2][:, :]),
                  reads=[f"rs{a2}"], writes=[f"rs{a2}"])
                if isq:
                    q2 = rot("qn")
                    O("dve", lambda e, a2=a2, r2=r2, q2=q2: e.scalar_tensor_tensor(out=qn[q2][:, :], in0=kraw[r2][:, :], scalar=c_cols[:, 0:1],
                                                                                 in1=rs[a2][:, :], op0=ALU.mult, op1=ALU.mult),
                      reads=[f"kraw{r2}", f"rs{a2}", "const"], writes=[f"qn{q2}"])

                    @_nd(2)
                    def stq(e, dma, q2=q2, hp=hp, t0=t0):
                        for hh in range(2):
                            dma(QT[0:64, hp * 2 + hh, t0:t0 + 512], qn[q2][hh * 64:(hh + 1) * 64, :])
                    O("sp", stq, reads=[f"qn{q2}"], writes=[f"QT{tg}"], dma=True)
                    return None
                k2 = rot3("kf")
                O("dve", lambda e, a2=a2, r2=r2, k2=k2: e.scalar_tensor_tensor(out=kf[k2][:, :], in0=kraw[r2][:, :], scalar=c_fd[:, 1:2],
                                                                             in1=rs[a2][:, :], op0=ALU.mult, op1=ALU.mult),
                  reads=[f"kraw{r2}", f"rs{a2}", "cfd1"], writes=[f"kf{k2}"])
                return (k2, hp)

            def qk2(st, tg=tg, t0=t0):
                if st is None:
                    return
                k2, hp = st
                q2 = rot("qn")
                O("act", lambda e, k2=k2, q2=q2: e.copy(out=qn[q2][:, :], in_=kf[k2][:, :]),
                  reads=[f"kf{k2}"], writes=[f"qn{q2}"])

                @_nd(2)
                def stk(e, dma, q2=q2, hp=hp, t0=t0):
                    for hh in range(2):
                        dma(KTn[0:64, hp * 2 + hh, t0:t0 + 512], qn[q2][hh * 64:(hh + 1) * 64, :])
                O("sp", stk, reads=[f"qn{q2}"], writes=[f"KTn{tg}"], dma=True)
                f2 = rot("psF")
                O("pe", tr4_f32(kf[k2], f2), reads=[f"kf{k2}", "const"], writes=[f"psF{f2}"])
                O("act", lambda e, f2=f2: e.copy(out=tokf[f2][:, :, :], in_=psF[f2][:, :, :]),
                  reads=[f"psF{f2}"], writes=[f"tokf{f2}"])

                @_nd(1)
                def stko(e, dma, f2=f2, hp=hp, t0=t0):
                    dma(k_out[t0:t0 + 512, hp * 128:(hp + 1) * 128].rearrange("(j p) c -> p j c", p=128), tokf[f2][:, :, :])
                O("sp", stko, reads=[f"tokf{f2}"], writes=["k_out"], dma=True)

            def v1(a3, hp):
                k2 = rot3("kf")
                O("act", lambda e, k2=k2, a3=a3: e.copy(out=kf[k2][:, :], in_=psA[a3][:, :]),
                  reads=[f"psA{a3}"], writes=[f"kf{k2}"])
                return (k2, hp)

            def v2(st, tg=tg, t0=t0):
                k2, hp = st
                f2 = rot("psF")
                O("pe", tr4_f32(kf[k2], f2), reads=[f"kf{k2}", "const"], writes=[f"psF{f2}"])
                O("act", lambda e, f2=f2: e.copy(out=tokf[f2][:, :, :], in_=psF[f2][:, :, :]),
                  reads=[f"psF{f2}"], writes=[f"tokf{f2}"])

                @_nd(1)
                def stvo(e, dma, f2=f2, hp=hp, t0=t0):
                    dma(v_out[t0:t0 + 512, hp * 128:(hp + 1) * 128].rearrange("(j p) c -> p j c", p=128), tokf[f2][:, :, :])
                O("sp", stvo, reads=[f"tokf{f2}"], writes=[f"v_out{tg}_{hp}"], dma=True)

                @_nd(1)
                def stvb(e, dma, hp=hp, t0=t0):
                    dma(Vb[t0:t0 + 512, hp * 128:(hp + 1) * 128], v_out[t0:t0 + 512, hp * 128:(hp + 1) * 128])
                O("pool", stvb, reads=[f"v_out{tg}_{hp}"], writes=[f"Vb{tg}"], dma=True)

            def z1(a3, hp):
                k2 = rot3("kf")
                O("act", lambda e, k2=k2, a3=a3: e.activation(out=kf[k2][:, :], in_=psA[a3][:, :], func=AF.Silu),
                  reads=[f"psA{a3}"], writes=[f"kf{k2}"])
                return (k2, hp)

            def z2(st, tg=tg, t0=t0):
                k2, hp = st
                f2 = rot("psF")
                O("pe", tr4_f32(kf[k2], f2), reads=[f"kf{k2}", "const"], writes=[f"psF{f2}"])
                O("act", lambda e, f2=f2: e.copy(out=tokf[f2][:, :, :], in_=psF[f2][:, :, :]),
                  reads=[f"psF{f2}"], writes=[f"tokf{f2}"])

                @_nd(1)
                def stz(e, dma, f2=f2, hp=hp, t0=t0):
                    dma(ZS[t0:t0 + 512, hp * 128:(hp + 1) * 128].rearrange("(j p) c -> p j c", p=128), tokf[f2][:, :, :])
                O("sp", stz, reads=[f"tokf{f2}"], writes=[f"ZS{tg}"], dma=True)

            nseg, L = (NSQ, TS) if is_s else (1, 512)
            W = L + 3

            def pre_conv(ci, nseg=nseg, W=W, is_s=is_s):
                if is_s:
                    xp3 = xpad[:, ci, 0:nseg * W].rearrange("p (s l) -> p s l", l=W)

                    @_nd(1)
                    def ldhalo(e, dma, ci=ci, xp3=xp3):
                        dma(xp3[:, :, 0:3], c_convT[ci * 128:(ci + 1) * 128, :, :])
                    O("sp", ldhalo, writes=[f"xpad{ci}"], dma=True)

            def conv1(a3, ci, tg=tg, t0=t0, nseg=nseg, L=L, W=W, is_s=is_s):
                xp3 = xpad[:, ci, 0:nseg * W].rearrange("p (s l) -> p s l", l=W)
                O("act", lambda e, a3=a3, xp3=xp3, L=L: e.copy(out=xp3[:, :, 3:3 + L],
                                                              in_=psA[a3][:, :].rearrange("p (s l) -> p s l", l=L)),
                  reads=[f"psA{a3}"], writes=[f"xpad{ci}"])
                c2 = rot3("acc")
                acc3 = acc[c2][:, :].rearrange("p (s l) -> p s l", l=L)
                O("dve", lambda e, acc3=acc3, xp3=xp3, ci=ci, L=L: e.tensor_scalar(
                    out=acc3, in0=xp3[:, :, 3:3 + L], scalar1=c_wc[:, ci, 3:4], scalar2=c_wc[:, ci, 4:5],
                    op0=ALU.mult, op1=ALU.add), reads=[f"xpad{ci}", "const"], writes=[f"acc{c2}"])
                for j in range(1, 4):
                    O("dve", lambda e, acc3=acc3, xp3=xp3, ci=ci, L=L, j=j: e.scalar_tensor_tensor(
                        out=acc3, in0=xp3[:, :, 3 - j:3 - j + L], scalar=c_wc[:, ci, 3 - j:4 - j], in1=acc3,
                        op0=ALU.mult, op1=ALU.add), reads=[f"xpad{ci}", "const", f"acc{c2}"], writes=[f"acc{c2}"])
                O("act", lambda e, c2=c2: e.activation(out=ub[c2][:, :], in_=acc[c2][:, :], func=AF.Silu),
                  reads=[f"acc{c2}"], writes=[f"ub{c2}"])
                if ci >= 2:
                    @_nd(1)
                    def stut(e, dma, c2=c2, ci=ci, t0=t0):
                        dma(UT[(ci - 2) * 128:(ci - 1) * 128, t0:t0 + 512], ub[c2][:, :])
                    O("sp", stut, reads=[f"ub{c2}"], writes=[f"UT{tg}"], dma=True)
                if is_s:
                    @_nd(1)
                    def stcs(e, dma, ci=ci, xp3=xp3):
                        dma(conv_s[ci * 128:(ci + 1) * 128, :, :], xp3[:, :, TS:TS + 3])
                    O("sp", stcs, reads=[f"xpad{ci}"], writes=["conv_s"], dma=True)
                else:
                    if tg == NGP - 1:
                        @_nd(1)
                        def stcp(e, dma, ci=ci):
                            dma(conv_p[ci * 128:(ci + 1) * 128, :], xpad[:, ci, 512:515])
                        O("sp", stcp, reads=[f"xpad{ci}"], writes=["conv_p"], dma=True)
                    O("dve", lambda e, ci=ci: e.tensor_copy(out=xpad[:, ci, 0:3], in_=xpad[:, ci, 512:515]),
                      reads=[f"xpad{ci}"], writes=[f"xpad{ci}"])
                return (c2, ci)

            def conv2(st, tg=tg, t0=t0):
                c2, ci = st
                if ci < 3:
                    p2 = rot("psT")

                    def tru(e, c2=c2, p2=p2):
                        for j in range(4):
                            ins = e.transpose(out=psT[p2][:, j, :], in_=ub[c2][:, j * 128:(j + 1) * 128], identity=c_idb[:, :])
                        return ins
                    O("pe", tru, reads=[f"ub{c2}", "const"], writes=[f"psT{p2}"])
                    b2 = rot("tokb")
                    O("act", lambda e, p2=p2, b2=b2: e.copy(out=tokb[b2][:, :, :], in_=psT[p2][:, :, :]),
                      reads=[f"psT{p2}"], writes=[f"tokb{b2}"])

                    @_nd(1)
                    def stutok(e, dma, b2=b2, ci=ci, t0=t0):
                        dma(Utok[t0:t0 + 512, ci * 128:(ci + 1) * 128].rearrange("(j p) c -> p j c", p=128), tokb[b2][:, :, :])
                    O("sp", stutok, reads=[f"tokb{b2}"], writes=[f"Utok{tg}"], dma=True)

            def fd1(a3):
                O("act", lambda e, a3=a3: e.activation(out=fd[:, :], in_=psA[a3][0:8, :], func=AF.Exp,
                                                      bias=c_fd[0:8, 0:1], scale=c_cols[0:8, 2:3]),
                  reads=[f"psA{a3}", "cfd0", "const"], writes=["fd"])
                O("act", lambda e: e.activation(out=fd[:, :], in_=fd[:, :], func=AF.Ln, bias=1.0, scale=1.0),
                  reads=["fd"], writes=["fd"])
                O("dve", lambda e: e.tensor_scalar(out=fd[:, :], in0=fd[:, :], scalar1=c_cols[0:8, 4:5], scalar2=None, op0=ALU.mult),
                  reads=["fd", "const"], writes=["fd"])
                return 0

            def fd2(st, tg=tg, t0=t0):
                f2 = rot("psF")

                def trf(e, f2=f2):
                    for j in range(4):
                        ins = e.transpose(out=psF[f2][:, j, 0:8], in_=fd[0:8, j * 128:(j + 1) * 128], identity=c_idf[0:8, 0:8])
                    return ins
                O("pe", trf, reads=["fd", "const"], writes=[f"psF{f2}"])
                O("act", lambda e, f2=f2: e.copy(out=fdt[:, :, :], in_=psF[f2][:, :, 0:8]),
                  reads=[f"psF{f2}"], writes=["fdt"])

                @_nd(2)
                def stfd(e, dma, t0=t0):
                    dma(FD[t0:t0 + 512, :].rearrange("(j p) c -> p j c", p=128), fdt[:, :, :])
                    dma(lf_out[t0:t0 + 512, :].rearrange("(j p) c -> p j c", p=128), fdt[:, :, 0:4])
                O("sp", stfd, reads=["fdt"], writes=[f"FD{tg}", "lf_out"], dma=True)

            for cc in range(4):
                chunks.append((cc * 128, 128, None, (lambda a3, cc=cc: qk1(a3, cc)), qk2))
            for hp in range(2):
                chunks.append((512 + hp * 128, 128, None, (lambda a3, hp=hp: v1(a3, hp)), v2))
            for hp in range(2):
                chunks.append((768 + hp * 128, 128, None, (lambda a3, hp=hp: z1(a3, hp)), z2))
            for ci in range(4):
                chunks.append((1024 + ci * 128, 128, (lambda ci=ci: pre_conv(ci)), (lambda a3, ci=ci: conv1(a3, ci)), conv2))
            chunks.append((1536, 8, None, (lambda a3: fd1(a3)), fd2))

            def emit_mm(ci_):
                c0, m, pre, p1_, p2_ = chunks[ci_]
                a3 = mmc[0] % 3
                mmc[0] += 1
                if pre is not None:
                    pre()
                O("pe", mm_chunk(c0, m, psA[a3]), reads=hT_res + ["wb"], writes=[f"psA{a3}"])
                return a3
            ncn = len(chunks)
            a3s = {0: emit_mm(0), 1: emit_mm(1)}
            sts = {}
            for ci_ in range(ncn + 2):
                if ci_ + 2 < ncn:
                    a3s[ci_ + 2] = emit_mm(ci_ + 2)
                if ci_ == 3 and tg + 1 < NG:
                    part_a(tg + 1)
                if ci_ < ncn:
                    sts[ci_] = chunks[ci_][3](a3s[ci_])
                if ci_ >= 2:
                    chunks[ci_ - 2][4](sts[ci_ - 2])
        tk.barrier()
        tk.flush(sems, dsems)
        es.close()

    phase1()

    def phase1b():
        es = ExitStack()

        def sb(name, shape, dt=F32):
            return es.enter_context(nc.sbuf_tensor(f'ph1_' + name, list(shape), dt))

        def ps(name, shape, dt=F32):
            return es.enter_context(nc.psum_tensor(f'ph1_' + name, list(shape), dt))

        NBmax = max(T // 128, NBS)
        LF = sb("LF", [128, NBmax, 4])
        A = [sb(f"scanA{i}", [128, NBmax, 4]) for i in range(2)]
        Ff = sb("Ff", [128, NBmax, 4])
        R1 = sb("R1", [128, NBmax, 4])
        HF = sb("HF", [128, NBmax, 4])
        AQ = sb("AQ", [128, NBmax, 24], BF16)
        AK = sb("AK", [128, NBmax, 24], BF16)
        stg = [sb(f"stg{i}", [24, 4, 128], BF16) for i in range(2)]
        psW = ps("psW", [128, 512])
        psTot = ps("psTot", [128, 512])
        psT = [ps(f"psT{i}", [128, 4, 128], BF16) for i in range(2)]
        ck = [sb(f"ck{i}", [128, 4, 256]) for i in range(2)]
        ckb = [sb(f"ckb{i}", [128, 4, 256], BF16) for i in range(2)]
        kst = [sb(f"kst{i}", [128, 4, 128], BF16) for i in range(2)]
        AQv = AQ[:, :, :].rearrange("p b (i h) -> p b i h", h=4)
        AKv = AK[:, :, :].rearrange("p b (i h) -> p b i h", h=4)
        O("dve", lambda e: e.memset(AQ[:, :, :], -1.0), writes=["AQ"])
        O("dve", lambda e: e.memset(AK[:, :, :], 1.0), writes=["AK"])

        def cumsum_and_aug(nb, lf_reads):
            n4 = nb * 4
            O("pe", lambda e: e.matmul(psW[:, 0:n4], lhsT=c_trf[:, :], rhs=LF[:, 0:nb, :], start=True, stop=True),
              reads=lf_reads + ["const"], writes=["psW"])
            O("pe", lambda e: e.matmul(psTot[:, 0:n4], lhsT=c_onf[:, :], rhs=LF[:, 0:nb, :], start=True, stop=True),
              reads=lf_reads + ["const"], writes=["psTot"])
            O("act", lambda e: e.copy(out=A[0][:, 0:nb, :], in_=psTot[:, 0:n4].rearrange("p (b h) -> p b h", h=4)),
              reads=["psTot"], writes=["scanA0"])
            cur = 0
            d = 1
            while d < nb:
                o = 1 - cur
                O("dve", lambda e, cur=cur, o=o, d=d: e.tensor_tensor(out=A[o][:, d:nb, :], in0=A[cur][:, d:nb, :],
                                                                    in1=A[cur][:, 0:nb - d, :], op=ALU.add),
                  reads=[f"scanA{cur}"], writes=[f"scanA{o}"])
                O("dve", lambda e, cur=cur, o=o, d=d: e.tensor_copy(out=A[o][:, 0:d, :], in_=A[cur][:, 0:d, :]),
                  reads=[f"scanA{cur}", f"scanA{o}"], writes=[f"scanA{o}"])
                cur = o
                d *= 2
            O("dve", lambda e, cur=cur: e.tensor_tensor(out=R1[:, 0:nb, :], in0=A[cur][:, 0:nb, :],
                                                       in1=psTot[:, 0:n4].rearrange("p (b h) -> p b h", h=4), op=ALU.subtract),
              reads=[f"scanA{cur}", "psTot"], writes=["R1"])
            O("dve", lambda e: e.tensor_tensor(out=Ff[:, 0:nb, :], in0=R1[:, 0:nb, :],
                                              in1=psW[:, 0:n4].rearrange("p (b h) -> p b h", h=4), op=ALU.add),
              reads=["R1", "psW"], writes=["Ff"])
            O("act", lambda e: e.copy(out=AQv[:, 0:nb, 0, :], in_=Ff[:, 0:nb, :]), reads=["Ff", "AQ"], writes=["AQ"])
            O("act", lambda e: e.copy(out=HF[:, 0:nb, :], in_=AQv[:, 0:nb, 0, :]), reads=["AQ"], writes=["HF"])
            O("dve", lambda e: e.tensor_tensor(out=R1[:, 0:nb, :], in0=Ff[:, 0:nb, :], in1=HF[:, 0:nb, :], op=ALU.subtract),
              reads=["Ff", "HF"], writes=["R1"])
            O("act", lambda e: e.copy(out=AQv[:, 0:nb, 1, :], in_=R1[:, 0:nb, :]), reads=["R1", "AQ"], writes=["AQ"])
            O("act", lambda e: e.copy(out=HF[:, 0:nb, :], in_=AQv[:, 0:nb, 1, :]), reads=["AQ"], writes=["HF"])
            O("dve", lambda e: e.tensor_tensor(out=R1[:, 0:nb, :], in0=R1[:, 0:nb, :], in1=HF[:, 0:nb, :], op=ALU.subtract),
              reads=["R1", "HF"], writes=["R1"])
            O("act", lambda e: e.copy(out=AQv[:, 0:nb, 2, :], in_=R1[:, 0:nb, :]), reads=["R1", "AQ"], writes=["AQ"])
            O("act", lambda e: e.copy(out=AKv[:, 0:nb, 3:6, :], in_=AQv[:, 0:nb, 0:3, :]), reads=["AQ", "AK"], writes=["AK"])

        def aug_store(src, srcname, b0, nbb, dst_fn, npart=128, ncol=128):
            p2 = nxt() % 2

            def trp(e, p2=p2):
                for j in range(nbb):
                    ins = e.transpose(out=psT[p2][0:24, j, 0:npart], in_=src[0:npart, b0 + j, :], identity=c_idb[0:npart, 0:npart])
                return ins
            O("pe", trp, reads=[srcname, "const"], writes=[f"psT{p2}"])
            O("act", lambda e, p2=p2: e.copy(out=stg[p2][:, 0:nbb, 0:ncol], in_=psT[p2][0:24, 0:nbb, 0:ncol]),
              reads=[f"psT{p2}"], writes=[f"stg{p2}"])
            O("sp", _nd(1)(lambda e, dma, p2=p2: dma(dst_fn(), stg[p2][:, 0:nbb, 0:ncol])),
              reads=[f"stg{p2}"], writes=[f"aug_{nxt()}"], dma=True)

        NB = T // 128
        for b0 in range(0, NB, 8):
            O("sp", _nd(1)(lambda e, dma, b0=b0: dma(
                LF[:, b0:b0 + 8, :], FD[b0 * 128:(b0 + 8) * 128, 0:4].rearrange("(b p) c -> p b c", p=128))),
              reads=[f"FD{tg}" for tg in range(NGP)] + (["LF"] if b0 else []), writes=["LF"], dma=True)
        cumsum_and_aug(NB, ["LF"])
        for b0 in range(0, NB, 4):
            aug_store(AQ, "AQ", b0, 4, lambda b0=b0: QT[64:70, :, b0 * 128:(b0 + 4) * 128].rearrange("i h (j p) -> (i h) j p", p=128))
            aug_store(AK, "AK", b0, 4, lambda b0=b0: KTn[64:70, :, b0 * 128:(b0 + 4) * 128].rearrange("i h (j p) -> (i h) j p", p=128))
        NBP = past // 128
        for s in range(NSQ):
            ts0 = T + s * TS
            O("dve", lambda e: e.memset(LF[:, 0:NBS, :], 0.0), reads=["LF"], writes=["LF"])

            @_nd(3)
            def ldlf(e, dma, s=s, ts0=ts0):
                dma(LF[:, 0:NBP // 2, :], c_lf[s, 0:past // 2, :].rearrange("(b p) c -> p b c", p=128))
                dma(LF[:, NBP // 2:NBP, :], c_lf[s, past // 2:past, :].rearrange("(b p) c -> p b c", p=128))
                dma(LF[0:TS, NBP, :], FD[ts0:ts0 + TS, 0:4])
            O("sp", ldlf, reads=[f"FD{NG - 1}"], writes=["LF"], dma=True)
            cumsum_and_aug(NBS, ["LF"])
            for b0 in range(0, NBS, 4):
                nbb = min(4, NBS - b0)
                aug_store(AK, "AK", b0, nbb, lambda b0=b0, nbb=nbb, s=s:
                          KTs[s, 64:70, :, b0 * 128:(b0 + nbb) * 128].rearrange("i h (j p) -> (i h) j p", p=128))
            aug_store(AQ, "AQ", NBP, 1, lambda ts0=ts0: QT[64:70, :, ts0:ts0 + TS].rearrange("i h (j p) -> (i h) j p", p=TS),
                      npart=TS, ncol=TS)
            for b0 in range(0, NBP, 4):
                i2 = nxt() % 2
                O("sp", _nd(1)(lambda e, dma, i2=i2, s=s, b0=b0: dma(
                    ck[i2][:, :, :], c_k[s, b0 * 128:(b0 + 4) * 128, :].rearrange("(j p) c -> p j c", p=128))),
                  writes=[f"ck{i2}"], dma=True)
                O("act", lambda e, i2=i2: e.copy(out=ckb[i2][:, :, :], in_=ck[i2][:, :, :]), reads=[f"ck{i2}"], writes=[f"ckb{i2}"])
                for hp in range(2):
                    p2 = nxt() % 2

                    def trk(e, i2=i2, hp=hp, p2=p2):
                        for j in range(4):
                            ins = e.transpose(out=psT[p2][:, j, :], in_=ckb[i2][:, j, hp * 128:(hp + 1) * 128], identity=c_idb[:, :])
                        return ins
                    O("pe", trk, reads=[f"ckb{i2}", "const"], writes=[f"psT{p2}"])
                    k2 = nxt() % 2
                    O("dve", lambda e, p2=p2, k2=k2: e.tensor_copy(out=kst[k2][:, :, :], in_=psT[p2][:, :, :]),
                      reads=[f"psT{p2}"], writes=[f"kst{k2}"])

                    @_nd(2)
                    def stks(e, dma, k2=k2, hp=hp, s=s, b0=b0):
                        for hh in range(2):
                            dma(KTs[s, 0:64, hp * 2 + hh, b0 * 128:(b0 + 4) * 128].rearrange("d (j p) -> d j p", p=128),
                                kst[k2][hh * 64:(hh + 1) * 64, :, :])
                    O("sp", stks, reads=[f"kst{k2}"], writes=[f"KTs{s}"], dma=True)
            O("sp", _nd(1)(lambda e, dma, s=s, ts0=ts0: dma(KTs[s, 0:64, :, past:past + TS], KTn[0:64, :, ts0:ts0 + TS])),
              reads=[f"KTn{NG - 1}"], writes=[f"KTs{s}"], dma=True)
            O("pool", _nd(1)(lambda e, dma, s=s: dma(Vs[s, 0:past, :], c_v[s, :, :])), writes=[f"Vs{s}"], dma=True)
            O("sp", _nd(1)(lambda e, dma, s=s, ts0=ts0: dma(Vs[s, past:past + TS, :], Vb[ts0:ts0 + TS, :])),
              reads=[f"Vb{NG - 1}"], writes=[f"Vs{s}"], dma=True)
        tk.barrier()
        tk.flush(sems, dsems)
        es.close()

    if STAGE >= 2:
        phase1b()

    RG = [[0, 1, 2, 3], [4, 5, 6, 7]]
    EARLY_AG = False
    deferred = list(range(NCH))

    def phase23():
        es = ExitStack()

        def sb(name, shape, dt=F32):
            return es.enter_context(nc.sbuf_tensor('p23_' + name, list(shape), dt))

        def ps(name, shape, dt=F32):
            return es.enter_context(nc.psum_tensor('p23_' + name, list(shape), dt))

        TK = max(T, TKS)
        qt = sb("qt", [128, T], BF16)
        kt = sb("kt", [128, TK], BF16)
        vt = sb("vt", [128, TK // 128, 128], BF16)
        pb = [sb(f"pb{i}", [128, 512], BF16) for i in range(4)]
        rc = sb("rc", [128, 512])
        at = sb("at", [64, 512], BF16)
        att = [sb(f"att{i}", [128, 4, 64], BF16) for i in range(2)]
        psS = [ps(f"psS{i}", [128, 512]) for i in range(2)]
        psO = [ps(f"psO{i}", [128, 512]) for i in range(1)]
        tA = ps("ssd_tA", [128, 4, 64])
        tB = ps("ssd_tB", [128, 256])
        tC = ps("ssd_tC", [64, 256])
        psTa = ps("psTa", [128, 4, 64], BF16)
        O("dve", lambda e: e.memset(vt[:, :, 64:128], 1.0), writes=["vt1"])
        O("dve", lambda e: e.memset(qt[64:128, :], 0.0), writes=["qt"])
        O("dve", lambda e: e.memset(kt[64:128, :], 0.0), writes=["kt"])
        sbc = [0]

        def job(q_ap, k_ap, v_ap, Tq, nk, pst, e_row0, h):
            nkb = (nk + 127) // 128

            @_nd(2)
            def ld(e, dma):
                dma(qt[0:70, 0:Tq], q_ap)
                dma(kt[0:70, 0:nk], k_ap)
            O("sp", ld, writes=["qt", "kt"], dma=True)
            nfb = nk // 128
            VB = 8
            for b0 in range(0, nfb, VB):
                b1 = min(nfb, b0 + VB)
                O("sp", _nd(1)(lambda e, dma, b0=b0, b1=b1: dma(
                    vt[:, b0:b1, 0:64], v_ap[b0 * 128:b1 * 128, :].rearrange("(b p) c -> p b c", p=128))),
                  reads=["vt"] if b0 else [], writes=["vt"], dma=True)
            if nk % 128:
                nf = nk // 128
                O("sp", _nd(1)(lambda e, dma: dma(vt[0:nk - nf * 128, nf, 0:64], v_ap[nf * 128:nk, :])),
                  reads=["vt"], writes=["vt"], dma=True)
            SBQ = min(512, Tq)
            LA = 2
            tiles = []
            sbs = []
            for q0 in range(0, Tq, SBQ):
                nq = SBQ
                sbc[0] += 1
                o2 = 0
                blocks = []
                for kb in range(nkb):
                    ks = kb * 128
                    kn = min(128, nk - ks)
                    if ks + kn - 1 <= pst + q0:
                        blocks.append((kb, ks, kn, q0, False))
                    elif ks <= pst + q0 + nq - 1:
                        blocks.append((kb, ks, kn, ks - pst, True))
                sbs.append((q0, nq, o2, len(blocks)))
                for bi, blk in enumerate(blocks):
                    tiles.append((len(sbs) - 1, bi, blk))

            def emit_qk(ti):
                si, bi, (kb, ks, kn, qlo, diag) = tiles[ti]
                q0, nq, o2, nb_ = sbs[si]
                n1 = q0 + nq - qlo
                s3 = ti % 2
                p4 = ti % 4
                O("pe", lambda e, s3=s3, ks=ks, kn=kn, qlo=qlo, n1=n1: e.matmul(
                    psS[s3][0:kn, 0:n1], lhsT=kt[0:128, ks:ks + kn], rhs=qt[0:128, qlo:qlo + n1], start=True, stop=True),
                  reads=["qt", "kt"], writes=[f"psS{s3}"])
                O("act", lambda e, s3=s3, p4=p4, kn=kn, n1=n1: e.activation(
                    out=pb[p4][0:kn, 0:n1], in_=psS[s3][0:kn, 0:n1], func=AF.Exp),
                  reads=[f"psS{s3}"], writes=[f"pb{p4}"])
                if diag:
                    w = min(128, n1)
                    O("pool", lambda e, p4=p4, kn=kn, w=w: e.tensor_tensor(
                        out=pb[p4][0:kn, 0:w], in0=pb[p4][0:kn, 0:w], in1=c_trb[0:kn, 0:w], op=ALU.mult),
                      reads=[f"pb{p4}", "const"], writes=[f"pb{p4}"])

            def emit_pv(ti):
                si, bi, (kb, ks, kn, qlo, diag) = tiles[ti]
                q0, nq, o2, nb_ = sbs[si]
                n1 = q0 + nq - qlo
                p4 = ti % 4
                O("pe", lambda e, o2=o2, p4=p4, kb=kb, kn=kn, qlo=qlo, n1=n1, q0=q0, nq=nq, bi=bi, nb_=nb_: e.matmul(
                    psO[o2][:, qlo - q0:nq], lhsT=vt[0:kn, kb, :], rhs=pb[p4][0:kn, 0:n1],
                    start=(bi == 0), stop=(bi == nb_ - 1)),
                  reads=[f"pb{p4}", "vt", "vt1"], writes=[f"psO{o2}"])
                if bi == nb_ - 1:
                    finish(si)

            def finish(si):
                q0, nq, o2, nb_ = sbs[si]
                O("dve", lambda e, o2=o2, nq=nq: e.reciprocal(out=rc[64:128, 0:nq], in_=psO[o2][64:128, 0:nq]),
                  reads=[f"psO{o2}"], writes=["rc"])
                O("dve", lambda e, o2=o2, nq=nq: e.tensor_tensor(out=at[:, 0:nq], in0=psO[o2][0:64, 0:nq], in1=rc[64:128, 0:nq], op=ALU.mult),
                  reads=[f"psO{o2}", "rc"], writes=["at"])
                nj = (nq + 127) // 128
                wj = min(128, nq)
                p2 = 0

                def tra(e, p2=p2, nj=nj, wj=wj):
                    for j in range(nj):
                        ins = e.transpose(out=psTa[0:wj, j, 0:64], in_=at[:, j * 128:j * 128 + wj], identity=c_idb[0:64, 0:64])
                    return ins
                O("pe", tra, reads=["at", "const"], writes=["psTa"])
                a2 = nxt() % 2
                O("act", lambda e, p2=p2, a2=a2, nj=nj, wj=wj: e.copy(out=att[a2][0:wj, 0:nj, :], in_=psTa[0:wj, 0:nj, 0:64]),
                  reads=["psTa"], writes=[f"att{a2}"])
                r0 = e_row0 + q0
                O("sp", _nd(1)(lambda e, dma, a2=a2, r0=r0, nq=nq, nj=nj, wj=wj, h=h: dma(
                    E[r0:r0 + nq, h * 64:(h + 1) * 64].rearrange("(j p) c -> p j c", p=wj), att[a2][0:wj, 0:nj, :])),
                  reads=[f"att{a2}"], writes=[f"Ea_{r0}_{h}"], dma=True)
                if EARLY_AG and Tq == T and h == HG - 1 and (q0 // 512) % 2 == 1 and STAGE >= 5:
                    q = q0 // 1024
                    rd = [f"Ea_{rr}_{hh}" for rr in (q * 1024, q * 1024 + 512) for hh in range(HG)] + \
                         [f"Eu_{rr}" for rr in (q * 1024, q * 1024 + 512)]
                    if all(r_ in tk.res for r_ in rd):
                        O("pool", lambda e, q=q: e.collective_compute("AllGather", ALU.bypass, replica_groups=RG,
                                                                      ins=[E[q * 1024:(q + 1) * 1024, :]], outs=[Gq[q][:, :]]),
                          reads=rd, writes=[f"G{q}"])
                    else:
                        deferred.append(q)

            nt_ = len(tiles)
            for idx in range(nt_ + LA):
                if idx < nt_:
                    emit_qk(idx)
                if idx - LA >= 0:
                    emit_pv(idx - LA)
                yield

        bt = [sb(f"bt{i}", [128, 512], BF16) for i in range(2)]
        ct = [sb(f"ct{i}", [128, 512], BF16) for i in range(2)]
        xk = [sb(f"xk{i}", [64, 8, 384], BF16) for i in range(2)]
        fdm = [sb(f"fdm{i}", [64, 8, 8]) for i in range(2)]
        zs = [sb(f"zs{i}", [64, 8, 256]) for i in range(2)]
        u8 = [sb(f"u8{i}", [64, 8, 256], BF16) for i in range(2)]
        Hf = sb("Hf", [128, 256])
        Hb = sb("Hb", [128, 256], BF16)
        hio = sb("hio", [128, 2, 128])
        dA = sb("dA", [64, 4])
        Rr = sb("Rr", [64, 4, 64])
        cum = sb("cum", [64, 4])
        ecum = sb("ecum", [64, 4])
        seg = sb("seg", [64, 4, 64])
        CBm = sb("CBm", [64, 64])
        MT = sb("MT", [64, 4, 64], BF16)
        ysb = sb("ysb", [64, 256])
        wl = sb("wl", [64, 4])
        xw = sb("xw", [64, 256], BF16)
        dec = sb("dec", [128, 4])
        psR = tA
        psH = tB[:, :]
        psY = tB[0:64, :]
        psC = tB[0:64, 0:4]
        psF = tB[:, :].rearrange("p (a n) -> p a n", n=128)
        psYo = tC[:, :]
        psCB = tC[:, 0:64]

        def seq_job(r_base, ntok, init_ap, out_ap, tag):
            if init_ap is None:
                O("dve", lambda e: e.memset(Hf[:, :], 0.0), writes=["Hf"])
            else:
                O("sp", _nd(1)(lambda e, dma: dma(hio[:, :, :], init_ap.rearrange("(a p) n -> p a n", p=128))),
                  writes=["hio"], dma=True)

                def trin(e):
                    for a in range(2):
                        ins = e.transpose(out=psF[:, a, :], in_=hio[:, a, :], identity=c_idf[:, :])
                    return ins
                O("pe", trin, reads=["hio", "const"], writes=["tB"])
                O("act", lambda e: e.copy(out=Hf[:, :], in_=tB[:, :]), reads=["tB"], writes=["Hf"])
            O("act", lambda e: e.copy(out=Hb[:, :], in_=Hf[:, :]), reads=["Hf"], writes=["Hb"])
            SC = min(512, ntok)
            nchk = SC // 64
            for r0 in range(r_base, r_base + ntok, SC):
                i2 = nxt() % 2

                @_nd(5)
                def ld(e, dma, i2=i2, r0=r0):
                    dma(bt[i2][:, 0:SC], UT[0:128, r0:r0 + SC])
                    dma(ct[i2][:, 0:SC], UT[128:256, r0:r0 + SC])
                    dma(xk[i2][:, 0:nchk, :], Utok[r0:r0 + SC, :].rearrange("(c p) f -> p c f", p=64))
                    dma(fdm[i2][:, 0:nchk, :], FD[r0:r0 + SC, :].rearrange("(c p) f -> p c f", p=64))
                    dma(zs[i2][:, 0:nchk, :], ZS[r0:r0 + SC, :].rearrange("(c p) f -> p c f", p=64))
                O("sp", ld, writes=[f"ssdin{i2}"], dma=True)
                IN = f"ssdin{i2}"
                for c in range(nchk):
                    dt_ = fdm[i2][:, c, 4:8]
                    cs = slice(c * 64, (c + 1) * 64)
                    O("dve", lambda e, dt_=dt_: e.tensor_tensor(out=dA[:, :], in0=dt_, in1=c_r4[0:64, 0:4], op=ALU.mult),
                      reads=[IN, "cr4"], writes=["dA"])
                    for h in range(HG):
                        O("dve", lambda e, h=h: e.tensor_scalar(out=Rr[:, h, :], in0=c_trf[0:64, 0:64], scalar1=dA[:, h:h + 1],
                                                                scalar2=None, op0=ALU.mult),
                          reads=["dA", "const"], writes=["Rr"])
                    yield
                    O("pe", lambda e: e.matmul(psC[:, :], lhsT=c_trf[0:64, 0:64], rhs=dA[:, :], start=True, stop=True),
                      reads=["dA", "const"], writes=["tB"])
                    O("pe", lambda e: e.matmul(psR[:, :, :], lhsT=c_onf[0:64, :], rhs=Rr[:, :, :], start=True, stop=True),
                      reads=["Rr", "const"], writes=["tA"])
                    yield
                    O("act", lambda e: e.copy(out=cum[:, :], in_=psC[:, :]), reads=["tB"], writes=["cum"])
                    O("act", lambda e: e.activation(out=ecum[:, :], in_=psC[:, :], func=AF.Exp), reads=["tB"], writes=["ecum"])
                    for h in range(HG):
                        O("dve", lambda e, h=h: e.tensor_scalar(out=seg[:, h, :], in0=psR[0:64, h, :], scalar1=cum[:, h:h + 1],
                                                                scalar2=0.0, op0=ALU.subtract, op1=ALU.min),
                          reads=["tA", "cum"], writes=["seg"])
                    yield
                    O("act", lambda e: e.activation(out=seg[:, :, :], in_=seg[:, :, :], func=AF.Exp), reads=["seg"], writes=["seg"])
                    O("pe", lambda e, i2=i2, cs=cs: e.matmul(psCB[:, :], lhsT=bt[i2][:, cs], rhs=ct[i2][:, cs], start=True, stop=True),
                      reads=[IN], writes=["tC"])
                    O("dve", lambda e: e.tensor_tensor(out=CBm[:, :], in0=psCB[:, :], in1=c_trf[0:64, 0:64], op=ALU.mult),
                      reads=["tC", "const"], writes=["CBm"])
                    yield
                    for h in range(HG):
                        O("dve", lambda e, h=h, dt_=dt_: e.scalar_tensor_tensor(out=MT[:, h, :], in0=seg[:, h, :], scalar=dt_[:, h:h + 1],
                                                                               in1=CBm[:, :], op0=ALU.mult, op1=ALU.mult),
                          reads=["seg", "CBm", IN], writes=["MT"])

                    yield
                    def ydiag(e, i2=i2, c=c):
                        for h in range(HG):
                            ins = e.matmul(psY[:, h * 64:(h + 1) * 64], lhsT=MT[:, h, :], rhs=xk[i2][:, c, h * 64:(h + 1) * 64],
                                           start=True, stop=True)
                        return ins
                    O("pe", ydiag, reads=["MT", IN], writes=["tB"])
                    O("pe", lambda e, i2=i2, cs=cs: e.matmul(psYo[:, :], lhsT=ct[i2][:, cs], rhs=Hb[:, :], start=True, stop=True),
                      reads=[IN, "Hb"], writes=["tC"])
                    yield
                    O("act", lambda e: e.copy(out=ysb[:, :], in_=psY[:, :]), reads=["tB"], writes=["ysb"])
                    for h in range(HG):
                        hs = slice(h * 64, (h + 1) * 64)
                        O("dve", lambda e, h=h, hs=hs: e.scalar_tensor_tensor(out=ysb[:, hs], in0=psYo[:, hs], scalar=ecum[:, h:h + 1],
                                                                             in1=ysb[:, hs], op0=ALU.mult, op1=ALU.add),
                          reads=["tC", "ecum", "ysb"], writes=["ysb"])
                        O("dve", lambda e, h=h, hs=hs, i2=i2, c=c: e.scalar_tensor_tensor(
                            out=ysb[:, hs], in0=xk[i2][:, c, hs], scalar=c_r4[0:64, 4 + h:5 + h], in1=ysb[:, hs],
                            op0=ALU.mult, op1=ALU.add), reads=[IN, "const", "ysb"], writes=["ysb"])
                    O("dve", lambda e, i2=i2, c=c: e.tensor_tensor(out=u8[i2][:, c, :], in0=ysb[:, :], in1=zs[i2][:, c, :], op=ALU.mult),
                      reads=["ysb", IN], writes=[f"u8{i2}"])
                    yield
                    O("dve", lambda e: e.tensor_tensor(out=wl[:, :], in0=psR[0:64, :, 63], in1=cum[:, :], op=ALU.subtract),
                      reads=["tA", "cum"], writes=["wl"])
                    O("act", lambda e: e.activation(out=wl[:, :], in_=wl[:, :], func=AF.Exp), reads=["wl"], writes=["wl"])
                    O("dve", lambda e, dt_=dt_: e.tensor_tensor(out=wl[:, :], in0=wl[:, :], in1=dt_, op=ALU.mult),
                      reads=["wl", IN], writes=["wl"])
                    for h in range(HG):
                        hs = slice(h * 64, (h + 1) * 64)
                        O("dve", lambda e, h=h, hs=hs, i2=i2, c=c: e.tensor_scalar(out=xw[:, hs], in0=xk[i2][:, c, hs], scalar1=wl[:, h:h + 1],
                                                                                  scalar2=None, op0=ALU.mult),
                          reads=[IN, "wl"], writes=["xw"])
                    yield
                    O("pe", lambda e, i2=i2, c=c: e.matmul(psH[:, :], lhsT=xk[i2][:, c, 256:384], rhs=xw[:, :], start=True, stop=True),
                      reads=[IN, "xw"], writes=["tB"])
                    O("act", lambda e: e.activation(out=dec[:, :], in_=psR[:, :, 63], func=AF.Exp), reads=["tA"], writes=["dec"])
                    yield
                    for h in range(HG):
                        hs = slice(h * 64, (h + 1) * 64)
                        O("dve", lambda e, h=h, hs=hs: e.scalar_tensor_tensor(out=Hf[:, hs], in0=Hf[:, hs], scalar=dec[:, h:h + 1],
                                                                             in1=psH[:, hs], op0=ALU.mult, op1=ALU.add),
                          reads=["Hf", "dec", "tB"], writes=["Hf"])
                    O("act", lambda e: e.copy(out=Hb[:, :], in_=Hf[:, :]), reads=["Hf"], writes=["Hb"])
                yield
                O("sp", _nd(1)(lambda e, dma, i2=i2, r0=r0: dma(
                    E[r0:r0 + SC, 256:512].rearrange("(c p) f -> p c f", p=64), u8[i2][:, 0:nchk, :])),
                  reads=[f"u8{i2}"], writes=[f"Eu_{r0}"], dma=True)
            def trout(e):
                for a in range(2):
                    ins = e.transpose(out=psF[:, a, :], in_=Hf[:, a * 128:(a + 1) * 128], identity=c_idf[:, :])
                return ins
            O("pe", trout, reads=["Hf", "const"], writes=["tB"])
            O("act", lambda e: e.copy(out=hio[:, :, :], in_=psF[:, :, :]), reads=["tB"], writes=["hio"])
            O("sp", _nd(1)(lambda e, dma: dma(out_ap.rearrange("(a p) n -> p a n", p=128), hio[:, :, :])),
              reads=["hio"], writes=[f"ssm_{tag}"], dma=True)


        def att_gen():
            for h in range(HG):
                yield from job(QT[:, h, 0:T], KTn[:, h, 0:T], Vb[0:T, h * 64:(h + 1) * 64], T, T, 0, 0, h)
            for s in range(NSQ):
                ts0 = T + s * TS
                for h in range(HG):
                    yield from job(QT[:, h, ts0:ts0 + TS], KTs[s, :, h, 0:past + TS], Vs[s, 0:past + TS, h * 64:(h + 1) * 64],
                                   TS, past + TS, past, ts0, h)

        def ssd_gen():
            yield from seq_job(0, T, None, ssm_p[:, :], "p")
            for s in range(NSQ):
                yield from seq_job(T + s * TS, TS, c_ssm[s, :, :], ssm_s[s, :, :], f"s{s}")

        n_att = HG * sum((i + 1) * 4 + 4 for i in range(T // 512)) + NSQ * HG * (past // 128 + 3)
        n_ssd = (T // 64 + NSQ) * 11
        ga, gs_ = att_gen(), ssd_gen()
        da = ds = 0
        a_alive = s_alive = True
        while a_alive or s_alive:
            if a_alive:
                try:
                    next(ga)
                    da += 1
                except StopIteration:
                    a_alive = False
            while s_alive and (not a_alive or (os.environ.get('INTERLEAVE', '1') == '1' and ds * n_att <= da * n_ssd)):
                try:
                    next(gs_)
                    ds += 1
                except StopIteration:
                    s_alive = False
        tk.barrier()
        tk.flush(sems, dsems)
        es.close()

    if STAGE >= 3:
        phase23()


    if STAGE >= 5:
        for q in deferred:
            O("pool", lambda e, q=q: e.collective_compute("AllGather", ALU.bypass, replica_groups=RG,
                                                          ins=[E[q * 1024:(q + 1) * 1024, :]], outs=[Gq[q][:, :]]),
              writes=[f"G{q}"])
        O("pool", lambda e: e.collective_compute("AllGather", ALU.bypass, replica_groups=RG,
                                                 ins=[E[T:T + 512, :]], outs=[Gs[:, :]]), writes=["Gs"])

    def phase5():
        es = ExitStack()

        def sb(name, shape, dt=F32):
            return es.enter_context(nc.sbuf_tensor(f'ph4_' + name, list(shape), dt))

        def ps(name, shape, dt=F32):
            return es.enter_context(nc.psum_tensor(f'ph4_' + name, list(shape), dt))

        mixT = sb("mixT", [128, 16, 512], BF16)
        x1 = sb("x1", [128, 4, D])
        AT = sb("AT", [128, 44, 512], BF16)
        NWB = 3
        wbuf = [sb(f"wbuf{i}", [128, 8192], BF16) for i in range(NWB)]
        selT = sb("selT", [128, 8, 256], BF16)
        selS = sb("selS", [128, 4, 128], BF16)
        c_gffn = sb("c_gffn", [128, D])
        c_gssd = sb("c_gssd", [128, 8])
        junk = sb("junk", [128, D])
        hb = [sb(f"hb{i}", [128, D], BF16) for i in range(2)]
        ss = [sb(f"ss{i}", [128, 2]) for i in range(2)]
        sq = sb("sq", [128, 512], BF16)
        rstd = sb("rstd", [128, 512])
        gt = [sb(f"gt{i}", [128, 512]) for i in range(2)]
        pz = [ps(f"pz{i}", [128, 512]) for i in range(6)]
        psT = [ps(f"psT{i}", [128, 4, 128], BF16) for i in range(2)]
        zc = [0]

        def nz():
            zc[0] += 1
            return zc[0] % 6
        wc = [0]

        def wload(view_fn, src_ap, src_reads):
            i = wc[0] % NWB
            wc[0] += 1
            O("sp", _nd(1)(lambda e, dma, i=i: dma(view_fn(wbuf[i]), src_ap)), reads=src_reads, writes=[f"wbuf{i}"], dma=True)
            return i

        @_nd(4)
        def ldc(e, dma):
            dma(selT[:, :, :], sel[:, :, :])
            dma(selS[:, :, :], sel_s[:, :, :])
            dma(c_gffn[:, :], gffn[0:1, :].partition_broadcast(128))
            dma(c_gssd[:, :], gssd[:, :])
        O("sp", ldc, writes=["p5c"], dma=True)

        items = [(Gq[q], 8, selT, 2, f"G{q}") for q in range(NCH)]
        groups = [items[i:i + 2] for i in range(0, NCH, 2)] + [[(Gs, 4, selS, 1, "Gs")]]
        def do_group(grp, row):
            ntile = sum(it[3] for it in grp)
            ntok = ntile * 128
            O("sp", _nd(1)(lambda e, dma, row=row, ntile=ntile: dma(
                x1[:, 0:ntile, :], x2[row:row + ntok, :].rearrange("(j p) c -> p j c", p=128))), writes=["x1"], dma=True)
            toff = 0
            for (G, nblk, st, nt, gname) in grp:
                rows = nblk * 128
                for r in range(4):
                    wi = wload(lambda w, nblk=nblk: w[:, 0:nblk * 512].rearrange("p (b c) -> p b c", c=512),
                               G[r * rows:(r + 1) * rows, :].rearrange("(b p) c -> p b c", p=128), [gname])
                    gv = wbuf[wi][:, 0:nblk * 512].rearrange("p (b c) -> p b c", c=512)
                    for i in range(nt):
                        z = nz()
                        pzv = pz[z][:, :].rearrange("p (a t) -> p a t", t=128)

                        def selmm(e, gv=gv, st=st, nblk=nblk, i=i, pzv=pzv):
                            for cch in range(4):
                                for blk in range(nblk):
                                    ins = e.matmul(pzv[:, cch, :], lhsT=gv[:, blk, cch * 128:(cch + 1) * 128],
                                                   rhs=st[:, blk, i * 128:(i + 1) * 128], start=(blk == 0), stop=(blk == nblk - 1))
                            return ins
                        O("pe", selmm, reads=[f"wbuf{wi}", "p5c"], writes=[f"pz{z}"])
                        c0 = toff + i * 128
                        eng = "act" if (r + i) % 2 else "dve"
                        if eng == "act":
                            O("act", lambda e, r=r, c0=c0, pzv=pzv: e.copy(out=mixT[:, r * 4:r * 4 + 4, c0:c0 + 128], in_=pzv),
                              reads=[f"pz{z}"], writes=["mixT"])
                        else:
                            O("dve", lambda e, r=r, c0=c0, pzv=pzv: e.tensor_copy(out=mixT[:, r * 4:r * 4 + 4, c0:c0 + 128], in_=pzv),
                              reads=[f"pz{z}"], writes=["mixT"])
                toff += nt * 128
            for gi in range(2):
                kcs = [(2 * gi) * 4 + 2, (2 * gi) * 4 + 3, (2 * gi + 1) * 4 + 2, (2 * gi + 1) * 4 + 3]
                z = nz()
                for j, kc in enumerate(kcs):
                    O("act", lambda e, kc=kc: e.activation(out=sq[:, 0:ntok], in_=mixT[:, kc, 0:ntok], func=AF.Square),
                      reads=["mixT"], writes=["sq"])
                    O("pe", lambda e, z=z, j=j: e.matmul(pz[z][:, 0:ntok], lhsT=c_onb[:, :], rhs=sq[:, 0:ntok], start=(j == 0), stop=(j == 3)),
                      reads=["sq", "const"], writes=[f"pz{z}"])
                O("act", lambda e, z=z: e.activation(out=rstd[:, 0:ntok], in_=pz[z][:, 0:ntok], func=AF.Sqrt, bias=EPS, scale=1.0 / 512),
                  reads=[f"pz{z}"], writes=["rstd"])
                O("dve", lambda e: e.reciprocal(out=rstd[:, 0:ntok], in_=rstd[:, 0:ntok]), reads=["rstd"], writes=["rstd"])
                for j, kc in enumerate(kcs):
                    O("dve", lambda e, kc=kc, j=j, gi=gi: e.scalar_tensor_tensor(
                        out=mixT[:, kc, 0:ntok], in0=mixT[:, kc, 0:ntok], scalar=c_gssd[:, gi * 4 + j:gi * 4 + j + 1],
                        in1=rstd[:, 0:ntok], op0=ALU.mult, op1=ALU.mult), reads=["mixT", "rstd", "p5c"], writes=["mixT"])
            for cg in range(4):
                wi = wload(lambda w: w[:, :].rearrange("p (k n) -> p k n", n=512),
                           wo_b[:, cg * 512:(cg + 1) * 512].rearrange("(k p) n -> p k n", p=128), [f"wo_b_{r}" for r in range(0, D, 512)])
                wv = wbuf[wi][:, :].rearrange("p (k n) -> p k n", n=512)
                for i in range(ntile):
                    z = nz()

                    def womm(e, wv=wv, i=i, z=z):
                        for kc in range(16):
                            ins = e.matmul(pz[z][:, :], lhsT=mixT[:, kc, i * 128:(i + 1) * 128], rhs=wv[:, kc, :],
                                           start=(kc == 0), stop=(kc == 15))
                        return ins
                    O("pe", womm, reads=[f"wbuf{wi}", "mixT"], writes=[f"pz{z}"])
                    O("dve", lambda e, i=i, cg=cg, z=z: e.tensor_tensor(out=x1[:, i, cg * 512:(cg + 1) * 512], in0=pz[z][:, :],
                                                                       in1=x1[:, i, cg * 512:(cg + 1) * 512], op=ALU.add),
                      reads=[f"pz{z}", "x1"], writes=["x1"])
            for i in range(ntile):
                i2 = nxt() % 2
                O("act", lambda e, i=i, i2=i2: e.activation(out=junk[:, :], in_=x1[:, i, :], func=AF.Square, accum_out=ss[i2][:, 0:1]),
                  reads=["x1"], writes=["junk", f"ss{i2}"])
                O("act", lambda e, i2=i2: e.activation(out=ss[i2][:, 1:2], in_=ss[i2][:, 0:1], func=AF.Sqrt, bias=EPS, scale=1.0 / D),
                  reads=[f"ss{i2}"], writes=[f"ss{i2}"])
                O("dve", lambda e, i2=i2: e.reciprocal(out=ss[i2][:, 1:2], in_=ss[i2][:, 1:2]), reads=[f"ss{i2}"], writes=[f"ss{i2}"])
                O("dve", lambda e, i=i, i2=i2: e.scalar_tensor_tensor(out=hb[i2][:, :], in0=x1[:, i, :], scalar=ss[i2][:, 1:2],
                                                                     in1=c_gffn[:, :], op0=ALU.mult, op1=ALU.mult),
                  reads=["x1", f"ss{i2}", "p5c"], writes=[f"hb{i2}"])
                for k4 in range(4):
                    p2 = nxt() % 2

                    def tr(e, i2=i2, k4=k4, p2=p2):
                        for j in range(4):
                            kc = k4 * 4 + j
                            ins = e.transpose(out=psT[p2][:, j, :], in_=hb[i2][:, kc * 128:(kc + 1) * 128], identity=c_idb[:, :])
                        return ins
                    O("pe", tr, reads=[f"hb{i2}", "const"], writes=[f"psT{p2}"])
                    if k4 % 2:
                        O("act", lambda e, k4=k4, p2=p2, i=i: e.copy(out=mixT[:, k4 * 4:(k4 + 1) * 4, i * 128:(i + 1) * 128], in_=psT[p2][:, :, :]),
                          reads=[f"psT{p2}"], writes=["mixT"])
                    else:
                        O("dve", lambda e, k4=k4, p2=p2, i=i: e.tensor_copy(out=mixT[:, k4 * 4:(k4 + 1) * 4, i * 128:(i + 1) * 128], in_=psT[p2][:, :, :]),
                          reads=[f"psT{p2}"], writes=["mixT"])
            wg_reads = [f"wg_b_{r}" for r in range(0, D, 512)]
            wu_reads = [f"wu_b_{r}" for r in range(0, D, 512)]
            for fb in range(DFF // 512):
                wgi = wload(lambda w: w[:, :].rearrange("p (k n) -> p k n", n=512),
                            wg_b[:, fb * 512:(fb + 1) * 512].rearrange("(k p) n -> p k n", p=128), wg_reads)
                wui = wload(lambda w: w[:, :].rearrange("p (k n) -> p k n", n=512),
                            wu_b[:, fb * 512:(fb + 1) * 512].rearrange("(k p) n -> p k n", p=128), wu_reads)
                wgv = wbuf[wgi][:, :].rearrange("p (k n) -> p k n", n=512)
                wuv = wbuf[wui][:, :].rearrange("p (k n) -> p k n", n=512)
                for c4 in range(4):
                    zg, zu = nz(), nz()

                    def gmm(e, wv=wgv, c4=c4, z=zg):
                        for kc in range(16):
                            ins = e.matmul(pz[z][:, 0:ntok], lhsT=wv[:, kc, c4 * 128:(c4 + 1) * 128], rhs=mixT[:, kc, 0:ntok],
                                           start=(kc == 0), stop=(kc == 15))
                        return ins

                    def umm(e, wv=wuv, c4=c4, z=zu):
                        for kc in range(16):
                            ins = e.matmul(pz[z][:, 0:ntok], lhsT=wv[:, kc, c4 * 128:(c4 + 1) * 128], rhs=mixT[:, kc, 0:ntok],
                                           start=(kc == 0), stop=(kc == 15))
                        return ins
                    O("pe", gmm, reads=[f"wbuf{wgi}", "mixT"], writes=[f"pz{zg}"])
                    O("pe", umm, reads=[f"wbuf{wui}", "mixT"], writes=[f"pz{zu}"])
                    g2 = nxt() % 2
                    O("act", lambda e, z=zg, g2=g2: e.activation(out=gt[g2][:, 0:ntok], in_=pz[z][:, 0:ntok], func=AF.Silu),
                      reads=[f"pz{zg}"], writes=[f"gt{g2}"])
                    O("dve", lambda e, z=zu, g2=g2, fb=fb, c4=c4: e.tensor_tensor(out=AT[:, fb * 4 + c4, 0:ntok], in0=pz[z][:, 0:ntok],
                                                                                 in1=gt[g2][:, 0:ntok], op=ALU.mult),
                      reads=[f"pz{zu}", f"gt{g2}"], writes=["AT"])
            wd_reads = [f"wd_b_{r}" for r in range(0, DFF, 512)]
            for cg in range(4):
                zs_ = [nz() for _ in range(ntile)]
                for blk in range(4):
                    wi = wload(lambda w: w[:, 0:11 * 512].rearrange("p (f n) -> p f n", n=512),
                               wd_b[blk * 1408:(blk + 1) * 1408, cg * 512:(cg + 1) * 512].rearrange("(f p) n -> p f n", p=128), wd_reads)
                    wv = wbuf[wi][:, 0:11 * 512].rearrange("p (f n) -> p f n", n=512)
                    for i in range(ntile):
                        def dmm(e, wv=wv, i=i, blk=blk, z=zs_[i]):
                            for f in range(11):
                                ins = e.matmul(pz[z][:, :], lhsT=AT[:, blk * 11 + f, i * 128:(i + 1) * 128], rhs=wv[:, f, :],
                                               start=(blk == 0 and f == 0), stop=(blk == 3 and f == 10))
                            return ins
                        O("pe", dmm, reads=[f"wbuf{wi}", "AT"], writes=[f"pz{zs_[i]}"])
                for i in range(ntile):
                    O("dve", lambda e, i=i, cg=cg, z=zs_[i]: e.tensor_tensor(out=x1[:, i, cg * 512:(cg + 1) * 512], in0=pz[z][:, :],
                                                                            in1=x1[:, i, cg * 512:(cg + 1) * 512], op=ALU.add),
                      reads=[f"pz{zs_[i]}", "x1"], writes=["x1"])
            O("sp", _nd(1)(lambda e, dma, row=row, ntile=ntile, ntok=ntok: dma(
                y2[row:row + ntok, :].rearrange("(j p) c -> p j c", p=128), x1[:, 0:ntile, :])), reads=["x1"], writes=["y2"], dma=True)
            return ntok

        row = 0
        for grp in groups:
            row += do_group(grp, row)
        tk.barrier()
        tk.flush(sems, dsems)
        es.close()

    if STAGE >= 8:
        phase5()
    if os.environ.get("DBG_E"):
        E_dbg = dout("E_dbg", [TT, 512], BF16)
        O("sp", _nd(1)(lambda e, dma: dma(E_dbg[:, :], E[:, :])), writes=["E_dbg"], dma=True)
    tk.barrier()
    tk.flush(sems, dsems)
    ges.close()
    sstack.close()
    return nc


def make_in_maps(inp, T, NSQ=8, TS=64, past=PAST):
    bf = ml_dtypes.bfloat16
    f32 = np.float32
    NCH = T // 1024
    ident = np.eye(128, dtype=f32)
    bd = np.zeros((128, 128), f32)
    bd[:64, :64] = 1.0
    bd[64:, 64:] = 1.0
    triu = np.triu(np.ones((128, 128), f32))
    ones = np.ones((128, 128), f32)
    w_in = inp["w_in"][0]
    w_conv = inp["w_conv"][0]
    b_conv = inp["b_conv"][0]
    o_f = 3 * 1024
    o_z = o_f + 16
    o_x = o_z + 1024
    o_B = o_x + 1024
    o_C = o_B + 256
    o_dt = o_C + 256
    perm = []
    for kc in range(16):
        r, cch = kc // 4, kc % 4
        base = 256 * r + 128 * cch if cch < 2 else 1024 + 256 * r + 128 * (cch - 2)
        perm.extend(range(base, base + 128))
    w_out_p = np.ascontiguousarray(inp["w_out"][0][perm])
    maps = []
    for c in range(8):
        b, g = c // 4, c % 4
        j = g
        xs = inp["x_sample"][8 * b:8 * b + 8].reshape(NSQ * TS, D)
        x_all = np.concatenate([inp["x_prompt"][b], xs], axis=0)
        hs = slice(256 * g, 256 * g + 256)
        grp = g // 2
        wsl = np.concatenate([
            w_in[:, 0:1024][:, hs], w_in[:, 1024:2048][:, hs], w_in[:, 2048:3072][:, hs],
            w_in[:, o_z:o_z + 1024][:, hs], w_in[:, o_x:o_x + 1024][:, hs],
            w_in[:, o_B + 128 * grp:o_B + 128 * grp + 128], w_in[:, o_C + 128 * grp:o_C + 128 * grp + 128],
            w_in[:, o_f + 4 * g:o_f + 4 * g + 4], w_in[:, o_dt + 4 * g:o_dt + 4 * g + 4]], axis=1)
        cols = np.zeros((128, 16), f32)
        cols[:, 0] = np.tile(inp["g_q"][0], 2)
        cols[:, 1] = np.tile(inp["g_k"][0], 2)
        cols[0:4, 2] = -1.0
        cols[4:8, 2] = 1.0
        cols[0:4, 3] = inp["f_bias"][0][4 * g:4 * g + 4]
        cols[4:8, 3] = inp["dt_bias"][0][4 * g:4 * g + 4]
        cols[0:4, 4] = -1.0
        cols[4:8, 4] = 1.0
        ccols = np.concatenate([np.arange(256 * g, 256 * g + 256), 1024 + 128 * grp + np.arange(128),
                                1280 + 128 * grp + np.arange(128)])
        wconv = np.zeros((128, 4, 5), f32)
        for ci in range(4):
            cc = ccols[ci * 128:(ci + 1) * 128]
            wconv[:, ci, 0:4] = w_conv[:, cc].T
            wconv[:, ci, 4] = b_conv[cc]
        row4 = np.concatenate([inp["a_log"][0][4 * g:4 * g + 4], inp["d_skip"][0][4 * g:4 * g + 4]])[None, :].astype(f32)
        sq = slice(8 * b, 8 * b + 8)
        c_k = inp["cache_fox_k"][0][sq][:, :, 4 * g:4 * g + 4, :].reshape(NSQ, past, 256)
        c_v = inp["cache_fox_v"][0][sq][:, :, 4 * g:4 * g + 4, :].reshape(NSQ, past, 256)
        c_lf = inp["cache_fox_logf"][0][sq][:, :, 4 * g:4 * g + 4]
        c_convT = np.transpose(inp["cache_conv"][0][sq][:, :, ccols], (2, 0, 1))
        c_ssm = inp["state_ssm"][0][sq][:, 4 * g:4 * g + 4].reshape(NSQ, 256, 128)
        xp = inp["x_prompt"][b].reshape(NCH, 4, 256, D)[:, j].reshape(NCH * 256, D)
        x2 = np.concatenate([xp, inp["x_sample"][8 * b + 2 * j:8 * b + 2 * j + 2].reshape(2 * TS, D)], axis=0)
        sel = np.zeros((128, 8, 256), f32)
        for m in range(256):
            t = 256 * j + m
            sel[t % 128, t // 128, m] = 1.0
        sel_s = np.zeros((128, 4, 128), f32)
        for m in range(128):
            t = 128 * j + m
            sel_s[t % 128, t // 128, m] = 1.0
        gs = inp["g_ssd"][0]
        gssd = np.zeros((128, 8), f32)
        for gi in range(2):
            for jj in range(4):
                base = 256 * (2 * gi + jj // 2) + 128 * (jj % 2)
                gssd[:, gi * 4 + jj] = gs[base:base + 128]
        ca = np.ascontiguousarray
        maps.append({
            "x_all": ca(x_all), "w_in": ca(wsl), "gmix": ca(inp["g_mix"]), "cols": cols, "wconv": wconv, "row4": row4,
            "ident_f": ident, "ident_b": ident.astype(bf), "bd_ones": bd.astype(bf),
            "triu_f": triu, "triu_b": triu.astype(bf), "ones_f": ones, "ones_b": ones.astype(bf),
            "c_k": ca(c_k), "c_v": ca(c_v), "c_lf": ca(c_lf), "c_convT": ca(c_convT), "c_ssm": ca(c_ssm),
            "x2": ca(x2), "sel": sel.astype(bf), "sel_s": sel_s.astype(bf), "gffn": ca(inp["g_ffn"]), "gssd": gssd,
            "w_out": w_out_p, "w_gate": inp["w_gate"][0], "w_up": inp["w_up"][0], "w_down": inp["w_down"][0],
        })
    return maps


def assemble(res, T, B=2, NSQ=8, TS=64):
    NCH = T // 1024
    f32 = np.float32
    DB = 8 * B
    yp = np.zeros((B, T, D), f32)
    ys = np.zeros((DB, TS, D), f32)
    p_conv = np.zeros((1, B, 3, 1536), f32)
    p_ssm = np.zeros((1, B, 16, 64, 128), f32)
    p_k = np.zeros((1, B, T, 16, 64), f32)
    p_v = np.zeros((1, B, T, 16, 64), f32)
    p_f = np.zeros((1, B, T, 16), f32)
    s_conv = np.zeros((1, DB, 3, 1536), f32)
    s_ssm = np.zeros((1, DB, 16, 64, 128), f32)
    s_k = np.zeros((1, DB, TS, 16, 64), f32)
    s_v = np.zeros((1, DB, TS, 16, 64), f32)
    s_f = np.zeros((1, DB, TS, 16), f32)
    for c in range(8):
        b, g = c // 4, c % 4
        j = g
        grp = g // 2
        r = res[c]
        y2 = r["y2"]
        yp[b].reshape(NCH, 4, 256, D)[:, j] = y2[:NCH * 256].reshape(NCH, 256, D)
        ys[8 * b + 2 * j:8 * b + 2 * j + 2] = y2[NCH * 256:].reshape(2, TS, D)
        sq = slice(8 * b, 8 * b + 8)
        p_k[0, b, :, 4 * g:4 * g + 4] = r["k_out"][:T].reshape(T, 4, 64)
        p_v[0, b, :, 4 * g:4 * g + 4] = r["v_out"][:T].reshape(T, 4, 64)
        p_f[0, b, :, 4 * g:4 * g + 4] = r["lf_out"][:T]
        s_k[0, sq, :, 4 * g:4 * g + 4] = r["k_out"][T:].reshape(NSQ, TS, 4, 64)
        s_v[0, sq, :, 4 * g:4 * g + 4] = r["v_out"][T:].reshape(NSQ, TS, 4, 64)
        s_f[0, sq, :, 4 * g:4 * g + 4] = r["lf_out"][T:].reshape(NSQ, TS, 4)
        p_ssm[0, b, 4 * g:4 * g + 4] = r["ssm_p"].reshape(4, 64, 128)
        s_ssm[0, sq, 4 * g:4 * g + 4] = r["ssm_s"].reshape(NSQ, 4, 64, 128)
        cp = r["conv_p"].T
        cs = np.transpose(r["conv_s"], (1, 2, 0))
        p_conv[0, b, :, 256 * g:256 * g + 256] = cp[:, 0:256]
        s_conv[0, sq, :, 256 * g:256 * g + 256] = cs[:, :, 0:256]
        if g % 2 == 0:
            p_conv[0, b, :, 1024 + 128 * grp:1024 + 128 * grp + 128] = cp[:, 256:384]
            p_conv[0, b, :, 1280 + 128 * grp:1280 + 128 * grp + 128] = cp[:, 384:512]
            s_conv[0, sq, :, 1024 + 128 * grp:1024 + 128 * grp + 128] = cs[:, :, 256:384]
            s_conv[0, sq, :, 1280 + 128 * grp:1280 + 128 * grp + 128] = cs[:, :, 384:512]
    return (yp, ys, p_conv, p_ssm, p_k, p_v, p_f, s_conv, s_ssm, s_k, s_v, s_f)


_NC_CACHE = {}


def kernel(**inputs):
    inp = {k: np.asarray(v) for k, v in inputs.items()}
    T = inp["x_prompt"].shape[1]
    if T not in _NC_CACHE:
        _NC_CACHE[T] = build(T)
    nc = _NC_CACHE[T]
    maps = make_in_maps(inp, T)
    res = run_bass_kernel_spmd(nc, maps, core_ids=list(range(8)))
    return assemble(res.results, T)
```

```python
import numpy as np
import ml_dtypes
import concourse.bass as bass
import concourse.mybir as mybir
from concourse.bass_utils import run_bass_kernel_spmd

F32 = mybir.dt.float32
BF16 = mybir.dt.bfloat16
ALU = mybir.AluOpType
AF = mybir.ActivationFunctionType
ET = mybir.EngineType

D = 2048
EPS = 1e-6
HG = 4
NCOL = 1544
PAST = 2048
DFF = 5632


class Tracker:
    CE = ("pe", "act", "dve", "pool", "sp")

    def __init__(self, nc, nslot=8):
        self.nc = nc
        self.nslot = nslot
        self.ops = {e: [] for e in ("pe", "act", "dve", "pool", "sp")}
        self.cnt = {e: 0 for e in self.CE}
        self.ndma = {"sp": 0, "pool": 0}
        self.slotval = {}
        self.res = {}
        self.waited = {e: {} for e in self.ops}

    def _need(self, eng, tok, waits):
        if tok is None:
            return
        kind, key, val = tok
        if kind == "c" and key == eng and eng == "pe":
            return
        k = (kind, key)
        if self.waited[eng].get(k, 0) >= val:
            return
        waits[k] = max(waits.get(k, 0), val)

    def op(self, eng, fn, reads=(), writes=(), dma=False):
        waits = {}
        for r in reads:
            st = self.res.get(r)
            if st:
                self._need(eng, st["w"], waits)
        for w in writes:
            st = self.res.get(w)
            if st:
                self._need(eng, st["w"], waits)
                for t in st["r"]:
                    self._need(eng, t, waits)
        if dma:
            slot = self.ndma[eng] % self.nslot
            self.ndma[eng] += 1
            key = (eng, slot)
            prev = self.slotval.get(key, 0)
            if prev:
                self._need(eng, ("d", key, prev), waits)
            tok = ["d", key, prev]
        else:
            self.cnt[eng] += 1
            tok = ("c", eng, self.cnt[eng])
        for k, v in waits.items():
            self.waited[eng][k] = v
        rec = {"fn": fn, "waits": dict(waits), "tok": tok, "dma": dma, "nd": 0}
        if dma:
            n = getattr(fn, "ndma", 1)
            rec["nd"] = n
            self.slotval[key] = prev + 16 * n
            tok[2] = prev + 16 * n
            tok = tuple(tok)
            rec["tok"] = tok
        self.ops[eng].append(rec)
        for r in reads:
            self.res.setdefault(r, {"w": None, "r": []})["r"].append(tok)
        for w in writes:
            self.res[w] = {"w": tok, "r": []}
        return tok

    def barrier(self):
        allr = list(self.res.keys())
        self.op("sp", lambda e: e.nop(), reads=allr, writes=allr + ["__bar"])
        for en in ("pe", "act", "dve", "pool"):
            self.op(en, lambda e: e.nop(), reads=["__bar"])

    def flush(self, sems, dsems):
        nc = self.nc
        ops, self.ops = self.ops, {e: [] for e in self.ops}

        def run(ename, e):
            for rec in ops[ename]:
                for (kind, key), val in rec["waits"].items():
                    if kind == "c":
                        e.wait_ge(sems[key], val)
                    else:
                        e.wait_ge(dsems[key], val)
                if rec["dma"]:
                    sem = dsems[rec["tok"][1]]
                    cnt = [0]

                    def dma(out, in_, _sem=sem, _cnt=cnt, **kw):
                        _cnt[0] += 1
                        return e.dma_start(out=out, in_=in_, **kw).then_inc(_sem, 16)
                    rec["fn"](e, dma)
                    assert cnt[0] == rec["nd"], (cnt[0], rec["nd"])
                else:
                    ins = rec["fn"](e)
                    ins.then_inc(sems[ename], 1)
        with nc.Block() as block:
            @block.sync
            def _(e):
                run("sp", e)

            @block.tensor
            def _(e):
                run("pe", e)

            @block.scalar
            def _(e):
                run("act", e)

            @block.vector
            def _(e):
                run("dve", e)

            @block.gpsimd
            def _(e):
                run("pool", e)


def _nd(n):
    def deco(f):
        f.ndma = n
        return f
    return deco


import os
from contextlib import ExitStack


def build(T, NSQ=8, TS=64, past=PAST):
    assert T % 1024 == 0 and NSQ * TS == 512 and TS == 64 and past % 512 == 0
    STAGE = int(os.environ.get("STAGE", "99"))
    nc = bass.Bass("TRN2", target_bir_lowering=False)
    TT = T + NSQ * TS
    NG = TT // 512
    NGP = T // 512
    NCH = T // 1024
    TKS = past + 128
    NBS = TKS // 128
    NT2 = NCH * 256 + 128

    def din(name, shape, dt=F32):
        return nc.dram_tensor(name, list(shape), dt, kind="ExternalInput").ap()

    def dout(name, shape, dt=F32):
        return nc.dram_tensor(name, list(shape), dt, kind="ExternalOutput").ap()

    def dscr(name, shape, dt):
        return nc.dram_tensor(name, list(shape), dt, kind="Internal").ap()

    x_all = din("x_all", [TT, D])
    w_in = din("w_in", [D, NCOL])
    gmix = din("gmix", [1, D])
    cols = din("cols", [128, 16])
    wconv = din("wconv", [128, 4, 5])
    row4 = din("row4", [1, 8])
    ident_f = din("ident_f", [128, 128])
    ident_b = din("ident_b", [128, 128], BF16)
    bd_ones = din("bd_ones", [128, 128], BF16)
    triu_f = din("triu_f", [128, 128])
    triu_b = din("triu_b", [128, 128], BF16)
    ones_f = din("ones_f", [128, 128])
    ones_b = din("ones_b", [128, 128], BF16)
    c_k = din("c_k", [NSQ, past, 256])
    c_v = din("c_v", [NSQ, past, 256])
    c_lf = din("c_lf", [NSQ, past, 4])
    c_convT = din("c_convT", [512, NSQ, 3])
    c_ssm = din("c_ssm", [NSQ, 256, 128])
    x2 = din("x2", [NT2, D])
    sel = din("sel", [128, 8, 256], BF16)
    sel_s = din("sel_s", [128, 4, 128], BF16)
    gffn = din("gffn", [1, D])
    gssd = din("gssd", [128, 8])
    w_out = din("w_out", [D, D])
    w_gate = din("w_gate", [D, DFF])
    w_up = din("w_up", [D, DFF])
    w_down = din("w_down", [DFF, D])
    k_out = dout("k_out", [TT, 256])
    v_out = dout("v_out", [TT, 256])
    lf_out = dout("lf_out", [TT, 4])
    conv_p = dout("conv_p", [512, 3])
    conv_s = dout("conv_s", [512, NSQ, 3])
    ssm_p = dout("ssm_p", [256, 128])
    ssm_s = dout("ssm_s", [NSQ, 256, 128])
    y2 = dout("y2", [NT2, D])

    w_in_b = dscr("w_in_b", [D, NCOL], BF16)
    QT = dscr("QT", [70, HG, TT], BF16)
    KTn = dscr("KTn", [70, HG, TT], BF16)
    KTs = dscr("KTs", [NSQ, 70, HG, TKS], BF16)
    Vb = dscr("Vb", [TT, 256], BF16)
    Vs = dscr("Vs", [NSQ, TKS, 256], BF16)
    FD = dscr("FD", [TT, 8], F32)
    ZS = dscr("ZS", [TT, 256], F32)
    UT = dscr("UT", [256, TT], BF16)
    Utok = dscr("Utok", [TT, 384], BF16)
    E = dscr("E", [TT, 512], BF16)
    Gq = [dscr(f"G{q}", [4 * 1024, 512], BF16) for q in range(NCH)]
    Gs = dscr("Gs", [4 * 512, 512], BF16)
    wo_b = dscr("wo_b", [D, D], BF16)
    wg_b = dscr("wg_b", [D, DFF], BF16)
    wu_b = dscr("wu_b", [D, DFF], BF16)
    wd_b = dscr("wd_b", [DFF, D], BF16)

    tk = Tracker(nc)
    O = tk.op
    uid = [0]

    def nxt():
        uid[0] += 1
        return uid[0]

    sstack = ExitStack()
    sems = {e: sstack.enter_context(nc.semaphore(f"s_{e}")) for e in Tracker.CE}
    dsems = {}
    for q in ("sp", "pool"):
        for s in range(tk.nslot):
            dsems[(q, s)] = sstack.enter_context(nc.semaphore(f"d_{q}{s}"))

    ges = ExitStack()

    def gsb(name, shape, dt=F32):
        return ges.enter_context(nc.sbuf_tensor(name, list(shape), dt))

    c_cols = gsb("c_cols", [128, 16])
    c_idf = gsb("c_idf", [128, 128])
    c_idb = gsb("c_idb", [128, 128], BF16)
    c_bd = gsb("c_bd", [128, 128], BF16)
    c_trf = gsb("c_trf", [128, 128])
    c_trb = gsb("c_trb", [128, 128], BF16)
    c_onf = gsb("c_onf", [128, 128])
    c_onb = gsb("c_onb", [128, 128], BF16)
    c_fd = gsb("c_fd", [128, 4])
    c_wc = gsb("c_wc", [128, 4, 5])
    c_r4 = gsb("c_r4", [128, 8])

    @_nd(10)
    def ld_const(e, dma):
        dma(c_cols[:, :], cols[:, :])
        dma(c_idf[:, :], ident_f[:, :])
        dma(c_idb[:, :], ident_b[:, :])
        dma(c_bd[:, :], bd_ones[:, :])
        dma(c_trf[:, :], triu_f[:, :])
        dma(c_trb[:, :], triu_b[:, :])
        dma(c_onf[:, :], ones_f[:, :])
        dma(c_onb[:, :], ones_b[:, :])
        dma(c_wc[:, :, :], wconv[:, :, :])
        dma(c_r4[:, :], row4[0:1, :].partition_broadcast(128))
    O("sp", ld_const, writes=["const"], dma=True)
    O("dve", lambda e: e.tensor_tensor(out=c_fd[0:8, 0:1], in0=c_cols[0:8, 3:4], in1=c_cols[0:8, 2:3], op=ALU.mult),
      reads=["const"], writes=["cfd0"])
    O("dve", lambda e: e.tensor_scalar(out=c_fd[:, 1:2], in0=c_cols[:, 1:2], scalar1=8.0, scalar2=None, op0=ALU.mult),
      reads=["const"], writes=["cfd1"])
    O("act", lambda e: e.activation(out=c_r4[:, 0:4], in_=c_r4[:, 0:4], func=AF.Exp), reads=["const"], writes=["cr4"])
    O("dve", lambda e: e.tensor_scalar(out=c_r4[:, 0:4], in0=c_r4[:, 0:4], scalar1=-1.0, scalar2=None, op0=ALU.mult),
      reads=["cr4"], writes=["cr4"])

    @_nd(1)
    def cast_w(e, dma):
        dma(w_in_b[:, :], w_in[:, :])
    O("pool", cast_w, writes=["w_in_b"], dma=True)

    def cast_big(dst, src, rows, name):
        step = 512
        for r in range(0, rows, step):
            @_nd(1)
            def f(e, dma, r=r):
                dma(dst[r:r + step, :], src[r:r + step, :])
            O("pool", f, writes=[name + f"_{r}"], dma=True)
    if STAGE >= 8:
        cast_big(wo_b, w_out, D, "wo_b")
        cast_big(wg_b, w_gate, D, "wg_b")
        cast_big(wu_b, w_up, D, "wu_b")
        cast_big(wd_b, w_down, DFF, "wd_b")

    def phase1():
        es = ExitStack()

        def sb(name, shape, dt=F32):
            return es.enter_context(nc.sbuf_tensor(f'ph0_' + name, list(shape), dt))

        def ps(name, shape, dt=F32):
            return es.enter_context(nc.psum_tensor(f'ph0_' + name, list(shape), dt))

        wb = sb("wb", [128, 16, NCOL], BF16)
        c_gmix = sb("c_gmix", [128, D])

        @_nd(2)
        def ld_w(e, dma):
            dma(wb[:, :, :], w_in_b.rearrange("(kc p) n -> p kc n", p=128))
            dma(c_gmix[:, :], gmix[0:1, :].partition_broadcast(128))
        O("sp", ld_w, reads=["w_in_b"], writes=["wb"], dma=True)

        hT = [sb(f"hT{i}", [128, 16, 512], BF16) for i in range(2)]
        mmc = [0]
        rc_ = {}
        xt = [sb(f"xt{i}", [128, D]) for i in range(4)]
        hb = [sb(f"hb{i}", [128, D], BF16) for i in range(4)]
        ss = [sb(f"ss{i}", [128, 2]) for i in range(4)]
        psT = [ps(f"psT{i}", [128, 4, 128], BF16) for i in range(2)]
        psA = [ps(f"psA{i}", [128, 512]) for i in range(3)]
        psB = ps("psB", [128, 512])
        psF = [ps(f"psF{i}", [128, 4, 128]) for i in range(2)]
        sq = [sb(f"sq{i}", [128, 512], BF16) for i in range(2)]
        rs = [sb(f"rs{i}", [128, 512]) for i in range(2)]
        qn = [sb(f"qn{i}", [128, 512], BF16) for i in range(2)]
        kf = [sb(f"kf{i}", [128, 512]) for i in range(3)]
        kraw = [sb(f"kraw{i}", [128, 512]) for i in range(2)]
        tokf = [sb(f"tokf{i}", [128, 4, 128]) for i in range(2)]
        tokb = [sb(f"tokb{i}", [128, 4, 128], BF16) for i in range(2)]
        fd = sb("fd", [8, 512])
        fdt = sb("fdt", [128, 4, 8])
        xpad = sb("xpad", [128, 4, 536])
        acc = [sb(f"acc{i}", [128, 512]) for i in range(3)]
        ub = [sb(f"ub{i}", [128, 512], BF16) for i in range(3)]
        O("dve", lambda e: e.memset(xpad[:, :, :], 0.0), writes=[f"xpad{ci}" for ci in range(4)])

        def part_a_norm(tg):
            t0 = tg * 512
            for ti in range(4):
                r0 = t0 + ti * 128

                @_nd(1)
                def ldx(e, dma, ti=ti, r0=r0):
                    dma(xt[ti][:, :], x_all[r0:r0 + 128, :])
                O("sp", ldx, writes=[f"xt{ti}"], dma=True)
                O("act", lambda e, ti=ti: e.activation(out=hb[ti][:, :], in_=xt[ti][:, :], func=AF.Square,
                                                      accum_out=ss[ti][:, 0:1]),
                  reads=[f"xt{ti}"], writes=[f"hb{ti}", f"ss{ti}a"])
                O("act", lambda e, ti=ti: e.activation(out=ss[ti][:, 1:2], in_=ss[ti][:, 0:1], func=AF.Sqrt,
                                                      bias=EPS, scale=1.0 / D),
                  reads=[f"ss{ti}a"], writes=[f"ss{ti}b"])
                O("dve", lambda e, ti=ti: e.reciprocal(out=ss[ti][:, 1:2], in_=ss[ti][:, 1:2]),
                  reads=[f"ss{ti}b"], writes=[f"ss{ti}b"])
                O("dve", lambda e, ti=ti: e.scalar_tensor_tensor(out=hb[ti][:, :], in0=xt[ti][:, :], scalar=ss[ti][:, 1:2],
                                                                 in1=c_gmix[:, :], op0=ALU.mult, op1=ALU.mult),
                  reads=[f"xt{ti}", f"ss{ti}b", "wb"], writes=[f"hb{ti}"])

        def part_a_tr(tg):
            hTb = hT[tg % 2]
            for ti in range(4):
                for k4 in range(4):
                    p2 = nxt() % 2

                    def tr(e, ti=ti, k4=k4, p2=p2):
                        for j in range(4):
                            kc = k4 * 4 + j
                            ins = e.transpose(out=psT[p2][:, j, :], in_=hb[ti][:, kc * 128:(kc + 1) * 128],
                                              identity=c_idb[:, :])
                        return ins
                    O("pe", tr, reads=[f"hb{ti}", "const"], writes=[f"psT{p2}"])
                    if k4 % 2:
                        O("act", lambda e, k4=k4, p2=p2, ti=ti, hTb=hTb: e.copy(
                            out=hTb[:, k4 * 4:(k4 + 1) * 4, ti * 128:(ti + 1) * 128], in_=psT[p2][:, :, :]),
                          reads=[f"psT{p2}"], writes=[f"hT{tg % 2}_{ti}_{k4}"])
                    else:
                        O("dve", lambda e, k4=k4, p2=p2, ti=ti, hTb=hTb: e.tensor_copy(
                            out=hTb[:, k4 * 4:(k4 + 1) * 4, ti * 128:(ti + 1) * 128], in_=psT[p2][:, :, :]),
                          reads=[f"psT{p2}"], writes=[f"hT{tg % 2}_{ti}_{k4}"])

        part_a_norm(0)
        part_a_tr(0)
        for tg in range(NG):
            t0 = tg * 512
            is_s = tg >= NGP
            hTb = hT[tg % 2]
            hT_res = [f"hT{tg % 2}_{ti}_{k4}" for ti in range(4) for k4 in range(4)]

            def mm_chunk(c0, m, pst, hTb=hTb):
                def f(e):
                    for kc in range(16):
                        ins = e.matmul(pst[0:m, :], lhsT=wb[:, kc, c0:c0 + m], rhs=hTb[:, kc, :],
                                       start=(kc == 0), stop=(kc == 15))
                    return ins
                return f

            def tr4_f32(src, f2):
                def f(e):
                    for j in range(4):
                        ins = e.transpose(out=psF[f2][:, j, :], in_=src[:, j * 128:(j + 1) * 128], identity=c_idf[:, :])
                    return ins
                return f

            chunks = []

            def rot(name):
                rc_[name] = rc_.get(name, 0) + 1
                return rc_[name] % 2

            def rot3(name):
                rc_[name] = rc_.get(name, 0) + 1
                return rc_[name] % 3

            def qk1(a3, cc, tg=tg, t0=t0):
                isq = cc < 2
                hp = cc % 2
                a2 = rot("sq")
                O("act", lambda e, a2=a2, a3=a3: e.activation(out=sq[a2][:, :], in_=psA[a3][:, :], func=AF.Square),
                  reads=[f"psA{a3}"], writes=[f"sq{a2}"])
                r2 = rot("kraw")
                O("act", lambda e, r2=r2, a3=a3: e.copy(out=kraw[r2][:, :], in_=psA[a3][:, :]),
                  reads=[f"psA{a3}"], writes=[f"kraw{r2}"])
                O("pe", lambda e, a2=a2: e.matmul(psB[:, :], lhsT=c_bd[:, :], rhs=sq[a2][:, :], start=True, stop=True),
                  reads=[f"sq{a2}", "const"], writes=["psB"])
                O("act", lambda e, a2=a2: e.activation(out=rs[a2][:, :], in_=psB[:, :], func=AF.Sqrt,
                                                      bias=64.0 * EPS, scale=1.0),
                  reads=["psB"], writes=[f"rs{a2}"])
                O("dve", lambda e, a2=a2: e.reciprocal(out=rs[a2][:, :], in_=rs[a2][:, :]),
                  reads=[f"rs{a2}"], writes=[f"rs{a2}"])
                if isq:
                    q2 = rot("qn")
                    O("dve", lambda e, a2=a2, r2=r2, q2=q2: e.scalar_tensor_tensor(out=qn[q2][:, :], in0=kraw[r2][:, :], scalar=c_cols[:, 0:1],
                                                                                 in1=rs[a2][:, :], op0=ALU.mult, op1=ALU.mult),
                      reads=[f"kraw{r2}", f"rs{a2}", "const"], writes=[f"qn{q2}"])

                    @_nd(2)
                    def stq(e, dma, q2=q2, hp=hp, t0=t0):
                        for hh in range(2):
                            dma(QT[0:64, hp * 2 + hh, t0:t0 + 512], qn[q2][hh * 64:(hh + 1) * 64, :])
                    O("sp", stq, reads=[f"qn{q2}"], writes=[f"QT{tg}"], dma=True)
                    return None
                k2 = rot3("kf")
                O("dve", lambda e, a2=a2, r2=r2, k2=k2: e.scalar_tensor_tensor(out=kf[k2][:, :], in0=kraw[r2][:, :], scalar=c_fd[:, 1:2],
                                                                             in1=rs[a2][:, :], op0=ALU.mult, op1=ALU.mult),
                  reads=[f"kraw{r2}", f"rs{a2}", "cfd1"], writes=[f"kf{k2}"])
                return (k2, hp)

            def qk2(st, tg=tg, t0=t0):
                if st is None:
                    return
                k2, hp = st
                q2 = rot("qn")
                O("act", lambda e, k2=k2, q2=q2: e.copy(out=qn[q2][:, :], in_=kf[k2][:, :]),
                  reads=[f"kf{k2}"], writes=[f"qn{q2}"])

                @_nd(2)
                def stk(e, dma, q2=q2, hp=hp, t0=t0):
                    for hh in range(2):
                        dma(KTn[0:64, hp * 2 + hh, t0:t0 + 512], qn[q2][hh * 64:(hh + 1) * 64, :])
                O("sp", stk, reads=[f"qn{q2}"], writes=[f"KTn{tg}"], dma=True)
                f2 = rot("psF")
                O("pe", tr4_f32(kf[k2], f2), reads=[f"kf{k2}", "const"], writes=[f"psF{f2}"])
                O("act", lambda e, f2=f2: e.copy(out=tokf[f2][:, :, :], in_=psF[f2][:, :, :]),
                  reads=[f"psF{f2}"], writes=[f"tokf{f2}"])

                @_nd(1)
                def stko(e, dma, f2=f2, hp=hp, t0=t0):
                    dma(k_out[t0:t0 + 512, hp * 128:(hp + 1) * 128].rearrange("(j p) c -> p j c", p=128), tokf[f2][:, :, :])
                O("sp", stko, reads=[f"tokf{f2}"], writes=["k_out"], dma=True)

            def v1(a3, hp):
                k2 = rot3("kf")
                O("act", lambda e, k2=k2, a3=a3: e.copy(out=kf[k2][:, :], in_=psA[a3][:, :]),
                  reads=[f"psA{a3}"], writes=[f"kf{k2}"])
                return (k2, hp)

            def v2(st, tg=tg, t0=t0):
                k2, hp = st
                f2 = rot("psF")
                O("pe", tr4_f32(kf[k2], f2), reads=[f"kf{k2}", "const"], writes=[f"psF{f2}"])
                O("act", lambda e, f2=f2: e.copy(out=tokf[f2][:, :, :], in_=psF[f2][:, :, :]),
                  reads=[f"psF{f2}"], writes=[f"tokf{f2}"])

                @_nd(1)
                def stvo(e, dma, f2=f2, hp=hp, t0=t0):
                    dma(v_out[t0:t0 + 512, hp * 128:(hp + 1) * 128].rearrange("(j p) c -> p j c", p=128), tokf[f2][:, :, :])
                O("sp", stvo, reads=[f"tokf{f2}"], writes=[f"v_out{tg}_{hp}"], dma=True)

                @_nd(1)
                def stvb(e, dma, hp=hp, t0=t0):
                    dma(Vb[t0:t0 + 512, hp * 128:(hp + 1) * 128], v_out[t0:t0 + 512, hp * 128:(hp + 1) * 128])
                O("pool", stvb, reads=[f"v_out{tg}_{hp}"], writes=[f"Vb{tg}"], dma=True)

            def z1(a3, hp):
                k2 = rot3("kf")
                O("act", lambda e, k2=k2, a3=a3: e.activation(out=kf[k2][:, :], in_=psA[a3][:, :], func=AF.Silu),
                  reads=[f"psA{a3}"], writes=[f"kf{k2}"])
                return (k2, hp)

            def z2(st, tg=tg, t0=t0):
                k2, hp = st
                f2 = rot("psF")
                O("pe", tr4_f32(kf[k2], f2), reads=[f"kf{k2}", "const"], writes=[f"psF{f2}"])
                O("act", lambda e, f2=f2: e.copy(out=tokf[f2][:, :, :], in_=psF[f2][:, :, :]),
                  reads=[f"psF{f2}"], writes=[f"tokf{f2}"])

                @_nd(1)
                def stz(e, dma, f2=f2, hp=hp, t0=t0):
                    dma(ZS[t0:t0 + 512, hp * 128:(hp + 1) * 128].rearrange("(j p) c -> p j c", p=128), tokf[f2][:, :, :])
                O("sp", stz, reads=[f"tokf{f2}"], writes=[f"ZS{tg}"], dma=True)

            nseg, L = (NSQ, TS) if is_s else (1, 512)
            W = L + 3

            def pre_conv(ci, nseg=nseg, W=W, is_s=is_s):
                if is_s:
                    xp3 = xpad[:, ci, 0:nseg * W].rearrange("p (s l) -> p s l", l=W)

                    @_nd(1)
                    def ldhalo(e, dma, ci=ci, xp3=xp3):
                        dma(xp3[:, :, 0:3], c_convT[ci * 128:(ci + 1) * 128, :, :])
                    O("sp", ldhalo, writes=[f"xpad{ci}"], dma=True)

            def conv1(a3, ci, tg=tg, t0=t0, nseg=nseg, L=L, W=W, is_s=is_s):
                xp3 = xpad[:, ci, 0:nseg * W].rearrange("p (s l) -> p s l", l=W)
                O("act", lambda e, a3=a3, xp3=xp3, L=L: e.copy(out=xp3[:, :, 3:3 + L],
                                                              in_=psA[a3][:, :].rearrange("p (s l) -> p s l", l=L)),
                  reads=[f"psA{a3}"], writes=[f"xpad{ci}"])
                c2 = rot3("acc")
                acc3 = acc[c2][:, :].rearrange("p (s l) -> p s l", l=L)
                O("dve", lambda e, acc3=acc3, xp3=xp3, ci=ci, L=L: e.tensor_scalar(
                    out=acc3, in0=xp3[:, :, 3:3 + L], scalar1=c_wc[:, ci, 3:4], scalar2=c_wc[:, ci, 4:5],
                    op0=ALU.mult, op1=ALU.add), reads=[f"xpad{ci}", "const"], writes=[f"acc{c2}"])
                for j in range(1, 4):
                    O("dve", lambda e, acc3=acc3, xp3=xp3, ci=ci, L=L, j=j: e.scalar_tensor_tensor(
                        out=acc3, in0=xp3[:, :, 3 - j:3 - j + L], scalar=c_wc[:, ci, 3 - j:4 - j], in1=acc3,
                        op0=ALU.mult, op1=ALU.add), reads=[f"xpad{ci}", "const", f"acc{c2}"], writes=[f"acc{c2}"])
                O("act", lambda e, c2=c2: e.activation(out=ub[c2][:, :], in_=acc[c2][:, :], func=AF.Silu),
                  reads=[f"acc{c2}"], writes=[f"ub{c2}"])
                if ci >= 2:
                    @_nd(1)
                    def stut(e, dma, c2=c2, ci=ci, t0=t0):
                        dma(UT[(ci - 2) * 128:(ci - 1) * 128, t0:t0 + 512], ub[c2][:, :])
                    O("sp", stut, reads=[f"ub{c2}"], writes=[f"UT{tg}"], dma=True)
                if is_s:
                    @_nd(1)
                    def stcs(e, dma, ci=ci, xp3=xp3):
                        dma(conv_s[ci * 128:(ci + 1) * 128, :, :], xp3[:, :, TS:TS + 3])
                    O("sp", stcs, reads=[f"xpad{ci}"], writes=["conv_s"], dma=True)
                else:
                    if tg == NGP - 1:
                        @_nd(1)
                        def stcp(e, dma, ci=ci):
                            dma(conv_p[ci * 128:(ci + 1) * 128, :], xpad[:, ci, 512:515])
                        O("sp", stcp, reads=[f"xpad{ci}"], writes=["conv_p"], dma=True)
                    O("dve", lambda e, ci=ci: e.tensor_copy(out=xpad[:, ci, 0:3], in_=xpad[:, ci, 512:515]),
                      reads=[f"xpad{ci}"], writes=[f"xpad{ci}"])
                return (c2, ci)

            def conv2(st, tg=tg, t0=t0):
                c2, ci = st
                if ci < 3:
                    p2 = rot("psT")

                    def tru(e, c2=c2, p2=p2):
                        for j in range(4):
                            ins = e.transpose(out=psT[p2][:, j, :], in_=ub[c2][:, j * 128:(j + 1) * 128], identity=c_idb[:, :])
                        return ins
                    O("pe", tru, reads=[f"ub{c2}", "const"], writes=[f"psT{p2}"])
                    b2 = rot("tokb")
                    O("act", lambda e, p2=p2, b2=b2: e.copy(out=tokb[b2][:, :, :], in_=psT[p2][:, :, :]),
                      reads=[f"psT{p2}"], writes=[f"tokb{b2}"])

                    @_nd(1)
                    def stutok(e, dma, b2=b2, ci=ci, t0=t0):
                        dma(Utok[t0:t0 + 512, ci * 128:(ci + 1) * 128].rearrange("(j p) c -> p j c", p=128), tokb[b2][:, :, :])
                    O("sp", stutok, reads=[f"tokb{b2}"], writes=[f"Utok{tg}"], dma=True)

            def fd1(a3):
                O("act", lambda e, a3=a3: e.activation(out=fd[:, :], in_=psA[a3][0:8, :], func=AF.Exp,
                                                      bias=c_fd[0:8, 0:1], scale=c_cols[0:8, 2:3]),
                  reads=[f"psA{a3}", "cfd0", "const"], writes=["fd"])
                O("act", lambda e: e.activation(out=fd[:, :], in_=fd[:, :], func=AF.Ln, bias=1.0, scale=1.0),
                  reads=["fd"], writes=["fd"])
                O("dve", lambda e: e.tensor_scalar(out=fd[:, :], in0=fd[:, :], scalar1=c_cols[0:8, 4:5], scalar2=None, op0=ALU.mult),
                  reads=["fd", "const"], writes=["fd"])
                return 0

            def fd2(st, tg=tg, t0=t0):
                f2 = rot("psF")

                def trf(e, f2=f2):
                    for j in range(4):
                        ins = e.transpose(out=psF[f2][:, j, 0:8], in_=fd[0:8, j * 128:(j + 1) * 128], identity=c_idf[0:8, 0:8])
                    return ins
                O("pe", trf, reads=["fd", "const"], writes=[f"psF{f2}"])
                O("act", lambda e, f2=f2: e.copy(out=fdt[:, :, :], in_=psF[f2][:, :, 0:8]),
                  reads=[f"psF{f2}"], writes=["fdt"])

                @_nd(2)
                def stfd(e, dma, t0=t0):
                    dma(FD[t0:t0 + 512, :].rearrange("(j p) c -> p j c", p=128), fdt[:, :, :])
                    dma(lf_out[t0:t0 + 512, :].rearrange("(j p) c -> p j c", p=128), fdt[:, :, 0:4])
                O("sp", stfd, reads=["fdt"], writes=[f"FD{tg}", "lf_out"], dma=True)

            chunks.append((1536, 8, None, (lambda a3: fd1(a3)), fd2))
            for cc in range(4):
                chunks.append((cc * 128, 128, None, (lambda a3, cc=cc: qk1(a3, cc)), qk2))
            for hp in range(2):
                chunks.append((512 + hp * 128, 128, None, (lambda a3, hp=hp: v1(a3, hp)), v2))
            for hp in range(2):
                chunks.append((768 + hp * 128, 128, None, (lambda a3, hp=hp: z1(a3, hp)), z2))
            for ci in range(4):
                chunks.append((1024 + ci * 128, 128, (lambda ci=ci: pre_conv(ci)), (lambda a3, ci=ci: conv1(a3, ci)), conv2))

            def emit_mm(ci_):
                c0, m, pre, p1_, p2_ = chunks[ci_]
                a3 = mmc[0] % 3
                mmc[0] += 1
                if pre is not None:
                    pre()
                O("pe", mm_chunk(c0, m, psA[a3]), reads=hT_res + ["wb"], writes=[f"psA{a3}"])
                return a3
            ncn = len(chunks)
            a3s = {0: emit_mm(0), 1: emit_mm(1)}
            sts = {}
            for ci_ in range(ncn + 2):
                if ci_ + 2 < ncn:
                    a3s[ci_ + 2] = emit_mm(ci_ + 2)
                if ci_ == 1 and tg + 1 < NG:
                    part_a_norm(tg + 1)
                if ci_ == 8 and tg + 1 < NG:
                    part_a_tr(tg + 1)
                if ci_ < ncn:
                    sts[ci_] = chunks[ci_][3](a3s[ci_])
                if ci_ >= 2:
                    chunks[ci_ - 2][4](sts[ci_ - 2])
        tk.barrier()
        tk.flush(sems, dsems)
        es.close()

    phase1()

    def phase1b():
        es = ExitStack()

        def sb(name, shape, dt=F32):
            return es.enter_context(nc.sbuf_tensor(f'ph1_' + name, list(shape), dt))

        def ps(name, shape, dt=F32):
            return es.enter_context(nc.psum_tensor(f'ph1_' + name, list(shape), dt))

        NBmax = max(T // 128, NBS)
        LF = sb("LF", [128, NBmax, 4])
        A = [sb(f"scanA{i}", [128, NBmax, 4]) for i in range(2)]
        Ff = sb("Ff", [128, NBmax, 4])
        R1 = sb("R1", [128, NBmax, 4])
        HF = sb("HF", [128, NBmax, 4])
        AQ = sb("AQ", [128, NBmax, 24], BF16)
        AK = sb("AK", [128, NBmax, 24], BF16)
        stg = [sb(f"stg{i}", [24, 4, 128], BF16) for i in range(2)]
        psW = ps("psW", [128, 512])
        psTot = ps("psTot", [128, 512])
        psT = [ps(f"psT{i}", [128, 4, 128], BF16) for i in range(2)]
        ck = [sb(f"ck{i}", [128, 4, 256]) for i in range(2)]
        ckb = [sb(f"ckb{i}", [128, 4, 256], BF16) for i in range(2)]
        kst = [sb(f"kst{i}", [128, 4, 128], BF16) for i in range(2)]
        AQv = AQ[:, :, :].rearrange("p b (i h) -> p b i h", h=4)
        AKv = AK[:, :, :].rearrange("p b (i h) -> p b i h", h=4)
        O("dve", lambda e: e.memset(AQ[:, :, :], -1.0), writes=["AQ"])
        O("dve", lambda e: e.memset(AK[:, :, :], 1.0), writes=["AK"])

        def cumsum_and_aug(nb, lf_reads):
            n4 = nb * 4
            O("pe", lambda e: e.matmul(psW[:, 0:n4], lhsT=c_trf[:, :], rhs=LF[:, 0:nb, :], start=True, stop=True),
              reads=lf_reads + ["const"], writes=["psW"])
            O("pe", lambda e: e.matmul(psTot[:, 0:n4], lhsT=c_onf[:, :], rhs=LF[:, 0:nb, :], start=True, stop=True),
              reads=lf_reads + ["const"], writes=["psTot"])
            O("act", lambda e: e.copy(out=A[0][:, 0:nb, :], in_=psTot[:, 0:n4].rearrange("p (b h) -> p b h", h=4)),
              reads=["psTot"], writes=["scanA0"])
            cur = 0
            d = 1
            while d < nb:
                o = 1 - cur
                O("dve", lambda e, cur=cur, o=o, d=d: e.tensor_tensor(out=A[o][:, d:nb, :], in0=A[cur][:, d:nb, :],
                                                                    in1=A[cur][:, 0:nb - d, :], op=ALU.add),
                  reads=[f"scanA{cur}"], writes=[f"scanA{o}"])
                O("dve", lambda e, cur=cur, o=o, d=d: e.tensor_copy(out=A[o][:, 0:d, :], in_=A[cur][:, 0:d, :]),
                  reads=[f"scanA{cur}", f"scanA{o}"], writes=[f"scanA{o}"])
                cur = o
                d *= 2
            O("dve", lambda e, cur=cur: e.tensor_tensor(out=R1[:, 0:nb, :], in0=A[cur][:, 0:nb, :],
                                                       in1=psTot[:, 0:n4].rearrange("p (b h) -> p b h", h=4), op=ALU.subtract),
              reads=[f"scanA{cur}", "psTot"], writes=["R1"])
            O("dve", lambda e: e.tensor_tensor(out=Ff[:, 0:nb, :], in0=R1[:, 0:nb, :],
                                              in1=psW[:, 0:n4].rearrange("p (b h) -> p b h", h=4), op=ALU.add),
              reads=["R1", "psW"], writes=["Ff"])
            O("act", lambda e: e.copy(out=AQv[:, 0:nb, 0, :], in_=Ff[:, 0:nb, :]), reads=["Ff", "AQ"], writes=["AQ"])
            O("act", lambda e: e.copy(out=HF[:, 0:nb, :], in_=AQv[:, 0:nb, 0, :]), reads=["AQ"], writes=["HF"])
            O("dve", lambda e: e.tensor_tensor(out=R1[:, 0:nb, :], in0=Ff[:, 0:nb, :], in1=HF[:, 0:nb, :], op=ALU.subtract),
              reads=["Ff", "HF"], writes=["R1"])
            O("act", lambda e: e.copy(out=AQv[:, 0:nb, 1, :], in_=R1[:, 0:nb, :]), reads=["R1", "AQ"], writes=["AQ"])
            O("act", lambda e: e.copy(out=HF[:, 0:nb, :], in_=AQv[:, 0:nb, 1, :]), reads=["AQ"], writes=["HF"])
            O("dve", lambda e: e.tensor_tensor(out=R1[:, 0:nb, :], in0=R1[:, 0:nb, :], in1=HF[:, 0:nb, :], op=ALU.subtract),
              reads=["R1", "HF"], writes=["R1"])
            O("act", lambda e: e.copy(out=AQv[:, 0:nb, 2, :], in_=R1[:, 0:nb, :]), reads=["R1", "AQ"], writes=["AQ"])
            O("act", lambda e: e.copy(out=AKv[:, 0:nb, 3:6, :], in_=AQv[:, 0:nb, 0:3, :]), reads=["AQ", "AK"], writes=["AK"])

        def aug_store(src, srcname, b0, nbb, dst_fn, npart=128, ncol=128):
            p2 = nxt() % 2

            def trp(e, p2=p2):
                for j in range(nbb):
                    ins = e.transpose(out=psT[p2][0:24, j, 0:npart], in_=src[0:npart, b0 + j, :], identity=c_idb[0:npart, 0:npart])
                return ins
            O("pe", trp, reads=[srcname, "const"], writes=[f"psT{p2}"])
            O("act", lambda e, p2=p2: e.copy(out=stg[p2][:, 0:nbb, 0:ncol], in_=psT[p2][0:24, 0:nbb, 0:ncol]),
              reads=[f"psT{p2}"], writes=[f"stg{p2}"])
            O("sp", _nd(1)(lambda e, dma, p2=p2: dma(dst_fn(), stg[p2][:, 0:nbb, 0:ncol])),
              reads=[f"stg{p2}"], writes=[f"aug_{nxt()}"], dma=True)

        NB = T // 128
        for b0 in range(0, NB, 8):
            O("sp", _nd(1)(lambda e, dma, b0=b0: dma(
                LF[:, b0:b0 + 8, :], FD[b0 * 128:(b0 + 8) * 128, 0:4].rearrange("(b p) c -> p b c", p=128))),
              reads=[f"FD{tg}" for tg in range(NGP)] + (["LF"] if b0 else []), writes=["LF"], dma=True)
        cumsum_and_aug(NB, ["LF"])
        for b0 in range(0, NB, 4):
            aug_store(AQ, "AQ", b0, 4, lambda b0=b0: QT[64:70, :, b0 * 128:(b0 + 4) * 128].rearrange("i h (j p) -> (i h) j p", p=128))
            aug_store(AK, "AK", b0, 4, lambda b0=b0: KTn[64:70, :, b0 * 128:(b0 + 4) * 128].rearrange("i h (j p) -> (i h) j p", p=128))
        NBP = past // 128
        for s in range(NSQ):
            ts0 = T + s * TS
            O("dve", lambda e: e.memset(LF[:, 0:NBS, :], 0.0), reads=["LF"], writes=["LF"])

            @_nd(3)
            def ldlf(e, dma, s=s, ts0=ts0):
                dma(LF[:, 0:NBP // 2, :], c_lf[s, 0:past // 2, :].rearrange("(b p) c -> p b c", p=128))
                dma(LF[:, NBP // 2:NBP, :], c_lf[s, past // 2:past, :].rearrange("(b p) c -> p b c", p=128))
                dma(LF[0:TS, NBP, :], FD[ts0:ts0 + TS, 0:4])
            O("sp", ldlf, reads=[f"FD{NG - 1}"], writes=["LF"], dma=True)
            cumsum_and_aug(NBS, ["LF"])
            for b0 in range(0, NBS, 4):
                nbb = min(4, NBS - b0)
                aug_store(AK, "AK", b0, nbb, lambda b0=b0, nbb=nbb, s=s:
                          KTs[s, 64:70, :, b0 * 128:(b0 + nbb) * 128].rearrange("i h (j p) -> (i h) j p", p=128))
            aug_store(AQ, "AQ", NBP, 1, lambda ts0=ts0: QT[64:70, :, ts0:ts0 + TS].rearrange("i h (j p) -> (i h) j p", p=TS),
                      npart=TS, ncol=TS)
            for b0 in range(0, NBP, 4):
                i2 = nxt() % 2
                O("sp", _nd(1)(lambda e, dma, i2=i2, s=s, b0=b0: dma(
                    ck[i2][:, :, :], c_k[s, b0 * 128:(b0 + 4) * 128, :].rearrange("(j p) c -> p j c", p=128))),
                  writes=[f"ck{i2}"], dma=True)
                O("act", lambda e, i2=i2: e.copy(out=ckb[i2][:, :, :], in_=ck[i2][:, :, :]), reads=[f"ck{i2}"], writes=[f"ckb{i2}"])
                for hp in range(2):
                    p2 = nxt() % 2

                    def trk(e, i2=i2, hp=hp, p2=p2):
                        for j in range(4):
                            ins = e.transpose(out=psT[p2][:, j, :], in_=ckb[i2][:, j, hp * 128:(hp + 1) * 128], identity=c_idb[:, :])
                        return ins
                    O("pe", trk, reads=[f"ckb{i2}", "const"], writes=[f"psT{p2}"])
                    k2 = nxt() % 2
                    O("dve", lambda e, p2=p2, k2=k2: e.tensor_copy(out=kst[k2][:, :, :], in_=psT[p2][:, :, :]),
                      reads=[f"psT{p2}"], writes=[f"kst{k2}"])

                    @_nd(2)
                    def stks(e, dma, k2=k2, hp=hp, s=s, b0=b0):
                        for hh in range(2):
                            dma(KTs[s, 0:64, hp * 2 + hh, b0 * 128:(b0 + 4) * 128].rearrange("d (j p) -> d j p", p=128),
                                kst[k2][hh * 64:(hh + 1) * 64, :, :])
                    O("sp", stks, reads=[f"kst{k2}"], writes=[f"KTs{s}"], dma=True)
            O("sp", _nd(1)(lambda e, dma, s=s, ts0=ts0: dma(KTs[s, 0:64, :, past:past + TS], KTn[0:64, :, ts0:ts0 + TS])),
              reads=[f"KTn{NG - 1}"], writes=[f"KTs{s}"], dma=True)
            O("pool", _nd(1)(lambda e, dma, s=s: dma(Vs[s, 0:past, :], c_v[s, :, :])), writes=[f"Vs{s}"], dma=True)
            O("sp", _nd(1)(lambda e, dma, s=s, ts0=ts0: dma(Vs[s, past:past + TS, :], Vb[ts0:ts0 + TS, :])),
              reads=[f"Vb{NG - 1}"], writes=[f"Vs{s}"], dma=True)
        tk.barrier()
        tk.flush(sems, dsems)
        es.close()

    if STAGE >= 2:
        phase1b()

    RG = [[0, 1, 2, 3], [4, 5, 6, 7]]
    EARLY_AG = False
    deferred = list(range(NCH))

    def phase23():
        es = ExitStack()

        def sb(name, shape, dt=F32):
            return es.enter_context(nc.sbuf_tensor('p23_' + name, list(shape), dt))

        def ps(name, shape, dt=F32):
            return es.enter_context(nc.psum_tensor('p23_' + name, list(shape), dt))

        TK = max(T, TKS)
        qt = sb("qt", [128, T], BF16)
        kt = sb("kt", [128, TK], BF16)
        vt = sb("vt", [128, TK // 128, 128], BF16)
        pb = [sb(f"pb{i}", [128, 512], BF16) for i in range(4)]
        rc = sb("rc", [128, 512])
        at = sb("at", [64, 512], BF16)
        att = [sb(f"att{i}", [128, 4, 64], BF16) for i in range(2)]
        psS = [ps(f"psS{i}", [128, 512]) for i in range(2)]
        psO = [ps(f"psO{i}", [128, 512]) for i in range(1)]
        tA = ps("ssd_tA", [128, 4, 64])
        tB = ps("ssd_tB", [128, 256])
        tC = ps("ssd_tC", [64, 256])
        psTa = ps("psTa", [128, 4, 64], BF16)
        O("dve", lambda e: e.memset(vt[:, :, 64:128], 1.0), writes=["vt1"])
        O("dve", lambda e: e.memset(qt[64:128, :], 0.0), writes=["qt"])
        O("dve", lambda e: e.memset(kt[64:128, :], 0.0), writes=["kt"])
        sbc = [0]

        def job(q_ap, k_ap, v_ap, Tq, nk, pst, e_row0, h):
            nkb = (nk + 127) // 128

            @_nd(2)
            def ld(e, dma):
                dma(qt[0:70, 0:Tq], q_ap)
                dma(kt[0:70, 0:nk], k_ap)
            O("sp", ld, writes=["qt", "kt"], dma=True)
            nfb = nk // 128
            VB = 8
            for b0 in range(0, nfb, VB):
                b1 = min(nfb, b0 + VB)
                O("sp", _nd(1)(lambda e, dma, b0=b0, b1=b1: dma(
                    vt[:, b0:b1, 0:64], v_ap[b0 * 128:b1 * 128, :].rearrange("(b p) c -> p b c", p=128))),
                  reads=["vt"] if b0 else [], writes=["vt"], dma=True)
            if nk % 128:
                nf = nk // 128
                O("sp", _nd(1)(lambda e, dma: dma(vt[0:nk - nf * 128, nf, 0:64], v_ap[nf * 128:nk, :])),
                  reads=["vt"], writes=["vt"], dma=True)
            SBQ = min(512, Tq)
            LA = 2
            tiles = []
            sbs = []
            for q0 in range(0, Tq, SBQ):
                nq = SBQ
                sbc[0] += 1
                o2 = 0
                blocks = []
                for kb in range(nkb):
                    ks = kb * 128
                    kn = min(128, nk - ks)
                    if ks + kn - 1 <= pst + q0:
                        blocks.append((kb, ks, kn, q0, False))
                    elif ks <= pst + q0 + nq - 1:
                        blocks.append((kb, ks, kn, ks - pst, True))
                sbs.append((q0, nq, o2, len(blocks)))
                for bi, blk in enumerate(blocks):
                    tiles.append((len(sbs) - 1, bi, blk))

            def emit_qk(ti):
                si, bi, (kb, ks, kn, qlo, diag) = tiles[ti]
                q0, nq, o2, nb_ = sbs[si]
                n1 = q0 + nq - qlo
                s3 = ti % 2
                p4 = ti % 4
                O("pe", lambda e, s3=s3, ks=ks, kn=kn, qlo=qlo, n1=n1: e.matmul(
                    psS[s3][0:kn, 0:n1], lhsT=kt[0:128, ks:ks + kn], rhs=qt[0:128, qlo:qlo + n1], start=True, stop=True),
                  reads=["qt", "kt"], writes=[f"psS{s3}"])
                O("act", lambda e, s3=s3, p4=p4, kn=kn, n1=n1: e.activation(
                    out=pb[p4][0:kn, 0:n1], in_=psS[s3][0:kn, 0:n1], func=AF.Exp),
                  reads=[f"psS{s3}"], writes=[f"pb{p4}"])
                if diag:
                    w = min(128, n1)
                    O("pool", lambda e, p4=p4, kn=kn, w=w: e.tensor_tensor(
                        out=pb[p4][0:kn, 0:w], in0=pb[p4][0:kn, 0:w], in1=c_trb[0:kn, 0:w], op=ALU.mult),
                      reads=[f"pb{p4}", "const"], writes=[f"pb{p4}"])

            def emit_pv(ti):
                si, bi, (kb, ks, kn, qlo, diag) = tiles[ti]
                q0, nq, o2, nb_ = sbs[si]
                n1 = q0 + nq - qlo
                p4 = ti % 4
                O("pe", lambda e, o2=o2, p4=p4, kb=kb, kn=kn, qlo=qlo, n1=n1, q0=q0, nq=nq, bi=bi, nb_=nb_: e.matmul(
                    psO[o2][:, qlo - q0:nq], lhsT=vt[0:kn, kb, :], rhs=pb[p4][0:kn, 0:n1],
                    start=(bi == 0), stop=(bi == nb_ - 1)),
                  reads=[f"pb{p4}", "vt", "vt1"], writes=[f"psO{o2}"])
                if bi == nb_ - 1:
                    finish(si)

            def finish(si):
                q0, nq, o2, nb_ = sbs[si]
                O("dve", lambda e, o2=o2, nq=nq: e.reciprocal(out=rc[64:128, 0:nq], in_=psO[o2][64:128, 0:nq]),
                  reads=[f"psO{o2}"], writes=["rc"])
                O("dve", lambda e, o2=o2, nq=nq: e.tensor_tensor(out=at[:, 0:nq], in0=psO[o2][0:64, 0:nq], in1=rc[64:128, 0:nq], op=ALU.mult),
                  reads=[f"psO{o2}", "rc"], writes=["at"])
                nj = (nq + 127) // 128
                wj = min(128, nq)
                p2 = 0

                def tra(e, p2=p2, nj=nj, wj=wj):
                    for j in range(nj):
                        ins = e.transpose(out=psTa[0:wj, j, 0:64], in_=at[:, j * 128:j * 128 + wj], identity=c_idb[0:64, 0:64])
                    return ins
                O("pe", tra, reads=["at", "const"], writes=["psTa"])
                a2 = nxt() % 2
                O("act", lambda e, p2=p2, a2=a2, nj=nj, wj=wj: e.copy(out=att[a2][0:wj, 0:nj, :], in_=psTa[0:wj, 0:nj, 0:64]),
                  reads=["psTa"], writes=[f"att{a2}"])
                r0 = e_row0 + q0
                O("sp", _nd(1)(lambda e, dma, a2=a2, r0=r0, nq=nq, nj=nj, wj=wj, h=h: dma(
                    E[r0:r0 + nq, h * 64:(h + 1) * 64].rearrange("(j p) c -> p j c", p=wj), att[a2][0:wj, 0:nj, :])),
                  reads=[f"att{a2}"], writes=[f"Ea_{r0}_{h}"], dma=True)
                if EARLY_AG and Tq == T and h == HG - 1 and (q0 // 512) % 2 == 1 and STAGE >= 5:
                    q = q0 // 1024
                    rd = [f"Ea_{rr}_{hh}" for rr in (q * 1024, q * 1024 + 512) for hh in range(HG)] + \
                         [f"Eu_{rr}" for rr in (q * 1024, q * 1024 + 512)]
                    if all(r_ in tk.res for r_ in rd):
                        O("pool", lambda e, q=q: e.collective_compute("AllGather", ALU.bypass, replica_groups=RG,
                                                                      ins=[E[q * 1024:(q + 1) * 1024, :]], outs=[Gq[q][:, :]]),
                          reads=rd, writes=[f"G{q}"])
                    else:
                        deferred.append(q)

            nt_ = len(tiles)
            for idx in range(nt_ + LA):
                if idx < nt_:
                    emit_qk(idx)
                if idx - LA >= 0:
                    emit_pv(idx - LA)
                yield

        bt = [sb(f"bt{i}", [128, 512], BF16) for i in range(2)]
        ct = [sb(f"ct{i}", [128, 512], BF16) for i in range(2)]
        xk = [sb(f"xk{i}", [64, 8, 384], BF16) for i in range(2)]
        fdm = [sb(f"fdm{i}", [64, 8, 8]) for i in range(2)]
        zs = [sb(f"zs{i}", [64, 8, 256]) for i in range(2)]
        u8 = [sb(f"u8{i}", [64, 8, 256], BF16) for i in range(2)]
        Hf = sb("Hf", [128, 256])
        Hb = sb("Hb", [128, 256], BF16)
        hio = sb("hio", [128, 2, 128])
        dA = sb("dA", [64, 4])
        Rr = sb("Rr", [64, 4, 64])
        cum = sb("cum", [64, 4])
        ecum = sb("ecum", [64, 4])
        seg = sb("seg", [64, 4, 64])
        CBm = sb("CBm", [64, 64])
        MT = sb("MT", [64, 4, 64], BF16)
        ysb = sb("ysb", [64, 256])
        wl = sb("wl", [64, 4])
        xw = sb("xw", [64, 256], BF16)
        dec = sb("dec", [128, 4])
        psR = tA
        psH = tB[:, :]
        psY = tB[0:64, :]
        psC = tB[0:64, 0:4]
        psF = tB[:, :].rearrange("p (a n) -> p a n", n=128)
        psYo = tC[:, :]
        psCB = tC[:, 0:64]

        def seq_job(r_base, ntok, init_ap, out_ap, tag):
            if init_ap is None:
                O("dve", lambda e: e.memset(Hf[:, :], 0.0), writes=["Hf"])
            else:
                O("sp", _nd(1)(lambda e, dma: dma(hio[:, :, :], init_ap.rearrange("(a p) n -> p a n", p=128))),
                  writes=["hio"], dma=True)

                def trin(e):
                    for a in range(2):
                        ins = e.transpose(out=psF[:, a, :], in_=hio[:, a, :], identity=c_idf[:, :])
                    return ins
                O("pe", trin, reads=["hio", "const"], writes=["tB"])
                O("act", lambda e: e.copy(out=Hf[:, :], in_=tB[:, :]), reads=["tB"], writes=["Hf"])
            O("act", lambda e: e.copy(out=Hb[:, :], in_=Hf[:, :]), reads=["Hf"], writes=["Hb"])
            SC = min(512, ntok)
            nchk = SC // 64
            for r0 in range(r_base, r_base + ntok, SC):
                i2 = nxt() % 2

                @_nd(5)
                def ld(e, dma, i2=i2, r0=r0):
                    dma(bt[i2][:, 0:SC], UT[0:128, r0:r0 + SC])
                    dma(ct[i2][:, 0:SC], UT[128:256, r0:r0 + SC])
                    dma(xk[i2][:, 0:nchk, :], Utok[r0:r0 + SC, :].rearrange("(c p) f -> p c f", p=64))
                    dma(fdm[i2][:, 0:nchk, :], FD[r0:r0 + SC, :].rearrange("(c p) f -> p c f", p=64))
                    dma(zs[i2][:, 0:nchk, :], ZS[r0:r0 + SC, :].rearrange("(c p) f -> p c f", p=64))
                O("sp", ld, writes=[f"ssdin{i2}"], dma=True)
                IN = f"ssdin{i2}"
                for c in range(nchk):
                    dt_ = fdm[i2][:, c, 4:8]
                    cs = slice(c * 64, (c + 1) * 64)
                    O("dve", lambda e, dt_=dt_: e.tensor_tensor(out=dA[:, :], in0=dt_, in1=c_r4[0:64, 0:4], op=ALU.mult),
                      reads=[IN, "cr4"], writes=["dA"])
                    for h in range(HG):
                        O("dve", lambda e, h=h: e.tensor_scalar(out=Rr[:, h, :], in0=c_trf[0:64, 0:64], scalar1=dA[:, h:h + 1],
                                                                scalar2=None, op0=ALU.mult),
                          reads=["dA", "const"], writes=["Rr"])
                    yield
                    O("pe", lambda e: e.matmul(psC[:, :], lhsT=c_trf[0:64, 0:64], rhs=dA[:, :], start=True, stop=True),
                      reads=["dA", "const"], writes=["tB"])
                    O("pe", lambda e: e.matmul(psR[:, :, :], lhsT=c_onf[0:64, :], rhs=Rr[:, :, :], start=True, stop=True),
                      reads=["Rr", "const"], writes=["tA"])
                    yield
                    O("act", lambda e: e.copy(out=cum[:, :], in_=psC[:, :]), reads=["tB"], writes=["cum"])
                    O("act", lambda e: e.activation(out=ecum[:, :], in_=psC[:, :], func=AF.Exp), reads=["tB"], writes=["ecum"])
                    for h in range(HG):
                        O("dve", lambda e, h=h: e.tensor_scalar(out=seg[:, h, :], in0=psR[0:64, h, :], scalar1=cum[:, h:h + 1],
                                                                scalar2=0.0, op0=ALU.subtract, op1=ALU.min),
                          reads=["tA", "cum"], writes=["seg"])
                    yield
                    O("act", lambda e: e.activation(out=seg[:, :, :], in_=seg[:, :, :], func=AF.Exp), reads=["seg"], writes=["seg"])
                    O("pe", lambda e, i2=i2, cs=cs: e.matmul(psCB[:, :], lhsT=bt[i2][:, cs], rhs=ct[i2][:, cs], start=True, stop=True),
                      reads=[IN], writes=["tC"])
                    O("dve", lambda e: e.tensor_tensor(out=CBm[:, :], in0=psCB[:, :], in1=c_trf[0:64, 0:64], op=ALU.mult),
                      reads=["tC", "const"], writes=["CBm"])
                    yield
                    for h in range(HG):
                        O("dve", lambda e, h=h, dt_=dt_: e.scalar_tensor_tensor(out=MT[:, h, :], in0=seg[:, h, :], scalar=dt_[:, h:h + 1],
                                                                               in1=CBm[:, :], op0=ALU.mult, op1=ALU.mult),
                          reads=["seg", "CBm", IN], writes=["MT"])

                    yield
                    def ydiag(e, i2=i2, c=c):
                        for h in range(HG):
                            ins = e.matmul(psY[:, h * 64:(h + 1) * 64], lhsT=MT[:, h, :], rhs=xk[i2][:, c, h * 64:(h + 1) * 64],
                                           start=True, stop=True)
                        return ins
                    O("pe", ydiag, reads=["MT", IN], writes=["tB"])
                    O("pe", lambda e, i2=i2, cs=cs: e.matmul(psYo[:, :], lhsT=ct[i2][:, cs], rhs=Hb[:, :], start=True, stop=True),
                      reads=[IN, "Hb"], writes=["tC"])
                    yield
                    O("act", lambda e: e.copy(out=ysb[:, :], in_=psY[:, :]), reads=["tB"], writes=["ysb"])
                    for h in range(HG):
                        hs = slice(h * 64, (h + 1) * 64)
                        O("dve", lambda e, h=h, hs=hs: e.scalar_tensor_tensor(out=ysb[:, hs], in0=psYo[:, hs], scalar=ecum[:, h:h + 1],
                                                                             in1=ysb[:, hs], op0=ALU.mult, op1=ALU.add),
                          reads=["tC", "ecum", "ysb"], writes=["ysb"])
                        O("dve", lambda e, h=h, hs=hs, i2=i2, c=c: e.scalar_tensor_tensor(
                            out=ysb[:, hs], in0=xk[i2][:, c, hs], scalar=c_r4[0:64, 4 + h:5 + h], in1=ysb[:, hs],
                            op0=ALU.mult, op1=ALU.add), reads=[IN, "const", "ysb"], writes=["ysb"])
                    O("dve", lambda e, i2=i2, c=c: e.tensor_tensor(out=u8[i2][:, c, :], in0=ysb[:, :], in1=zs[i2][:, c, :], op=ALU.mult),
                      reads=["ysb", IN], writes=[f"u8{i2}"])
                    yield
                    O("dve", lambda e: e.tensor_tensor(out=wl[:, :], in0=psR[0:64, :, 63], in1=cum[:, :], op=ALU.subtract),
                      reads=["tA", "cum"], writes=["wl"])
                    O("act", lambda e: e.activation(out=wl[:, :], in_=wl[:, :], func=AF.Exp), reads=["wl"], writes=["wl"])
                    O("dve", lambda e, dt_=dt_: e.tensor_tensor(out=wl[:, :], in0=wl[:, :], in1=dt_, op=ALU.mult),
                      reads=["wl", IN], writes=["wl"])
                    for h in range(HG):
                        hs = slice(h * 64, (h + 1) * 64)
                        O("dve", lambda e, h=h, hs=hs, i2=i2, c=c: e.tensor_scalar(out=xw[:, hs], in0=xk[i2][:, c, hs], scalar1=wl[:, h:h + 1],
                                                                                  scalar2=None, op0=ALU.mult),
                          reads=[IN, "wl"], writes=["xw"])
                    yield
                    O("pe", lambda e, i2=i2, c=c: e.matmul(psH[:, :], lhsT=xk[i2][:, c, 256:384], rhs=xw[:, :], start=True, stop=True),
                      reads=[IN, "xw"], writes=["tB"])
                    O("act", lambda e: e.activation(out=dec[:, :], in_=psR[:, :, 63], func=AF.Exp), reads=["tA"], writes=["dec"])
                    yield
                    for h in range(HG):
                        hs = slice(h * 64, (h + 1) * 64)
                        O("dve", lambda e, h=h, hs=hs: e.scalar_tensor_tensor(out=Hf[:, hs], in0=Hf[:, hs], scalar=dec[:, h:h + 1],
                                                                             in1=psH[:, hs], op0=ALU.mult, op1=ALU.add),
                          reads=["Hf", "dec", "tB"], writes=["Hf"])
                    O("act", lambda e: e.copy(out=Hb[:, :], in_=Hf[:, :]), reads=["Hf"], writes=["Hb"])
                yield
                O("sp", _nd(1)(lambda e, dma, i2=i2, r0=r0: dma(
                    E[r0:r0 + SC, 256:512].rearrange("(c p) f -> p c f", p=64), u8[i2][:, 0:nchk, :])),
                  reads=[f"u8{i2}"], writes=[f"Eu_{r0}"], dma=True)
            def trout(e):
                for a in range(2):
                    ins = e.transpose(out=psF[:, a, :], in_=Hf[:, a * 128:(a + 1) * 128], identity=c_idf[:, :])
                return ins
            O("pe", trout, reads=["Hf", "const"], writes=["tB"])
            O("act", lambda e: e.copy(out=hio[:, :, :], in_=psF[:, :, :]), reads=["tB"], writes=["hio"])
            O("sp", _nd(1)(lambda e, dma: dma(out_ap.rearrange("(a p) n -> p a n", p=128), hio[:, :, :])),
              reads=["hio"], writes=[f"ssm_{tag}"], dma=True)


        def att_gen():
            for h in range(HG):
                yield from job(QT[:, h, 0:T], KTn[:, h, 0:T], Vb[0:T, h * 64:(h + 1) * 64], T, T, 0, 0, h)
            for s in range(NSQ):
                ts0 = T + s * TS
                for h in range(HG):
                    yield from job(QT[:, h, ts0:ts0 + TS], KTs[s, :, h, 0:past + TS], Vs[s, 0:past + TS, h * 64:(h + 1) * 64],
                                   TS, past + TS, past, ts0, h)

        def ssd_gen():
            yield from seq_job(0, T, None, ssm_p[:, :], "p")
            for s in range(NSQ):
                yield from seq_job(T + s * TS, TS, c_ssm[s, :, :], ssm_s[s, :, :], f"s{s}")

        n_att = HG * sum((i + 1) * 4 + 4 for i in range(T // 512)) + NSQ * HG * (past // 128 + 3)
        n_ssd = (T // 64 + NSQ) * 11
        ga, gs_ = att_gen(), ssd_gen()
        da = ds = 0
        a_alive = s_alive = True
        while a_alive or s_alive:
            if a_alive:
                try:
                    next(ga)
                    da += 1
                except StopIteration:
                    a_alive = False
            while s_alive and (not a_alive or (os.environ.get('INTERLEAVE', '1') == '1' and ds * n_att <= da * n_ssd)):
                try:
                    next(gs_)
                    ds += 1
                except StopIteration:
                    s_alive = False
        tk.barrier()
        tk.flush(sems, dsems)
        es.close()

    if STAGE >= 3:
        phase23()


    if STAGE >= 5:
        for q in deferred:
            O("pool", lambda e, q=q: e.collective_compute("AllGather", ALU.bypass, replica_groups=RG,
                                                          ins=[E[q * 1024:(q + 1) * 1024, :]], outs=[Gq[q][:, :]]),
              writes=[f"G{q}"])
        O("pool", lambda e: e.collective_compute("AllGather", ALU.bypass, replica_groups=RG,
                                                 ins=[E[T:T + 512, :]], outs=[Gs[:, :]]), writes=["Gs"])

    def phase5():
        es = ExitStack()

        def sb(name, shape, dt=F32):
            return es.enter_context(nc.sbuf_tensor(f'ph4_' + name, list(shape), dt))

        def ps(name, shape, dt=F32):
            return es.enter_context(nc.psum_tensor(f'ph4_' + name, list(shape), dt))

        mixT = sb("mixT", [128, 16, 512], BF16)
        x1 = sb("x1", [128, 4, D])
        AT = sb("AT", [128, 44, 512], BF16)
        NWB = 3
        wbuf = [sb(f"wbuf{i}", [128, 8192], BF16) for i in range(NWB)]
        selT = sb("selT", [128, 8, 256], BF16)
        selS = sb("selS", [128, 4, 128], BF16)
        c_gffn = sb("c_gffn", [128, D])
        c_gssd = sb("c_gssd", [128, 8])
        junk = sb("junk", [128, D])
        hb = [sb(f"hb{i}", [128, D], BF16) for i in range(2)]
        ss = [sb(f"ss{i}", [128, 2]) for i in range(2)]
        sq = sb("sq", [128, 512], BF16)
        rstd = sb("rstd", [128, 512])
        gt = [sb(f"gt{i}", [128, 512]) for i in range(2)]
        pz = [ps(f"pz{i}", [128, 512]) for i in range(6)]
        psT = [ps(f"psT{i}", [128, 4, 128], BF16) for i in range(2)]
        zc = [0]

        def nz():
            zc[0] += 1
            return zc[0] % 6
        wc = [0]

        def wload(view_fn, src_ap, src_reads):
            i = wc[0] % NWB
            wc[0] += 1
            O("sp", _nd(1)(lambda e, dma, i=i: dma(view_fn(wbuf[i]), src_ap)), reads=src_reads, writes=[f"wbuf{i}"], dma=True)
            return i

        @_nd(4)
        def ldc(e, dma):
            dma(selT[:, :, :], sel[:, :, :])
            dma(selS[:, :, :], sel_s[:, :, :])
            dma(c_gffn[:, :], gffn[0:1, :].partition_broadcast(128))
            dma(c_gssd[:, :], gssd[:, :])
        O("sp", ldc, writes=["p5c"], dma=True)

        items = [(Gq[q], 8, selT, 2, f"G{q}") for q in range(NCH)]
        groups = [items[i:i + 2] for i in range(0, NCH, 2)] + [[(Gs, 4, selS, 1, "Gs")]]
        def do_group(grp, row):
            ntile = sum(it[3] for it in grp)
            ntok = ntile * 128
            O("sp", _nd(1)(lambda e, dma, row=row, ntile=ntile: dma(
                x1[:, 0:ntile, :], x2[row:row + ntok, :].rearrange("(j p) c -> p j c", p=128))), writes=["x1"], dma=True)
            toff = 0
            for (G, nblk, st, nt, gname) in grp:
                rows = nblk * 128
                for r in range(4):
                    wi = wload(lambda w, nblk=nblk: w[:, 0:nblk * 512].rearrange("p (b c) -> p b c", c=512),
                               G[r * rows:(r + 1) * rows, :].rearrange("(b p) c -> p b c", p=128), [gname])
                    gv = wbuf[wi][:, 0:nblk * 512].rearrange("p (b c) -> p b c", c=512)
                    for i in range(nt):
                        z = nz()
                        pzv = pz[z][:, :].rearrange("p (a t) -> p a t", t=128)

                        def selmm(e, gv=gv, st=st, nblk=nblk, i=i, pzv=pzv):
                            for cch in range(4):
                                for blk in range(nblk):
                                    ins = e.matmul(pzv[:, cch, :], lhsT=gv[:, blk, cch * 128:(cch + 1) * 128],
                                                   rhs=st[:, blk, i * 128:(i + 1) * 128], start=(blk == 0), stop=(blk == nblk - 1))
                            return ins
                        O("pe", selmm, reads=[f"wbuf{wi}", "p5c"], writes=[f"pz{z}"])
                        c0 = toff + i * 128
                        eng = "act" if (r + i) % 2 else "dve"
                        if eng == "act":
                            O("act", lambda e, r=r, c0=c0, pzv=pzv: e.copy(out=mixT[:, r * 4:r * 4 + 4, c0:c0 + 128], in_=pzv),
                              reads=[f"pz{z}"], writes=["mixT"])
                        else:
                            O("dve", lambda e, r=r, c0=c0, pzv=pzv: e.tensor_copy(out=mixT[:, r * 4:r * 4 + 4, c0:c0 + 128], in_=pzv),
                              reads=[f"pz{z}"], writes=["mixT"])
                toff += nt * 128
            for gi in range(2):
                kcs = [(2 * gi) * 4 + 2, (2 * gi) * 4 + 3, (2 * gi + 1) * 4 + 2, (2 * gi + 1) * 4 + 3]
                z = nz()
                for j, kc in enumerate(kcs):
                    O("act", lambda e, kc=kc: e.activation(out=sq[:, 0:ntok], in_=mixT[:, kc, 0:ntok], func=AF.Square),
                      reads=["mixT"], writes=["sq"])
                    O("pe", lambda e, z=z, j=j: e.matmul(pz[z][:, 0:ntok], lhsT=c_onb[:, :], rhs=sq[:, 0:ntok], start=(j == 0), stop=(j == 3)),
                      reads=["sq", "const"], writes=[f"pz{z}"])
                O("act", lambda e, z=z: e.activation(out=rstd[:, 0:ntok], in_=pz[z][:, 0:ntok], func=AF.Sqrt, bias=EPS, scale=1.0 / 512),
                  reads=[f"pz{z}"], writes=["rstd"])
                O("dve", lambda e: e.reciprocal(out=rstd[:, 0:ntok], in_=rstd[:, 0:ntok]), reads=["rstd"], writes=["rstd"])
                for j, kc in enumerate(kcs):
                    O("dve", lambda e, kc=kc, j=j, gi=gi: e.scalar_tensor_tensor(
                        out=mixT[:, kc, 0:ntok], in0=mixT[:, kc, 0:ntok], scalar=c_gssd[:, gi * 4 + j:gi * 4 + j + 1],
                        in1=rstd[:, 0:ntok], op0=ALU.mult, op1=ALU.mult), reads=["mixT", "rstd", "p5c"], writes=["mixT"])
            for cg in range(4):
                wi = wload(lambda w: w[:, :].rearrange("p (k n) -> p k n", n=512),
                           wo_b[:, cg * 512:(cg + 1) * 512].rearrange("(k p) n -> p k n", p=128), [f"wo_b_{r}" for r in range(0, D, 512)])
                wv = wbuf[wi][:, :].rearrange("p (k n) -> p k n", n=512)
                for i in range(ntile):
                    z = nz()

                    def womm(e, wv=wv, i=i, z=z):
                        for kc in range(16):
                            ins = e.matmul(pz[z][:, :], lhsT=mixT[:, kc, i * 128:(i + 1) * 128], rhs=wv[:, kc, :],
                                           start=(kc == 0), stop=(kc == 15))
                        return ins
                    O("pe", womm, reads=[f"wbuf{wi}", "mixT"], writes=[f"pz{z}"])
                    O("dve", lambda e, i=i, cg=cg, z=z: e.tensor_tensor(out=x1[:, i, cg * 512:(cg + 1) * 512], in0=pz[z][:, :],
                                                                       in1=x1[:, i, cg * 512:(cg + 1) * 512], op=ALU.add),
                      reads=[f"pz{z}", "x1"], writes=["x1"])
            for i in range(ntile):
                i2 = nxt() % 2
                O("act", lambda e, i=i, i2=i2: e.activation(out=junk[:, :], in_=x1[:, i, :], func=AF.Square, accum_out=ss[i2][:, 0:1]),
                  reads=["x1"], writes=["junk", f"ss{i2}"])
                O("act", lambda e, i2=i2: e.activation(out=ss[i2][:, 1:2], in_=ss[i2][:, 0:1], func=AF.Sqrt, bias=EPS, scale=1.0 / D),
                  reads=[f"ss{i2}"], writes=[f"ss{i2}"])
                O("dve", lambda e, i2=i2: e.reciprocal(out=ss[i2][:, 1:2], in_=ss[i2][:, 1:2]), reads=[f"ss{i2}"], writes=[f"ss{i2}"])
                O("dve", lambda e, i=i, i2=i2: e.scalar_tensor_tensor(out=hb[i2][:, :], in0=x1[:, i, :], scalar=ss[i2][:, 1:2],
                                                                     in1=c_gffn[:, :], op0=ALU.mult, op1=ALU.mult),
                  reads=["x1", f"ss{i2}", "p5c"], writes=[f"hb{i2}"])
                for k4 in range(4):
                    p2 = nxt() % 2

                    def tr(e, i2=i2, k4=k4, p2=p2):
                        for j in range(4):
                            kc = k4 * 4 + j
                            ins = e.transpose(out=psT[p2][:, j, :], in_=hb[i2][:, kc * 128:(kc + 1) * 128], identity=c_idb[:, :])
                        return ins
                    O("pe", tr, reads=[f"hb{i2}", "const"], writes=[f"psT{p2}"])
                    if k4 % 2:
                        O("act", lambda e, k4=k4, p2=p2, i=i: e.copy(out=mixT[:, k4 * 4:(k4 + 1) * 4, i * 128:(i + 1) * 128], in_=psT[p2][:, :, :]),
                          reads=[f"psT{p2}"], writes=["mixT"])
                    else:
                        O("dve", lambda e, k4=k4, p2=p2, i=i: e.tensor_copy(out=mixT[:, k4 * 4:(k4 + 1) * 4, i * 128:(i + 1) * 128], in_=psT[p2][:, :, :]),
                          reads=[f"psT{p2}"], writes=["mixT"])
            wg_reads = [f"wg_b_{r}" for r in range(0, D, 512)]
            wu_reads = [f"wu_b_{r}" for r in range(0, D, 512)]
            for fb in range(DFF // 512):
                wgi = wload(lambda w: w[:, :].rearrange("p (k n) -> p k n", n=512),
                            wg_b[:, fb * 512:(fb + 1) * 512].rearrange("(k p) n -> p k n", p=128), wg_reads)
                wui = wload(lambda w: w[:, :].rearrange("p (k n) -> p k n", n=512),
                            wu_b[:, fb * 512:(fb + 1) * 512].rearrange("(k p) n -> p k n", p=128), wu_reads)
                wgv = wbuf[wgi][:, :].rearrange("p (k n) -> p k n", n=512)
                wuv = wbuf[wui][:, :].rearrange("p (k n) -> p k n", n=512)
                for c4 in range(4):
                    zg, zu = nz(), nz()

                    def gmm(e, wv=wgv, c4=c4, z=zg):
                        for kc in range(16):
                            ins = e.matmul(pz[z][:, 0:ntok], lhsT=wv[:, kc, c4 * 128:(c4 + 1) * 128], rhs=mixT[:, kc, 0:ntok],
                                           start=(kc == 0), stop=(kc == 15))
                        return ins

                    def umm(e, wv=wuv, c4=c4, z=zu):
                        for kc in range(16):
                            ins = e.matmul(pz[z][:, 0:ntok], lhsT=wv[:, kc, c4 * 128:(c4 + 1) * 128], rhs=mixT[:, kc, 0:ntok],
                                           start=(kc == 0), stop=(kc == 15))
                        return ins
                    O("pe", gmm, reads=[f"wbuf{wgi}", "mixT"], writes=[f"pz{zg}"])
                    O("pe", umm, reads=[f"wbuf{wui}", "mixT"], writes=[f"pz{zu}"])
                    g2 = nxt() % 2
                    O("act", lambda e, z=zg, g2=g2: e.activation(out=gt[g2][:, 0:ntok], in_=pz[z][:, 0:ntok], func=AF.Silu),
                      reads=[f"pz{zg}"], writes=[f"gt{g2}"])
                    O("dve", lambda e, z=zu, g2=g2, fb=fb, c4=c4: e.tensor_tensor(out=AT[:, fb * 4 + c4, 0:ntok], in0=pz[z][:, 0:ntok],
                                                                                 in1=gt[g2][:, 0:ntok], op=ALU.mult),
                      reads=[f"pz{zu}", f"gt{g2}"], writes=["AT"])
            wd_reads = [f"wd_b_{r}" for r in range(0, DFF, 512)]
            for cg in range(4):
                zs_ = [nz() for _ in range(ntile)]
                for blk in range(4):
                    wi = wload(lambda w: w[:, 0:11 * 512].rearrange("p (f n) -> p f n", n=512),
                               wd_b[blk * 1408:(blk + 1) * 1408, cg * 512:(cg + 1) * 512].rearrange("(f p) n -> p f n", p=128), wd_reads)
                    wv = wbuf[wi][:, 0:11 * 512].rearrange("p (f n) -> p f n", n=512)
                    for i in range(ntile):
                        def dmm(e, wv=wv, i=i, blk=blk, z=zs_[i]):
                            for f in range(11):
                                ins = e.matmul(pz[z][:, :], lhsT=AT[:, blk * 11 + f, i * 128:(i + 1) * 128], rhs=wv[:, f, :],
                                               start=(blk == 0 and f == 0), stop=(blk == 3 and f == 10))
                            return ins
                        O("pe", dmm, reads=[f"wbuf{wi}", "AT"], writes=[f"pz{zs_[i]}"])
                for i in range(ntile):
                    O("dve", lambda e, i=i, cg=cg, z=zs_[i]: e.tensor_tensor(out=x1[:, i, cg * 512:(cg + 1) * 512], in0=pz[z][:, :],
                                                                            in1=x1[:, i, cg * 512:(cg + 1) * 512], op=ALU.add),
                      reads=[f"pz{zs_[i]}", "x1"], writes=["x1"])
            O("sp", _nd(1)(lambda e, dma, row=row, ntile=ntile, ntok=ntok: dma(
                y2[row:row + ntok, :].rearrange("(j p) c -> p j c", p=128), x1[:, 0:ntile, :])), reads=["x1"], writes=["y2"], dma=True)
            return ntok

        row = 0
        for grp in groups:
            row += do_group(grp, row)
        tk.barrier()
        tk.flush(sems, dsems)
        es.close()

    if STAGE >= 8:
        phase5()
    if os.environ.get("DBG_E"):
        E_dbg = dout("E_dbg", [TT, 512], BF16)
        O("sp", _nd(1)(lambda e, dma: dma(E_dbg[:, :], E[:, :])), writes=["E_dbg"], dma=True)
    tk.barrier()
    tk.flush(sems, dsems)
    ges.close()
    sstack.close()
    return nc


def make_in_maps(inp, T, NSQ=8, TS=64, past=PAST):
    bf = ml_dtypes.bfloat16
    f32 = np.float32
    NCH = T // 1024
    ident = np.eye(128, dtype=f32)
    bd = np.zeros((128, 128), f32)
    bd[:64, :64] = 1.0
    bd[64:, 64:] = 1.0
    triu = np.triu(np.ones((128, 128), f32))
    ones = np.ones((128, 128), f32)
    w_in = inp["w_in"][0]
    w_conv = inp["w_conv"][0]
    b_conv = inp["b_conv"][0]
    o_f = 3 * 1024
    o_z = o_f + 16
    o_x = o_z + 1024
    o_B = o_x + 1024
    o_C = o_B + 256
    o_dt = o_C + 256
    perm = []
    for kc in range(16):
        r, cch = kc // 4, kc % 4
        base = 256 * r + 128 * cch if cch < 2 else 1024 + 256 * r + 128 * (cch - 2)
        perm.extend(range(base, base + 128))
    w_out_p = np.ascontiguousarray(inp["w_out"][0][perm])
    maps = []
    for c in range(8):
        b, g = c // 4, c % 4
        j = g
        xs = inp["x_sample"][8 * b:8 * b + 8].reshape(NSQ * TS, D)
        x_all = np.concatenate([inp["x_prompt"][b], xs], axis=0)
        hs = slice(256 * g, 256 * g + 256)
        grp = g // 2
        wsl = np.concatenate([
            w_in[:, 0:1024][:, hs], w_in[:, 1024:2048][:, hs], w_in[:, 2048:3072][:, hs],
            w_in[:, o_z:o_z + 1024][:, hs], w_in[:, o_x:o_x + 1024][:, hs],
            w_in[:, o_B + 128 * grp:o_B + 128 * grp + 128], w_in[:, o_C + 128 * grp:o_C + 128 * grp + 128],
            w_in[:, o_f + 4 * g:o_f + 4 * g + 4], w_in[:, o_dt + 4 * g:o_dt + 4 * g + 4]], axis=1)
        cols = np.zeros((128, 16), f32)
        cols[:, 0] = np.tile(inp["g_q"][0], 2)
        cols[:, 1] = np.tile(inp["g_k"][0], 2)
        cols[0:4, 2] = -1.0
        cols[4:8, 2] = 1.0
        cols[0:4, 3] = inp["f_bias"][0][4 * g:4 * g + 4]
        cols[4:8, 3] = inp["dt_bias"][0][4 * g:4 * g + 4]
        cols[0:4, 4] = -1.0
        cols[4:8, 4] = 1.0
        ccols = np.concatenate([np.arange(256 * g, 256 * g + 256), 1024 + 128 * grp + np.arange(128),
                                1280 + 128 * grp + np.arange(128)])
        wconv = np.zeros((128, 4, 5), f32)
        for ci in range(4):
            cc = ccols[ci * 128:(ci + 1) * 128]
            wconv[:, ci, 0:4] = w_conv[:, cc].T
            wconv[:, ci, 4] = b_conv[cc]
        row4 = np.concatenate([inp["a_log"][0][4 * g:4 * g + 4], inp["d_skip"][0][4 * g:4 * g + 4]])[None, :].astype(f32)
        sq = slice(8 * b, 8 * b + 8)
        c_k = inp["cache_fox_k"][0][sq][:, :, 4 * g:4 * g + 4, :].reshape(NSQ, past, 256)
        c_v = inp["cache_fox_v"][0][sq][:, :, 4 * g:4 * g + 4, :].reshape(NSQ, past, 256)
        c_lf = inp["cache_fox_logf"][0][sq][:, :, 4 * g:4 * g + 4]
        c_convT = np.transpose(inp["cache_conv"][0][sq][:, :, ccols], (2, 0, 1))
        c_ssm = inp["state_ssm"][0][sq][:, 4 * g:4 * g + 4].reshape(NSQ, 256, 128)
        xp = inp["x_prompt"][b].reshape(NCH, 4, 256, D)[:, j].reshape(NCH * 256, D)
        x2 = np.concatenate([xp, inp["x_sample"][8 * b + 2 * j:8 * b + 2 * j + 2].reshape(2 * TS, D)], axis=0)
        sel = np.zeros((128, 8, 256), f32)
        for m in range(256):
            t = 256 * j + m
            sel[t % 128, t // 128, m] = 1.0
        sel_s = np.zeros((128, 4, 128), f32)
        for m in range(128):
            t = 128 * j + m
            sel_s[t % 128, t // 128, m] = 1.0
        gs = inp["g_ssd"][0]
        gssd = np.zeros((128, 8), f32)
        for gi in range(2):
            for jj in range(4):
                base = 256 * (2 * gi + jj // 2) + 128 * (jj % 2)
                gssd[:, gi * 4 + jj] = gs[base:base + 128]
        ca = np.ascontiguousarray
        maps.append({
            "x_all": ca(x_all), "w_in": ca(wsl), "gmix": ca(inp["g_mix"]), "cols": cols, "wconv": wconv, "row4": row4,
            "ident_f": ident, "ident_b": ident.astype(bf), "bd_ones": bd.astype(bf),
            "triu_f": triu, "triu_b": triu.astype(bf), "ones_f": ones, "ones_b": ones.astype(bf),
            "c_k": ca(c_k), "c_v": ca(c_v), "c_lf": ca(c_lf), "c_convT": ca(c_convT), "c_ssm": ca(c_ssm),
            "x2": ca(x2), "sel": sel.astype(bf), "sel_s": sel_s.astype(bf), "gffn": ca(inp["g_ffn"]), "gssd": gssd,
            "w_out": w_out_p, "w_gate": inp["w_gate"][0], "w_up": inp["w_up"][0], "w_down": inp["w_down"][0],
        })
    return maps


def assemble(res, T, B=2, NSQ=8, TS=64):
    NCH = T // 1024
    f32 = np.float32
    DB = 8 * B
    yp = np.zeros((B, T, D), f32)
    ys = np.zeros((DB, TS, D), f32)
    p_conv = np.zeros((1, B, 3, 1536), f32)
    p_ssm = np.zeros((1, B, 16, 64, 128), f32)
    p_k = np.zeros((1, B, T, 16, 64), f32)
    p_v = np.zeros((1, B, T, 16, 64), f32)
    p_f = np.zeros((1, B, T, 16), f32)
    s_conv = np.zeros((1, DB, 3, 1536), f32)
    s_ssm = np.zeros((1, DB, 16, 64, 128), f32)
    s_k = np.zeros((1, DB, TS, 16, 64), f32)
    s_v = np.zeros((1, DB, TS, 16, 64), f32)
    s_f = np.zeros((1, DB, TS, 16), f32)
    for c in range(8):
        b, g = c // 4, c % 4
        j = g
        grp = g // 2
        r = res[c]
        y2 = r["y2"]
        yp[b].reshape(NCH, 4, 256, D)[:, j] = y2[:NCH * 256].reshape(NCH, 256, D)
        ys[8 * b + 2 * j:8 * b + 2 * j + 2] = y2[NCH * 256:].reshape(2, TS, D)
        sq = slice(8 * b, 8 * b + 8)
        p_k[0, b, :, 4 * g:4 * g + 4] = r["k_out"][:T].reshape(T, 4, 64)
        p_v[0, b, :, 4 * g:4 * g + 4] = r["v_out"][:T].reshape(T, 4, 64)
        p_f[0, b, :, 4 * g:4 * g + 4] = r["lf_out"][:T]
        s_k[0, sq, :, 4 * g:4 * g + 4] = r["k_out"][T:].reshape(NSQ, TS, 4, 64)
        s_v[0, sq, :, 4 * g:4 * g + 4] = r["v_out"][T:].reshape(NSQ, TS, 4, 64)
        s_f[0, sq, :, 4 * g:4 * g + 4] = r["lf_out"][T:].reshape(NSQ, TS, 4)
        p_ssm[0, b, 4 * g:4 * g + 4] = r["ssm_p"].reshape(4, 64, 128)
        s_ssm[0, sq, 4 * g:4 * g + 4] = r["ssm_s"].reshape(NSQ, 4, 64, 128)
        cp = r["conv_p"].T
        cs = np.transpose(r["conv_s"], (1, 2, 0))
        p_conv[0, b, :, 256 * g:256 * g + 256] = cp[:, 0:256]
        s_conv[0, sq, :, 256 * g:256 * g + 256] = cs[:, :, 0:256]
        if g % 2 == 0:
            p_conv[0, b, :, 1024 + 128 * grp:1024 + 128 * grp + 128] = cp[:, 256:384]
            p_conv[0, b, :, 1280 + 128 * grp:1280 + 128 * grp + 128] = cp[:, 384:512]
            s_conv[0, sq, :, 1024 + 128 * grp:1024 + 128 * grp + 128] = cs[:, :, 256:384]
            s_conv[0, sq, :, 1280 + 128 * grp:1280 + 128 * grp + 128] = cs[:, :, 384:512]
    return (yp, ys, p_conv, p_ssm, p_k, p_v, p_f, s_conv, s_ssm, s_k, s_v, s_f)


_NC_CACHE = {}


def kernel(**inputs):
    inp = {k: np.asarray(v) for k, v in inputs.items()}
    T = inp["x_prompt"].shape[1]
    if T not in _NC_CACHE:
        _NC_CACHE[T] = build(T)
    nc = _NC_CACHE[T]
    maps = make_in_maps(inp, T)
    res = run_bass_kernel_spmd(nc, maps, core_ids=list(range(8)))
    return assemble(res.results, T)
```

```python
import numpy as np
import ml_dtypes
import concourse.bass as bass
import concourse.mybir as mybir
from concourse.bass_utils import run_bass_kernel_spmd

F32 = mybir.dt.float32
BF16 = mybir.dt.bfloat16
ALU = mybir.AluOpType
AF = mybir.ActivationFunctionType
ET = mybir.EngineType

D = 2048
EPS = 1e-6
HG = 4
NCOL = 1544
PAST = 2048
DFF = 5632


class Tracker:
    CE = ("pe", "act", "dve", "pool", "sp")

    def __init__(self, nc, nslot=8):
        self.nc = nc
        self.nslot = nslot
        self.ops = {e: [] for e in ("pe", "act", "dve", "pool", "sp")}
        self.cnt = {e: 0 for e in self.CE}
        self.ndma = {"sp": 0, "pool": 0}
        self.slotval = {}
        self.res = {}
        self.waited = {e: {} for e in self.ops}

    def _need(self, eng, tok, waits):
        if tok is None:
            return
        kind, key, val = tok
        if kind == "c" and key == eng and eng == "pe":
            return
        k = (kind, key)
        if self.waited[eng].get(k, 0) >= val:
            return
        waits[k] = max(waits.get(k, 0), val)

    def op(self, eng, fn, reads=(), writes=(), dma=False):
        waits = {}
        for r in reads:
            st = self.res.get(r)
            if st:
                self._need(eng, st["w"], waits)
        for w in writes:
            st = self.res.get(w)
            if st:
                self._need(eng, st["w"], waits)
                for t in st["r"]:
                    self._need(eng, t, waits)
        if dma:
            slot = self.ndma[eng] % self.nslot
            self.ndma[eng] += 1
            key = (eng, slot)
            prev = self.slotval.get(key, 0)
            if prev:
                self._need(eng, ("d", key, prev), waits)
            tok = ["d", key, prev]
        else:
            self.cnt[eng] += 1
            tok = ("c", eng, self.cnt[eng])
        for k, v in waits.items():
            self.waited[eng][k] = v
        rec = {"fn": fn, "waits": dict(waits), "tok": tok, "dma": dma, "nd": 0}
        if dma:
            n = getattr(fn, "ndma", 1)
            rec["nd"] = n
            self.slotval[key] = prev + 16 * n
            tok[2] = prev + 16 * n
            tok = tuple(tok)
            rec["tok"] = tok
        self.ops[eng].append(rec)
        for r in reads:
            self.res.setdefault(r, {"w": None, "r": []})["r"].append(tok)
        for w in writes:
            self.res[w] = {"w": tok, "r": []}
        return tok

    def barrier(self):
        allr = list(self.res.keys())
        self.op("sp", lambda e: e.nop(), reads=allr, writes=allr + ["__bar"])
        for en in ("pe", "act", "dve", "pool"):
            self.op(en, lambda e: e.nop(), reads=["__bar"])

    def flush(self, sems, dsems):
        nc = self.nc
        ops, self.ops = self.ops, {e: [] for e in self.ops}

        def run(ename, e):
            for rec in ops[ename]:
                for (kind, key), val in rec["waits"].items():
                    if kind == "c":
                        e.wait_ge(sems[key], val)
                    else:
                        e.wait_ge(dsems[key], val)
                if rec["dma"]:
                    sem = dsems[rec["tok"][1]]
                    cnt = [0]

                    def dma(out, in_, _sem=sem, _cnt=cnt, **kw):
                        _cnt[0] += 1
                        return e.dma_start(out=out, in_=in_, **kw).then_inc(_sem, 16)
                    rec["fn"](e, dma)
                    assert cnt[0] == rec["nd"], (cnt[0], rec["nd"])
                else:
                    ins = rec["fn"](e)
                    ins.then_inc(sems[ename], 1)
        with nc.Block() as block:
            @block.sync
            def _(e):
                run("sp", e)

            @block.tensor
            def _(e):
                run("pe", e)

            @block.scalar
            def _(e):
                run("act", e)

            @block.vector
            def _(e):
                run("dve", e)

            @block.gpsimd
            def _(e):
                run("pool", e)


def _nd(n):
    def deco(f):
        f.ndma = n
        return f
    return deco


import os
from contextlib import ExitStack


def build(T, NSQ=8, TS=64, past=PAST):
    assert T % 1024 == 0 and NSQ * TS == 512 and TS == 64 and past % 512 == 0
    STAGE = int(os.environ.get("STAGE", "99"))
    nc = bass.Bass("TRN2", target_bir_lowering=False)
    TT = T + NSQ * TS
    NG = TT // 512
    NGP = T // 512
    NCH = T // 1024
    TKS = past + 128
    NBS = TKS // 128
    NT2 = NCH * 256 + 128

    def din(name, shape, dt=F32):
        return nc.dram_tensor(name, list(shape), dt, kind="ExternalInput").ap()

    def dout(name, shape, dt=F32):
        return nc.dram_tensor(name, list(shape), dt, kind="ExternalOutput").ap()

    def dscr(name, shape, dt):
        return nc.dram_tensor(name, list(shape), dt, kind="Internal").ap()

    x_all = din("x_all", [TT, D])
    w_in = din("w_in", [D, NCOL])
    gmix = din("gmix", [1, D])
    cols = din("cols", [128, 16])
    wconv = din("wconv", [128, 4, 5])
    row4 = din("row4", [1, 8])
    ident_f = din("ident_f", [128, 128])
    ident_b = din("ident_b", [128, 128], BF16)
    bd_ones = din("bd_ones", [128, 128], BF16)
    triu_f = din("triu_f", [128, 128])
    triu_b = din("triu_b", [128, 128], BF16)
    ones_f = din("ones_f", [128, 128])
    ones_b = din("ones_b", [128, 128], BF16)
    c_k = din("c_k", [NSQ, past, 256])
    c_v = din("c_v", [NSQ, past, 256])
    c_lf = din("c_lf", [NSQ, past, 4])
    c_convT = din("c_convT", [512, NSQ, 3])
    c_ssm = din("c_ssm", [NSQ, 256, 128])
    x2 = din("x2", [NT2, D])
    sel = din("sel", [128, 8, 256], BF16)
    sel_s = din("sel_s", [128, 4, 128], BF16)
    gffn = din("gffn", [1, D])
    gssd = din("gssd", [128, 8])
    w_out = din("w_out", [D, D])
    w_gate = din("w_gate", [D, DFF])
    w_up = din("w_up", [D, DFF])
    w_down = din("w_down", [DFF, D])
    k_out = dout("k_out", [TT, 256])
    v_out = dout("v_out", [TT, 256])
    lf_out = dout("lf_out", [TT, 4])
    conv_p = dout("conv_p", [512, 3])
    conv_s = dout("conv_s", [512, NSQ, 3])
    ssm_p = dout("ssm_p", [256, 128])
    ssm_s = dout("ssm_s", [NSQ, 256, 128])
    y2 = dout("y2", [NT2, D])

    w_in_b = dscr("w_in_b", [D, NCOL], BF16)
    QT = dscr("QT", [70, HG, TT], BF16)
    KTn = dscr("KTn", [70, HG, TT], BF16)
    KTs = dscr("KTs", [NSQ, 70, HG, TKS], BF16)
    Vb = dscr("Vb", [TT, 256], BF16)
    Vs = dscr("Vs", [NSQ, TKS, 256], BF16)
    FD = dscr("FD", [TT, 8], F32)
    ZS = dscr("ZS", [TT, 256], F32)
    UT = dscr("UT", [256, TT], BF16)
    Utok = dscr("Utok", [TT, 384], BF16)
    E = dscr("E", [TT, 512], BF16)
    Gq = [dscr(f"G{q}", [4 * 1024, 512], BF16) for q in range(NCH)]
    Gs = dscr("Gs", [4 * 512, 512], BF16)
    wo_b = dscr("wo_b", [D, D], BF16)
    wg_b = dscr("wg_b", [D, DFF], BF16)
    wu_b = dscr("wu_b", [D, DFF], BF16)
    wd_b = dscr("wd_b", [DFF, D], BF16)

    tk = Tracker(nc)
    O = tk.op
    uid = [0]

    def nxt():
        uid[0] += 1
        return uid[0]

    sstack = ExitStack()
    sems = {e: sstack.enter_context(nc.semaphore(f"s_{e}")) for e in Tracker.CE}
    dsems = {}
    for q in ("sp", "pool"):
        for s in range(tk.nslot):
            dsems[(q, s)] = sstack.enter_context(nc.semaphore(f"d_{q}{s}"))

    ges = ExitStack()

    def gsb(name, shape, dt=F32):
        return ges.enter_context(nc.sbuf_tensor(name, list(shape), dt))

    c_cols = gsb("c_cols", [128, 16])
    c_idf = gsb("c_idf", [128, 128])
    c_idb = gsb("c_idb", [128, 128], BF16)
    c_bd = gsb("c_bd", [128, 128], BF16)
    c_trf = gsb("c_trf", [128, 128])
    c_trb = gsb("c_trb", [128, 128], BF16)
    c_onf = gsb("c_onf", [128, 128])
    c_onb = gsb("c_onb", [128, 128], BF16)
    c_fd = gsb("c_fd", [128, 4])
    c_wc = gsb("c_wc", [128, 4, 5])
    c_r4 = gsb("c_r4", [128, 8])

    @_nd(10)
    def ld_const(e, dma):
        dma(c_cols[:, :], cols[:, :])
        dma(c_idf[:, :], ident_f[:, :])
        dma(c_idb[:, :], ident_b[:, :])
        dma(c_bd[:, :], bd_ones[:, :])
        dma(c_trf[:, :], triu_f[:, :])
        dma(c_trb[:, :], triu_b[:, :])
        dma(c_onf[:, :], ones_f[:, :])
        dma(c_onb[:, :], ones_b[:, :])
        dma(c_wc[:, :, :], wconv[:, :, :])
        dma(c_r4[:, :], row4[0:1, :].partition_broadcast(128))
    O("sp", ld_const, writes=["const"], dma=True)
    O("dve", lambda e: e.tensor_tensor(out=c_fd[0:8, 0:1], in0=c_cols[0:8, 3:4], in1=c_cols[0:8, 2:3], op=ALU.mult),
      reads=["const"], writes=["cfd0"])
    O("dve", lambda e: e.tensor_scalar(out=c_fd[:, 1:2], in0=c_cols[:, 1:2], scalar1=8.0, scalar2=None, op0=ALU.mult),
      reads=["const"], writes=["cfd1"])
    O("act", lambda e: e.activation(out=c_r4[:, 0:4], in_=c_r4[:, 0:4], func=AF.Exp), reads=["const"], writes=["cr4"])
    O("dve", lambda e: e.tensor_scalar(out=c_r4[:, 0:4], in0=c_r4[:, 0:4], scalar1=-1.0, scalar2=None, op0=ALU.mult),
      reads=["cr4"], writes=["cr4"])

    @_nd(1)
    def cast_w(e, dma):
        dma(w_in_b[:, :], w_in[:, :])
    O("pool", cast_w, writes=["w_in_b"], dma=True)

    def cast_big(dst, src, rows, name):
        step = 512
        for r in range(0, rows, step):
            @_nd(1)
            def f(e, dma, r=r):
                dma(dst[r:r + step, :], src[r:r + step, :])
            O("pool", f, writes=[name + f"_{r}"], dma=True)
    if STAGE >= 8:
        cast_big(wo_b, w_out, D, "wo_b")
        cast_big(wg_b, w_gate, D, "wg_b")
        cast_big(wu_b, w_up, D, "wu_b")
        cast_big(wd_b, w_down, DFF, "wd_b")

    def phase1():
        es = ExitStack()

        def sb(name, shape, dt=F32):
            return es.enter_context(nc.sbuf_tensor(f'ph0_' + name, list(shape), dt))

        def ps(name, shape, dt=F32):
            return es.enter_context(nc.psum_tensor(f'ph0_' + name, list(shape), dt))

        wb = sb("wb", [128, 16, NCOL], BF16)
        c_gmix = sb("c_gmix", [128, D])

        @_nd(2)
        def ld_w(e, dma):
            dma(wb[:, :, :], w_in_b.rearrange("(kc p) n -> p kc n", p=128))
            dma(c_gmix[:, :], gmix[0:1, :].partition_broadcast(128))
        O("sp", ld_w, reads=["w_in_b"], writes=["wb"], dma=True)

        hT = [sb(f"hT{i}", [128, 16, 512], BF16) for i in range(2)]
        mmc = [0]
        rc_ = {}
        xt = [sb(f"xt{i}", [128, D]) for i in range(4)]
        hb = [sb(f"hb{i}", [128, D], BF16) for i in range(4)]
        ss = [sb(f"ss{i}", [128, 2]) for i in range(4)]
        psT = [ps(f"psT{i}", [128, 4, 128], BF16) for i in range(2)]
        psA = [ps(f"psA{i}", [128, 512]) for i in range(3)]
        psB = ps("psB", [128, 512])
        psF = [ps(f"psF{i}", [128, 4, 128]) for i in range(2)]
        sq = [sb(f"sq{i}", [128, 512], BF16) for i in range(2)]
        rs = [sb(f"rs{i}", [128, 512]) for i in range(2)]
        qn = [sb(f"qn{i}", [128, 512], BF16) for i in range(2)]
        kf = [sb(f"kf{i}", [128, 512]) for i in range(3)]
        kraw = [sb(f"kraw{i}", [128, 512]) for i in range(2)]
        tokf = [sb(f"tokf{i}", [128, 4, 128]) for i in range(2)]
        tokb = [sb(f"tokb{i}", [128, 4, 128], BF16) for i in range(2)]
        fd = sb("fd", [8, 512])
        fdt = sb("fdt", [128, 4, 8])
        xpad = sb("xpad", [128, 4, 536])
        acc = [sb(f"acc{i}", [128, 512]) for i in range(3)]
        ub = [sb(f"ub{i}", [128, 512], BF16) for i in range(3)]
        O("dve", lambda e: e.memset(xpad[:, :, :], 0.0), writes=[f"xpad{ci}" for ci in range(4)])

        def part_a_norm(tg):
            t0 = tg * 512
            for ti in range(4):
                r0 = t0 + ti * 128

                @_nd(1)
                def ldx(e, dma, ti=ti, r0=r0):
                    dma(xt[ti][:, :], x_all[r0:r0 + 128, :])
                O("sp", ldx, writes=[f"xt{ti}"], dma=True)
                O("act", lambda e, ti=ti: e.activation(out=hb[ti][:, :], in_=xt[ti][:, :], func=AF.Square,
                                                      accum_out=ss[ti][:, 0:1]),
                  reads=[f"xt{ti}"], writes=[f"hb{ti}", f"ss{ti}a"])
                O("act", lambda e, ti=ti: e.activation(out=ss[ti][:, 1:2], in_=ss[ti][:, 0:1], func=AF.Sqrt,
                                                      bias=EPS, scale=1.0 / D),
                  reads=[f"ss{ti}a"], writes=[f"ss{ti}b"])
                O("dve", lambda e, ti=ti: e.reciprocal(out=ss[ti][:, 1:2], in_=ss[ti][:, 1:2]),
                  reads=[f"ss{ti}b"], writes=[f"ss{ti}b"])
                O("dve", lambda e, ti=ti: e.scalar_tensor_tensor(out=hb[ti][:, :], in0=xt[ti][:, :], scalar=ss[ti][:, 1:2],
                                                                 in1=c_gmix[:, :], op0=ALU.mult, op1=ALU.mult),
                  reads=[f"xt{ti}", f"ss{ti}b", "wb"], writes=[f"hb{ti}"])

        def part_a_tr(tg):
            hTb = hT[tg % 2]
            for ti in range(4):
                for k4 in range(4):
                    p2 = nxt() % 2

                    def tr(e, ti=ti, k4=k4, p2=p2):
                        for j in range(4):
                            kc = k4 * 4 + j
                            ins = e.transpose(out=psT[p2][:, j, :], in_=hb[ti][:, kc * 128:(kc + 1) * 128],
                                              identity=c_idb[:, :])
                        return ins
                    O("pe", tr, reads=[f"hb{ti}", "const"], writes=[f"psT{p2}"])
                    if k4 % 2:
                        O("act", lambda e, k4=k4, p2=p2, ti=ti, hTb=hTb: e.copy(
                            out=hTb[:, k4 * 4:(k4 + 1) * 4, ti * 128:(ti + 1) * 128], in_=psT[p2][:, :, :]),
                          reads=[f"psT{p2}"], writes=[f"hT{tg % 2}_{ti}_{k4}"])
                    else:
                        O("dve", lambda e, k4=k4, p2=p2, ti=ti, hTb=hTb: e.tensor_copy(
                            out=hTb[:, k4 * 4:(k4 + 1) * 4, ti * 128:(ti + 1) * 128], in_=psT[p2][:, :, :]),
                          reads=[f"psT{p2}"], writes=[f"hT{tg % 2}_{ti}_{k4}"])

        part_a_norm(0)
        part_a_tr(0)
        for tg in range(NG):
            t0 = tg * 512
            is_s = tg >= NGP
            hTb = hT[tg % 2]
            hT_res = [f"hT{tg % 2}_{ti}_{k4}" for ti in range(4) for k4 in range(4)]

            def mm_chunk(c0, m, pst, hTb=hTb):
                def f(e):
                    for kc in range(16):
                        ins = e.matmul(pst[0:m, :], lhsT=wb[:, kc, c0:c0 + m], rhs=hTb[:, kc, :],
                                       start=(kc == 0), stop=(kc == 15))
                    return ins
                return f

            def tr4_f32(src, f2):
                def f(e):
                    for j in range(4):
                        ins = e.transpose(out=psF[f2][:, j, :], in_=src[:, j * 128:(j + 1) * 128], identity=c_idf[:, :])
                    return ins
                return f

            chunks = []

            def rot(name):
                rc_[name] = rc_.get(name, 0) + 1
                return rc_[name] % 2

            def rot3(name):
                rc_[name] = rc_.get(name, 0) + 1
                return rc_[name] % 3

            def qk1(a3, cc, tg=tg, t0=t0):
                isq = cc < 2
                hp = cc % 2
                a2 = rot("sq")
                O("act", lambda e, a2=a2, a3=a3: e.activation(out=sq[a2][:, :], in_=psA[a3][:, :], func=AF.Square),
                  reads=[f"psA{a3}"], writes=[f"sq{a2}"])
                r2 = rot("kraw")
                O("act", lambda e, r2=r2, a3=a3: e.copy(out=kraw[r2][:, :], in_=psA[a3][:, :]),
                  reads=[f"psA{a3}"], writes=[f"kraw{r2}"])
                O("pe", lambda e, a2=a2: e.matmul(psB[:, :], lhsT=c_bd[:, :], rhs=sq[a2][:, :], start=True, stop=True),
                  reads=[f"sq{a2}", "const"], writes=["psB"])
                O("act", lambda e, a2=a2: e.activation(out=rs[a2][:, :], in_=psB[:, :], func=AF.Sqrt,
                                                      bias=64.0 * EPS, scale=1.0),
                  reads=["psB"], writes=[f"rs{a2}"])
                O("dve", lambda e, a2=a2: e.reciprocal(out=rs[a2][:, :], in_=rs[a2][:, :]),
                  reads=[f"rs{a2}"], writes=[f"rs{a2}"])
                if isq:
                    q2 = rot("qn")
                    O("dve", lambda e, a2=a2, r2=r2, q2=q2: e.scalar_tensor_tensor(out=qn[q2][:, :], in0=kraw[r2][:, :], scalar=c_cols[:, 0:1],
                                                                                 in1=rs[a2][:, :], op0=ALU.mult, op1=ALU.mult),
                      reads=[f"kraw{r2}", f"rs{a2}", "const"], writes=[f"qn{q2}"])

                    @_nd(2)
                    def stq(e, dma, q2=q2, hp=hp, t0=t0):
                        for hh in range(2):
                            dma(QT[0:64, hp * 2 + hh, t0:t0 + 512], qn[q2][hh * 64:(hh + 1) * 64, :])
                    O("sp", stq, reads=[f"qn{q2}"], writes=[f"QT{tg}"], dma=True)
                    return None
                k2 = rot3("kf")
                O("dve", lambda e, a2=a2, r2=r2, k2=k2: e.scalar_tensor_tensor(out=kf[k2][:, :], in0=kraw[r2][:, :], scalar=c_fd[:, 1:2],
                                                                             in1=rs[a2][:, :], op0=ALU.mult, op1=ALU.mult),
                  reads=[f"kraw{r2}", f"rs{a2}", "cfd1"], writes=[f"kf{k2}"])
                return (k2, hp)

            def qk2(st, tg=tg, t0=t0):
                if st is None:
                    return
                k2, hp = st
                q2 = rot("qn")
                O("act", lambda e, k2=k2, q2=q2: e.copy(out=qn[q2][:, :], in_=kf[k2][:, :]),
                  reads=[f"kf{k2}"], writes=[f"qn{q2}"])

                @_nd(2)
                def stk(e, dma, q2=q2, hp=hp, t0=t0):
                    for hh in range(2):
                        dma(KTn[0:64, hp * 2 + hh, t0:t0 + 512], qn[q2][hh * 64:(hh + 1) * 64, :])
                O("sp", stk, reads=[f"qn{q2}"], writes=[f"KTn{tg}"], dma=True)
                f2 = rot("psF")
                O("pe", tr4_f32(kf[k2], f2), reads=[f"kf{k2}", "const"], writes=[f"psF{f2}"])
                O("act", lambda e, f2=f2: e.copy(out=tokf[f2][:, :, :], in_=psF[f2][:, :, :]),
                  reads=[f"psF{f2}"], writes=[f"tokf{f2}"])

                @_nd(1)
                def stko(e, dma, f2=f2, hp=hp, t0=t0):
                    dma(k_out[t0:t0 + 512, hp * 128:(hp + 1) * 128].rearrange("(j p) c -> p j c", p=128), tokf[f2][:, :, :])
                O("sp", stko, reads=[f"tokf{f2}"], writes=["k_out"], dma=True)

            def v1(a3, hp):
                k2 = rot3("kf")
                O("act", lambda e, k2=k2, a3=a3: e.copy(out=kf[k2][:, :], in_=psA[a3][:, :]),
                  reads=[f"psA{a3}"], writes=[f"kf{k2}"])
                return (k2, hp)

            def v2(st, tg=tg, t0=t0):
                k2, hp = st
                f2 = rot("psF")
                O("pe", tr4_f32(kf[k2], f2), reads=[f"kf{k2}", "const"], writes=[f"psF{f2}"])
                O("act", lambda e, f2=f2: e.copy(out=tokf[f2][:, :, :], in_=psF[f2][:, :, :]),
                  reads=[f"psF{f2}"], writes=[f"tokf{f2}"])

                @_nd(1)
                def stvo(e, dma, f2=f2, hp=hp, t0=t0):
                    dma(v_out[t0:t0 + 512, hp * 128:(hp + 1) * 128].rearrange("(j p) c -> p j c", p=128), tokf[f2][:, :, :])
                O("sp", stvo, reads=[f"tokf{f2}"], writes=[f"v_out{tg}_{hp}"], dma=True)

                @_nd(1)
                def stvb(e, dma, hp=hp, t0=t0):
                    dma(Vb[t0:t0 + 512, hp * 128:(hp + 1) * 128], v_out[t0:t0 + 512, hp * 128:(hp + 1) * 128])
                O("pool", stvb, reads=[f"v_out{tg}_{hp}"], writes=[f"Vb{tg}"], dma=True)

            def z1(a3, hp):
                k2 = rot3("kf")
                O("act", lambda e, k2=k2, a3=a3: e.activation(out=kf[k2][:, :], in_=psA[a3][:, :], func=AF.Silu),
                  reads=[f"psA{a3}"], writes=[f"kf{k2}"])
                return (k2, hp)

            def z2(st, tg=tg, t0=t0):
                k2, hp = st
                f2 = rot("psF")
                O("pe", tr4_f32(kf[k2], f2), reads=[f"kf{k2}", "const"], writes=[f"psF{f2}"])
                O("act", lambda e, f2=f2: e.copy(out=tokf[f2][:, :, :], in_=psF[f2][:, :, :]),
                  reads=[f"psF{f2}"], writes=[f"tokf{f2}"])

                @_nd(1)
                def stz(e, dma, f2=f2, hp=hp, t0=t0):
                    dma(ZS[t0:t0 + 512, hp * 128:(hp + 1) * 128].rearrange("(j p) c -> p j c", p=128), tokf[f2][:, :, :])
                O("sp", stz, reads=[f"tokf{f2}"], writes=[f"ZS{tg}"], dma=True)

            nseg, L = (NSQ, TS) if is_s else (1, 512)
            W = L + 3

            def pre_conv(ci, nseg=nseg, W=W, is_s=is_s):
                if is_s:
                    xp3 = xpad[:, ci, 0:nseg * W].rearrange("p (s l) -> p s l", l=W)

                    @_nd(1)
                    def ldhalo(e, dma, ci=ci, xp3=xp3):
                        dma(xp3[:, :, 0:3], c_convT[ci * 128:(ci + 1) * 128, :, :])
                    O("sp", ldhalo, writes=[f"xpad{ci}"], dma=True)

            def conv1(a3, ci, tg=tg, t0=t0, nseg=nseg, L=L, W=W, is_s=is_s):
                xp3 = xpad[:, ci, 0:nseg * W].rearrange("p (s l) -> p s l", l=W)
                O("act", lambda e, a3=a3, xp3=xp3, L=L: e.copy(out=xp3[:, :, 3:3 + L],
                                                              in_=psA[a3][:, :].rearrange("p (s l) -> p s l", l=L)),
                  reads=[f"psA{a3}"], writes=[f"xpad{ci}"])
                c2 = rot3("acc")
                acc3 = acc[c2][:, :].rearrange("p (s l) -> p s l", l=L)
                O("dve", lambda e, acc3=acc3, xp3=xp3, ci=ci, L=L: e.tensor_scalar(
                    out=acc3, in0=xp3[:, :, 3:3 + L], scalar1=c_wc[:, ci, 3:4], scalar2=c_wc[:, ci, 4:5],
                    op0=ALU.mult, op1=ALU.add), reads=[f"xpad{ci}", "const"], writes=[f"acc{c2}"])
                for j in range(1, 4):
                    O("dve", lambda e, acc3=acc3, xp3=xp3, ci=ci, L=L, j=j: e.scalar_tensor_tensor(
                        out=acc3, in0=xp3[:, :, 3 - j:3 - j + L], scalar=c_wc[:, ci, 3 - j:4 - j], in1=acc3,
                        op0=ALU.mult, op1=ALU.add), reads=[f"xpad{ci}", "const", f"acc{c2}"], writes=[f"acc{c2}"])
                O("act", lambda e, c2=c2: e.activation(out=ub[c2][:, :], in_=acc[c2][:, :], func=AF.Silu),
                  reads=[f"acc{c2}"], writes=[f"ub{c2}"])
                if ci >= 2:
                    @_nd(1)
                    def stut(e, dma, c2=c2, ci=ci, t0=t0):
                        dma(UT[(ci - 2) * 128:(ci - 1) * 128, t0:t0 + 512], ub[c2][:, :])
                    O("sp", stut, reads=[f"ub{c2}"], writes=[f"UT{tg}"], dma=True)
                if is_s:
                    @_nd(1)
                    def stcs(e, dma, ci=ci, xp3=xp3):
                        dma(conv_s[ci * 128:(ci + 1) * 128, :, :], xp3[:, :, TS:TS + 3])
                    O("sp", stcs, reads=[f"xpad{ci}"], writes=["conv_s"], dma=True)
                else:
                    if tg == NGP - 1:
                        @_nd(1)
                        def stcp(e, dma, ci=ci):
                            dma(conv_p[ci * 128:(ci + 1) * 128, :], xpad[:, ci, 512:515])
                        O("sp", stcp, reads=[f"xpad{ci}"], writes=["conv_p"], dma=True)
                    O("dve", lambda e, ci=ci: e.tensor_copy(out=xpad[:, ci, 0:3], in_=xpad[:, ci, 512:515]),
                      reads=[f"xpad{ci}"], writes=[f"xpad{ci}"])
                return (c2, ci)

            def conv2(st, tg=tg, t0=t0):
                c2, ci = st
                if ci < 3:
                    p2 = rot("psT")

                    def tru(e, c2=c2, p2=p2):
                        for j in range(4):
                            ins = e.transpose(out=psT[p2][:, j, :], in_=ub[c2][:, j * 128:(j + 1) * 128], identity=c_idb[:, :])
                        return ins
                    O("pe", tru, reads=[f"ub{c2}", "const"], writes=[f"psT{p2}"])
                    b2 = rot("tokb")
                    O("act", lambda e, p2=p2, b2=b2: e.copy(out=tokb[b2][:, :, :], in_=psT[p2][:, :, :]),
                      reads=[f"psT{p2}"], writes=[f"tokb{b2}"])

                    @_nd(1)
                    def stutok(e, dma, b2=b2, ci=ci, t0=t0):
                        dma(Utok[t0:t0 + 512, ci * 128:(ci + 1) * 128].rearrange("(j p) c -> p j c", p=128), tokb[b2][:, :, :])
                    O("sp", stutok, reads=[f"tokb{b2}"], writes=[f"Utok{tg}"], dma=True)

            def fd1(a3):
                O("act", lambda e, a3=a3: e.activation(out=fd[:, :], in_=psA[a3][0:8, :], func=AF.Exp,
                                                      bias=c_fd[0:8, 0:1], scale=c_cols[0:8, 2:3]),
                  reads=[f"psA{a3}", "cfd0", "const"], writes=["fd"])
                O("act", lambda e: e.activation(out=fd[:, :], in_=fd[:, :], func=AF.Ln, bias=1.0, scale=1.0),
                  reads=["fd"], writes=["fd"])
                O("dve", lambda e: e.tensor_scalar(out=fd[:, :], in0=fd[:, :], scalar1=c_cols[0:8, 4:5], scalar2=None, op0=ALU.mult),
                  reads=["fd", "const"], writes=["fd"])
                return 0

            def fd2(st, tg=tg, t0=t0):
                f2 = rot("psF")

                def trf(e, f2=f2):
                    for j in range(4):
                        ins = e.transpose(out=psF[f2][:, j, 0:8], in_=fd[0:8, j * 128:(j + 1) * 128], identity=c_idf[0:8, 0:8])
                    return ins
                O("pe", trf, reads=["fd", "const"], writes=[f"psF{f2}"])
                O("act", lambda e, f2=f2: e.copy(out=fdt[:, :, :], in_=psF[f2][:, :, 0:8]),
                  reads=[f"psF{f2}"], writes=["fdt"])

                @_nd(2)
                def stfd(e, dma, t0=t0):
                    dma(FD[t0:t0 + 512, :].rearrange("(j p) c -> p j c", p=128), fdt[:, :, :])
                    dma(lf_out[t0:t0 + 512, :].rearrange("(j p) c -> p j c", p=128), fdt[:, :, 0:4])
                O("sp", stfd, reads=["fdt"], writes=[f"FD{tg}", "lf_out"], dma=True)

            chunks.append((1536, 8, None, (lambda a3: fd1(a3)), fd2))
            for cc in range(4):
                chunks.append((cc * 128, 128, None, (lambda a3, cc=cc: qk1(a3, cc)), qk2))
            for hp in range(2):
                chunks.append((512 + hp * 128, 128, None, (lambda a3, hp=hp: v1(a3, hp)), v2))
            for hp in range(2):
                chunks.append((768 + hp * 128, 128, None, (lambda a3, hp=hp: z1(a3, hp)), z2))
            for ci in range(4):
                chunks.append((1024 + ci * 128, 128, (lambda ci=ci: pre_conv(ci)), (lambda a3, ci=ci: conv1(a3, ci)), conv2))

            def emit_mm(ci_):
                c0, m, pre, p1_, p2_ = chunks[ci_]
                a3 = mmc[0] % 3
                mmc[0] += 1
                if pre is not None:
                    pre()
                O("pe", mm_chunk(c0, m, psA[a3]), reads=hT_res + ["wb"], writes=[f"psA{a3}"])
                return a3
            ncn = len(chunks)
            a3s = {0: emit_mm(0), 1: emit_mm(1)}
            sts = {}
            for ci_ in range(ncn + 2):
                if ci_ + 2 < ncn:
                    a3s[ci_ + 2] = emit_mm(ci_ + 2)
                if ci_ == 1 and tg + 1 < NG:
                    part_a_norm(tg + 1)
                if ci_ == 8 and tg + 1 < NG:
                    part_a_tr(tg + 1)
                if ci_ < ncn:
                    sts[ci_] = chunks[ci_][3](a3s[ci_])
                if ci_ >= 2:
                    chunks[ci_ - 2][4](sts[ci_ - 2])
        tk.barrier()
        tk.flush(sems, dsems)
        es.close()

    phase1()

    def phase1b():
        es = ExitStack()

        def sb(name, shape, dt=F32):
            return es.enter_context(nc.sbuf_tensor(f'ph1_' + name, list(shape), dt))

        def ps(name, shape, dt=F32):
            return es.enter_context(nc.psum_tensor(f'ph1_' + name, list(shape), dt))

        NBmax = max(T // 128, NBS)
        LF = sb("LF", [128, NBmax, 4])
        A = [sb(f"scanA{i}", [128, NBmax, 4]) for i in range(2)]
        Ff = sb("Ff", [128, NBmax, 4])
        R1 = sb("R1", [128, NBmax, 4])
        HF = sb("HF", [128, NBmax, 4])
        AQ = sb("AQ", [128, NBmax, 24], BF16)
        AK = sb("AK", [128, NBmax, 24], BF16)
        stg = [sb(f"stg{i}", [24, 4, 128], BF16) for i in range(2)]
        psW = ps("psW", [128, 512])
        psTot = ps("psTot", [128, 512])
        psT = [ps(f"psT{i}", [128, 4, 128], BF16) for i in range(2)]
        ck = [sb(f"ck{i}", [128, 4, 256]) for i in range(4)]
        ckb = [sb(f"ckb{i}", [128, 4, 256], BF16) for i in range(2)]
        kst = [sb(f"kst{i}", [128, 4, 128], BF16) for i in range(2)]
        ckc = [0]
        AQv = AQ[:, :, :].rearrange("p b (i h) -> p b i h", h=4)
        AKv = AK[:, :, :].rearrange("p b (i h) -> p b i h", h=4)
        O("dve", lambda e: e.memset(AQ[:, :, :], -1.0), writes=["AQ"])
        O("dve", lambda e: e.memset(AK[:, :, :], 1.0), writes=["AK"])

        def cumsum_and_aug(nb, lf_reads):
            n4 = nb * 4
            O("pe", lambda e: e.matmul(psW[:, 0:n4], lhsT=c_trf[:, :], rhs=LF[:, 0:nb, :], start=True, stop=True),
              reads=lf_reads + ["const"], writes=["psW"])
            O("pe", lambda e: e.matmul(psTot[:, 0:n4], lhsT=c_onf[:, :], rhs=LF[:, 0:nb, :], start=True, stop=True),
              reads=lf_reads + ["const"], writes=["psTot"])
            O("act", lambda e: e.copy(out=A[0][:, 0:nb, :], in_=psTot[:, 0:n4].rearrange("p (b h) -> p b h", h=4)),
              reads=["psTot"], writes=["scanA0"])
            cur = 0
            d = 1
            while d < nb:
                o = 1 - cur
                O("dve", lambda e, cur=cur, o=o, d=d: e.tensor_tensor(out=A[o][:, d:nb, :], in0=A[cur][:, d:nb, :],
                                                                    in1=A[cur][:, 0:nb - d, :], op=ALU.add),
                  reads=[f"scanA{cur}"], writes=[f"scanA{o}"])
                O("dve", lambda e, cur=cur, o=o, d=d: e.tensor_copy(out=A[o][:, 0:d, :], in_=A[cur][:, 0:d, :]),
                  reads=[f"scanA{cur}", f"scanA{o}"], writes=[f"scanA{o}"])
                cur = o
                d *= 2
            O("dve", lambda e, cur=cur: e.tensor_tensor(out=R1[:, 0:nb, :], in0=A[cur][:, 0:nb, :],
                                                       in1=psTot[:, 0:n4].rearrange("p (b h) -> p b h", h=4), op=ALU.subtract),
              reads=[f"scanA{cur}", "psTot"], writes=["R1"])
            O("dve", lambda e: e.tensor_tensor(out=Ff[:, 0:nb, :], in0=R1[:, 0:nb, :],
                                              in1=psW[:, 0:n4].rearrange("p (b h) -> p b h", h=4), op=ALU.add),
              reads=["R1", "psW"], writes=["Ff"])
            O("act", lambda e: e.copy(out=AQv[:, 0:nb, 0, :], in_=Ff[:, 0:nb, :]), reads=["Ff", "AQ"], writes=["AQ"])
            O("act", lambda e: e.copy(out=HF[:, 0:nb, :], in_=AQv[:, 0:nb, 0, :]), reads=["AQ"], writes=["HF"])
            O("dve", lambda e: e.tensor_tensor(out=R1[:, 0:nb, :], in0=Ff[:, 0:nb, :], in1=HF[:, 0:nb, :], op=ALU.subtract),
              reads=["Ff", "HF"], writes=["R1"])
            O("act", lambda e: e.copy(out=AQv[:, 0:nb, 1, :], in_=R1[:, 0:nb, :]), reads=["R1", "AQ"], writes=["AQ"])
            O("act", lambda e: e.copy(out=HF[:, 0:nb, :], in_=AQv[:, 0:nb, 1, :]), reads=["AQ"], writes=["HF"])
            O("dve", lambda e: e.tensor_tensor(out=R1[:, 0:nb, :], in0=R1[:, 0:nb, :], in1=HF[:, 0:nb, :], op=ALU.subtract),
              reads=["R1", "HF"], writes=["R1"])
            O("act", lambda e: e.copy(out=AQv[:, 0:nb, 2, :], in_=R1[:, 0:nb, :]), reads=["R1", "AQ"], writes=["AQ"])
            O("act", lambda e: e.copy(out=AKv[:, 0:nb, 3:6, :], in_=AQv[:, 0:nb, 0:3, :]), reads=["AQ", "AK"], writes=["AK"])

        def aug_store(src, srcname, b0, nbb, dst_fn, npart=128, ncol=128):
            p2 = nxt() % 2

            def trp(e, p2=p2):
                for j in range(nbb):
                    ins = e.transpose(out=psT[p2][0:24, j, 0:npart], in_=src[0:npart, b0 + j, :], identity=c_idb[0:npart, 0:npart])
                return ins
            O("pe", trp, reads=[srcname, "const"], writes=[f"psT{p2}"])
            O("act", lambda e, p2=p2: e.copy(out=stg[p2][:, 0:nbb, 0:ncol], in_=psT[p2][0:24, 0:nbb, 0:ncol]),
              reads=[f"psT{p2}"], writes=[f"stg{p2}"])
            O("sp", _nd(1)(lambda e, dma, p2=p2: dma(dst_fn(), stg[p2][:, 0:nbb, 0:ncol])),
              reads=[f"stg{p2}"], writes=[f"aug_{nxt()}"], dma=True)

        NB = T // 128
        for b0 in range(0, NB, 8):
            O("sp", _nd(1)(lambda e, dma, b0=b0: dma(
                LF[:, b0:b0 + 8, :], FD[b0 * 128:(b0 + 8) * 128, 0:4].rearrange("(b p) c -> p b c", p=128))),
              reads=[f"FD{tg}" for tg in range(NGP)] + (["LF"] if b0 else []), writes=["LF"], dma=True)
        cumsum_and_aug(NB, ["LF"])
        for b0 in range(0, NB, 4):
            aug_store(AQ, "AQ", b0, 4, lambda b0=b0: QT[64:70, :, b0 * 128:(b0 + 4) * 128].rearrange("i h (j p) -> (i h) j p", p=128))
            aug_store(AK, "AK", b0, 4, lambda b0=b0: KTn[64:70, :, b0 * 128:(b0 + 4) * 128].rearrange("i h (j p) -> (i h) j p", p=128))
        NBP = past // 128
        for s in range(NSQ):
            ts0 = T + s * TS
            O("dve", lambda e: e.memset(LF[:, 0:NBS, :], 0.0), reads=["LF"], writes=["LF"])

            @_nd(3)
            def ldlf(e, dma, s=s, ts0=ts0):
                dma(LF[:, 0:NBP // 2, :], c_lf[s, 0:past // 2, :].rearrange("(b p) c -> p b c", p=128))
                dma(LF[:, NBP // 2:NBP, :], c_lf[s, past // 2:past, :].rearrange("(b p) c -> p b c", p=128))
                dma(LF[0:TS, NBP, :], FD[ts0:ts0 + TS, 0:4])
            O("sp", ldlf, reads=[f"FD{NG - 1}"], writes=["LF"], dma=True)
            cumsum_and_aug(NBS, ["LF"])
            for b0 in range(0, NBS, 4):
                nbb = min(4, NBS - b0)
                aug_store(AK, "AK", b0, nbb, lambda b0=b0, nbb=nbb, s=s:
                          KTs[s, 64:70, :, b0 * 128:(b0 + nbb) * 128].rearrange("i h (j p) -> (i h) j p", p=128))
            aug_store(AQ, "AQ", NBP, 1, lambda ts0=ts0: QT[64:70, :, ts0:ts0 + TS].rearrange("i h (j p) -> (i h) j p", p=TS),
                      npart=TS, ncol=TS)
            for b0 in range(0, NBP, 4):
                ckc[0] += 1
                i4 = ckc[0] % 4
                i2 = ckc[0] % 2
                O("pool", _nd(1)(lambda e, dma, i4=i4, s=s, b0=b0: dma(
                    ck[i4][:, :, :], c_k[s, b0 * 128:(b0 + 4) * 128, :].rearrange("(j p) c -> p j c", p=128))),
                  writes=[f"ck{i4}"], dma=True)
                O("act", lambda e, i2=i2, i4=i4: e.copy(out=ckb[i2][:, :, :], in_=ck[i4][:, :, :]), reads=[f"ck{i4}"], writes=[f"ckb{i2}"])
                for hp in range(2):
                    p2 = nxt() % 2

                    def trk(e, i2=i2, hp=hp, p2=p2):
                        for j in range(4):
                            ins = e.transpose(out=psT[p2][:, j, :], in_=ckb[i2][:, j, hp * 128:(hp + 1) * 128], identity=c_idb[:, :])
                        return ins
                    O("pe", trk, reads=[f"ckb{i2}", "const"], writes=[f"psT{p2}"])
                    k2 = nxt() % 2
                    O("dve", lambda e, p2=p2, k2=k2: e.tensor_copy(out=kst[k2][:, :, :], in_=psT[p2][:, :, :]),
                      reads=[f"psT{p2}"], writes=[f"kst{k2}"])

                    @_nd(2)
                    def stks(e, dma, k2=k2, hp=hp, s=s, b0=b0):
                        for hh in range(2):
                            dma(KTs[s, 0:64, hp * 2 + hh, b0 * 128:(b0 + 4) * 128].rearrange("d (j p) -> d j p", p=128),
                                kst[k2][hh * 64:(hh + 1) * 64, :, :])
                    O("sp", stks, reads=[f"kst{k2}"], writes=[f"KTs{s}"], dma=True)
            O("sp", _nd(1)(lambda e, dma, s=s, ts0=ts0: dma(KTs[s, 0:64, :, past:past + TS], KTn[0:64, :, ts0:ts0 + TS])),
              reads=[f"KTn{NG - 1}"], writes=[f"KTs{s}"], dma=True)
            O("pool", _nd(1)(lambda e, dma, s=s: dma(Vs[s, 0:past, :], c_v[s, :, :])), writes=[f"Vs{s}"], dma=True)
            O("sp", _nd(1)(lambda e, dma, s=s, ts0=ts0: dma(Vs[s, past:past + TS, :], Vb[ts0:ts0 + TS, :])),
              reads=[f"Vb{NG - 1}"], writes=[f"Vs{s}"], dma=True)
        tk.barrier()
        tk.flush(sems, dsems)
        es.close()

    if STAGE >= 2:
        phase1b()

    RG = [[0, 1, 2, 3], [4, 5, 6, 7]]
    EARLY_AG = False
    deferred = list(range(NCH))

    def phase23():
        es = ExitStack()

        def sb(name, shape, dt=F32):
            return es.enter_context(nc.sbuf_tensor('p23_' + name, list(shape), dt))

        def ps(name, shape, dt=F32):
            return es.enter_context(nc.psum_tensor('p23_' + name, list(shape), dt))

        TK = max(T, TKS)
        qt = sb("qt", [128, T], BF16)
        kt = sb("kt", [128, TK], BF16)
        vt = sb("vt", [128, TK // 128, 128], BF16)
        pb = [sb(f"pb{i}", [128, 512], BF16) for i in range(4)]
        rc = sb("rc", [128, 512])
        at = sb("at", [64, 512], BF16)
        att = [sb(f"att{i}", [128, 4, 64], BF16) for i in range(2)]
        psS = [ps(f"psS{i}", [128, 512]) for i in range(2)]
        psO = [ps(f"psO{i}", [128, 512]) for i in range(1)]
        tA = ps("ssd_tA", [128, 4, 64])
        tB = ps("ssd_tB", [128, 256])
        tC = ps("ssd_tC", [64, 256])
        psTa = ps("psTa", [128, 4, 64], BF16)
        O("dve", lambda e: e.memset(vt[:, :, 64:128], 1.0), writes=["vt1"])
        O("dve", lambda e: e.memset(qt[64:128, :], 0.0), writes=["qt"])
        O("dve", lambda e: e.memset(kt[64:128, :], 0.0), writes=["kt"])
        sbc = [0]

        def job(q_ap, k_ap, v_ap, Tq, nk, pst, e_row0, h):
            nkb = (nk + 127) // 128

            @_nd(2)
            def ld(e, dma):
                dma(qt[0:70, 0:Tq], q_ap)
                dma(kt[0:70, 0:nk], k_ap)
            O("sp", ld, writes=["qt", "kt"], dma=True)
            nfb = nk // 128
            VB = 8
            for b0 in range(0, nfb, VB):
                b1 = min(nfb, b0 + VB)
                O("sp", _nd(1)(lambda e, dma, b0=b0, b1=b1: dma(
                    vt[:, b0:b1, 0:64], v_ap[b0 * 128:b1 * 128, :].rearrange("(b p) c -> p b c", p=128))),
                  reads=["vt"] if b0 else [], writes=["vt"], dma=True)
            if nk % 128:
                nf = nk // 128
                O("sp", _nd(1)(lambda e, dma: dma(vt[0:nk - nf * 128, nf, 0:64], v_ap[nf * 128:nk, :])),
                  reads=["vt"], writes=["vt"], dma=True)
            SBQ = min(512, Tq)
            LA = 2
            tiles = []
            sbs = []
            for q0 in range(0, Tq, SBQ):
                nq = SBQ
                sbc[0] += 1
                o2 = 0
                blocks = []
                for kb in range(nkb):
                    ks = kb * 128
                    kn = min(128, nk - ks)
                    if ks + kn - 1 <= pst + q0:
                        blocks.append((kb, ks, kn, q0, False))
                    elif ks <= pst + q0 + nq - 1:
                        blocks.append((kb, ks, kn, ks - pst, True))
                sbs.append((q0, nq, o2, len(blocks)))
                for bi, blk in enumerate(blocks):
                    tiles.append((len(sbs) - 1, bi, blk))

            def emit_qk(ti):
                si, bi, (kb, ks, kn, qlo, diag) = tiles[ti]
                q0, nq, o2, nb_ = sbs[si]
                n1 = q0 + nq - qlo
                s3 = ti % 2
                p4 = ti % 4
                O("pe", lambda e, s3=s3, ks=ks, kn=kn, qlo=qlo, n1=n1: e.matmul(
                    psS[s3][0:kn, 0:n1], lhsT=kt[0:128, ks:ks + kn], rhs=qt[0:128, qlo:qlo + n1], start=True, stop=True),
                  reads=["qt", "kt"], writes=[f"psS{s3}"])
                O("act", lambda e, s3=s3, p4=p4, kn=kn, n1=n1: e.activation(
                    out=pb[p4][0:kn, 0:n1], in_=psS[s3][0:kn, 0:n1], func=AF.Exp),
                  reads=[f"psS{s3}"], writes=[f"pb{p4}"])
                if diag:
                    w = min(128, n1)
                    O("pool", lambda e, p4=p4, kn=kn, w=w: e.tensor_tensor(
                        out=pb[p4][0:kn, 0:w], in0=pb[p4][0:kn, 0:w], in1=c_trb[0:kn, 0:w], op=ALU.mult),
                      reads=[f"pb{p4}", "const"], writes=[f"pb{p4}"])

            def emit_pv(ti):
                si, bi, (kb, ks, kn, qlo, diag) = tiles[ti]
                q0, nq, o2, nb_ = sbs[si]
                n1 = q0 + nq - qlo
                p4 = ti % 4
                O("pe", lambda e, o2=o2, p4=p4, kb=kb, kn=kn, qlo=qlo, n1=n1, q0=q0, nq=nq, bi=bi, nb_=nb_: e.matmul(
                    psO[o2][:, qlo - q0:nq], lhsT=vt[0:kn, kb, :], rhs=pb[p4][0:kn, 0:n1],
                    start=(bi == 0), stop=(bi == nb_ - 1)),
                  reads=[f"pb{p4}", "vt", "vt1"], writes=[f"psO{o2}"])
                if bi == nb_ - 1:
                    finish(si)

            def finish(si):
                q0, nq, o2, nb_ = sbs[si]
                O("dve", lambda e, o2=o2, nq=nq: e.reciprocal(out=rc[64:128, 0:nq], in_=psO[o2][64:128, 0:nq]),
                  reads=[f"psO{o2}"], writes=["rc"])
                O("dve", lambda e, o2=o2, nq=nq: e.tensor_tensor(out=at[:, 0:nq], in0=psO[o2][0:64, 0:nq], in1=rc[64:128, 0:nq], op=ALU.mult),
                  reads=[f"psO{o2}", "rc"], writes=["at"])
                nj = (nq + 127) // 128
                wj = min(128, nq)
                p2 = 0

                def tra(e, p2=p2, nj=nj, wj=wj):
                    for j in range(nj):
                        ins = e.transpose(out=psTa[0:wj, j, 0:64], in_=at[:, j * 128:j * 128 + wj], identity=c_idb[0:64, 0:64])
                    return ins
                O("pe", tra, reads=["at", "const"], writes=["psTa"])
                a2 = nxt() % 2
                O("act", lambda e, p2=p2, a2=a2, nj=nj, wj=wj: e.copy(out=att[a2][0:wj, 0:nj, :], in_=psTa[0:wj, 0:nj, 0:64]),
                  reads=["psTa"], writes=[f"att{a2}"])
                r0 = e_row0 + q0
                O("sp", _nd(1)(lambda e, dma, a2=a2, r0=r0, nq=nq, nj=nj, wj=wj, h=h: dma(
                    E[r0:r0 + nq, h * 64:(h + 1) * 64].rearrange("(j p) c -> p j c", p=wj), att[a2][0:wj, 0:nj, :])),
                  reads=[f"att{a2}"], writes=[f"Ea_{r0}_{h}"], dma=True)
                if EARLY_AG and Tq == T and h == HG - 1 and (q0 // 512) % 2 == 1 and STAGE >= 5:
                    q = q0 // 1024
                    rd = [f"Ea_{rr}_{hh}" for rr in (q * 1024, q * 1024 + 512) for hh in range(HG)] + \
                         [f"Eu_{rr}" for rr in (q * 1024, q * 1024 + 512)]
                    if all(r_ in tk.res for r_ in rd):
                        O("pool", lambda e, q=q: e.collective_compute("AllGather", ALU.bypass, replica_groups=RG,
                                                                      ins=[E[q * 1024:(q + 1) * 1024, :]], outs=[Gq[q][:, :]]),
                          reads=rd, writes=[f"G{q}"])
                    else:
                        deferred.append(q)

            nt_ = len(tiles)
            for idx in range(nt_ + LA):
                if idx < nt_:
                    emit_qk(idx)
                if idx - LA >= 0:
                    emit_pv(idx - LA)
                yield

        bt = [sb(f"bt{i}", [128, 512], BF16) for i in range(2)]
        ct = [sb(f"ct{i}", [128, 512], BF16) for i in range(2)]
        xk = [sb(f"xk{i}", [64, 8, 384], BF16) for i in range(2)]
        fdm = [sb(f"fdm{i}", [64, 8, 8]) for i in range(2)]
        zs = [sb(f"zs{i}", [64, 8, 256]) for i in range(2)]
        u8 = [sb(f"u8{i}", [64, 8, 256], BF16) for i in range(2)]
        Hf = sb("Hf", [128, 256])
        Hb = sb("Hb", [128, 256], BF16)
        hio = sb("hio", [128, 2, 128])
        dA = sb("dA", [64, 4])
        Rr = sb("Rr", [64, 4, 64])
        cum = sb("cum", [64, 4])
        ecum = sb("ecum", [64, 4])
        seg = sb("seg", [64, 4, 64])
        CBm = sb("CBm", [64, 64])
        MT = sb("MT", [64, 4, 64], BF16)
        ysb = sb("ysb", [64, 256])
        wl = sb("wl", [64, 4])
        xw = sb("xw", [64, 256], BF16)
        dec = sb("dec", [128, 4])
        psR = tA
        psH = tB[:, :]
        psY = tB[0:64, :]
        psC = tB[0:64, 0:4]
        psF = tB[:, :].rearrange("p (a n) -> p a n", n=128)
        psYo = tC[:, :]
        psCB = tC[:, 0:64]

        def seq_job(r_base, ntok, init_ap, out_ap, tag):
            if init_ap is None:
                O("dve", lambda e: e.memset(Hf[:, :], 0.0), writes=["Hf"])
            else:
                O("sp", _nd(1)(lambda e, dma: dma(hio[:, :, :], init_ap.rearrange("(a p) n -> p a n", p=128))),
                  writes=["hio"], dma=True)

                def trin(e):
                    for a in range(2):
                        ins = e.transpose(out=psF[:, a, :], in_=hio[:, a, :], identity=c_idf[:, :])
                    return ins
                O("pe", trin, reads=["hio", "const"], writes=["tB"])
                O("act", lambda e: e.copy(out=Hf[:, :], in_=tB[:, :]), reads=["tB"], writes=["Hf"])
            O("act", lambda e: e.copy(out=Hb[:, :], in_=Hf[:, :]), reads=["Hf"], writes=["Hb"])
            SC = min(512, ntok)
            nchk = SC // 64
            for r0 in range(r_base, r_base + ntok, SC):
                i2 = nxt() % 2

                @_nd(5)
                def ld(e, dma, i2=i2, r0=r0):
                    dma(bt[i2][:, 0:SC], UT[0:128, r0:r0 + SC])
                    dma(ct[i2][:, 0:SC], UT[128:256, r0:r0 + SC])
                    dma(xk[i2][:, 0:nchk, :], Utok[r0:r0 + SC, :].rearrange("(c p) f -> p c f", p=64))
                    dma(fdm[i2][:, 0:nchk, :], FD[r0:r0 + SC, :].rearrange("(c p) f -> p c f", p=64))
                    dma(zs[i2][:, 0:nchk, :], ZS[r0:r0 + SC, :].rearrange("(c p) f -> p c f", p=64))
                O("sp", ld, writes=[f"ssdin{i2}"], dma=True)
                IN = f"ssdin{i2}"
                for c in range(nchk):
                    dt_ = fdm[i2][:, c, 4:8]
                    cs = slice(c * 64, (c + 1) * 64)
                    O("dve", lambda e, dt_=dt_: e.tensor_tensor(out=dA[:, :], in0=dt_, in1=c_r4[0:64, 0:4], op=ALU.mult),
                      reads=[IN, "cr4"], writes=["dA"])
                    for h in range(HG):
                        O("dve", lambda e, h=h: e.tensor_scalar(out=Rr[:, h, :], in0=c_trf[0:64, 0:64], scalar1=dA[:, h:h + 1],
                                                                scalar2=None, op0=ALU.mult),
                          reads=["dA", "const"], writes=["Rr"])
                    yield
                    O("pe", lambda e: e.matmul(psC[:, :], lhsT=c_trf[0:64, 0:64], rhs=dA[:, :], start=True, stop=True),
                      reads=["dA", "const"], writes=["tB"])
                    O("pe", lambda e: e.matmul(psR[:, :, :], lhsT=c_onf[0:64, :], rhs=Rr[:, :, :], start=True, stop=True),
                      reads=["Rr", "const"], writes=["tA"])
                    yield
                    O("act", lambda e: e.copy(out=cum[:, :], in_=psC[:, :]), reads=["tB"], writes=["cum"])
                    O("act", lambda e: e.activation(out=ecum[:, :], in_=psC[:, :], func=AF.Exp), reads=["tB"], writes=["ecum"])
                    for h in range(HG):
                        O("dve", lambda e, h=h: e.tensor_scalar(out=seg[:, h, :], in0=psR[0:64, h, :], scalar1=cum[:, h:h + 1],
                                                                scalar2=0.0, op0=ALU.subtract, op1=ALU.min),
                          reads=["tA", "cum"], writes=["seg"])
                    yield
                    O("act", lambda e: e.activation(out=seg[:, :, :], in_=seg[:, :, :], func=AF.Exp), reads=["seg"], writes=["seg"])
                    O("pe", lambda e, i2=i2, cs=cs: e.matmul(psCB[:, :], lhsT=bt[i2][:, cs], rhs=ct[i2][:, cs], start=True, stop=True),
                      reads=[IN], writes=["tC"])
                    O("dve", lambda e: e.tensor_tensor(out=CBm[:, :], in0=psCB[:, :], in1=c_trf[0:64, 0:64], op=ALU.mult),
                      reads=["tC", "const"], writes=["CBm"])
                    yield
                    for h in range(HG):
                        O("dve", lambda e, h=h, dt_=dt_: e.scalar_tensor_tensor(out=MT[:, h, :], in0=seg[:, h, :], scalar=dt_[:, h:h + 1],
                                                                               in1=CBm[:, :], op0=ALU.mult, op1=ALU.mult),
                          reads=["seg", "CBm", IN], writes=["MT"])

                    yield
                    def ydiag(e, i2=i2, c=c):
                        for h in range(HG):
                            ins = e.matmul(psY[:, h * 64:(h + 1) * 64], lhsT=MT[:, h, :], rhs=xk[i2][:, c, h * 64:(h + 1) * 64],
                                           start=True, stop=True)
                        return ins
                    O("pe", ydiag, reads=["MT", IN], writes=["tB"])
                    O("pe", lambda e, i2=i2, cs=cs: e.matmul(psYo[:, :], lhsT=ct[i2][:, cs], rhs=Hb[:, :], start=True, stop=True),
                      reads=[IN, "Hb"], writes=["tC"])
                    yield
                    O("act", lambda e: e.copy(out=ysb[:, :], in_=psY[:, :]), reads=["tB"], writes=["ysb"])
                    for h in range(HG):
                        hs = slice(h * 64, (h + 1) * 64)
                        O("dve", lambda e, h=h, hs=hs: e.scalar_tensor_tensor(out=ysb[:, hs], in0=psYo[:, hs], scalar=ecum[:, h:h + 1],
                                                                             in1=ysb[:, hs], op0=ALU.mult, op1=ALU.add),
                          reads=["tC", "ecum", "ysb"], writes=["ysb"])
                        O("dve", lambda e, h=h, hs=hs, i2=i2, c=c: e.scalar_tensor_tensor(
                            out=ysb[:, hs], in0=xk[i2][:, c, hs], scalar=c_r4[0:64, 4 + h:5 + h], in1=ysb[:, hs],
                            op0=ALU.mult, op1=ALU.add), reads=[IN, "const", "ysb"], writes=["ysb"])
                    O("dve", lambda e, i2=i2, c=c: e.tensor_tensor(out=u8[i2][:, c, :], in0=ysb[:, :], in1=zs[i2][:, c, :], op=ALU.mult),
                      reads=["ysb", IN], writes=[f"u8{i2}"])
                    yield
                    O("dve", lambda e: e.tensor_tensor(out=wl[:, :], in0=psR[0:64, :, 63], in1=cum[:, :], op=ALU.subtract),
                      reads=["tA", "cum"], writes=["wl"])
                    O("act", lambda e: e.activation(out=wl[:, :], in_=wl[:, :], func=AF.Exp), reads=["wl"], writes=["wl"])
                    O("dve", lambda e, dt_=dt_: e.tensor_tensor(out=wl[:, :], in0=wl[:, :], in1=dt_, op=ALU.mult),
                      reads=["wl", IN], writes=["wl"])
                    for h in range(HG):
                        hs = slice(h * 64, (h + 1) * 64)
                        O("dve", lambda e, h=h, hs=hs, i2=i2, c=c: e.tensor_scalar(out=xw[:, hs], in0=xk[i2][:, c, hs], scalar1=wl[:, h:h + 1],
                                                                                  scalar2=None, op0=ALU.mult),
                          reads=[IN, "wl"], writes=["xw"])
                    yield
                    O("pe", lambda e, i2=i2, c=c: e.matmul(psH[:, :], lhsT=xk[i2][:, c, 256:384], rhs=xw[:, :], start=True, stop=True),
                      reads=[IN, "xw"], writes=["tB"])
                    O("act", lambda e: e.activation(out=dec[:, :], in_=psR[:, :, 63], func=AF.Exp), reads=["tA"], writes=["dec"])
                    yield
                    for h in range(HG):
                        hs = slice(h * 64, (h + 1) * 64)
                        O("dve", lambda e, h=h, hs=hs: e.scalar_tensor_tensor(out=Hf[:, hs], in0=Hf[:, hs], scalar=dec[:, h:h + 1],
                                                                             in1=psH[:, hs], op0=ALU.mult, op1=ALU.add),
                          reads=["Hf", "dec", "tB"], writes=["Hf"])
                    O("act", lambda e: e.copy(out=Hb[:, :], in_=Hf[:, :]), reads=["Hf"], writes=["Hb"])
                yield
                O("sp", _nd(1)(lambda e, dma, i2=i2, r0=r0: dma(
                    E[r0:r0 + SC, 256:512].rearrange("(c p) f -> p c f", p=64), u8[i2][:, 0:nchk, :])),
                  reads=[f"u8{i2}"], writes=[f"Eu_{r0}"], dma=True)
            def trout(e):
                for a in range(2):
                    ins = e.transpose(out=psF[:, a, :], in_=Hf[:, a * 128:(a + 1) * 128], identity=c_idf[:, :])
                return ins
            O("pe", trout, reads=["Hf", "const"], writes=["tB"])
            O("act", lambda e: e.copy(out=hio[:, :, :], in_=psF[:, :, :]), reads=["tB"], writes=["hio"])
            O("sp", _nd(1)(lambda e, dma: dma(out_ap.rearrange("(a p) n -> p a n", p=128), hio[:, :, :])),
              reads=["hio"], writes=[f"ssm_{tag}"], dma=True)


        def att_gen():
            for h in range(HG):
                yield from job(QT[:, h, 0:T], KTn[:, h, 0:T], Vb[0:T, h * 64:(h + 1) * 64], T, T, 0, 0, h)
            for s in range(NSQ):
                ts0 = T + s * TS
                for h in range(HG):
                    yield from job(QT[:, h, ts0:ts0 + TS], KTs[s, :, h, 0:past + TS], Vs[s, 0:past + TS, h * 64:(h + 1) * 64],
                                   TS, past + TS, past, ts0, h)

        def ssd_gen():
            yield from seq_job(0, T, None, ssm_p[:, :], "p")
            for s in range(NSQ):
                yield from seq_job(T + s * TS, TS, c_ssm[s, :, :], ssm_s[s, :, :], f"s{s}")

        n_att = HG * sum((i + 1) * 4 + 4 for i in range(T // 512)) + NSQ * HG * (past // 128 + 3)
        n_ssd = (T // 64 + NSQ) * 11
        ga, gs_ = att_gen(), ssd_gen()
        da = ds = 0
        a_alive = s_alive = True
        while a_alive or s_alive:
            if a_alive:
                try:
                    next(ga)
                    da += 1
                except StopIteration:
                    a_alive = False
            while s_alive and (not a_alive or (os.environ.get('INTERLEAVE', '1') == '1' and ds * n_att <= da * n_ssd)):
                try:
                    next(gs_)
                    ds += 1
                except StopIteration:
                    s_alive = False
        tk.barrier()
        tk.flush(sems, dsems)
        es.close()

    if STAGE >= 3:
        phase23()


    if STAGE >= 5:
        for q in deferred:
            O("pool", lambda e, q=q: e.collective_compute("AllGather", ALU.bypass, replica_groups=RG,
                                                          ins=[E[q * 1024:(q + 1) * 1024, :]], outs=[Gq[q][:, :]]),
              writes=[f"G{q}"])
        O("pool", lambda e: e.collective_compute("AllGather", ALU.bypass, replica_groups=RG,
                                                 ins=[E[T:T + 512, :]], outs=[Gs[:, :]]), writes=["Gs"])

    def phase5():
        es = ExitStack()

        def sb(name, shape, dt=F32):
            return es.enter_context(nc.sbuf_tensor(f'ph4_' + name, list(shape), dt))

        def ps(name, shape, dt=F32):
            return es.enter_context(nc.psum_tensor(f'ph4_' + name, list(shape), dt))

        mixT = sb("mixT", [128, 16, 512], BF16)
        x1 = sb("x1", [128, 4, D])
        AT = sb("AT", [128, 44, 512], BF16)
        NWB = 3
        wbuf = [sb(f"wbuf{i}", [128, 8192], BF16) for i in range(NWB)]
        selT = sb("selT", [128, 8, 256], BF16)
        selS = sb("selS", [128, 4, 128], BF16)
        c_gffn = sb("c_gffn", [128, D])
        c_gssd = sb("c_gssd", [128, 8])
        junk = sb("junk", [128, D])
        hb = [sb(f"hb{i}", [128, D], BF16) for i in range(2)]
        ss = [sb(f"ss{i}", [128, 2]) for i in range(2)]
        sq = sb("sq", [128, 512], BF16)
        rstd = sb("rstd", [128, 512])
        gt = [sb(f"gt{i}", [128, 512]) for i in range(2)]
        pz = [ps(f"pz{i}", [128, 512]) for i in range(6)]
        psT = [ps(f"psT{i}", [128, 4, 128], BF16) for i in range(2)]
        zc = [0]

        def nz():
            zc[0] += 1
            return zc[0] % 6
        wc = [0]

        def wload(view_fn, src_ap, src_reads):
            i = wc[0] % NWB
            wc[0] += 1
            O("sp", _nd(1)(lambda e, dma, i=i: dma(view_fn(wbuf[i]), src_ap)), reads=src_reads, writes=[f"wbuf{i}"], dma=True)
            return i

        @_nd(4)
        def ldc(e, dma):
            dma(selT[:, :, :], sel[:, :, :])
            dma(selS[:, :, :], sel_s[:, :, :])
            dma(c_gffn[:, :], gffn[0:1, :].partition_broadcast(128))
            dma(c_gssd[:, :], gssd[:, :])
        O("sp", ldc, writes=["p5c"], dma=True)

        items = [(Gq[q], 8, selT, 2, f"G{q}") for q in range(NCH)]
        groups = [items[i:i + 2] for i in range(0, NCH, 2)] + [[(Gs, 4, selS, 1, "Gs")]]
        def do_group(grp, row):
            ntile = sum(it[3] for it in grp)
            ntok = ntile * 128
            O("sp", _nd(1)(lambda e, dma, row=row, ntile=ntile: dma(
                x1[:, 0:ntile, :], x2[row:row + ntok, :].rearrange("(j p) c -> p j c", p=128))), writes=["x1"], dma=True)
            toff = 0
            for (G, nblk, st, nt, gname) in grp:
                rows = nblk * 128
                for r in range(4):
                    wi = wload(lambda w, nblk=nblk: w[:, 0:nblk * 512].rearrange("p (b c) -> p b c", c=512),
                               G[r * rows:(r + 1) * rows, :].rearrange("(b p) c -> p b c", p=128), [gname])
                    gv = wbuf[wi][:, 0:nblk * 512].rearrange("p (b c) -> p b c", c=512)
                    for i in range(nt):
                        z = nz()
                        pzv = pz[z][:, :].rearrange("p (a t) -> p a t", t=128)

                        def selmm(e, gv=gv, st=st, nblk=nblk, i=i, pzv=pzv):
                            for cch in range(4):
                                for blk in range(nblk):
                                    ins = e.matmul(pzv[:, cch, :], lhsT=gv[:, blk, cch * 128:(cch + 1) * 128],
                                                   rhs=st[:, blk, i * 128:(i + 1) * 128], start=(blk == 0), stop=(blk == nblk - 1))
                            return ins
                        O("pe", selmm, reads=[f"wbuf{wi}", "p5c"], writes=[f"pz{z}"])
                        c0 = toff + i * 128
                        eng = "act" if (r + i) % 2 else "dve"
                        if eng == "act":
                            O("act", lambda e, r=r, c0=c0, pzv=pzv: e.copy(out=mixT[:, r * 4:r * 4 + 4, c0:c0 + 128], in_=pzv),
                              reads=[f"pz{z}"], writes=["mixT"])
                        else:
                            O("dve", lambda e, r=r, c0=c0, pzv=pzv: e.tensor_copy(out=mixT[:, r * 4:r * 4 + 4, c0:c0 + 128], in_=pzv),
                              reads=[f"pz{z}"], writes=["mixT"])
                toff += nt * 128
            for gi in range(2):
                kcs = [(2 * gi) * 4 + 2, (2 * gi) * 4 + 3, (2 * gi + 1) * 4 + 2, (2 * gi + 1) * 4 + 3]
                z = nz()
                for j, kc in enumerate(kcs):
                    O("act", lambda e, kc=kc: e.activation(out=sq[:, 0:ntok], in_=mixT[:, kc, 0:ntok], func=AF.Square),
                      reads=["mixT"], writes=["sq"])
                    O("pe", lambda e, z=z, j=j: e.matmul(pz[z][:, 0:ntok], lhsT=c_onb[:, :], rhs=sq[:, 0:ntok], start=(j == 0), stop=(j == 3)),
                      reads=["sq", "const"], writes=[f"pz{z}"])
                O("act", lambda e, z=z: e.activation(out=rstd[:, 0:ntok], in_=pz[z][:, 0:ntok], func=AF.Sqrt, bias=EPS, scale=1.0 / 512),
                  reads=[f"pz{z}"], writes=["rstd"])
                O("dve", lambda e: e.reciprocal(out=rstd[:, 0:ntok], in_=rstd[:, 0:ntok]), reads=["rstd"], writes=["rstd"])
                for j, kc in enumerate(kcs):
                    O("dve", lambda e, kc=kc, j=j, gi=gi: e.scalar_tensor_tensor(
                        out=mixT[:, kc, 0:ntok], in0=mixT[:, kc, 0:ntok], scalar=c_gssd[:, gi * 4 + j:gi * 4 + j + 1],
                        in1=rstd[:, 0:ntok], op0=ALU.mult, op1=ALU.mult), reads=["mixT", "rstd", "p5c"], writes=["mixT"])
            for cg in range(4):
                wi = wload(lambda w: w[:, :].rearrange("p (k n) -> p k n", n=512),
                           wo_b[:, cg * 512:(cg + 1) * 512].rearrange("(k p) n -> p k n", p=128), [f"wo_b_{r}" for r in range(0, D, 512)])
                wv = wbuf[wi][:, :].rearrange("p (k n) -> p k n", n=512)
                for i in range(ntile):
                    z = nz()

                    def womm(e, wv=wv, i=i, z=z):
                        for kc in range(16):
                            ins = e.matmul(pz[z][:, :], lhsT=mixT[:, kc, i * 128:(i + 1) * 128], rhs=wv[:, kc, :],
                                           start=(kc == 0), stop=(kc == 15))
                        return ins
                    O("pe", womm, reads=[f"wbuf{wi}", "mixT"], writes=[f"pz{z}"])
                    O("dve", lambda e, i=i, cg=cg, z=z: e.tensor_tensor(out=x1[:, i, cg * 512:(cg + 1) * 512], in0=pz[z][:, :],
                                                                       in1=x1[:, i, cg * 512:(cg + 1) * 512], op=ALU.add),
                      reads=[f"pz{z}", "x1"], writes=["x1"])
            for i in range(ntile):
                i2 = nxt() % 2
                O("act", lambda e, i=i, i2=i2: e.activation(out=junk[:, :], in_=x1[:, i, :], func=AF.Square, accum_out=ss[i2][:, 0:1]),
                  reads=["x1"], writes=["junk", f"ss{i2}"])
                O("act", lambda e, i2=i2: e.activation(out=ss[i2][:, 1:2], in_=ss[i2][:, 0:1], func=AF.Sqrt, bias=EPS, scale=1.0 / D),
                  reads=[f"ss{i2}"], writes=[f"ss{i2}"])
                O("dve", lambda e, i2=i2: e.reciprocal(out=ss[i2][:, 1:2], in_=ss[i2][:, 1:2]), reads=[f"ss{i2}"], writes=[f"ss{i2}"])
                O("dve", lambda e, i=i, i2=i2: e.scalar_tensor_tensor(out=hb[i2][:, :], in0=x1[:, i, :], scalar=ss[i2][:, 1:2],
                                                                     in1=c_gffn[:, :], op0=ALU.mult, op1=ALU.mult),
                  reads=["x1", f"ss{i2}", "p5c"], writes=[f"hb{i2}"])
                for k4 in range(4):
                    p2 = nxt() % 2

                    def tr(e, i2=i2, k4=k4, p2=p2):
                        for j in range(4):
                            kc = k4 * 4 + j
                            ins = e.transpose(out=psT[p2][:, j, :], in_=hb[i2][:, kc * 128:(kc + 1) * 128], identity=c_idb[:, :])
                        return ins
                    O("pe", tr, reads=[f"hb{i2}", "const"], writes=[f"psT{p2}"])
                    if k4 % 2:
                        O("act", lambda e, k4=k4, p2=p2, i=i: e.copy(out=mixT[:, k4 * 4:(k4 + 1) * 4, i * 128:(i + 1) * 128], in_=psT[p2][:, :, :]),
                          reads=[f"psT{p2}"], writes=["mixT"])
                    else:
                        O("dve", lambda e, k4=k4, p2=p2, i=i: e.tensor_copy(out=mixT[:, k4 * 4:(k4 + 1) * 4, i * 128:(i + 1) * 128], in_=psT[p2][:, :, :]),
                          reads=[f"psT{p2}"], writes=["mixT"])
            wg_reads = [f"wg_b_{r}" for r in range(0, D, 512)]
            wu_reads = [f"wu_b_{r}" for r in range(0, D, 512)]
            for fb in range(DFF // 512):
                wgi = wload(lambda w: w[:, :].rearrange("p (k n) -> p k n", n=512),
                            wg_b[:, fb * 512:(fb + 1) * 512].rearrange("(k p) n -> p k n", p=128), wg_reads)
                wui = wload(lambda w: w[:, :].rearrange("p (k n) -> p k n", n=512),
                            wu_b[:, fb * 512:(fb + 1) * 512].rearrange("(k p) n -> p k n", p=128), wu_reads)
                wgv = wbuf[wgi][:, :].rearrange("p (k n) -> p k n", n=512)
                wuv = wbuf[wui][:, :].rearrange("p (k n) -> p k n", n=512)
                for c4 in range(4):
                    zg, zu = nz(), nz()

                    def gmm(e, wv=wgv, c4=c4, z=zg):
                        for kc in range(16):
                            ins = e.matmul(pz[z][:, 0:ntok], lhsT=wv[:, kc, c4 * 128:(c4 + 1) * 128], rhs=mixT[:, kc, 0:ntok],
                                           start=(kc == 0), stop=(kc == 15))
                        return ins

                    def umm(e, wv=wuv, c4=c4, z=zu):
                        for kc in range(16):
                            ins = e.matmul(pz[z][:, 0:ntok], lhsT=wv[:, kc, c4 * 128:(c4 + 1) * 128], rhs=mixT[:, kc, 0:ntok],
                                           start=(kc == 0), stop=(kc == 15))
                        return ins
                    O("pe", gmm, reads=[f"wbuf{wgi}", "mixT"], writes=[f"pz{zg}"])
                    O("pe", umm, reads=[f"wbuf{wui}", "mixT"], writes=[f"pz{zu}"])
                    g2 = nxt() % 2
                    O("act", lambda e, z=zg, g2=g2: e.activation(out=gt[g2][:, 0:ntok], in_=pz[z][:, 0:ntok], func=AF.Silu),
                      reads=[f"pz{zg}"], writes=[f"gt{g2}"])
                    O("dve", lambda e, z=zu, g2=g2, fb=fb, c4=c4: e.tensor_tensor(out=AT[:, fb * 4 + c4, 0:ntok], in0=pz[z][:, 0:ntok],
                                                                                 in1=gt[g2][:, 0:ntok], op=ALU.mult),
                      reads=[f"pz{zu}", f"gt{g2}"], writes=["AT"])
            wd_reads = [f"wd_b_{r}" for r in range(0, DFF, 512)]
            for cg in range(4):
                zs_ = [nz() for _ in range(ntile)]
                for blk in range(4):
                    wi = wload(lambda w: w[:, 0:11 * 512].rearrange("p (f n) -> p f n", n=512),
                               wd_b[blk * 1408:(blk + 1) * 1408, cg * 512:(cg + 1) * 512].rearrange("(f p) n -> p f n", p=128), wd_reads)
                    wv = wbuf[wi][:, 0:11 * 512].rearrange("p (f n) -> p f n", n=512)
                    for i in range(ntile):
                        def dmm(e, wv=wv, i=i, blk=blk, z=zs_[i]):
                            for f in range(11):
                                ins = e.matmul(pz[z][:, :], lhsT=AT[:, blk * 11 + f, i * 128:(i + 1) * 128], rhs=wv[:, f, :],
                                               start=(blk == 0 and f == 0), stop=(blk == 3 and f == 10))
                            return ins
                        O("pe", dmm, reads=[f"wbuf{wi}", "AT"], writes=[f"pz{zs_[i]}"])
                for i in range(ntile):
                    O("dve", lambda e, i=i, cg=cg, z=zs_[i]: e.tensor_tensor(out=x1[:, i, cg * 512:(cg + 1) * 512], in0=pz[z][:, :],
                                                                            in1=x1[:, i, cg * 512:(cg + 1) * 512], op=ALU.add),
                      reads=[f"pz{zs_[i]}", "x1"], writes=["x1"])
            O("sp", _nd(1)(lambda e, dma, row=row, ntile=ntile, ntok=ntok: dma(
                y2[row:row + ntok, :].rearrange("(j p) c -> p j c", p=128), x1[:, 0:ntile, :])), reads=["x1"], writes=["y2"], dma=True)
            return ntok

        row = 0
        for grp in groups:
            row += do_group(grp, row)
        tk.barrier()
        tk.flush(sems, dsems)
        es.close()

    if STAGE >= 8:
        phase5()
    if os.environ.get("DBG_E"):
        E_dbg = dout("E_dbg", [TT, 512], BF16)
        O("sp", _nd(1)(lambda e, dma: dma(E_dbg[:, :], E[:, :])), writes=["E_dbg"], dma=True)
    tk.barrier()
    tk.flush(sems, dsems)
    ges.close()
    sstack.close()
    return nc


def make_in_maps(inp, T, NSQ=8, TS=64, past=PAST):
    bf = ml_dtypes.bfloat16
    f32 = np.float32
    NCH = T // 1024
    ident = np.eye(128, dtype=f32)
    bd = np.zeros((128, 128), f32)
    bd[:64, :64] = 1.0
    bd[64:, 64:] = 1.0
    triu = np.triu(np.ones((128, 128), f32))
    ones = np.ones((128, 128), f32)
    w_in = inp["w_in"][0]
    w_conv = inp["w_conv"][0]
    b_conv = inp["b_conv"][0]
    o_f = 3 * 1024
    o_z = o_f + 16
    o_x = o_z + 1024
    o_B = o_x + 1024
    o_C = o_B + 256
    o_dt = o_C + 256
    perm = []
    for kc in range(16):
        r, cch = kc // 4, kc % 4
        base = 256 * r + 128 * cch if cch < 2 else 1024 + 256 * r + 128 * (cch - 2)
        perm.extend(range(base, base + 128))
    w_out_p = np.ascontiguousarray(inp["w_out"][0][perm])
    maps = []
    for c in range(8):
        b, g = c // 4, c % 4
        j = g
        xs = inp["x_sample"][8 * b:8 * b + 8].reshape(NSQ * TS, D)
        x_all = np.concatenate([inp["x_prompt"][b], xs], axis=0)
        hs = slice(256 * g, 256 * g + 256)
        grp = g // 2
        wsl = np.concatenate([
            w_in[:, 0:1024][:, hs], w_in[:, 1024:2048][:, hs], w_in[:, 2048:3072][:, hs],
            w_in[:, o_z:o_z + 1024][:, hs], w_in[:, o_x:o_x + 1024][:, hs],
            w_in[:, o_B + 128 * grp:o_B + 128 * grp + 128], w_in[:, o_C + 128 * grp:o_C + 128 * grp + 128],
            w_in[:, o_f + 4 * g:o_f + 4 * g + 4], w_in[:, o_dt + 4 * g:o_dt + 4 * g + 4]], axis=1)
        cols = np.zeros((128, 16), f32)
        cols[:, 0] = np.tile(inp["g_q"][0], 2)
        cols[:, 1] = np.tile(inp["g_k"][0], 2)
        cols[0:4, 2] = -1.0
        cols[4:8, 2] = 1.0
        cols[0:4, 3] = inp["f_bias"][0][4 * g:4 * g + 4]
        cols[4:8, 3] = inp["dt_bias"][0][4 * g:4 * g + 4]
        cols[0:4, 4] = -1.0
        cols[4:8, 4] = 1.0
        ccols = np.concatenate([np.arange(256 * g, 256 * g + 256), 1024 + 128 * grp + np.arange(128),
                                1280 + 128 * grp + np.arange(128)])
        wconv = np.zeros((128, 4, 5), f32)
        for ci in range(4):
            cc = ccols[ci * 128:(ci + 1) * 128]
            wconv[:, ci, 0:4] = w_conv[:, cc].T
            wconv[:, ci, 4] = b_conv[cc]
        row4 = np.concatenate([inp["a_log"][0][4 * g:4 * g + 4], inp["d_skip"][0][4 * g:4 * g + 4]])[None, :].astype(f32)
        sq = slice(8 * b, 8 * b + 8)
        c_k = inp["cache_fox_k"][0][sq][:, :, 4 * g:4 * g + 4, :].reshape(NSQ, past, 256)
        c_v = inp["cache_fox_v"][0][sq][:, :, 4 * g:4 * g + 4, :].reshape(NSQ, past, 256)
        c_lf = inp["cache_fox_logf"][0][sq][:, :, 4 * g:4 * g + 4]
        c_convT = np.transpose(inp["cache_conv"][0][sq][:, :, ccols], (2, 0, 1))
        c_ssm = inp["state_ssm"][0][sq][:, 4 * g:4 * g + 4].reshape(NSQ, 256, 128)
        xp = inp["x_prompt"][b].reshape(NCH, 4, 256, D)[:, j].reshape(NCH * 256, D)
        x2 = np.concatenate([xp, inp["x_sample"][8 * b + 2 * j:8 * b + 2 * j + 2].reshape(2 * TS, D)], axis=0)
        sel = np.zeros((128, 8, 256), f32)
        for m in range(256):
            t = 256 * j + m
            sel[t % 128, t // 128, m] = 1.0
        sel_s = np.zeros((128, 4, 128), f32)
        for m in range(128):
            t = 128 * j + m
            sel_s[t % 128, t // 128, m] = 1.0
        gs = inp["g_ssd"][0]
        gssd = np.zeros((128, 8), f32)
        for gi in range(2):
            for jj in range(4):
                base = 256 * (2 * gi + jj // 2) + 128 * (jj % 2)
                gssd[:, gi * 4 + jj] = gs[base:base + 128]
        ca = np.ascontiguousarray
        maps.append({
            "x_all": ca(x_all), "w_in": ca(wsl), "gmix": ca(inp["g_mix"]), "cols": cols, "wconv": wconv, "row4": row4,
            "ident_f": ident, "ident_b": ident.astype(bf), "bd_ones": bd.astype(bf),
            "triu_f": triu, "triu_b": triu.astype(bf), "ones_f": ones, "ones_b": ones.astype(bf),
            "c_k": ca(c_k), "c_v": ca(c_v), "c_lf": ca(c_lf), "c_convT": ca(c_convT), "c_ssm": ca(c_ssm),
            "x2": ca(x2), "sel": sel.astype(bf), "sel_s": sel_s.astype(bf), "gffn": ca(inp["g_ffn"]), "gssd": gssd,
            "w_out": w_out_p, "w_gate": inp["w_gate"][0], "w_up": inp["w_up"][0], "w_down": inp["w_down"][0],
        })
    return maps


def assemble(res, T, B=2, NSQ=8, TS=64):
    NCH = T // 1024
    f32 = np.float32
    DB = 8 * B
    yp = np.zeros((B, T, D), f32)
    ys = np.zeros((DB, TS, D), f32)
    p_conv = np.zeros((1, B, 3, 1536), f32)
    p_ssm = np.zeros((1, B, 16, 64, 128), f32)
    p_k = np.zeros((1, B, T, 16, 64), f32)
    p_v = np.zeros((1, B, T, 16, 64), f32)
    p_f = np.zeros((1, B, T, 16), f32)
    s_conv = np.zeros((1, DB, 3, 1536), f32)
    s_ssm = np.zeros((1, DB, 16, 64, 128), f32)
    s_k = np.zeros((1, DB, TS, 16, 64), f32)
    s_v = np.zeros((1, DB, TS, 16, 64), f32)
    s_f = np.zeros((1, DB, TS, 16), f32)
    for c in range(8):
        b, g = c // 4, c % 4
        j = g
        grp = g // 2
        r = res[c]
        y2 = r["y2"]
        yp[b].reshape(NCH, 4, 256, D)[:, j] = y2[:NCH * 256].reshape(NCH, 256, D)
        ys[8 * b + 2 * j:8 * b + 2 * j + 2] = y2[NCH * 256:].reshape(2, TS, D)
        sq = slice(8 * b, 8 * b + 8)
        p_k[0, b, :, 4 * g:4 * g + 4] = r["k_out"][:T].reshape(T, 4, 64)
        p_v[0, b, :, 4 * g:4 * g + 4] = r["v_out"][:T].reshape(T, 4, 64)
        p_f[0, b, :, 4 * g:4 * g + 4] = r["lf_out"][:T]
        s_k[0, sq, :, 4 * g:4 * g + 4] = r["k_out"][T:].reshape(NSQ, TS, 4, 64)
        s_v[0, sq, :, 4 * g:4 * g + 4] = r["v_out"][T:].reshape(NSQ, TS, 4, 64)
        s_f[0, sq, :, 4 * g:4 * g + 4] = r["lf_out"][T:].reshape(NSQ, TS, 4)
        p_ssm[0, b, 4 * g:4 * g + 4] = r["ssm_p"].reshape(4, 64, 128)
        s_ssm[0, sq, 4 * g:4 * g + 4] = r["ssm_s"].reshape(NSQ, 4, 64, 128)
        cp = r["conv_p"].T
        cs = np.transpose(r["conv_s"], (1, 2, 0))
        p_conv[0, b, :, 256 * g:256 * g + 256] = cp[:, 0:256]
        s_conv[0, sq, :, 256 * g:256 * g + 256] = cs[:, :, 0:256]
        if g % 2 == 0:
            p_conv[0, b, :, 1024 + 128 * grp:1024 + 128 * grp + 128] = cp[:, 256:384]
            p_conv[0, b, :, 1280 + 128 * grp:1280 + 128 * grp + 128] = cp[:, 384:512]
            s_conv[0, sq, :, 1024 + 128 * grp:1024 + 128 * grp + 128] = cs[:, :, 256:384]
            s_conv[0, sq, :, 1280 + 128 * grp:1280 + 128 * grp + 128] = cs[:, :, 384:512]
    return (yp, ys, p_conv, p_ssm, p_k, p_v, p_f, s_conv, s_ssm, s_k, s_v, s_f)


_NC_CACHE = {}


def kernel(**inputs):
    inp = {k: np.asarray(v) for k, v in inputs.items()}
    T = inp["x_prompt"].shape[1]
    if T not in _NC_CACHE:
        _NC_CACHE[T] = build(T)
    nc = _NC_CACHE[T]
    maps = make_in_maps(inp, T)
    res = run_bass_kernel_spmd(nc, maps, core_ids=list(range(8)))
    return assemble(res.results, T)
```

```python
import numpy as np
import ml_dtypes
import concourse.bass as bass
import concourse.mybir as mybir
from concourse.bass_utils import run_bass_kernel_spmd

F32 = mybir.dt.float32
BF16 = mybir.dt.bfloat16
ALU = mybir.AluOpType
AF = mybir.ActivationFunctionType
ET = mybir.EngineType

D = 2048
EPS = 1e-6
HG = 4
NCOL = 1544
PAST = 2048
DFF = 5632


class Tracker:
    CE = ("pe", "act", "dve", "pool", "sp")

    def __init__(self, nc, nslot=8):
        self.nc = nc
        self.nslot = nslot
        self.ops = {e: [] for e in ("pe", "act", "dve", "pool", "sp")}
        self.cnt = {e: 0 for e in self.CE}
        self.ndma = {"sp": 0, "pool": 0}
        self.slotval = {}
        self.res = {}
        self.waited = {e: {} for e in self.ops}

    def _need(self, eng, tok, waits):
        if tok is None:
            return
        kind, key, val = tok
        if kind == "c" and key == eng and eng == "pe":
            return
        k = (kind, key)
        if self.waited[eng].get(k, 0) >= val:
            return
        waits[k] = max(waits.get(k, 0), val)

    def op(self, eng, fn, reads=(), writes=(), dma=False):
        waits = {}
        for r in reads:
            st = self.res.get(r)
            if st:
                self._need(eng, st["w"], waits)
        for w in writes:
            st = self.res.get(w)
            if st:
                self._need(eng, st["w"], waits)
                for t in st["r"]:
                    self._need(eng, t, waits)
        if dma:
            slot = self.ndma[eng] % self.nslot
            self.ndma[eng] += 1
            key = (eng, slot)
            prev = self.slotval.get(key, 0)
            if prev:
                self._need(eng, ("d", key, prev), waits)
            tok = ["d", key, prev]
        else:
            self.cnt[eng] += 1
            tok = ("c", eng, self.cnt[eng])
        for k, v in waits.items():
            self.waited[eng][k] = v
        rec = {"fn": fn, "waits": dict(waits), "tok": tok, "dma": dma, "nd": 0}
        if dma:
            n = getattr(fn, "ndma", 1)
            rec["nd"] = n
            self.slotval[key] = prev + 16 * n
            tok[2] = prev + 16 * n
            tok = tuple(tok)
            rec["tok"] = tok
        self.ops[eng].append(rec)
        for r in reads:
            self.res.setdefault(r, {"w": None, "r": []})["r"].append(tok)
        for w in writes:
            self.res[w] = {"w": tok, "r": []}
        return tok

    def barrier(self):
        allr = list(self.res.keys())
        self.op("sp", lambda e: e.nop(), reads=allr, writes=allr + ["__bar"])
        for en in ("pe", "act", "dve", "pool"):
            self.op(en, lambda e: e.nop(), reads=["__bar"])

    def flush(self, sems, dsems):
        nc = self.nc
        ops, self.ops = self.ops, {e: [] for e in self.ops}

        def run(ename, e):
            for rec in ops[ename]:
                for (kind, key), val in rec["waits"].items():
                    if kind == "c":
                        e.wait_ge(sems[key], val)
                    else:
                        e.wait_ge(dsems[key], val)
                if rec["dma"]:
                    sem = dsems[rec["tok"][1]]
                    cnt = [0]

                    def dma(out, in_, _sem=sem, _cnt=cnt, **kw):
                        _cnt[0] += 1
                        return e.dma_start(out=out, in_=in_, **kw).then_inc(_sem, 16)
                    rec["fn"](e, dma)
                    assert cnt[0] == rec["nd"], (cnt[0], rec["nd"])
                else:
                    ins = rec["fn"](e)
                    ins.then_inc(sems[ename], 1)
        with nc.Block() as block:
            @block.sync
            def _(e):
                run("sp", e)

            @block.tensor
            def _(e):
                run("pe", e)

            @block.scalar
            def _(e):
                run("act", e)

            @block.vector
            def _(e):
                run("dve", e)

            @block.gpsimd
            def _(e):
                run("pool", e)


def _nd(n):
    def deco(f):
        f.ndma = n
        return f
    return deco


import os
from contextlib import ExitStack


def build(T, NSQ=8, TS=64, past=PAST):
    assert T % 1024 == 0 and NSQ * TS == 512 and TS == 64 and past % 512 == 0
    STAGE = int(os.environ.get("STAGE", "99"))
    nc = bass.Bass("TRN2", target_bir_lowering=False)
    TT = T + NSQ * TS
    NG = TT // 512
    NGP = T // 512
    NCH = T // 1024
    TKS = past + 128
    NBS = TKS // 128
    NT2 = NCH * 256 + 128

    def din(name, shape, dt=F32):
        return nc.dram_tensor(name, list(shape), dt, kind="ExternalInput").ap()

    def dout(name, shape, dt=F32):
        return nc.dram_tensor(name, list(shape), dt, kind="ExternalOutput").ap()

    def dscr(name, shape, dt):
        return nc.dram_tensor(name, list(shape), dt, kind="Internal").ap()

    x_all = din("x_all", [TT, D])
    w_in = din("w_in", [D, NCOL])
    gmix = din("gmix", [1, D])
    cols = din("cols", [128, 16])
    wconv = din("wconv", [128, 4, 5])
    row4 = din("row4", [1, 8])
    ident_f = din("ident_f", [128, 128])
    ident_b = din("ident_b", [128, 128], BF16)
    bd_ones = din("bd_ones", [128, 128], BF16)
    triu_f = din("triu_f", [128, 128])
    triu_b = din("triu_b", [128, 128], BF16)
    ones_f = din("ones_f", [128, 128])
    ones_b = din("ones_b", [128, 128], BF16)
    c_k = din("c_k", [NSQ, past, 256])
    c_v = din("c_v", [NSQ, past, 256])
    c_lf = din("c_lf", [NSQ, past, 4])
    c_convT = din("c_convT", [512, NSQ, 3])
    c_ssm = din("c_ssm", [NSQ, 256, 128])
    x2 = din("x2", [NT2, D])
    sel = din("sel", [128, 8, 256], BF16)
    sel_s = din("sel_s", [128, 4, 128], BF16)
    gffn = din("gffn", [1, D])
    gssd = din("gssd", [128, 8])
    w_out = din("w_out", [D, D])
    w_gate = din("w_gate", [D, DFF])
    w_up = din("w_up", [D, DFF])
    w_down = din("w_down", [DFF, D])
    k_out = dout("k_out", [TT, 256])
    v_out = dout("v_out", [TT, 256])
    lf_out = dout("lf_out", [TT, 4])
    conv_p = dout("conv_p", [512, 3])
    conv_s = dout("conv_s", [512, NSQ, 3])
    ssm_p = dout("ssm_p", [256, 128])
    ssm_s = dout("ssm_s", [NSQ, 256, 128])
    y2 = dout("y2", [NT2, D])

    w_in_b = dscr("w_in_b", [D, NCOL], BF16)
    QT = dscr("QT", [70, HG, TT], BF16)
    KTn = dscr("KTn", [70, HG, TT], BF16)
    KTs = dscr("KTs", [NSQ, 70, HG, TKS], BF16)
    Vb = dscr("Vb", [TT, 256], BF16)
    Vs = dscr("Vs", [NSQ, TKS, 256], BF16)
    FD = dscr("FD", [TT, 8], F32)
    ZS = dscr("ZS", [TT, 256], F32)
    UT = dscr("UT", [256, TT], BF16)
    Utok = dscr("Utok", [TT, 384], BF16)
    E = dscr("E", [TT, 512], BF16)
    Gq = [dscr(f"G{q}", [4 * 1024, 512], BF16) for q in range(NCH)]
    Gs = dscr("Gs", [4 * 512, 512], BF16)
    wo_b = dscr("wo_b", [D, D], BF16)
    wg_b = dscr("wg_b", [D, DFF], BF16)
    wu_b = dscr("wu_b", [D, DFF], BF16)
    wd_b = dscr("wd_b", [DFF, D], BF16)

    tk = Tracker(nc)
    O = tk.op
    uid = [0]

    def nxt():
        uid[0] += 1
        return uid[0]

    sstack = ExitStack()
    sems = {e: sstack.enter_context(nc.semaphore(f"s_{e}")) for e in Tracker.CE}
    dsems = {}
    for q in ("sp", "pool"):
        for s in range(tk.nslot):
            dsems[(q, s)] = sstack.enter_context(nc.semaphore(f"d_{q}{s}"))

    ges = ExitStack()

    def gsb(name, shape, dt=F32):
        return ges.enter_context(nc.sbuf_tensor(name, list(shape), dt))

    c_cols = gsb("c_cols", [128, 16])
    c_idf = gsb("c_idf", [128, 128])
    c_idb = gsb("c_idb", [128, 128], BF16)
    c_bd = gsb("c_bd", [128, 128], BF16)
    c_trf = gsb("c_trf", [128, 128])
    c_trb = gsb("c_trb", [128, 128], BF16)
    c_onf = gsb("c_onf", [128, 128])
    c_onb = gsb("c_onb", [128, 128], BF16)
    c_fd = gsb("c_fd", [128, 4])
    c_wc = gsb("c_wc", [128, 4, 5])
    c_r4 = gsb("c_r4", [128, 8])

    @_nd(10)
    def ld_const(e, dma):
        dma(c_cols[:, :], cols[:, :])
        dma(c_idf[:, :], ident_f[:, :])
        dma(c_idb[:, :], ident_b[:, :])
        dma(c_bd[:, :], bd_ones[:, :])
        dma(c_trf[:, :], triu_f[:, :])
        dma(c_trb[:, :], triu_b[:, :])
        dma(c_onf[:, :], ones_f[:, :])
        dma(c_onb[:, :], ones_b[:, :])
        dma(c_wc[:, :, :], wconv[:, :, :])
        dma(c_r4[:, :], row4[0:1, :].partition_broadcast(128))
    O("sp", ld_const, writes=["const"], dma=True)
    O("dve", lambda e: e.tensor_tensor(out=c_fd[0:8, 0:1], in0=c_cols[0:8, 3:4], in1=c_cols[0:8, 2:3], op=ALU.mult),
      reads=["const"], writes=["cfd0"])
    O("dve", lambda e: e.tensor_scalar(out=c_fd[:, 1:2], in0=c_cols[:, 1:2], scalar1=8.0, scalar2=None, op0=ALU.mult),
      reads=["const"], writes=["cfd1"])
    O("act", lambda e: e.activation(out=c_r4[:, 0:4], in_=c_r4[:, 0:4], func=AF.Exp), reads=["const"], writes=["cr4"])
    O("dve", lambda e: e.tensor_scalar(out=c_r4[:, 0:4], in0=c_r4[:, 0:4], scalar1=-1.0, scalar2=None, op0=ALU.mult),
      reads=["cr4"], writes=["cr4"])

    @_nd(1)
    def cast_w(e, dma):
        dma(w_in_b[:, :], w_in[:, :])
    O("pool", cast_w, writes=["w_in_b"], dma=True)

    def cast_big(dst, src, rows, name):
        step = 512
        for r in range(0, rows, step):
            @_nd(1)
            def f(e, dma, r=r):
                dma(dst[r:r + step, :], src[r:r + step, :])
            O("pool", f, writes=[name + f"_{r}"], dma=True)
    if STAGE >= 8:
        cast_big(wo_b, w_out, D, "wo_b")
        cast_big(wg_b, w_gate, D, "wg_b")
        cast_big(wu_b, w_up, D, "wu_b")
        cast_big(wd_b, w_down, DFF, "wd_b")

    def phase1():
        es = ExitStack()

        def sb(name, shape, dt=F32):
            return es.enter_context(nc.sbuf_tensor(f'ph0_' + name, list(shape), dt))

        def ps(name, shape, dt=F32):
            return es.enter_context(nc.psum_tensor(f'ph0_' + name, list(shape), dt))

        wb = sb("wb", [128, 16, NCOL], BF16)
        c_gmix = sb("c_gmix", [128, D])

        @_nd(2)
        def ld_w(e, dma):
            dma(wb[:, :, :], w_in_b.rearrange("(kc p) n -> p kc n", p=128))
            dma(c_gmix[:, :], gmix[0:1, :].partition_broadcast(128))
        O("sp", ld_w, reads=["w_in_b"], writes=["wb"], dma=True)

        hT = [sb(f"hT{i}", [128, 16, 512], BF16) for i in range(2)]
        mmc = [0]
        rc_ = {}
        xt = [sb(f"xt{i}", [128, D]) for i in range(4)]
        hb = [sb(f"hb{i}", [128, D], BF16) for i in range(4)]
        ss = [sb(f"ss{i}", [128, 2]) for i in range(4)]
        psT = [ps(f"psT{i}", [128, 4, 128], BF16) for i in range(2)]
        psA = [ps(f"psA{i}", [128, 512]) for i in range(3)]
        psB = ps("psB", [128, 512])
        psF = [ps(f"psF{i}", [128, 4, 128]) for i in range(2)]
        sq = [sb(f"sq{i}", [128, 512], BF16) for i in range(2)]
        rs = [sb(f"rs{i}", [128, 512]) for i in range(2)]
        qn = [sb(f"qn{i}", [128, 512], BF16) for i in range(2)]
        kf = [sb(f"kf{i}", [128, 512]) for i in range(3)]
        kraw = [sb(f"kraw{i}", [128, 512]) for i in range(2)]
        tokf = [sb(f"tokf{i}", [128, 4, 128]) for i in range(2)]
        tokb = [sb(f"tokb{i}", [128, 4, 128], BF16) for i in range(2)]
        fd = sb("fd", [8, 512])
        fdt = sb("fdt", [128, 4, 8])
        xpad = sb("xpad", [128, 4, 536])
        acc = [sb(f"acc{i}", [128, 512]) for i in range(3)]
        ub = [sb(f"ub{i}", [128, 512], BF16) for i in range(3)]
        O("dve", lambda e: e.memset(xpad[:, :, :], 0.0), writes=[f"xpad{ci}" for ci in range(4)])

        def part_a_norm(tg):
            t0 = tg * 512
            for ti in range(4):
                r0 = t0 + ti * 128

                @_nd(1)
                def ldx(e, dma, ti=ti, r0=r0):
                    dma(xt[ti][:, :], x_all[r0:r0 + 128, :])
                O("sp", ldx, writes=[f"xt{ti}"], dma=True)
                O("act", lambda e, ti=ti: e.activation(out=hb[ti][:, :], in_=xt[ti][:, :], func=AF.Square,
                                                      accum_out=ss[ti][:, 0:1]),
                  reads=[f"xt{ti}"], writes=[f"hb{ti}", f"ss{ti}a"])
                O("act", lambda e, ti=ti: e.activation(out=ss[ti][:, 1:2], in_=ss[ti][:, 0:1], func=AF.Sqrt,
                                                      bias=EPS, scale=1.0 / D),
                  reads=[f"ss{ti}a"], writes=[f"ss{ti}b"])
                O("dve", lambda e, ti=ti: e.reciprocal(out=ss[ti][:, 1:2], in_=ss[ti][:, 1:2]),
                  reads=[f"ss{ti}b"], writes=[f"ss{ti}b"])
                O("dve", lambda e, ti=ti: e.scalar_tensor_tensor(out=hb[ti][:, :], in0=xt[ti][:, :], scalar=ss[ti][:, 1:2],
                                                                 in1=c_gmix[:, :], op0=ALU.mult, op1=ALU.mult),
                  reads=[f"xt{ti}", f"ss{ti}b", "wb"], writes=[f"hb{ti}"])

        def part_a_tr(tg):
            hTb = hT[tg % 2]
            for ti in range(4):
                for k4 in range(4):
                    p2 = nxt() % 2

                    def tr(e, ti=ti, k4=k4, p2=p2):
                        for j in range(4):
                            kc = k4 * 4 + j
                            ins = e.transpose(out=psT[p2][:, j, :], in_=hb[ti][:, kc * 128:(kc + 1) * 128],
                                              identity=c_idb[:, :])
                        return ins
                    O("pe", tr, reads=[f"hb{ti}", "const"], writes=[f"psT{p2}"])
                    if k4 % 2:
                        O("act", lambda e, k4=k4, p2=p2, ti=ti, hTb=hTb: e.copy(
                            out=hTb[:, k4 * 4:(k4 + 1) * 4, ti * 128:(ti + 1) * 128], in_=psT[p2][:, :, :]),
                          reads=[f"psT{p2}"], writes=[f"hT{tg % 2}_{ti}_{k4}"])
                    else:
                        O("dve", lambda e, k4=k4, p2=p2, ti=ti, hTb=hTb: e.tensor_copy(
                            out=hTb[:, k4 * 4:(k4 + 1) * 4, ti * 128:(ti + 1) * 128], in_=psT[p2][:, :, :]),
                          reads=[f"psT{p2}"], writes=[f"hT{tg % 2}_{ti}_{k4}"])

        part_a_norm(0)
        part_a_tr(0)
        for tg in range(NG):
            t0 = tg * 512
            is_s = tg >= NGP
            hTb = hT[tg % 2]
            hT_res = [f"hT{tg % 2}_{ti}_{k4}" for ti in range(4) for k4 in range(4)]

            def mm_chunk(c0, m, pst, hTb=hTb):
                def f(e):
                    for kc in range(16):
                        ins = e.matmul(pst[0:m, :], lhsT=wb[:, kc, c0:c0 + m], rhs=hTb[:, kc, :],
                                       start=(kc == 0), stop=(kc == 15))
                    return ins
                return f

            def tr4_f32(src, f2):
                def f(e):
                    for j in range(4):
                        ins = e.transpose(out=psF[f2][:, j, :], in_=src[:, j * 128:(j + 1) * 128], identity=c_idf[:, :])
                    return ins
                return f

            chunks = []

            def rot(name):
                rc_[name] = rc_.get(name, 0) + 1
                return rc_[name] % 2

            def rot3(name):
                rc_[name] = rc_.get(name, 0) + 1
                return rc_[name] % 3

            def qk1(a3, cc, tg=tg, t0=t0):
                isq = cc < 2
                hp = cc % 2
                a2 = rot("sq")
                O("act", lambda e, a2=a2, a3=a3: e.activation(out=sq[a2][:, :], in_=psA[a3][:, :], func=AF.Square),
                  reads=[f"psA{a3}"], writes=[f"sq{a2}"])
                r2 = rot("kraw")
                O("act", lambda e, r2=r2, a3=a3: e.copy(out=kraw[r2][:, :], in_=psA[a3][:, :]),
                  reads=[f"psA{a3}"], writes=[f"kraw{r2}"])
                O("pe", lambda e, a2=a2: e.matmul(psB[:, :], lhsT=c_bd[:, :], rhs=sq[a2][:, :], start=True, stop=True),
                  reads=[f"sq{a2}", "const"], writes=["psB"])
                O("act", lambda e, a2=a2: e.activation(out=rs[a2][:, :], in_=psB[:, :], func=AF.Sqrt,
                                                      bias=64.0 * EPS, scale=1.0),
                  reads=["psB"], writes=[f"rs{a2}"])
                O("dve", lambda e, a2=a2: e.reciprocal(out=rs[a2][:, :], in_=rs[a2][:, :]),
                  reads=[f"rs{a2}"], writes=[f"rs{a2}"])
                if isq:
                    q2 = rot("qn")
                    O("dve", lambda e, a2=a2, r2=r2, q2=q2: e.scalar_tensor_tensor(out=qn[q2][:, :], in0=kraw[r2][:, :], scalar=c_cols[:, 0:1],
                                                                                 in1=rs[a2][:, :], op0=ALU.mult, op1=ALU.mult),
                      reads=[f"kraw{r2}", f"rs{a2}", "const"], writes=[f"qn{q2}"])

                    @_nd(2)
                    def stq(e, dma, q2=q2, hp=hp, t0=t0):
                        for hh in range(2):
                            dma(QT[0:64, hp * 2 + hh, t0:t0 + 512], qn[q2][hh * 64:(hh + 1) * 64, :])
                    O("sp", stq, reads=[f"qn{q2}"], writes=[f"QT{tg}"], dma=True)
                    return None
                k2 = rot3("kf")
                O("dve", lambda e, a2=a2, r2=r2, k2=k2: e.scalar_tensor_tensor(out=kf[k2][:, :], in0=kraw[r2][:, :], scalar=c_fd[:, 1:2],
                                                                             in1=rs[a2][:, :], op0=ALU.mult, op1=ALU.mult),
                  reads=[f"kraw{r2}", f"rs{a2}", "cfd1"], writes=[f"kf{k2}"])
                return (k2, hp)

            def qk2(st, tg=tg, t0=t0):
                if st is None:
                    return
                k2, hp = st
                q2 = rot("qn")
                O("act", lambda e, k2=k2, q2=q2: e.copy(out=qn[q2][:, :], in_=kf[k2][:, :]),
                  reads=[f"kf{k2}"], writes=[f"qn{q2}"])

                @_nd(2)
                def stk(e, dma, q2=q2, hp=hp, t0=t0):
                    for hh in range(2):
                        dma(KTn[0:64, hp * 2 + hh, t0:t0 + 512], qn[q2][hh * 64:(hh + 1) * 64, :])
                O("sp", stk, reads=[f"qn{q2}"], writes=[f"KTn{tg}"], dma=True)
                f2 = rot("psF")
                O("pe", tr4_f32(kf[k2], f2), reads=[f"kf{k2}", "const"], writes=[f"psF{f2}"])
                O("act", lambda e, f2=f2: e.copy(out=tokf[f2][:, :, :], in_=psF[f2][:, :, :]),
                  reads=[f"psF{f2}"], writes=[f"tokf{f2}"])

                @_nd(1)
                def stko(e, dma, f2=f2, hp=hp, t0=t0):
                    dma(k_out[t0:t0 + 512, hp * 128:(hp + 1) * 128].rearrange("(j p) c -> p j c", p=128), tokf[f2][:, :, :])
                O("sp", stko, reads=[f"tokf{f2}"], writes=["k_out"], dma=True)

            def v1(a3, hp):
                k2 = rot3("kf")
                O("act", lambda e, k2=k2, a3=a3: e.copy(out=kf[k2][:, :], in_=psA[a3][:, :]),
                  reads=[f"psA{a3}"], writes=[f"kf{k2}"])
                return (k2, hp)

            def v2(st, tg=tg, t0=t0):
                k2, hp = st
                f2 = rot("psF")
                O("pe", tr4_f32(kf[k2], f2), reads=[f"kf{k2}", "const"], writes=[f"psF{f2}"])
                O("act", lambda e, f2=f2: e.copy(out=tokf[f2][:, :, :], in_=psF[f2][:, :, :]),
                  reads=[f"psF{f2}"], writes=[f"tokf{f2}"])

                @_nd(1)
                def stvo(e, dma, f2=f2, hp=hp, t0=t0):
                    dma(v_out[t0:t0 + 512, hp * 128:(hp + 1) * 128].rearrange("(j p) c -> p j c", p=128), tokf[f2][:, :, :])
                O("sp", stvo, reads=[f"tokf{f2}"], writes=[f"v_out{tg}_{hp}"], dma=True)

                @_nd(1)
                def stvb(e, dma, hp=hp, t0=t0):
                    dma(Vb[t0:t0 + 512, hp * 128:(hp + 1) * 128], v_out[t0:t0 + 512, hp * 128:(hp + 1) * 128])
                O("pool", stvb, reads=[f"v_out{tg}_{hp}"], writes=[f"Vb{tg}"], dma=True)

            def z1(a3, hp):
                k2 = rot3("kf")
                O("act", lambda e, k2=k2, a3=a3: e.activation(out=kf[k2][:, :], in_=psA[a3][:, :], func=AF.Silu),
                  reads=[f"psA{a3}"], writes=[f"kf{k2}"])
                return (k2, hp)

            def z2(st, tg=tg, t0=t0):
                k2, hp = st
                f2 = rot("psF")
                O("pe", tr4_f32(kf[k2], f2), reads=[f"kf{k2}", "const"], writes=[f"psF{f2}"])
                O("act", lambda e, f2=f2: e.copy(out=tokf[f2][:, :, :], in_=psF[f2][:, :, :]),
                  reads=[f"psF{f2}"], writes=[f"tokf{f2}"])

                @_nd(1)
                def stz(e, dma, f2=f2, hp=hp, t0=t0):
                    dma(ZS[t0:t0 + 512, hp * 128:(hp + 1) * 128].rearrange("(j p) c -> p j c", p=128), tokf[f2][:, :, :])
                O("sp", stz, reads=[f"tokf{f2}"], writes=[f"ZS{tg}"], dma=True)

            nseg, L = (NSQ, TS) if is_s else (1, 512)
            W = L + 3

            def pre_conv(ci, nseg=nseg, W=W, is_s=is_s):
                if is_s:
                    xp3 = xpad[:, ci, 0:nseg * W].rearrange("p (s l) -> p s l", l=W)

                    @_nd(1)
                    def ldhalo(e, dma, ci=ci, xp3=xp3):
                        dma(xp3[:, :, 0:3], c_convT[ci * 128:(ci + 1) * 128, :, :])
                    O("sp", ldhalo, writes=[f"xpad{ci}"], dma=True)

            def conv1(a3, ci, tg=tg, t0=t0, nseg=nseg, L=L, W=W, is_s=is_s):
                xp3 = xpad[:, ci, 0:nseg * W].rearrange("p (s l) -> p s l", l=W)
                O("act", lambda e, a3=a3, xp3=xp3, L=L: e.copy(out=xp3[:, :, 3:3 + L],
                                                              in_=psA[a3][:, :].rearrange("p (s l) -> p s l", l=L)),
                  reads=[f"psA{a3}"], writes=[f"xpad{ci}"])
                c2 = rot3("acc")
                acc3 = acc[c2][:, :].rearrange("p (s l) -> p s l", l=L)
                O("dve", lambda e, acc3=acc3, xp3=xp3, ci=ci, L=L: e.tensor_scalar(
                    out=acc3, in0=xp3[:, :, 3:3 + L], scalar1=c_wc[:, ci, 3:4], scalar2=c_wc[:, ci, 4:5],
                    op0=ALU.mult, op1=ALU.add), reads=[f"xpad{ci}", "const"], writes=[f"acc{c2}"])
                for j in range(1, 4):
                    O("dve", lambda e, acc3=acc3, xp3=xp3, ci=ci, L=L, j=j: e.scalar_tensor_tensor(
                        out=acc3, in0=xp3[:, :, 3 - j:3 - j + L], scalar=c_wc[:, ci, 3 - j:4 - j], in1=acc3,
                        op0=ALU.mult, op1=ALU.add), reads=[f"xpad{ci}", "const", f"acc{c2}"], writes=[f"acc{c2}"])
                O("act", lambda e, c2=c2: e.activation(out=ub[c2][:, :], in_=acc[c2][:, :], func=AF.Silu),
                  reads=[f"acc{c2}"], writes=[f"ub{c2}"])
                if ci >= 2:
                    @_nd(1)
                    def stut(e, dma, c2=c2, ci=ci, t0=t0):
                        dma(UT[(ci - 2) * 128:(ci - 1) * 128, t0:t0 + 512], ub[c2][:, :])
                    O("sp", stut, reads=[f"ub{c2}"], writes=[f"UT{tg}"], dma=True)
                if is_s:
                    @_nd(1)
                    def stcs(e, dma, ci=ci, xp3=xp3):
                        dma(conv_s[ci * 128:(ci + 1) * 128, :, :], xp3[:, :, TS:TS + 3])
                    O("sp", stcs, reads=[f"xpad{ci}"], writes=["conv_s"], dma=True)
                else:
                    if tg == NGP - 1:
                        @_nd(1)
                        def stcp(e, dma, ci=ci):
                            dma(conv_p[ci * 128:(ci + 1) * 128, :], xpad[:, ci, 512:515])
                        O("sp", stcp, reads=[f"xpad{ci}"], writes=["conv_p"], dma=True)
                    O("dve", lambda e, ci=ci: e.tensor_copy(out=xpad[:, ci, 0:3], in_=xpad[:, ci, 512:515]),
                      reads=[f"xpad{ci}"], writes=[f"xpad{ci}"])
                return (c2, ci)

            def conv2(st, tg=tg, t0=t0):
                c2, ci = st
                if ci < 3:
                    p2 = rot("psT")

                    def tru(e, c2=c2, p2=p2):
                        for j in range(4):
                            ins = e.transpose(out=psT[p2][:, j, :], in_=ub[c2][:, j * 128:(j + 1) * 128], identity=c_idb[:, :])
                        return ins
                    O("pe", tru, reads=[f"ub{c2}", "const"], writes=[f"psT{p2}"])
                    b2 = rot("tokb")
                    O("act", lambda e, p2=p2, b2=b2: e.copy(out=tokb[b2][:, :, :], in_=psT[p2][:, :, :]),
                      reads=[f"psT{p2}"], writes=[f"tokb{b2}"])

                    @_nd(1)
                    def stutok(e, dma, b2=b2, ci=ci, t0=t0):
                        dma(Utok[t0:t0 + 512, ci * 128:(ci + 1) * 128].rearrange("(j p) c -> p j c", p=128), tokb[b2][:, :, :])
                    O("sp", stutok, reads=[f"tokb{b2}"], writes=[f"Utok{tg}"], dma=True)

            def fd1(a3):
                O("act", lambda e, a3=a3: e.activation(out=fd[:, :], in_=psA[a3][0:8, :], func=AF.Exp,
                                                      bias=c_fd[0:8, 0:1], scale=c_cols[0:8, 2:3]),
                  reads=[f"psA{a3}", "cfd0", "const"], writes=["fd"])
                O("act", lambda e: e.activation(out=fd[:, :], in_=fd[:, :], func=AF.Ln, bias=1.0, scale=1.0),
                  reads=["fd"], writes=["fd"])
                O("dve", lambda e: e.tensor_scalar(out=fd[:, :], in0=fd[:, :], scalar1=c_cols[0:8, 4:5], scalar2=None, op0=ALU.mult),
                  reads=["fd", "const"], writes=["fd"])
                return 0

            def fd2(st, tg=tg, t0=t0):
                f2 = rot("psF")

                def trf(e, f2=f2):
                    for j in range(4):
                        ins = e.transpose(out=psF[f2][:, j, 0:8], in_=fd[0:8, j * 128:(j + 1) * 128], identity=c_idf[0:8, 0:8])
                    return ins
                O("pe", trf, reads=["fd", "const"], writes=[f"psF{f2}"])
                O("act", lambda e, f2=f2: e.copy(out=fdt[:, :, :], in_=psF[f2][:, :, 0:8]),
                  reads=[f"psF{f2}"], writes=["fdt"])

                @_nd(2)
                def stfd(e, dma, t0=t0):
                    dma(FD[t0:t0 + 512, :].rearrange("(j p) c -> p j c", p=128), fdt[:, :, :])
                    dma(lf_out[t0:t0 + 512, :].rearrange("(j p) c -> p j c", p=128), fdt[:, :, 0:4])
                O("sp", stfd, reads=["fdt"], writes=[f"FD{tg}", "lf_out"], dma=True)

            chunks.append((1536, 8, None, (lambda a3: fd1(a3)), fd2))
            for cc in range(4):
                chunks.append((cc * 128, 128, None, (lambda a3, cc=cc: qk1(a3, cc)), qk2))
            for hp in range(2):
                chunks.append((512 + hp * 128, 128, None, (lambda a3, hp=hp: v1(a3, hp)), v2))
            for hp in range(2):
                chunks.append((768 + hp * 128, 128, None, (lambda a3, hp=hp: z1(a3, hp)), z2))
            for ci in range(4):
                chunks.append((1024 + ci * 128, 128, (lambda ci=ci: pre_conv(ci)), (lambda a3, ci=ci: conv1(a3, ci)), conv2))

            def emit_mm(ci_):
                c0, m, pre, p1_, p2_ = chunks[ci_]
                a3 = mmc[0] % 3
                mmc[0] += 1
                if pre is not None:
                    pre()
                O("pe", mm_chunk(c0, m, psA[a3]), reads=hT_res + ["wb"], writes=[f"psA{a3}"])
                return a3
            ncn = len(chunks)
            a3s = {0: emit_mm(0), 1: emit_mm(1)}
            sts = {}
            for ci_ in range(ncn + 2):
                if ci_ + 2 < ncn:
                    a3s[ci_ + 2] = emit_mm(ci_ + 2)
                if ci_ == 1 and tg + 1 < NG:
                    part_a_norm(tg + 1)
                if ci_ == 8 and tg + 1 < NG:
                    part_a_tr(tg + 1)
                if ci_ < ncn:
                    sts[ci_] = chunks[ci_][3](a3s[ci_])
                if ci_ >= 2:
                    chunks[ci_ - 2][4](sts[ci_ - 2])
        tk.barrier()
        tk.flush(sems, dsems)
        es.close()

    phase1()

    def phase1b():
        es = ExitStack()

        def sb(name, shape, dt=F32):
            return es.enter_context(nc.sbuf_tensor(f'ph1_' + name, list(shape), dt))

        def ps(name, shape, dt=F32):
            return es.enter_context(nc.psum_tensor(f'ph1_' + name, list(shape), dt))

        NBmax = max(T // 128, NBS)
        LF = sb("LF", [128, NBmax, 4])
        A = [sb(f"scanA{i}", [128, NBmax, 4]) for i in range(2)]
        Ff = sb("Ff", [128, NBmax, 4])
        R1 = sb("R1", [128, NBmax, 4])
        HF = sb("HF", [128, NBmax, 4])
        AQ = sb("AQ", [128, NBmax, 24], BF16)
        AK = sb("AK", [128, NBmax, 24], BF16)
        stg = [sb(f"stg{i}", [24, 4, 128], BF16) for i in range(2)]
        psW = ps("psW", [128, 512])
        psTot = ps("psTot", [128, 512])
        psT = [ps(f"psT{i}", [128, 4, 128], BF16) for i in range(2)]
        ck = [sb(f"ck{i}", [128, 4, 256]) for i in range(4)]
        ckb = [sb(f"ckb{i}", [128, 4, 256], BF16) for i in range(2)]
        kst = [sb(f"kst{i}", [128, 4, 128], BF16) for i in range(2)]
        ckc = [0]
        AQv = AQ[:, :, :].rearrange("p b (i h) -> p b i h", h=4)
        AKv = AK[:, :, :].rearrange("p b (i h) -> p b i h", h=4)
        O("dve", lambda e: e.memset(AQ[:, :, :], -1.0), writes=["AQ"])
        O("dve", lambda e: e.memset(AK[:, :, :], 1.0), writes=["AK"])

        def cumsum_and_aug(nb, lf_reads):
            n4 = nb * 4
            O("pe", lambda e: e.matmul(psW[:, 0:n4], lhsT=c_trf[:, :], rhs=LF[:, 0:nb, :], start=True, stop=True),
              reads=lf_reads + ["const"], writes=["psW"])
            O("pe", lambda e: e.matmul(psTot[:, 0:n4], lhsT=c_onf[:, :], rhs=LF[:, 0:nb, :], start=True, stop=True),
              reads=lf_reads + ["const"], writes=["psTot"])
            O("act", lambda e: e.copy(out=A[0][:, 0:nb, :], in_=psTot[:, 0:n4].rearrange("p (b h) -> p b h", h=4)),
              reads=["psTot"], writes=["scanA0"])
            cur = 0
            d = 1
            while d < nb:
                o = 1 - cur
                O("dve", lambda e, cur=cur, o=o, d=d: e.tensor_tensor(out=A[o][:, d:nb, :], in0=A[cur][:, d:nb, :],
                                                                    in1=A[cur][:, 0:nb - d, :], op=ALU.add),
                  reads=[f"scanA{cur}"], writes=[f"scanA{o}"])
                O("dve", lambda e, cur=cur, o=o, d=d: e.tensor_copy(out=A[o][:, 0:d, :], in_=A[cur][:, 0:d, :]),
                  reads=[f"scanA{cur}", f"scanA{o}"], writes=[f"scanA{o}"])
                cur = o
                d *= 2
            O("dve", lambda e, cur=cur: e.tensor_tensor(out=R1[:, 0:nb, :], in0=A[cur][:, 0:nb, :],
                                                       in1=psTot[:, 0:n4].rearrange("p (b h) -> p b h", h=4), op=ALU.subtract),
              reads=[f"scanA{cur}", "psTot"], writes=["R1"])
            O("dve", lambda e: e.tensor_tensor(out=Ff[:, 0:nb, :], in0=R1[:, 0:nb, :],
                                              in1=psW[:, 0:n4].rearrange("p (b h) -> p b h", h=4), op=ALU.add),
              reads=["R1", "psW"], writes=["Ff"])
            O("act", lambda e: e.copy(out=AQv[:, 0:nb, 0, :], in_=Ff[:, 0:nb, :]), reads=["Ff", "AQ"], writes=["AQ"])
            O("act", lambda e: e.copy(out=HF[:, 0:nb, :], in_=AQv[:, 0:nb, 0, :]), reads=["AQ"], writes=["HF"])
            O("dve", lambda e: e.tensor_tensor(out=R1[:, 0:nb, :], in0=Ff[:, 0:nb, :], in1=HF[:, 0:nb, :], op=ALU.subtract),
              reads=["Ff", "HF"], writes=["R1"])
            O("act", lambda e: e.copy(out=AQv[:, 0:nb, 1, :], in_=R1[:, 0:nb, :]), reads=["R1", "AQ"], writes=["AQ"])
            O("act", lambda e: e.copy(out=HF[:, 0:nb, :], in_=AQv[:, 0:nb, 1, :]), reads=["AQ"], writes=["HF"])
            O("dve", lambda e: e.tensor_tensor(out=R1[:, 0:nb, :], in0=R1[:, 0:nb, :], in1=HF[:, 0:nb, :], op=ALU.subtract),
              reads=["R1", "HF"], writes=["R1"])
            O("act", lambda e: e.copy(out=AQv[:, 0:nb, 2, :], in_=R1[:, 0:nb, :]), reads=["R1", "AQ"], writes=["AQ"])
            O("act", lambda e: e.copy(out=AKv[:, 0:nb, 3:6, :], in_=AQv[:, 0:nb, 0:3, :]), reads=["AQ", "AK"], writes=["AK"])

        def aug_store(src, srcname, b0, nbb, dst_fn, npart=128, ncol=128):
            p2 = nxt() % 2

            def trp(e, p2=p2):
                for j in range(nbb):
                    ins = e.transpose(out=psT[p2][0:24, j, 0:npart], in_=src[0:npart, b0 + j, :], identity=c_idb[0:npart, 0:npart])
                return ins
            O("pe", trp, reads=[srcname, "const"], writes=[f"psT{p2}"])
            O("act", lambda e, p2=p2: e.copy(out=stg[p2][:, 0:nbb, 0:ncol], in_=psT[p2][0:24, 0:nbb, 0:ncol]),
              reads=[f"psT{p2}"], writes=[f"stg{p2}"])
            O("sp", _nd(1)(lambda e, dma, p2=p2: dma(dst_fn(), stg[p2][:, 0:nbb, 0:ncol])),
              reads=[f"stg{p2}"], writes=[f"aug_{nxt()}"], dma=True)

        NB = T // 128
        for b0 in range(0, NB, 8):
            O("sp", _nd(1)(lambda e, dma, b0=b0: dma(
                LF[:, b0:b0 + 8, :], FD[b0 * 128:(b0 + 8) * 128, 0:4].rearrange("(b p) c -> p b c", p=128))),
              reads=[f"FD{tg}" for tg in range(NGP)] + (["LF"] if b0 else []), writes=["LF"], dma=True)
        cumsum_and_aug(NB, ["LF"])
        for b0 in range(0, NB, 4):
            aug_store(AQ, "AQ", b0, 4, lambda b0=b0: QT[64:70, :, b0 * 128:(b0 + 4) * 128].rearrange("i h (j p) -> (i h) j p", p=128))
            aug_store(AK, "AK", b0, 4, lambda b0=b0: KTn[64:70, :, b0 * 128:(b0 + 4) * 128].rearrange("i h (j p) -> (i h) j p", p=128))
        NBP = past // 128
        for s in range(NSQ):
            ts0 = T + s * TS
            O("dve", lambda e: e.memset(LF[:, 0:NBS, :], 0.0), reads=["LF"], writes=["LF"])

            @_nd(3)
            def ldlf(e, dma, s=s, ts0=ts0):
                dma(LF[:, 0:NBP // 2, :], c_lf[s, 0:past // 2, :].rearrange("(b p) c -> p b c", p=128))
                dma(LF[:, NBP // 2:NBP, :], c_lf[s, past // 2:past, :].rearrange("(b p) c -> p b c", p=128))
                dma(LF[0:TS, NBP, :], FD[ts0:ts0 + TS, 0:4])
            O("sp", ldlf, reads=[f"FD{NG - 1}"], writes=["LF"], dma=True)
            cumsum_and_aug(NBS, ["LF"])
            for b0 in range(0, NBS, 4):
                nbb = min(4, NBS - b0)
                aug_store(AK, "AK", b0, nbb, lambda b0=b0, nbb=nbb, s=s:
                          KTs[s, 64:70, :, b0 * 128:(b0 + nbb) * 128].rearrange("i h (j p) -> (i h) j p", p=128))
            aug_store(AQ, "AQ", NBP, 1, lambda ts0=ts0: QT[64:70, :, ts0:ts0 + TS].rearrange("i h (j p) -> (i h) j p", p=TS),
                      npart=TS, ncol=TS)
            for b0 in range(0, NBP, 4):
                ckc[0] += 1
                i4 = ckc[0] % 4
                i2 = ckc[0] % 2
                O("pool", _nd(1)(lambda e, dma, i4=i4, s=s, b0=b0: dma(
                    ck[i4][:, :, :], c_k[s, b0 * 128:(b0 + 4) * 128, :].rearrange("(j p) c -> p j c", p=128))),
                  writes=[f"ck{i4}"], dma=True)
                O("act", lambda e, i2=i2, i4=i4: e.copy(out=ckb[i2][:, :, :], in_=ck[i4][:, :, :]), reads=[f"ck{i4}"], writes=[f"ckb{i2}"])
                for hp in range(2):
                    p2 = nxt() % 2

                    def trk(e, i2=i2, hp=hp, p2=p2):
                        for j in range(4):
                            ins = e.transpose(out=psT[p2][:, j, :], in_=ckb[i2][:, j, hp * 128:(hp + 1) * 128], identity=c_idb[:, :])
                        return ins
                    O("pe", trk, reads=[f"ckb{i2}", "const"], writes=[f"psT{p2}"])
                    k2 = nxt() % 2
                    O("dve", lambda e, p2=p2, k2=k2: e.tensor_copy(out=kst[k2][:, :, :], in_=psT[p2][:, :, :]),
                      reads=[f"psT{p2}"], writes=[f"kst{k2}"])

                    @_nd(2)
                    def stks(e, dma, k2=k2, hp=hp, s=s, b0=b0):
                        for hh in range(2):
                            dma(KTs[s, 0:64, hp * 2 + hh, b0 * 128:(b0 + 4) * 128].rearrange("d (j p) -> d j p", p=128),
                                kst[k2][hh * 64:(hh + 1) * 64, :, :])
                    O("sp", stks, reads=[f"kst{k2}"], writes=[f"KTs{s}"], dma=True)
            O("sp", _nd(1)(lambda e, dma, s=s, ts0=ts0: dma(KTs[s, 0:64, :, past:past + TS], KTn[0:64, :, ts0:ts0 + TS])),
              reads=[f"KTn{NG - 1}"], writes=[f"KTs{s}"], dma=True)
            O("pool", _nd(1)(lambda e, dma, s=s: dma(Vs[s, 0:past, :], c_v[s, :, :])), writes=[f"Vs{s}"], dma=True)
            O("sp", _nd(1)(lambda e, dma, s=s, ts0=ts0: dma(Vs[s, past:past + TS, :], Vb[ts0:ts0 + TS, :])),
              reads=[f"Vb{NG - 1}"], writes=[f"Vs{s}"], dma=True)
        tk.barrier()
        tk.flush(sems, dsems)
        es.close()

    if STAGE >= 2:
        phase1b()

    RG = [[0, 1, 2, 3], [4, 5, 6, 7]]
    EARLY_AG = False
    deferred = list(range(NCH))

    def phase23():
        es = ExitStack()

        def sb(name, shape, dt=F32):
            return es.enter_context(nc.sbuf_tensor('p23_' + name, list(shape), dt))

        def ps(name, shape, dt=F32):
            return es.enter_context(nc.psum_tensor('p23_' + name, list(shape), dt))

        TK = max(T, TKS)
        qt = sb("qt", [128, T], BF16)
        kt = sb("kt", [128, TK], BF16)
        vt = sb("vt", [128, TK // 128, 128], BF16)
        pb = [sb(f"pb{i}", [128, 512], BF16) for i in range(4)]
        rc = sb("rc", [128, 512])
        at = sb("at", [64, 512], BF16)
        att = [sb(f"att{i}", [128, 4, 64], BF16) for i in range(2)]
        psS = [ps(f"psS{i}", [128, 512]) for i in range(2)]
        psO = [ps(f"psO{i}", [128, 512]) for i in range(2)]
        tA = ps("ssd_tA", [128, 4, 64])
        tB = ps("ssd_tB", [128, 256])
        tC = ps("ssd_tC", [64, 256])
        psTa = ps("psTa", [128, 4, 64], BF16)
        O("dve", lambda e: e.memset(vt[:, :, 64:128], 1.0), writes=["vt1"])
        O("dve", lambda e: e.memset(qt[64:128, :], 0.0), writes=["qt"])
        O("dve", lambda e: e.memset(kt[64:128, :], 0.0), writes=["kt"])
        sbc = [0]

        def job(q_ap, k_ap, v_ap, Tq, nk, pst, e_row0, h):
            nkb = (nk + 127) // 128

            @_nd(2)
            def ld(e, dma):
                dma(qt[0:70, 0:Tq], q_ap)
                dma(kt[0:70, 0:nk], k_ap)
            O("sp", ld, writes=["qt", "kt"], dma=True)
            nfb = nk // 128
            VB = 8
            for b0 in range(0, nfb, VB):
                b1 = min(nfb, b0 + VB)
                O("sp", _nd(1)(lambda e, dma, b0=b0, b1=b1: dma(
                    vt[:, b0:b1, 0:64], v_ap[b0 * 128:b1 * 128, :].rearrange("(b p) c -> p b c", p=128))),
                  reads=["vt"] if b0 else [], writes=["vt"], dma=True)
            if nk % 128:
                nf = nk // 128
                O("sp", _nd(1)(lambda e, dma: dma(vt[0:nk - nf * 128, nf, 0:64], v_ap[nf * 128:nk, :])),
                  reads=["vt"], writes=["vt"], dma=True)
            SBQ = min(512, Tq)
            LA = 2
            tiles = []
            sbs = []
            for q0 in range(0, Tq, SBQ):
                nq = SBQ
                sbc[0] += 1
                o2 = sbc[0] % 2
                blocks = []
                for kb in range(nkb):
                    ks = kb * 128
                    kn = min(128, nk - ks)
                    if ks + kn - 1 <= pst + q0:
                        blocks.append((kb, ks, kn, q0, False))
                    elif ks <= pst + q0 + nq - 1:
                        blocks.append((kb, ks, kn, ks - pst, True))
                sbs.append((q0, nq, o2, len(blocks)))
                for bi, blk in enumerate(blocks):
                    tiles.append((len(sbs) - 1, bi, blk))

            def emit_qk(ti):
                si, bi, (kb, ks, kn, qlo, diag) = tiles[ti]
                q0, nq, o2, nb_ = sbs[si]
                n1 = q0 + nq - qlo
                s3 = ti % 2
                p4 = ti % 4
                O("pe", lambda e, s3=s3, ks=ks, kn=kn, qlo=qlo, n1=n1: e.matmul(
                    psS[s3][0:kn, 0:n1], lhsT=kt[0:128, ks:ks + kn], rhs=qt[0:128, qlo:qlo + n1], start=True, stop=True),
                  reads=["qt", "kt"], writes=[f"psS{s3}"])
                O("act", lambda e, s3=s3, p4=p4, kn=kn, n1=n1: e.activation(
                    out=pb[p4][0:kn, 0:n1], in_=psS[s3][0:kn, 0:n1], func=AF.Exp),
                  reads=[f"psS{s3}"], writes=[f"pb{p4}"])
                if diag:
                    w = min(128, n1)
                    O("pool", lambda e, p4=p4, kn=kn, w=w: e.tensor_tensor(
                        out=pb[p4][0:kn, 0:w], in0=pb[p4][0:kn, 0:w], in1=c_trb[0:kn, 0:w], op=ALU.mult),
                      reads=[f"pb{p4}", "const"], writes=[f"pb{p4}"])

            def emit_pv(ti):
                si, bi, (kb, ks, kn, qlo, diag) = tiles[ti]
                q0, nq, o2, nb_ = sbs[si]
                n1 = q0 + nq - qlo
                p4 = ti % 4
                O("pe", lambda e, o2=o2, p4=p4, kb=kb, kn=kn, qlo=qlo, n1=n1, q0=q0, nq=nq, bi=bi, nb_=nb_: e.matmul(
                    psO[o2][:, qlo - q0:nq], lhsT=vt[0:kn, kb, :], rhs=pb[p4][0:kn, 0:n1],
                    start=(bi == 0), stop=(bi == nb_ - 1)),
                  reads=[f"pb{p4}", "vt", "vt1"], writes=[f"psO{o2}"])
                if bi == nb_ - 1:
                    finish(si)

            def finish(si):
                q0, nq, o2, nb_ = sbs[si]
                O("dve", lambda e, o2=o2, nq=nq: e.reciprocal(out=rc[64:128, 0:nq], in_=psO[o2][64:128, 0:nq]),
                  reads=[f"psO{o2}"], writes=["rc"])
                O("dve", lambda e, o2=o2, nq=nq: e.tensor_tensor(out=at[:, 0:nq], in0=psO[o2][0:64, 0:nq], in1=rc[64:128, 0:nq], op=ALU.mult),
                  reads=[f"psO{o2}", "rc"], writes=["at"])
                pending.append((cur_idx[0], lambda si=si: finish2(si)))

            def finish2(si):
                q0, nq, o2, nb_ = sbs[si]
                nj = (nq + 127) // 128
                wj = min(128, nq)
                p2 = 0

                def tra(e, p2=p2, nj=nj, wj=wj):
                    for j in range(nj):
                        ins = e.transpose(out=psTa[0:wj, j, 0:64], in_=at[:, j * 128:j * 128 + wj], identity=c_idb[0:64, 0:64])
                    return ins
                O("pe", tra, reads=["at", "const"], writes=["psTa"])
                a2 = nxt() % 2
                O("act", lambda e, p2=p2, a2=a2, nj=nj, wj=wj: e.copy(out=att[a2][0:wj, 0:nj, :], in_=psTa[0:wj, 0:nj, 0:64]),
                  reads=["psTa"], writes=[f"att{a2}"])
                r0 = e_row0 + q0
                O("sp", _nd(1)(lambda e, dma, a2=a2, r0=r0, nq=nq, nj=nj, wj=wj, h=h: dma(
                    E[r0:r0 + nq, h * 64:(h + 1) * 64].rearrange("(j p) c -> p j c", p=wj), att[a2][0:wj, 0:nj, :])),
                  reads=[f"att{a2}"], writes=[f"Ea_{r0}_{h}"], dma=True)
                if EARLY_AG and Tq == T and h == HG - 1 and (q0 // 512) % 2 == 1 and STAGE >= 5:
                    q = q0 // 1024
                    rd = [f"Ea_{rr}_{hh}" for rr in (q * 1024, q * 1024 + 512) for hh in range(HG)] + \
                         [f"Eu_{rr}" for rr in (q * 1024, q * 1024 + 512)]
                    if all(r_ in tk.res for r_ in rd):
                        O("pool", lambda e, q=q: e.collective_compute("AllGather", ALU.bypass, replica_groups=RG,
                                                                      ins=[E[q * 1024:(q + 1) * 1024, :]], outs=[Gq[q][:, :]]),
                          reads=rd, writes=[f"G{q}"])
                    else:
                        deferred.append(q)

            nt_ = len(tiles)
            pending = []
            cur_idx = [0]
            for idx in range(nt_ + LA):
                cur_idx[0] = idx
                if idx < nt_:
                    emit_qk(idx)
                if idx - LA >= 0:
                    emit_pv(idx - LA)
                while pending and idx >= pending[0][0] + 6:
                    pending.pop(0)[1]()
                yield
            while pending:
                pending.pop(0)[1]()

        bt = [sb(f"bt{i}", [128, 512], BF16) for i in range(2)]
        ct = [sb(f"ct{i}", [128, 512], BF16) for i in range(2)]
        xk = [sb(f"xk{i}", [64, 8, 384], BF16) for i in range(2)]
        fdm = [sb(f"fdm{i}", [64, 8, 8]) for i in range(2)]
        zs = [sb(f"zs{i}", [64, 8, 256]) for i in range(2)]
        u8 = [sb(f"u8{i}", [64, 8, 256], BF16) for i in range(2)]
        Hf = sb("Hf", [128, 256])
        Hb = sb("Hb", [128, 256], BF16)
        hio = sb("hio", [128, 2, 128])
        dA = sb("dA", [64, 4])
        Rr = sb("Rr", [64, 4, 64])
        cum = sb("cum", [64, 4])
        ecum = sb("ecum", [64, 4])
        seg = sb("seg", [64, 4, 64])
        CBm = sb("CBm", [64, 64])
        MT = sb("MT", [64, 4, 64], BF16)
        ysb = sb("ysb", [64, 256])
        wl = sb("wl", [64, 4])
        xw = sb("xw", [64, 256], BF16)
        dec = sb("dec", [128, 4])
        psR = tA
        psH = tB[:, :]
        psY = tB[0:64, :]
        psC = tB[0:64, 0:4]
        psF = tB[:, :].rearrange("p (a n) -> p a n", n=128)
        psYo = tC[:, :]
        psCB = tC[:, 0:64]

        def seq_job(r_base, ntok, init_ap, out_ap, tag):
            if init_ap is None:
                O("dve", lambda e: e.memset(Hf[:, :], 0.0), writes=["Hf"])
            else:
                O("sp", _nd(1)(lambda e, dma: dma(hio[:, :, :], init_ap.rearrange("(a p) n -> p a n", p=128))),
                  writes=["hio"], dma=True)

                def trin(e):
                    for a in range(2):
                        ins = e.transpose(out=psF[:, a, :], in_=hio[:, a, :], identity=c_idf[:, :])
                    return ins
                O("pe", trin, reads=["hio", "const"], writes=["tB"])
                O("act", lambda e: e.copy(out=Hf[:, :], in_=tB[:, :]), reads=["tB"], writes=["Hf"])
            O("act", lambda e: e.copy(out=Hb[:, :], in_=Hf[:, :]), reads=["Hf"], writes=["Hb"])
            SC = min(512, ntok)
            nchk = SC // 64
            for r0 in range(r_base, r_base + ntok, SC):
                i2 = nxt() % 2

                @_nd(5)
                def ld(e, dma, i2=i2, r0=r0):
                    dma(bt[i2][:, 0:SC], UT[0:128, r0:r0 + SC])
                    dma(ct[i2][:, 0:SC], UT[128:256, r0:r0 + SC])
                    dma(xk[i2][:, 0:nchk, :], Utok[r0:r0 + SC, :].rearrange("(c p) f -> p c f", p=64))
                    dma(fdm[i2][:, 0:nchk, :], FD[r0:r0 + SC, :].rearrange("(c p) f -> p c f", p=64))
                    dma(zs[i2][:, 0:nchk, :], ZS[r0:r0 + SC, :].rearrange("(c p) f -> p c f", p=64))
                O("sp", ld, writes=[f"ssdin{i2}"], dma=True)
                IN = f"ssdin{i2}"
                for c in range(nchk):
                    dt_ = fdm[i2][:, c, 4:8]
                    cs = slice(c * 64, (c + 1) * 64)
                    O("dve", lambda e, dt_=dt_: e.tensor_tensor(out=dA[:, :], in0=dt_, in1=c_r4[0:64, 0:4], op=ALU.mult),
                      reads=[IN, "cr4"], writes=["dA"])
                    for h in range(HG):
                        O("dve", lambda e, h=h: e.tensor_scalar(out=Rr[:, h, :], in0=c_trf[0:64, 0:64], scalar1=dA[:, h:h + 1],
                                                                scalar2=None, op0=ALU.mult),
                          reads=["dA", "const"], writes=["Rr"])
                    yield
                    O("pe", lambda e: e.matmul(psC[:, :], lhsT=c_trf[0:64, 0:64], rhs=dA[:, :], start=True, stop=True),
                      reads=["dA", "const"], writes=["tB"])
                    O("pe", lambda e: e.matmul(psR[:, :, :], lhsT=c_onf[0:64, :], rhs=Rr[:, :, :], start=True, stop=True),
                      reads=["Rr", "const"], writes=["tA"])
                    yield
                    O("act", lambda e: e.copy(out=cum[:, :], in_=psC[:, :]), reads=["tB"], writes=["cum"])
                    O("act", lambda e: e.activation(out=ecum[:, :], in_=psC[:, :], func=AF.Exp), reads=["tB"], writes=["ecum"])
                    for h in range(HG):
                        O("dve", lambda e, h=h: e.tensor_scalar(out=seg[:, h, :], in0=psR[0:64, h, :], scalar1=cum[:, h:h + 1],
                                                                scalar2=0.0, op0=ALU.subtract, op1=ALU.min),
                          reads=["tA", "cum"], writes=["seg"])
                    yield
                    O("act", lambda e: e.activation(out=seg[:, :, :], in_=seg[:, :, :], func=AF.Exp), reads=["seg"], writes=["seg"])
                    O("pe", lambda e, i2=i2, cs=cs: e.matmul(psCB[:, :], lhsT=bt[i2][:, cs], rhs=ct[i2][:, cs], start=True, stop=True),
                      reads=[IN], writes=["tC"])
                    O("dve", lambda e: e.tensor_tensor(out=CBm[:, :], in0=psCB[:, :], in1=c_trf[0:64, 0:64], op=ALU.mult),
                      reads=["tC", "const"], writes=["CBm"])
                    yield
                    for h in range(HG):
                        O("dve", lambda e, h=h, dt_=dt_: e.scalar_tensor_tensor(out=MT[:, h, :], in0=seg[:, h, :], scalar=dt_[:, h:h + 1],
                                                                               in1=CBm[:, :], op0=ALU.mult, op1=ALU.mult),
                          reads=["seg", "CBm", IN], writes=["MT"])

                    yield
                    def ydiag(e, i2=i2, c=c):
                        for h in range(HG):
                            ins = e.matmul(psY[:, h * 64:(h + 1) * 64], lhsT=MT[:, h, :], rhs=xk[i2][:, c, h * 64:(h + 1) * 64],
                                           start=True, stop=True)
                        return ins
                    O("pe", ydiag, reads=["MT", IN], writes=["tB"])
                    O("pe", lambda e, i2=i2, cs=cs: e.matmul(psYo[:, :], lhsT=ct[i2][:, cs], rhs=Hb[:, :], start=True, stop=True),
                      reads=[IN, "Hb"], writes=["tC"])
                    yield
                    O("act", lambda e: e.copy(out=ysb[:, :], in_=psY[:, :]), reads=["tB"], writes=["ysb"])
                    for h in range(HG):
                        hs = slice(h * 64, (h + 1) * 64)
                        O("dve", lambda e, h=h, hs=hs: e.scalar_tensor_tensor(out=ysb[:, hs], in0=psYo[:, hs], scalar=ecum[:, h:h + 1],
                                                                             in1=ysb[:, hs], op0=ALU.mult, op1=ALU.add),
                          reads=["tC", "ecum", "ysb"], writes=["ysb"])
                        O("dve", lambda e, h=h, hs=hs, i2=i2, c=c: e.scalar_tensor_tensor(
                            out=ysb[:, hs], in0=xk[i2][:, c, hs], scalar=c_r4[0:64, 4 + h:5 + h], in1=ysb[:, hs],
                            op0=ALU.mult, op1=ALU.add), reads=[IN, "const", "ysb"], writes=["ysb"])
                    O("dve", lambda e, i2=i2, c=c: e.tensor_tensor(out=u8[i2][:, c, :], in0=ysb[:, :], in1=zs[i2][:, c, :], op=ALU.mult),
                      reads=["ysb", IN], writes=[f"u8{i2}"])
                    yield
                    O("dve", lambda e: e.tensor_tensor(out=wl[:, :], in0=psR[0:64, :, 63], in1=cum[:, :], op=ALU.subtract),
                      reads=["tA", "cum"], writes=["wl"])
                    O("act", lambda e: e.activation(out=wl[:, :], in_=wl[:, :], func=AF.Exp), reads=["wl"], writes=["wl"])
                    O("dve", lambda e, dt_=dt_: e.tensor_tensor(out=wl[:, :], in0=wl[:, :], in1=dt_, op=ALU.mult),
                      reads=["wl", IN], writes=["wl"])
                    for h in range(HG):
                        hs = slice(h * 64, (h + 1) * 64)
                        O("dve", lambda e, h=h, hs=hs, i2=i2, c=c: e.tensor_scalar(out=xw[:, hs], in0=xk[i2][:, c, hs], scalar1=wl[:, h:h + 1],
                                                                                  scalar2=None, op0=ALU.mult),
                          reads=[IN, "wl"], writes=["xw"])
                    yield
                    O("pe", lambda e, i2=i2, c=c: e.matmul(psH[:, :], lhsT=xk[i2][:, c, 256:384], rhs=xw[:, :], start=True, stop=True),
                      reads=[IN, "xw"], writes=["tB"])
                    O("act", lambda e: e.activation(out=dec[:, :], in_=psR[:, :, 63], func=AF.Exp), reads=["tA"], writes=["dec"])
                    yield
                    for h in range(HG):
                        hs = slice(h * 64, (h + 1) * 64)
                        O("dve", lambda e, h=h, hs=hs: e.scalar_tensor_tensor(out=Hf[:, hs], in0=Hf[:, hs], scalar=dec[:, h:h + 1],
                                                                             in1=psH[:, hs], op0=ALU.mult, op1=ALU.add),
                          reads=["Hf", "dec", "tB"], writes=["Hf"])
                    O("act", lambda e: e.copy(out=Hb[:, :], in_=Hf[:, :]), reads=["Hf"], writes=["Hb"])
                yield
                O("sp", _nd(1)(lambda e, dma, i2=i2, r0=r0: dma(
                    E[r0:r0 + SC, 256:512].rearrange("(c p) f -> p c f", p=64), u8[i2][:, 0:nchk, :])),
                  reads=[f"u8{i2}"], writes=[f"Eu_{r0}"], dma=True)
            def trout(e):
                for a in range(2):
                    ins = e.transpose(out=psF[:, a, :], in_=Hf[:, a * 128:(a + 1) * 128], identity=c_idf[:, :])
                return ins
            O("pe", trout, reads=["Hf", "const"], writes=["tB"])
            O("act", lambda e: e.copy(out=hio[:, :, :], in_=psF[:, :, :]), reads=["tB"], writes=["hio"])
            O("sp", _nd(1)(lambda e, dma: dma(out_ap.rearrange("(a p) n -> p a n", p=128), hio[:, :, :])),
              reads=["hio"], writes=[f"ssm_{tag}"], dma=True)


        def att_gen():
            for h in range(HG):
                yield from job(QT[:, h, 0:T], KTn[:, h, 0:T], Vb[0:T, h * 64:(h + 1) * 64], T, T, 0, 0, h)
            for s in range(NSQ):
                ts0 = T + s * TS
                for h in range(HG):
                    yield from job(QT[:, h, ts0:ts0 + TS], KTs[s, :, h, 0:past + TS], Vs[s, 0:past + TS, h * 64:(h + 1) * 64],
                                   TS, past + TS, past, ts0, h)

        def ssd_gen():
            yield from seq_job(0, T, None, ssm_p[:, :], "p")
            for s in range(NSQ):
                yield from seq_job(T + s * TS, TS, c_ssm[s, :, :], ssm_s[s, :, :], f"s{s}")

        n_att = HG * sum((i + 1) * 4 + 4 for i in range(T // 512)) + NSQ * HG * (past // 128 + 3)
        n_ssd = (T // 64 + NSQ) * 11
        ga, gs_ = att_gen(), ssd_gen()
        da = ds = 0
        a_alive = s_alive = True
        while a_alive or s_alive:
            if a_alive:
                try:
                    next(ga)
                    da += 1
                except StopIteration:
                    a_alive = False
            while s_alive and (not a_alive or (os.environ.get('INTERLEAVE', '1') == '1' and ds * n_att <= da * n_ssd)):
                try:
                    next(gs_)
                    ds += 1
                except StopIteration:
                    s_alive = False
        tk.barrier()
        tk.flush(sems, dsems)
        es.close()

    if STAGE >= 3:
        phase23()


    if STAGE >= 5:
        for q in deferred:
            O("pool", lambda e, q=q: e.collective_compute("AllGather", ALU.bypass, replica_groups=RG,
                                                          ins=[E[q * 1024:(q + 1) * 1024, :]], outs=[Gq[q][:, :]]),
              writes=[f"G{q}"])
        O("pool", lambda e: e.collective_compute("AllGather", ALU.bypass, replica_groups=RG,
                                                 ins=[E[T:T + 512, :]], outs=[Gs[:, :]]), writes=["Gs"])

    def phase5():
        es = ExitStack()

        def sb(name, shape, dt=F32):
            return es.enter_context(nc.sbuf_tensor(f'ph4_' + name, list(shape), dt))

        def ps(name, shape, dt=F32):
            return es.enter_context(nc.psum_tensor(f'ph4_' + name, list(shape), dt))

        mixT = sb("mixT", [128, 16, 512], BF16)
        x1 = sb("x1", [128, 4, D])
        AT = sb("AT", [128, 44, 512], BF16)
        NWB = 3
        wbuf = [sb(f"wbuf{i}", [128, 8192], BF16) for i in range(NWB)]
        selT = sb("selT", [128, 8, 256], BF16)
        selS = sb("selS", [128, 4, 128], BF16)
        c_gffn = sb("c_gffn", [128, D])
        c_gssd = sb("c_gssd", [128, 8])
        junk = sb("junk", [128, D])
        hb = [sb(f"hb{i}", [128, D], BF16) for i in range(2)]
        ss = [sb(f"ss{i}", [128, 2]) for i in range(2)]
        sq = sb("sq", [128, 512], BF16)
        rstd = sb("rstd", [128, 512])
        gt = [sb(f"gt{i}", [128, 512]) for i in range(2)]
        pz = [ps(f"pz{i}", [128, 512]) for i in range(6)]
        psT = [ps(f"psT{i}", [128, 4, 128], BF16) for i in range(2)]
        zc = [0]

        def nz():
            zc[0] += 1
            return zc[0] % 6
        wc = [0]

        def wload(view_fn, src_ap, src_reads):
            i = wc[0] % NWB
            wc[0] += 1
            O("sp", _nd(1)(lambda e, dma, i=i: dma(view_fn(wbuf[i]), src_ap)), reads=src_reads, writes=[f"wbuf{i}"], dma=True)
            return i

        @_nd(4)
        def ldc(e, dma):
            dma(selT[:, :, :], sel[:, :, :])
            dma(selS[:, :, :], sel_s[:, :, :])
            dma(c_gffn[:, :], gffn[0:1, :].partition_broadcast(128))
            dma(c_gssd[:, :], gssd[:, :])
        O("sp", ldc, writes=["p5c"], dma=True)

        items = [(Gq[q], 8, selT, 2, f"G{q}") for q in range(NCH)]
        groups = [items[i:i + 2] for i in range(0, NCH, 2)] + [[(Gs, 4, selS, 1, "Gs")]]
        def do_group(grp, row):
            ntile = sum(it[3] for it in grp)
            ntok = ntile * 128
            O("sp", _nd(1)(lambda e, dma, row=row, ntile=ntile: dma(
                x1[:, 0:ntile, :], x2[row:row + ntok, :].rearrange("(j p) c -> p j c", p=128))), writes=["x1"], dma=True)
            toff = 0
            for (G, nblk, st, nt, gname) in grp:
                rows = nblk * 128
                for r in range(4):
                    wi = wload(lambda w, nblk=nblk: w[:, 0:nblk * 512].rearrange("p (b c) -> p b c", c=512),
                               G[r * rows:(r + 1) * rows, :].rearrange("(b p) c -> p b c", p=128), [gname])
                    gv = wbuf[wi][:, 0:nblk * 512].rearrange("p (b c) -> p b c", c=512)
                    for i in range(nt):
                        z = nz()
                        pzv = pz[z][:, :].rearrange("p (a t) -> p a t", t=128)

                        def selmm(e, gv=gv, st=st, nblk=nblk, i=i, pzv=pzv):
                            for cch in range(4):
                                for blk in range(nblk):
                                    ins = e.matmul(pzv[:, cch, :], lhsT=gv[:, blk, cch * 128:(cch + 1) * 128],
                                                   rhs=st[:, blk, i * 128:(i + 1) * 128], start=(blk == 0), stop=(blk == nblk - 1))
                            return ins
                        O("pe", selmm, reads=[f"wbuf{wi}", "p5c"], writes=[f"pz{z}"])
                        c0 = toff + i * 128
                        eng = "act" if (r + i) % 2 else "dve"
                        if eng == "act":
                            O("act", lambda e, r=r, c0=c0, pzv=pzv: e.copy(out=mixT[:, r * 4:r * 4 + 4, c0:c0 + 128], in_=pzv),
                              reads=[f"pz{z}"], writes=["mixT"])
                        else:
                            O("dve", lambda e, r=r, c0=c0, pzv=pzv: e.tensor_copy(out=mixT[:, r * 4:r * 4 + 4, c0:c0 + 128], in_=pzv),
                              reads=[f"pz{z}"], writes=["mixT"])
                toff += nt * 128
            for gi in range(2):
                kcs = [(2 * gi) * 4 + 2, (2 * gi) * 4 + 3, (2 * gi + 1) * 4 + 2, (2 * gi + 1) * 4 + 3]
                z = nz()
                for j, kc in enumerate(kcs):
                    O("act", lambda e, kc=kc: e.activation(out=sq[:, 0:ntok], in_=mixT[:, kc, 0:ntok], func=AF.Square),
                      reads=["mixT"], writes=["sq"])
                    O("pe", lambda e, z=z, j=j: e.matmul(pz[z][:, 0:ntok], lhsT=c_onb[:, :], rhs=sq[:, 0:ntok], start=(j == 0), stop=(j == 3)),
                      reads=["sq", "const"], writes=[f"pz{z}"])
                O("act", lambda e, z=z: e.activation(out=rstd[:, 0:ntok], in_=pz[z][:, 0:ntok], func=AF.Sqrt, bias=EPS, scale=1.0 / 512),
                  reads=[f"pz{z}"], writes=["rstd"])
                O("dve", lambda e: e.reciprocal(out=rstd[:, 0:ntok], in_=rstd[:, 0:ntok]), reads=["rstd"], writes=["rstd"])
                for j, kc in enumerate(kcs):
                    O("dve", lambda e, kc=kc, j=j, gi=gi: e.scalar_tensor_tensor(
                        out=mixT[:, kc, 0:ntok], in0=mixT[:, kc, 0:ntok], scalar=c_gssd[:, gi * 4 + j:gi * 4 + j + 1],
                        in1=rstd[:, 0:ntok], op0=ALU.mult, op1=ALU.mult), reads=["mixT", "rstd", "p5c"], writes=["mixT"])
            for cg in range(4):
                wi = wload(lambda w: w[:, :].rearrange("p (k n) -> p k n", n=512),
                           wo_b[:, cg * 512:(cg + 1) * 512].rearrange("(k p) n -> p k n", p=128), [f"wo_b_{r}" for r in range(0, D, 512)])
                wv = wbuf[wi][:, :].rearrange("p (k n) -> p k n", n=512)
                for i in range(ntile):
                    z = nz()

                    def womm(e, wv=wv, i=i, z=z):
                        for kc in range(16):
                            ins = e.matmul(pz[z][:, :], lhsT=mixT[:, kc, i * 128:(i + 1) * 128], rhs=wv[:, kc, :],
                                           start=(kc == 0), stop=(kc == 15))
                        return ins
                    O("pe", womm, reads=[f"wbuf{wi}", "mixT"], writes=[f"pz{z}"])
                    O("dve", lambda e, i=i, cg=cg, z=z: e.tensor_tensor(out=x1[:, i, cg * 512:(cg + 1) * 512], in0=pz[z][:, :],
                                                                       in1=x1[:, i, cg * 512:(cg + 1) * 512], op=ALU.add),
                      reads=[f"pz{z}", "x1"], writes=["x1"])
            for i in range(ntile):
                i2 = nxt() % 2
                O("act", lambda e, i=i, i2=i2: e.activation(out=junk[:, :], in_=x1[:, i, :], func=AF.Square, accum_out=ss[i2][:, 0:1]),
                  reads=["x1"], writes=["junk", f"ss{i2}"])
                O("act", lambda e, i2=i2: e.activation(out=ss[i2][:, 1:2], in_=ss[i2][:, 0:1], func=AF.Sqrt, bias=EPS, scale=1.0 / D),
                  reads=[f"ss{i2}"], writes=[f"ss{i2}"])
                O("dve", lambda e, i2=i2: e.reciprocal(out=ss[i2][:, 1:2], in_=ss[i2][:, 1:2]), reads=[f"ss{i2}"], writes=[f"ss{i2}"])
                O("dve", lambda e, i=i, i2=i2: e.scalar_tensor_tensor(out=hb[i2][:, :], in0=x1[:, i, :], scalar=ss[i2][:, 1:2],
                                                                     in1=c_gffn[:, :], op0=ALU.mult, op1=ALU.mult),
                  reads=["x1", f"ss{i2}", "p5c"], writes=[f"hb{i2}"])
                for k4 in range(4):
                    p2 = nxt() % 2

                    def tr(e, i2=i2, k4=k4, p2=p2):
                        for j in range(4):
                            kc = k4 * 4 + j
                            ins = e.transpose(out=psT[p2][:, j, :], in_=hb[i2][:, kc * 128:(kc + 1) * 128], identity=c_idb[:, :])
                        return ins
                    O("pe", tr, reads=[f"hb{i2}", "const"], writes=[f"psT{p2}"])
                    if k4 % 2:
                        O("act", lambda e, k4=k4, p2=p2, i=i: e.copy(out=mixT[:, k4 * 4:(k4 + 1) * 4, i * 128:(i + 1) * 128], in_=psT[p2][:, :, :]),
                          reads=[f"psT{p2}"], writes=["mixT"])
                    else:
                        O("dve", lambda e, k4=k4, p2=p2, i=i: e.tensor_copy(out=mixT[:, k4 * 4:(k4 + 1) * 4, i * 128:(i + 1) * 128], in_=psT[p2][:, :, :]),
                          reads=[f"psT{p2}"], writes=["mixT"])
            wg_reads = [f"wg_b_{r}" for r in range(0, D, 512)]
            wu_reads = [f"wu_b_{r}" for r in range(0, D, 512)]
            for fb in range(DFF // 512):
                wgi = wload(lambda w: w[:, :].rearrange("p (k n) -> p k n", n=512),
                            wg_b[:, fb * 512:(fb + 1) * 512].rearrange("(k p) n -> p k n", p=128), wg_reads)
                wui = wload(lambda w: w[:, :].rearrange("p (k n) -> p k n", n=512),
                            wu_b[:, fb * 512:(fb + 1) * 512].rearrange("(k p) n -> p k n", p=128), wu_reads)
                wgv = wbuf[wgi][:, :].rearrange("p (k n) -> p k n", n=512)
                wuv = wbuf[wui][:, :].rearrange("p (k n) -> p k n", n=512)
                for c4 in range(4):
                    zg, zu = nz(), nz()

                    def gmm(e, wv=wgv, c4=c4, z=zg):
                        for kc in range(16):
                            ins = e.matmul(pz[z][:, 0:ntok], lhsT=wv[:, kc, c4 * 128:(c4 + 1) * 128], rhs=mixT[:, kc, 0:ntok],
                                           start=(kc == 0), stop=(kc == 15))
                        return ins

                    def umm(e, wv=wuv, c4=c4, z=zu):
                        for kc in range(16):
                            ins = e.matmul(pz[z][:, 0:ntok], lhsT=wv[:, kc, c4 * 128:(c4 + 1) * 128], rhs=mixT[:, kc, 0:ntok],
                                           start=(kc == 0), stop=(kc == 15))
                        return ins
                    O("pe", gmm, reads=[f"wbuf{wgi}", "mixT"], writes=[f"pz{zg}"])
                    O("pe", umm, reads=[f"wbuf{wui}", "mixT"], writes=[f"pz{zu}"])
                    g2 = nxt() % 2
                    O("act", lambda e, z=zg, g2=g2: e.activation(out=gt[g2][:, 0:ntok], in_=pz[z][:, 0:ntok], func=AF.Silu),
                      reads=[f"pz{zg}"], writes=[f"gt{g2}"])
                    O("dve", lambda e, z=zu, g2=g2, fb=fb, c4=c4: e.tensor_tensor(out=AT[:, fb * 4 + c4, 0:ntok], in0=pz[z][:, 0:ntok],
                                                                                 in1=gt[g2][:, 0:ntok], op=ALU.mult),
                      reads=[f"pz{zu}", f"gt{g2}"], writes=["AT"])
            wd_reads = [f"wd_b_{r}" for r in range(0, DFF, 512)]
            for cg in range(4):
                zs_ = [nz() for _ in range(ntile)]
                for blk in range(4):
                    wi = wload(lambda w: w[:, 0:11 * 512].rearrange("p (f n) -> p f n", n=512),
                               wd_b[blk * 1408:(blk + 1) * 1408, cg * 512:(cg + 1) * 512].rearrange("(f p) n -> p f n", p=128), wd_reads)
                    wv = wbuf[wi][:, 0:11 * 512].rearrange("p (f n) -> p f n", n=512)
                    for i in range(ntile):
                        def dmm(e, wv=wv, i=i, blk=blk, z=zs_[i]):
                            for f in range(11):
                                ins = e.matmul(pz[z][:, :], lhsT=AT[:, blk * 11 + f, i * 128:(i + 1) * 128], rhs=wv[:, f, :],
                                               start=(blk == 0 and f == 0), stop=(blk == 3 and f == 10))
                            return ins
                        O("pe", dmm, reads=[f"wbuf{wi}", "AT"], writes=[f"pz{zs_[i]}"])
                for i in range(ntile):
                    O("dve", lambda e, i=i, cg=cg, z=zs_[i]: e.tensor_tensor(out=x1[:, i, cg * 512:(cg + 1) * 512], in0=pz[z][:, :],
                                                                            in1=x1[:, i, cg * 512:(cg + 1) * 512], op=ALU.add),
                      reads=[f"pz{zs_[i]}", "x1"], writes=["x1"])
            O("sp", _nd(1)(lambda e, dma, row=row, ntile=ntile, ntok=ntok: dma(
                y2[row:row + ntok, :].rearrange("(j p) c -> p j c", p=128), x1[:, 0:ntile, :])), reads=["x1"], writes=["y2"], dma=True)
            return ntok

        row = 0
        for grp in groups:
            row += do_group(grp, row)
        tk.barrier()
        tk.flush(sems, dsems)
        es.close()

    if STAGE >= 8:
        phase5()
    if os.environ.get("DBG_E"):
        E_dbg = dout("E_dbg", [TT, 512], BF16)
        O("sp", _nd(1)(lambda e, dma: dma(E_dbg[:, :], E[:, :])), writes=["E_dbg"], dma=True)
    tk.barrier()
    tk.flush(sems, dsems)
    ges.close()
    sstack.close()
    return nc


def make_in_maps(inp, T, NSQ=8, TS=64, past=PAST):
    bf = ml_dtypes.bfloat16
    f32 = np.float32
    NCH = T // 1024
    ident = np.eye(128, dtype=f32)
    bd = np.zeros((128, 128), f32)
    bd[:64, :64] = 1.0
    bd[64:, 64:] = 1.0
    triu = np.triu(np.ones((128, 128), f32))
    ones = np.ones((128, 128), f32)
    w_in = inp["w_in"][0]
    w_conv = inp["w_conv"][0]
    b_conv = inp["b_conv"][0]
    o_f = 3 * 1024
    o_z = o_f + 16
    o_x = o_z + 1024
    o_B = o_x + 1024
    o_C = o_B + 256
    o_dt = o_C + 256
    perm = []
    for kc in range(16):
        r, cch = kc // 4, kc % 4
        base = 256 * r + 128 * cch if cch < 2 else 1024 + 256 * r + 128 * (cch - 2)
        perm.extend(range(base, base + 128))
    w_out_p = np.ascontiguousarray(inp["w_out"][0][perm])
    maps = []
    for c in range(8):
        b, g = c // 4, c % 4
        j = g
        xs = inp["x_sample"][8 * b:8 * b + 8].reshape(NSQ * TS, D)
        x_all = np.concatenate([inp["x_prompt"][b], xs], axis=0)
        hs = slice(256 * g, 256 * g + 256)
        grp = g // 2
        wsl = np.concatenate([
            w_in[:, 0:1024][:, hs], w_in[:, 1024:2048][:, hs], w_in[:, 2048:3072][:, hs],
            w_in[:, o_z:o_z + 1024][:, hs], w_in[:, o_x:o_x + 1024][:, hs],
            w_in[:, o_B + 128 * grp:o_B + 128 * grp + 128], w_in[:, o_C + 128 * grp:o_C + 128 * grp + 128],
            w_in[:, o_f + 4 * g:o_f + 4 * g + 4], w_in[:, o_dt + 4 * g:o_dt + 4 * g + 4]], axis=1)
        cols = np.zeros((128, 16), f32)
        cols[:, 0] = np.tile(inp["g_q"][0], 2)
        cols[:, 1] = np.tile(inp["g_k"][0], 2)
        cols[0:4, 2] = -1.0
        cols[4:8, 2] = 1.0
        cols[0:4, 3] = inp["f_bias"][0][4 * g:4 * g + 4]
        cols[4:8, 3] = inp["dt_bias"][0][4 * g:4 * g + 4]
        cols[0:4, 4] = -1.0
        cols[4:8, 4] = 1.0
        ccols = np.concatenate([np.arange(256 * g, 256 * g + 256), 1024 + 128 * grp + np.arange(128),
                                1280 + 128 * grp + np.arange(128)])
        wconv = np.zeros((128, 4, 5), f32)
        for ci in range(4):
            cc = ccols[ci * 128:(ci + 1) * 128]
            wconv[:, ci, 0:4] = w_conv[:, cc].T
            wconv[:, ci, 4] = b_conv[cc]
        row4 = np.concatenate([inp["a_log"][0][4 * g:4 * g + 4], inp["d_skip"][0][4 * g:4 * g + 4]])[None, :].astype(f32)
        sq = slice(8 * b, 8 * b + 8)
        c_k = inp["cache_fox_k"][0][sq][:, :, 4 * g:4 * g + 4, :].reshape(NSQ, past, 256)
        c_v = inp["cache_fox_v"][0][sq][:, :, 4 * g:4 * g + 4, :].reshape(NSQ, past, 256)
        c_lf = inp["cache_fox_logf"][0][sq][:, :, 4 * g:4 * g + 4]
        c_convT = np.transpose(inp["cache_conv"][0][sq][:, :, ccols], (2, 0, 1))
        c_ssm = inp["state_ssm"][0][sq][:, 4 * g:4 * g + 4].reshape(NSQ, 256, 128)
        xp = inp["x_prompt"][b].reshape(NCH, 4, 256, D)[:, j].reshape(NCH * 256, D)
        x2 = np.concatenate([xp, inp["x_sample"][8 * b + 2 * j:8 * b + 2 * j + 2].reshape(2 * TS, D)], axis=0)
        sel = np.zeros((128, 8, 256), f32)
        for m in range(256):
            t = 256 * j + m
            sel[t % 128, t // 128, m] = 1.0
        sel_s = np.zeros((128, 4, 128), f32)
        for m in range(128):
            t = 128 * j + m
            sel_s[t % 128, t // 128, m] = 1.0
        gs = inp["g_ssd"][0]
        gssd = np.zeros((128, 8), f32)
        for gi in range(2):
            for jj in range(4):
                base = 256 * (2 * gi + jj // 2) + 128 * (jj % 2)
                gssd[:, gi * 4 + jj] = gs[base:base + 128]
        ca = np.ascontiguousarray
        maps.append({
            "x_all": ca(x_all), "w_in": ca(wsl), "gmix": ca(inp["g_mix"]), "cols": cols, "wconv": wconv, "row4": row4,
            "ident_f": ident, "ident_b": ident.astype(bf), "bd_ones": bd.astype(bf),
            "triu_f": triu, "triu_b": triu.astype(bf), "ones_f": ones, "ones_b": ones.astype(bf),
            "c_k": ca(c_k), "c_v": ca(c_v), "c_lf": ca(c_lf), "c_convT": ca(c_convT), "c_ssm": ca(c_ssm),
            "x2": ca(x2), "sel": sel.astype(bf), "sel_s": sel_s.astype(bf), "gffn": ca(inp["g_ffn"]), "gssd": gssd,
            "w_out": w_out_p, "w_gate": inp["w_gate"][0], "w_up": inp["w_up"][0], "w_down": inp["w_down"][0],
        })
    return maps


def assemble(res, T, B=2, NSQ=8, TS=64):
    NCH = T // 1024
    f32 = np.float32
    DB = 8 * B
    yp = np.zeros((B, T, D), f32)
    ys = np.zeros((DB, TS, D), f32)
    p_conv = np.zeros((1, B, 3, 1536), f32)
    p_ssm = np.zeros((1, B, 16, 64, 128), f32)
    p_k = np.zeros((1, B, T, 16, 64), f32)
    p_v = np.zeros((1, B, T, 16, 64), f32)
    p_f = np.zeros((1, B, T, 16), f32)
    s_conv = np.zeros((1, DB, 3, 1536), f32)
    s_ssm = np.zeros((1, DB, 16, 64, 128), f32)
    s_k = np.zeros((1, DB, TS, 16, 64), f32)
    s_v = np.zeros((1, DB, TS, 16, 64), f32)
    s_f = np.zeros((1, DB, TS, 16), f32)
    for c in range(8):
        b, g = c // 4, c % 4
        j = g
        grp = g // 2
        r = res[c]
        y2 = r["y2"]
        yp[b].reshape(NCH, 4, 256, D)[:, j] = y2[:NCH * 256].reshape(NCH, 256, D)
        ys[8 * b + 2 * j:8 * b + 2 * j + 2] = y2[NCH * 256:].reshape(2, TS, D)
        sq = slice(8 * b, 8 * b + 8)
        p_k[0, b, :, 4 * g:4 * g + 4] = r["k_out"][:T].reshape(T, 4, 64)
        p_v[0, b, :, 4 * g:4 * g + 4] = r["v_out"][:T].reshape(T, 4, 64)
        p_f[0, b, :, 4 * g:4 * g + 4] = r["lf_out"][:T]
        s_k[0, sq, :, 4 * g:4 * g + 4] = r["k_out"][T:].reshape(NSQ, TS, 4, 64)
        s_v[0, sq, :, 4 * g:4 * g + 4] = r["v_out"][T:].reshape(NSQ, TS, 4, 64)
        s_f[0, sq, :, 4 * g:4 * g + 4] = r["lf_out"][T:].reshape(NSQ, TS, 4)
        p_ssm[0, b, 4 * g:4 * g + 4] = r["ssm_p"].reshape(4, 64, 128)
        s_ssm[0, sq, 4 * g:4 * g + 4] = r["ssm_s"].reshape(NSQ, 4, 64, 128)
        cp = r["conv_p"].T
        cs = np.transpose(r["conv_s"], (1, 2, 0))
        p_conv[0, b, :, 256 * g:256 * g + 256] = cp[:, 0:256]
        s_conv[0, sq, :, 256 * g:256 * g + 256] = cs[:, :, 0:256]
        if g % 2 == 0:
            p_conv[0, b, :, 1024 + 128 * grp:1024 + 128 * grp + 128] = cp[:, 256:384]
            p_conv[0, b, :, 1280 + 128 * grp:1280 + 128 * grp + 128] = cp[:, 384:512]
            s_conv[0, sq, :, 1024 + 128 * grp:1024 + 128 * grp + 128] = cs[:, :, 256:384]
            s_conv[0, sq, :, 1280 + 128 * grp:1280 + 128 * grp + 128] = cs[:, :, 384:512]
    return (yp, ys, p_conv, p_ssm, p_k, p_v, p_f, s_conv, s_ssm, s_k, s_v, s_f)


_NC_CACHE = {}


def kernel(**inputs):
    inp = {k: np.asarray(v) for k, v in inputs.items()}
    T = inp["x_prompt"].shape[1]
    if T not in _NC_CACHE:
        _NC_CACHE[T] = build(T)
    nc = _NC_CACHE[T]
    maps = make_in_maps(inp, T)
    res = run_bass_kernel_spmd(nc, maps, core_ids=list(range(8)))
    return assemble(res.results, T)
```

```python
import numpy as np
import ml_dtypes
import concourse.bass as bass
import concourse.mybir as mybir
from concourse.bass_utils import run_bass_kernel_spmd

F32 = mybir.dt.float32
BF16 = mybir.dt.bfloat16
ALU = mybir.AluOpType
AF = mybir.ActivationFunctionType
ET = mybir.EngineType

D = 2048
EPS = 1e-6
HG = 4
NCOL = 1544
PAST = 2048
DFF = 5632


class Tracker:
    CE = ("pe", "act", "dve", "pool", "sp")

    def __init__(self, nc, nslot=8):
        self.nc = nc
        self.nslot = nslot
        self.ops = {e: [] for e in ("pe", "act", "dve", "pool", "sp")}
        self.cnt = {e: 0 for e in self.CE}
        self.ndma = {"sp": 0, "pool": 0}
        self.slotval = {}
        self.res = {}
        self.waited = {e: {} for e in self.ops}

    def _need(self, eng, tok, waits):
        if tok is None:
            return
        kind, key, val = tok
        if kind == "c" and key == eng and eng == "pe":
            return
        k = (kind, key)
        if self.waited[eng].get(k, 0) >= val:
            return
        waits[k] = max(waits.get(k, 0), val)

    def op(self, eng, fn, reads=(), writes=(), dma=False):
        waits = {}
        for r in reads:
            st = self.res.get(r)
            if st:
                self._need(eng, st["w"], waits)
        for w in writes:
            st = self.res.get(w)
            if st:
                self._need(eng, st["w"], waits)
                for t in st["r"]:
                    self._need(eng, t, waits)
        if dma:
            slot = self.ndma[eng] % self.nslot
            self.ndma[eng] += 1
            key = (eng, slot)
            prev = self.slotval.get(key, 0)
            if prev:
                self._need(eng, ("d", key, prev), waits)
            tok = ["d", key, prev]
        else:
            self.cnt[eng] += 1
            tok = ("c", eng, self.cnt[eng])
        for k, v in waits.items():
            self.waited[eng][k] = v
        rec = {"fn": fn, "waits": dict(waits), "tok": tok, "dma": dma, "nd": 0}
        if dma:
            n = getattr(fn, "ndma", 1)
            rec["nd"] = n
            self.slotval[key] = prev + 16 * n
            tok[2] = prev + 16 * n
            tok = tuple(tok)
            rec["tok"] = tok
        self.ops[eng].append(rec)
        for r in reads:
            self.res.setdefault(r, {"w": None, "r": []})["r"].append(tok)
        for w in writes:
            self.res[w] = {"w": tok, "r": []}
        return tok

    def barrier(self):
        allr = list(self.res.keys())
        self.op("sp", lambda e: e.nop(), reads=allr, writes=allr + ["__bar"])
        for en in ("pe", "act", "dve", "pool"):
            self.op(en, lambda e: e.nop(), reads=["__bar"])

    def flush(self, sems, dsems):
        nc = self.nc
        ops, self.ops = self.ops, {e: [] for e in self.ops}

        def run(ename, e):
            for rec in ops[ename]:
                for (kind, key), val in rec["waits"].items():
                    if kind == "c":
                        e.wait_ge(sems[key], val)
                    else:
                        e.wait_ge(dsems[key], val)
                if rec["dma"]:
                    sem = dsems[rec["tok"][1]]
                    cnt = [0]

                    def dma(out, in_, _sem=sem, _cnt=cnt, **kw):
                        _cnt[0] += 1
                        return e.dma_start(out=out, in_=in_, **kw).then_inc(_sem, 16)
                    rec["fn"](e, dma)
                    assert cnt[0] == rec["nd"], (cnt[0], rec["nd"])
                else:
                    ins = rec["fn"](e)
                    ins.then_inc(sems[ename], 1)
        with nc.Block() as block:
            @block.sync
            def _(e):
                run("sp", e)

            @block.tensor
            def _(e):
                run("pe", e)

            @block.scalar
            def _(e):
                run("act", e)

            @block.vector
            def _(e):
                run("dve", e)

            @block.gpsimd
            def _(e):
                run("pool", e)


def _nd(n):
    def deco(f):
        f.ndma = n
        return f
    return deco


import os
from contextlib import ExitStack


def build(T, NSQ=8, TS=64, past=PAST):
    assert T % 1024 == 0 and NSQ * TS == 512 and TS == 64 and past % 512 == 0
    STAGE = int(os.environ.get("STAGE", "99"))
    nc = bass.Bass("TRN2", target_bir_lowering=False)
    TT = T + NSQ * TS
    NG = TT // 512
    NGP = T // 512
    NCH = T // 1024
    TKS = past + 128
    NBS = TKS // 128
    NT2 = NCH * 256 + 128

    def din(name, shape, dt=F32):
        return nc.dram_tensor(name, list(shape), dt, kind="ExternalInput").ap()

    def dout(name, shape, dt=F32):
        return nc.dram_tensor(name, list(shape), dt, kind="ExternalOutput").ap()

    def dscr(name, shape, dt):
        return nc.dram_tensor(name, list(shape), dt, kind="Internal").ap()

    x_all = din("x_all", [TT, D])
    w_in = din("w_in", [D, NCOL])
    gmix = din("gmix", [1, D])
    cols = din("cols", [128, 16])
    wconv = din("wconv", [128, 4, 5])
    row4 = din("row4", [1, 8])
    ident_f = din("ident_f", [128, 128])
    ident_b = din("ident_b", [128, 128], BF16)
    bd_ones = din("bd_ones", [128, 128], BF16)
    triu_f = din("triu_f", [128, 128])
    triu_b = din("triu_b", [128, 128], BF16)
    negm_b = din("negm_b", [128, 128], BF16)
    ones_f = din("ones_f", [128, 128])
    ones_b = din("ones_b", [128, 128], BF16)
    c_k = din("c_k", [NSQ, past, 256])
    c_v = din("c_v", [NSQ, past, 256])
    c_lf = din("c_lf", [NSQ, past, 4])
    c_convT = din("c_convT", [512, NSQ, 3])
    c_ssm = din("c_ssm", [NSQ, 256, 128])
    x2 = din("x2", [NT2, D])
    sel = din("sel", [128, 8, 256], BF16)
    sel_s = din("sel_s", [128, 4, 128], BF16)
    gffn = din("gffn", [1, D])
    gssd = din("gssd", [128, 8])
    w_out = din("w_out", [D, D])
    w_gate = din("w_gate", [D, DFF])
    w_up = din("w_up", [D, DFF])
    w_down = din("w_down", [DFF, D])
    k_out = dout("k_out", [TT, 256])
    v_out = dout("v_out", [TT, 256])
    lf_out = dout("lf_out", [TT, 4])
    conv_p = dout("conv_p", [512, 3])
    conv_s = dout("conv_s", [512, NSQ, 3])
    ssm_p = dout("ssm_p", [256, 128])
    ssm_s = dout("ssm_s", [NSQ, 256, 128])
    y2 = dout("y2", [NT2, D])

    w_in_b = dscr("w_in_b", [D, NCOL], BF16)
    QT = dscr("QT", [70, HG, TT], BF16)
    KTn = dscr("KTn", [70, HG, TT], BF16)
    KTs = dscr("KTs", [NSQ, 70, HG, TKS], BF16)
    Vb = dscr("Vb", [TT, 256], BF16)
    Vs = dscr("Vs", [NSQ, TKS, 256], BF16)
    FD = dscr("FD", [TT, 8], F32)
    ZS = dscr("ZS", [TT, 256], F32)
    UT = dscr("UT", [256, TT], BF16)
    Utok = dscr("Utok", [TT, 384], BF16)
    E = dscr("E", [TT, 512], BF16)
    Gq = [dscr(f"G{q}", [4 * 1024, 512], BF16) for q in range(NCH)]
    Gs = dscr("Gs", [4 * 512, 512], BF16)
    wo_b = dscr("wo_b", [D, D], BF16)
    wg_b = dscr("wg_b", [D, DFF], BF16)
    wu_b = dscr("wu_b", [D, DFF], BF16)
    wd_b = dscr("wd_b", [DFF, D], BF16)

    tk = Tracker(nc)
    O = tk.op
    uid = [0]

    def nxt():
        uid[0] += 1
        return uid[0]

    sstack = ExitStack()
    sems = {e: sstack.enter_context(nc.semaphore(f"s_{e}")) for e in Tracker.CE}
    dsems = {}
    for q in ("sp", "pool"):
        for s in range(tk.nslot):
            dsems[(q, s)] = sstack.enter_context(nc.semaphore(f"d_{q}{s}"))

    ges = ExitStack()

    def gsb(name, shape, dt=F32):
        return ges.enter_context(nc.sbuf_tensor(name, list(shape), dt))

    c_cols = gsb("c_cols", [128, 16])
    c_idf = gsb("c_idf", [128, 128])
    c_idb = gsb("c_idb", [128, 128], BF16)
    c_bd = gsb("c_bd", [128, 128], BF16)
    c_trf = gsb("c_trf", [128, 128])
    c_trb = gsb("c_trb", [128, 128], BF16)
    c_ngm = gsb("c_ngm", [128, 128], BF16)
    c_onf = gsb("c_onf", [128, 128])
    c_onb = gsb("c_onb", [128, 128], BF16)
    c_fd = gsb("c_fd", [128, 4])
    c_wc = gsb("c_wc", [128, 4, 5])
    c_r4 = gsb("c_r4", [128, 8])

    @_nd(11)
    def ld_const(e, dma):
        dma(c_ngm[:, :], negm_b[:, :])
        dma(c_cols[:, :], cols[:, :])
        dma(c_idf[:, :], ident_f[:, :])
        dma(c_idb[:, :], ident_b[:, :])
        dma(c_bd[:, :], bd_ones[:, :])
        dma(c_trf[:, :], triu_f[:, :])
        dma(c_trb[:, :], triu_b[:, :])
        dma(c_onf[:, :], ones_f[:, :])
        dma(c_onb[:, :], ones_b[:, :])
        dma(c_wc[:, :, :], wconv[:, :, :])
        dma(c_r4[:, :], row4[0:1, :].partition_broadcast(128))
    O("sp", ld_const, writes=["const"], dma=True)
    O("dve", lambda e: e.tensor_tensor(out=c_fd[0:8, 0:1], in0=c_cols[0:8, 3:4], in1=c_cols[0:8, 2:3], op=ALU.mult),
      reads=["const"], writes=["cfd0"])
    O("dve", lambda e: e.tensor_scalar(out=c_fd[:, 1:2], in0=c_cols[:, 1:2], scalar1=8.0, scalar2=None, op0=ALU.mult),
      reads=["const"], writes=["cfd1"])
    O("act", lambda e: e.activation(out=c_r4[:, 0:4], in_=c_r4[:, 0:4], func=AF.Exp), reads=["const"], writes=["cr4"])
    O("dve", lambda e: e.tensor_scalar(out=c_r4[:, 0:4], in0=c_r4[:, 0:4], scalar1=-1.0, scalar2=None, op0=ALU.mult),
      reads=["cr4"], writes=["cr4"])

    @_nd(1)
    def cast_w(e, dma):
        dma(w_in_b[:, :], w_in[:, :])
    O("pool", cast_w, writes=["w_in_b"], dma=True)

    def cast_big(dst, src, rows, name):
        step = 512
        for r in range(0, rows, step):
            @_nd(1)
            def f(e, dma, r=r):
                dma(dst[r:r + step, :], src[r:r + step, :])
            O("pool", f, writes=[name + f"_{r}"], dma=True)
    if STAGE >= 8:
        cast_big(wo_b, w_out, D, "wo_b")
        cast_big(wg_b, w_gate, D, "wg_b")
        cast_big(wu_b, w_up, D, "wu_b")
        cast_big(wd_b, w_down, DFF, "wd_b")

    def phase1():
        es = ExitStack()

        def sb(name, shape, dt=F32):
            return es.enter_context(nc.sbuf_tensor(f'ph0_' + name, list(shape), dt))

        def ps(name, shape, dt=F32):
            return es.enter_context(nc.psum_tensor(f'ph0_' + name, list(shape), dt))

        wb = sb("wb", [128, 16, NCOL], BF16)
        c_gmix = sb("c_gmix", [128, D])

        @_nd(2)
        def ld_w(e, dma):
            dma(wb[:, :, :], w_in_b.rearrange("(kc p) n -> p kc n", p=128))
            dma(c_gmix[:, :], gmix[0:1, :].partition_broadcast(128))
        O("sp", ld_w, reads=["w_in_b"], writes=["wb"], dma=True)

        hT = [sb(f"hT{i}", [128, 16, 512], BF16) for i in range(2)]
        mmc = [0]
        rc_ = {}
        xt = [sb(f"xt{i}", [128, D]) for i in range(4)]
        hb = [sb(f"hb{i}", [128, D], BF16) for i in range(4)]
        ss = [sb(f"ss{i}", [128, 2]) for i in range(4)]
        psT = [ps(f"psT{i}", [128, 4, 128], BF16) for i in range(2)]
        psA = [ps(f"psA{i}", [128, 512]) for i in range(3)]
        psB = ps("psB", [128, 512])
        psF = [ps(f"psF{i}", [128, 4, 128]) for i in range(2)]
        sq = [sb(f"sq{i}", [128, 512], BF16) for i in range(2)]
        rs = [sb(f"rs{i}", [128, 512]) for i in range(2)]
        qn = [sb(f"qn{i}", [128, 512], BF16) for i in range(2)]
        kf = [sb(f"kf{i}", [128, 512]) for i in range(3)]
        kraw = [sb(f"kraw{i}", [128, 512]) for i in range(2)]
        tokf = [sb(f"tokf{i}", [128, 4, 128]) for i in range(2)]
        tokb = [sb(f"tokb{i}", [128, 4, 128], BF16) for i in range(2)]
        fd = sb("fd", [8, 512])
        fdt = sb("fdt", [128, 4, 8])
        xpad = sb("xpad", [128, 4, 536])
        acc = [sb(f"acc{i}", [128, 512]) for i in range(3)]
        ub = [sb(f"ub{i}", [128, 512], BF16) for i in range(3)]
        O("dve", lambda e: e.memset(xpad[:, :, :], 0.0), writes=[f"xpad{ci}" for ci in range(4)])

        def part_a_norm(tg):
            t0 = tg * 512
            for ti in range(4):
                r0 = t0 + ti * 128

                @_nd(1)
                def ldx(e, dma, ti=ti, r0=r0):
                    dma(xt[ti][:, :], x_all[r0:r0 + 128, :])
                O("sp", ldx, writes=[f"xt{ti}"], dma=True)
                O("act", lambda e, ti=ti: e.activation(out=hb[ti][:, :], in_=xt[ti][:, :], func=AF.Square,
                                                      accum_out=ss[ti][:, 0:1]),
                  reads=[f"xt{ti}"], writes=[f"hb{ti}", f"ss{ti}a"])
                O("act", lambda e, ti=ti: e.activation(out=ss[ti][:, 1:2], in_=ss[ti][:, 0:1], func=AF.Sqrt,
                                                      bias=EPS, scale=1.0 / D),
                  reads=[f"ss{ti}a"], writes=[f"ss{ti}b"])
                O("dve", lambda e, ti=ti: e.reciprocal(out=ss[ti][:, 1:2], in_=ss[ti][:, 1:2]),
                  reads=[f"ss{ti}b"], writes=[f"ss{ti}b"])
                O("dve", lambda e, ti=ti: e.scalar_tensor_tensor(out=hb[ti][:, :], in0=xt[ti][:, :], scalar=ss[ti][:, 1:2],
                                                                 in1=c_gmix[:, :], op0=ALU.mult, op1=ALU.mult),
                  reads=[f"xt{ti}", f"ss{ti}b", "wb"], writes=[f"hb{ti}"])

        def part_a_tr(tg):
            hTb = hT[tg % 2]
            for ti in range(4):
                for k4 in range(4):
                    p2 = nxt() % 2

                    def tr(e, ti=ti, k4=k4, p2=p2):
                        for j in range(4):
                            kc = k4 * 4 + j
                            ins = e.transpose(out=psT[p2][:, j, :], in_=hb[ti][:, kc * 128:(kc + 1) * 128],
                                              identity=c_idb[:, :])
                        return ins
                    O("pe", tr, reads=[f"hb{ti}", "const"], writes=[f"psT{p2}"])
                    if k4 % 2:
                        O("act", lambda e, k4=k4, p2=p2, ti=ti, hTb=hTb: e.copy(
                            out=hTb[:, k4 * 4:(k4 + 1) * 4, ti * 128:(ti + 1) * 128], in_=psT[p2][:, :, :]),
                          reads=[f"psT{p2}"], writes=[f"hT{tg % 2}_{ti}_{k4}"])
                    else:
                        O("dve", lambda e, k4=k4, p2=p2, ti=ti, hTb=hTb: e.tensor_copy(
                            out=hTb[:, k4 * 4:(k4 + 1) * 4, ti * 128:(ti + 1) * 128], in_=psT[p2][:, :, :]),
                          reads=[f"psT{p2}"], writes=[f"hT{tg % 2}_{ti}_{k4}"])

        part_a_norm(0)
        part_a_tr(0)
        for tg in range(NG):
            t0 = tg * 512
            is_s = tg >= NGP
            hTb = hT[tg % 2]
            hT_res = [f"hT{tg % 2}_{ti}_{k4}" for ti in range(4) for k4 in range(4)]

            def mm_chunk(c0, m, pst, hTb=hTb):
                def f(e):
                    for kc in range(16):
                        ins = e.matmul(pst[0:m, :], lhsT=wb[:, kc, c0:c0 + m], rhs=hTb[:, kc, :],
                                       start=(kc == 0), stop=(kc == 15))
                    return ins
                return f

            def tr4_f32(src, f2):
                def f(e):
                    for j in range(4):
                        ins = e.transpose(out=psF[f2][:, j, :], in_=src[:, j * 128:(j + 1) * 128], identity=c_idf[:, :])
                    return ins
                return f

            chunks = []

            def rot(name):
                rc_[name] = rc_.get(name, 0) + 1
                return rc_[name] % 2

            def rot3(name):
                rc_[name] = rc_.get(name, 0) + 1
                return rc_[name] % 3

            def qk1(a3, cc, tg=tg, t0=t0):
                isq = cc < 2
                hp = cc % 2
                a2 = rot("sq")
                O("act", lambda e, a2=a2, a3=a3: e.activation(out=sq[a2][:, :], in_=psA[a3][:, :], func=AF.Square),
                  reads=[f"psA{a3}"], writes=[f"sq{a2}"])
                r2 = rot("kraw")
                O("act", lambda e, r2=r2, a3=a3: e.copy(out=kraw[r2][:, :], in_=psA[a3][:, :]),
                  reads=[f"psA{a3}"], writes=[f"kraw{r2}"])
                O("pe", lambda e, a2=a2: e.matmul(psB[:, :], lhsT=c_bd[:, :], rhs=sq[a2][:, :], start=True, stop=True),
                  reads=[f"sq{a2}", "const"], writes=["psB"])
                O("act", lambda e, a2=a2: e.activation(out=rs[a2][:, :], in_=psB[:, :], func=AF.Sqrt,
                                                      bias=64.0 * EPS, scale=1.0),
                  reads=["psB"], writes=[f"rs{a2}"])
                O("dve", lambda e, a2=a2: e.reciprocal(out=rs[a2][:, :], in_=rs[a2][:, :]),
                  reads=[f"rs{a2}"], writes=[f"rs{a2}"])
                if isq:
                    q2 = rot("qn")
                    O("dve", lambda e, a2=a2, r2=r2, q2=q2: e.scalar_tensor_tensor(out=qn[q2][:, :], in0=kraw[r2][:, :], scalar=c_cols[:, 0:1],
                                                                                 in1=rs[a2][:, :], op0=ALU.mult, op1=ALU.mult),
                      reads=[f"kraw{r2}", f"rs{a2}", "const"], writes=[f"qn{q2}"])

                    @_nd(2)
                    def stq(e, dma, q2=q2, hp=hp, t0=t0):
                        for hh in range(2):
                            dma(QT[0:64, hp * 2 + hh, t0:t0 + 512], qn[q2][hh * 64:(hh + 1) * 64, :])
                    O("sp", stq, reads=[f"qn{q2}"], writes=[f"QT{tg}"], dma=True)
                    return None
                k2 = rot3("kf")
                O("dve", lambda e, a2=a2, r2=r2, k2=k2: e.scalar_tensor_tensor(out=kf[k2][:, :], in0=kraw[r2][:, :], scalar=c_fd[:, 1:2],
                                                                             in1=rs[a2][:, :], op0=ALU.mult, op1=ALU.mult),
                  reads=[f"kraw{r2}", f"rs{a2}", "cfd1"], writes=[f"kf{k2}"])
                return (k2, hp)

            def qk2(st, tg=tg, t0=t0):
                if st is None:
                    return
                k2, hp = st
                q2 = rot("qn")
                O("act", lambda e, k2=k2, q2=q2: e.copy(out=qn[q2][:, :], in_=kf[k2][:, :]),
                  reads=[f"kf{k2}"], writes=[f"qn{q2}"])

                @_nd(2)
                def stk(e, dma, q2=q2, hp=hp, t0=t0):
                    for hh in range(2):
                        dma(KTn[0:64, hp * 2 + hh, t0:t0 + 512], qn[q2][hh * 64:(hh + 1) * 64, :])
                O("sp", stk, reads=[f"qn{q2}"], writes=[f"KTn{tg}"], dma=True)
                f2 = rot("psF")
                O("pe", tr4_f32(kf[k2], f2), reads=[f"kf{k2}", "const"], writes=[f"psF{f2}"])
                O("act", lambda e, f2=f2: e.copy(out=tokf[f2][:, :, :], in_=psF[f2][:, :, :]),
                  reads=[f"psF{f2}"], writes=[f"tokf{f2}"])

                @_nd(1)
                def stko(e, dma, f2=f2, hp=hp, t0=t0):
                    dma(k_out[t0:t0 + 512, hp * 128:(hp + 1) * 128].rearrange("(j p) c -> p j c", p=128), tokf[f2][:, :, :])
                O("sp", stko, reads=[f"tokf{f2}"], writes=["k_out"], dma=True)

            def v1(a3, hp):
                k2 = rot3("kf")
                O("act", lambda e, k2=k2, a3=a3: e.copy(out=kf[k2][:, :], in_=psA[a3][:, :]),
                  reads=[f"psA{a3}"], writes=[f"kf{k2}"])
                return (k2, hp)

            def v2(st, tg=tg, t0=t0):
                k2, hp = st
                f2 = rot("psF")
                O("pe", tr4_f32(kf[k2], f2), reads=[f"kf{k2}", "const"], writes=[f"psF{f2}"])
                O("act", lambda e, f2=f2: e.copy(out=tokf[f2][:, :, :], in_=psF[f2][:, :, :]),
                  reads=[f"psF{f2}"], writes=[f"tokf{f2}"])

                @_nd(1)
                def stvo(e, dma, f2=f2, hp=hp, t0=t0):
                    dma(v_out[t0:t0 + 512, hp * 128:(hp + 1) * 128].rearrange("(j p) c -> p j c", p=128), tokf[f2][:, :, :])
                O("sp", stvo, reads=[f"tokf{f2}"], writes=[f"v_out{tg}_{hp}"], dma=True)

                @_nd(1)
                def stvb(e, dma, hp=hp, t0=t0):
                    dma(Vb[t0:t0 + 512, hp * 128:(hp + 1) * 128], v_out[t0:t0 + 512, hp * 128:(hp + 1) * 128])
                O("pool", stvb, reads=[f"v_out{tg}_{hp}"], writes=[f"Vb{tg}"], dma=True)

            def z1(a3, hp):
                k2 = rot3("kf")
                O("act", lambda e, k2=k2, a3=a3: e.activation(out=kf[k2][:, :], in_=psA[a3][:, :], func=AF.Silu),
                  reads=[f"psA{a3}"], writes=[f"kf{k2}"])
                return (k2, hp)

            def z2(st, tg=tg, t0=t0):
                k2, hp = st
                f2 = rot("psF")
                O("pe", tr4_f32(kf[k2], f2), reads=[f"kf{k2}", "const"], writes=[f"psF{f2}"])
                O("act", lambda e, f2=f2: e.copy(out=tokf[f2][:, :, :], in_=psF[f2][:, :, :]),
                  reads=[f"psF{f2}"], writes=[f"tokf{f2}"])

                @_nd(1)
                def stz(e, dma, f2=f2, hp=hp, t0=t0):
                    dma(ZS[t0:t0 + 512, hp * 128:(hp + 1) * 128].rearrange("(j p) c -> p j c", p=128), tokf[f2][:, :, :])
                O("sp", stz, reads=[f"tokf{f2}"], writes=[f"ZS{tg}"], dma=True)

            nseg, L = (NSQ, TS) if is_s else (1, 512)
            W = L + 3

            def pre_conv(ci, nseg=nseg, W=W, is_s=is_s):
                if is_s:
                    xp3 = xpad[:, ci, 0:nseg * W].rearrange("p (s l) -> p s l", l=W)

                    @_nd(1)
                    def ldhalo(e, dma, ci=ci, xp3=xp3):
                        dma(xp3[:, :, 0:3], c_convT[ci * 128:(ci + 1) * 128, :, :])
                    O("sp", ldhalo, writes=[f"xpad{ci}"], dma=True)

            def conv1(a3, ci, tg=tg, t0=t0, nseg=nseg, L=L, W=W, is_s=is_s):
                xp3 = xpad[:, ci, 0:nseg * W].rearrange("p (s l) -> p s l", l=W)
                O("act", lambda e, a3=a3, xp3=xp3, L=L: e.copy(out=xp3[:, :, 3:3 + L],
                                                              in_=psA[a3][:, :].rearrange("p (s l) -> p s l", l=L)),
                  reads=[f"psA{a3}"], writes=[f"xpad{ci}"])
                c2 = rot3("acc")
                acc3 = acc[c2][:, :].rearrange("p (s l) -> p s l", l=L)
                O("dve", lambda e, acc3=acc3, xp3=xp3, ci=ci, L=L: e.tensor_scalar(
                    out=acc3, in0=xp3[:, :, 3:3 + L], scalar1=c_wc[:, ci, 3:4], scalar2=c_wc[:, ci, 4:5],
                    op0=ALU.mult, op1=ALU.add), reads=[f"xpad{ci}", "const"], writes=[f"acc{c2}"])
                for j in range(1, 4):
                    O("dve", lambda e, acc3=acc3, xp3=xp3, ci=ci, L=L, j=j: e.scalar_tensor_tensor(
                        out=acc3, in0=xp3[:, :, 3 - j:3 - j + L], scalar=c_wc[:, ci, 3 - j:4 - j], in1=acc3,
                        op0=ALU.mult, op1=ALU.add), reads=[f"xpad{ci}", "const", f"acc{c2}"], writes=[f"acc{c2}"])
                O("act", lambda e, c2=c2: e.activation(out=ub[c2][:, :], in_=acc[c2][:, :], func=AF.Silu),
                  reads=[f"acc{c2}"], writes=[f"ub{c2}"])
                if ci >= 2:
                    @_nd(1)
                    def stut(e, dma, c2=c2, ci=ci, t0=t0):
                        dma(UT[(ci - 2) * 128:(ci - 1) * 128, t0:t0 + 512], ub[c2][:, :])
                    O("sp", stut, reads=[f"ub{c2}"], writes=[f"UT{tg}"], dma=True)
                if is_s:
                    @_nd(1)
                    def stcs(e, dma, ci=ci, xp3=xp3):
                        dma(conv_s[ci * 128:(ci + 1) * 128, :, :], xp3[:, :, TS:TS + 3])
                    O("sp", stcs, reads=[f"xpad{ci}"], writes=["conv_s"], dma=True)
                else:
                    if tg == NGP - 1:
                        @_nd(1)
                        def stcp(e, dma, ci=ci):
                            dma(conv_p[ci * 128:(ci + 1) * 128, :], xpad[:, ci, 512:515])
                        O("sp", stcp, reads=[f"xpad{ci}"], writes=["conv_p"], dma=True)
                    O("dve", lambda e, ci=ci: e.tensor_copy(out=xpad[:, ci, 0:3], in_=xpad[:, ci, 512:515]),
                      reads=[f"xpad{ci}"], writes=[f"xpad{ci}"])
                return (c2, ci)

            def conv2(st, tg=tg, t0=t0):
                c2, ci = st
                if ci < 3:
                    p2 = rot("psT")

                    def tru(e, c2=c2, p2=p2):
                        for j in range(4):
                            ins = e.transpose(out=psT[p2][:, j, :], in_=ub[c2][:, j * 128:(j + 1) * 128], identity=c_idb[:, :])
                        return ins
                    O("pe", tru, reads=[f"ub{c2}", "const"], writes=[f"psT{p2}"])
                    b2 = rot("tokb")
                    O("act", lambda e, p2=p2, b2=b2: e.copy(out=tokb[b2][:, :, :], in_=psT[p2][:, :, :]),
                      reads=[f"psT{p2}"], writes=[f"tokb{b2}"])

                    @_nd(1)
                    def stutok(e, dma, b2=b2, ci=ci, t0=t0):
                        dma(Utok[t0:t0 + 512, ci * 128:(ci + 1) * 128].rearrange("(j p) c -> p j c", p=128), tokb[b2][:, :, :])
                    O("sp", stutok, reads=[f"tokb{b2}"], writes=[f"Utok{tg}"], dma=True)

            def fd1(a3):
                O("act", lambda e, a3=a3: e.activation(out=fd[:, :], in_=psA[a3][0:8, :], func=AF.Exp,
                                                      bias=c_fd[0:8, 0:1], scale=c_cols[0:8, 2:3]),
                  reads=[f"psA{a3}", "cfd0", "const"], writes=["fd"])
                O("act", lambda e: e.activation(out=fd[:, :], in_=fd[:, :], func=AF.Ln, bias=1.0, scale=1.0),
                  reads=["fd"], writes=["fd"])
                O("dve", lambda e: e.tensor_scalar(out=fd[:, :], in0=fd[:, :], scalar1=c_cols[0:8, 4:5], scalar2=None, op0=ALU.mult),
                  reads=["fd", "const"], writes=["fd"])
                return 0

            def fd2(st, tg=tg, t0=t0):
                f2 = rot("psF")

                def trf(e, f2=f2):
                    for j in range(4):
                        ins = e.transpose(out=psF[f2][:, j, 0:8], in_=fd[0:8, j * 128:(j + 1) * 128], identity=c_idf[0:8, 0:8])
                    return ins
                O("pe", trf, reads=["fd", "const"], writes=[f"psF{f2}"])
                O("act", lambda e, f2=f2: e.copy(out=fdt[:, :, :], in_=psF[f2][:, :, 0:8]),
                  reads=[f"psF{f2}"], writes=["fdt"])

                @_nd(2)
                def stfd(e, dma, t0=t0):
                    dma(FD[t0:t0 + 512, :].rearrange("(j p) c -> p j c", p=128), fdt[:, :, :])
                    dma(lf_out[t0:t0 + 512, :].rearrange("(j p) c -> p j c", p=128), fdt[:, :, 0:4])
                O("sp", stfd, reads=["fdt"], writes=[f"FD{tg}", "lf_out"], dma=True)

            chunks.append((1536, 8, None, (lambda a3: fd1(a3)), fd2))
            for cc in range(4):
                chunks.append((cc * 128, 128, None, (lambda a3, cc=cc: qk1(a3, cc)), qk2))
            for hp in range(2):
                chunks.append((512 + hp * 128, 128, None, (lambda a3, hp=hp: v1(a3, hp)), v2))
            for hp in range(2):
                chunks.append((768 + hp * 128, 128, None, (lambda a3, hp=hp: z1(a3, hp)), z2))
            for ci in range(4):
                chunks.append((1024 + ci * 128, 128, (lambda ci=ci: pre_conv(ci)), (lambda a3, ci=ci: conv1(a3, ci)), conv2))

            def emit_mm(ci_):
                c0, m, pre, p1_, p2_ = chunks[ci_]
                a3 = mmc[0] % 3
                mmc[0] += 1
                if pre is not None:
                    pre()
                O("pe", mm_chunk(c0, m, psA[a3]), reads=hT_res + ["wb"], writes=[f"psA{a3}"])
                return a3
            ncn = len(chunks)
            a3s = {0: emit_mm(0), 1: emit_mm(1)}
            sts = {}
            for ci_ in range(ncn + 2):
                if ci_ + 2 < ncn:
                    a3s[ci_ + 2] = emit_mm(ci_ + 2)
                if ci_ == 1 and tg + 1 < NG:
                    part_a_norm(tg + 1)
                if ci_ == 8 and tg + 1 < NG:
                    part_a_tr(tg + 1)
                if ci_ < ncn:
                    sts[ci_] = chunks[ci_][3](a3s[ci_])
                if ci_ >= 2:
                    chunks[ci_ - 2][4](sts[ci_ - 2])
        tk.barrier()
        tk.flush(sems, dsems)
        es.close()

    phase1()

    def phase1b():
        es = ExitStack()

        def sb(name, shape, dt=F32):
            return es.enter_context(nc.sbuf_tensor(f'ph1_' + name, list(shape), dt))

        def ps(name, shape, dt=F32):
            return es.enter_context(nc.psum_tensor(f'ph1_' + name, list(shape), dt))

        NBmax = max(T // 128, NBS)
        LF = sb("LF", [128, NBmax, 4])
        A = [sb(f"scanA{i}", [128, NBmax, 4]) for i in range(2)]
        Ff = sb("Ff", [128, NBmax, 4])
        R1 = sb("R1", [128, NBmax, 4])
        HF = sb("HF", [128, NBmax, 4])
        AQ = sb("AQ", [128, NBmax, 24], BF16)
        AK = sb("AK", [128, NBmax, 24], BF16)
        stg = [sb(f"stg{i}", [24, 4, 128], BF16) for i in range(2)]
        psW = ps("psW", [128, 512])
        psTot = ps("psTot", [128, 512])
        psT = [ps(f"psT{i}", [128, 4, 128], BF16) for i in range(2)]
        ck = [sb(f"ck{i}", [128, 4, 256]) for i in range(4)]
        ckb = [sb(f"ckb{i}", [128, 4, 256], BF16) for i in range(2)]
        kst = [sb(f"kst{i}", [128, 4, 128], BF16) for i in range(2)]
        ckc = [0]
        AQv = AQ[:, :, :].rearrange("p b (i h) -> p b i h", h=4)
        AKv = AK[:, :, :].rearrange("p b (i h) -> p b i h", h=4)
        O("dve", lambda e: e.memset(AQ[:, :, :], -1.0), writes=["AQ"])
        O("dve", lambda e: e.memset(AK[:, :, :], 1.0), writes=["AK"])

        def cumsum_and_aug(nb, lf_reads):
            n4 = nb * 4
            O("pe", lambda e: e.matmul(psW[:, 0:n4], lhsT=c_trf[:, :], rhs=LF[:, 0:nb, :], start=True, stop=True),
              reads=lf_reads + ["const"], writes=["psW"])
            O("pe", lambda e: e.matmul(psTot[:, 0:n4], lhsT=c_onf[:, :], rhs=LF[:, 0:nb, :], start=True, stop=True),
              reads=lf_reads + ["const"], writes=["psTot"])
            O("act", lambda e: e.copy(out=A[0][:, 0:nb, :], in_=psTot[:, 0:n4].rearrange("p (b h) -> p b h", h=4)),
              reads=["psTot"], writes=["scanA0"])
            cur = 0
            d = 1
            while d < nb:
                o = 1 - cur
                O("dve", lambda e, cur=cur, o=o, d=d: e.tensor_tensor(out=A[o][:, d:nb, :], in0=A[cur][:, d:nb, :],
                                                                    in1=A[cur][:, 0:nb - d, :], op=ALU.add),
                  reads=[f"scanA{cur}"], writes=[f"scanA{o}"])
                O("dve", lambda e, cur=cur, o=o, d=d: e.tensor_copy(out=A[o][:, 0:d, :], in_=A[cur][:, 0:d, :]),
                  reads=[f"scanA{cur}", f"scanA{o}"], writes=[f"scanA{o}"])
                cur = o
                d *= 2
            O("dve", lambda e, cur=cur: e.tensor_tensor(out=R1[:, 0:nb, :], in0=A[cur][:, 0:nb, :],
                                                       in1=psTot[:, 0:n4].rearrange("p (b h) -> p b h", h=4), op=ALU.subtract),
              reads=[f"scanA{cur}", "psTot"], writes=["R1"])
            O("dve", lambda e: e.tensor_tensor(out=Ff[:, 0:nb, :], in0=R1[:, 0:nb, :],
                                              in1=psW[:, 0:n4].rearrange("p (b h) -> p b h", h=4), op=ALU.add),
              reads=["R1", "psW"], writes=["Ff"])
            O("act", lambda e: e.copy(out=AQv[:, 0:nb, 0, :], in_=Ff[:, 0:nb, :]), reads=["Ff", "AQ"], writes=["AQ"])
            O("act", lambda e: e.copy(out=HF[:, 0:nb, :], in_=AQv[:, 0:nb, 0, :]), reads=["AQ"], writes=["HF"])
            O("dve", lambda e: e.tensor_tensor(out=R1[:, 0:nb, :], in0=Ff[:, 0:nb, :], in1=HF[:, 0:nb, :], op=ALU.subtract),
              reads=["Ff", "HF"], writes=["R1"])
            O("act", lambda e: e.copy(out=AQv[:, 0:nb, 1, :], in_=R1[:, 0:nb, :]), reads=["R1", "AQ"], writes=["AQ"])
            O("act", lambda e: e.copy(out=HF[:, 0:nb, :], in_=AQv[:, 0:nb, 1, :]), reads=["AQ"], writes=["HF"])
            O("dve", lambda e: e.tensor_tensor(out=R1[:, 0:nb, :], in0=R1[:, 0:nb, :], in1=HF[:, 0:nb, :], op=ALU.subtract),
              reads=["R1", "HF"], writes=["R1"])
            O("act", lambda e: e.copy(out=AQv[:, 0:nb, 2, :], in_=R1[:, 0:nb, :]), reads=["R1", "AQ"], writes=["AQ"])
            O("act", lambda e: e.copy(out=AKv[:, 0:nb, 3:6, :], in_=AQv[:, 0:nb, 0:3, :]), reads=["AQ", "AK"], writes=["AK"])

        def aug_store(src, srcname, b0, nbb, dst_fn, npart=128, ncol=128):
            p2 = nxt() % 2

            def trp(e, p2=p2):
                for j in range(nbb):
                    ins = e.transpose(out=psT[p2][0:24, j, 0:npart], in_=src[0:npart, b0 + j, :], identity=c_idb[0:npart, 0:npart])
                return ins
            O("pe", trp, reads=[srcname, "const"], writes=[f"psT{p2}"])
            O("act", lambda e, p2=p2: e.copy(out=stg[p2][:, 0:nbb, 0:ncol], in_=psT[p2][0:24, 0:nbb, 0:ncol]),
              reads=[f"psT{p2}"], writes=[f"stg{p2}"])
            O("sp", _nd(1)(lambda e, dma, p2=p2: dma(dst_fn(), stg[p2][:, 0:nbb, 0:ncol])),
              reads=[f"stg{p2}"], writes=[f"aug_{nxt()}"], dma=True)

        NB = T // 128
        for b0 in range(0, NB, 8):
            O("sp", _nd(1)(lambda e, dma, b0=b0: dma(
                LF[:, b0:b0 + 8, :], FD[b0 * 128:(b0 + 8) * 128, 0:4].rearrange("(b p) c -> p b c", p=128))),
              reads=[f"FD{tg}" for tg in range(NGP)] + (["LF"] if b0 else []), writes=["LF"], dma=True)
        cumsum_and_aug(NB, ["LF"])
        for b0 in range(0, NB, 4):
            aug_store(AQ, "AQ", b0, 4, lambda b0=b0: QT[64:70, :, b0 * 128:(b0 + 4) * 128].rearrange("i h (j p) -> (i h) j p", p=128))
            aug_store(AK, "AK", b0, 4, lambda b0=b0: KTn[64:70, :, b0 * 128:(b0 + 4) * 128].rearrange("i h (j p) -> (i h) j p", p=128))
        NBP = past // 128
        for s in range(NSQ):
            ts0 = T + s * TS
            O("dve", lambda e: e.memset(LF[:, 0:NBS, :], 0.0), reads=["LF"], writes=["LF"])

            @_nd(3)
            def ldlf(e, dma, s=s, ts0=ts0):
                dma(LF[:, 0:NBP // 2, :], c_lf[s, 0:past // 2, :].rearrange("(b p) c -> p b c", p=128))
                dma(LF[:, NBP // 2:NBP, :], c_lf[s, past // 2:past, :].rearrange("(b p) c -> p b c", p=128))
                dma(LF[0:TS, NBP, :], FD[ts0:ts0 + TS, 0:4])
            O("sp", ldlf, reads=[f"FD{NG - 1}"], writes=["LF"], dma=True)
            cumsum_and_aug(NBS, ["LF"])
            for b0 in range(0, NBS, 4):
                nbb = min(4, NBS - b0)
                aug_store(AK, "AK", b0, nbb, lambda b0=b0, nbb=nbb, s=s:
                          KTs[s, 64:70, :, b0 * 128:(b0 + nbb) * 128].rearrange("i h (j p) -> (i h) j p", p=128))
            aug_store(AQ, "AQ", NBP, 1, lambda ts0=ts0: QT[64:70, :, ts0:ts0 + TS].rearrange("i h (j p) -> (i h) j p", p=TS),
                      npart=TS, ncol=TS)
            for b0 in range(0, NBP, 4):
                ckc[0] += 1
                i4 = ckc[0] % 4
                i2 = ckc[0] % 2
                O("pool", _nd(1)(lambda e, dma, i4=i4, s=s, b0=b0: dma(
                    ck[i4][:, :, :], c_k[s, b0 * 128:(b0 + 4) * 128, :].rearrange("(j p) c -> p j c", p=128))),
                  writes=[f"ck{i4}"], dma=True)
                O("act", lambda e, i2=i2, i4=i4: e.copy(out=ckb[i2][:, :, :], in_=ck[i4][:, :, :]), reads=[f"ck{i4}"], writes=[f"ckb{i2}"])
                for hp in range(2):
                    p2 = nxt() % 2

                    def trk(e, i2=i2, hp=hp, p2=p2):
                        for j in range(4):
                            ins = e.transpose(out=psT[p2][:, j, :], in_=ckb[i2][:, j, hp * 128:(hp + 1) * 128], identity=c_idb[:, :])
                        return ins
                    O("pe", trk, reads=[f"ckb{i2}", "const"], writes=[f"psT{p2}"])
                    k2 = nxt() % 2
                    O("dve", lambda e, p2=p2, k2=k2: e.tensor_copy(out=kst[k2][:, :, :], in_=psT[p2][:, :, :]),
                      reads=[f"psT{p2}"], writes=[f"kst{k2}"])

                    @_nd(2)
                    def stks(e, dma, k2=k2, hp=hp, s=s, b0=b0):
                        for hh in range(2):
                            dma(KTs[s, 0:64, hp * 2 + hh, b0 * 128:(b0 + 4) * 128].rearrange("d (j p) -> d j p", p=128),
                                kst[k2][hh * 64:(hh + 1) * 64, :, :])
                    O("sp", stks, reads=[f"kst{k2}"], writes=[f"KTs{s}"], dma=True)
            O("sp", _nd(1)(lambda e, dma, s=s, ts0=ts0: dma(KTs[s, 0:64, :, past:past + TS], KTn[0:64, :, ts0:ts0 + TS])),
              reads=[f"KTn{NG - 1}"], writes=[f"KTs{s}"], dma=True)
            O("pool", _nd(1)(lambda e, dma, s=s: dma(Vs[s, 0:past, :], c_v[s, :, :])), writes=[f"Vs{s}"], dma=True)
            O("sp", _nd(1)(lambda e, dma, s=s, ts0=ts0: dma(Vs[s, past:past + TS, :], Vb[ts0:ts0 + TS, :])),
              reads=[f"Vb{NG - 1}"], writes=[f"Vs{s}"], dma=True)
        tk.barrier()
        tk.flush(sems, dsems)
        es.close()

    if STAGE >= 2:
        phase1b()

    RG = [[0, 1, 2, 3], [4, 5, 6, 7]]
    EARLY_AG = False
    deferred = list(range(NCH))

    def phase23():
        es = ExitStack()

        def sb(name, shape, dt=F32):
            return es.enter_context(nc.sbuf_tensor('p23_' + name, list(shape), dt))

        def ps(name, shape, dt=F32):
            return es.enter_context(nc.psum_tensor('p23_' + name, list(shape), dt))

        TK = max(T, TKS)
        qt = sb("qt", [128, T], BF16)
        kt = sb("kt", [128, TK], BF16)
        vt = sb("vt", [128, TK // 128, 128], BF16)
        pb = [sb(f"pb{i}", [128, 512], BF16) for i in range(4)]
        rc = sb("rc", [128, 512])
        at = sb("at", [64, 512], BF16)
        att = [sb(f"att{i}", [128, 4, 64], BF16) for i in range(2)]
        psS = [ps(f"psS{i}", [128, 512]) for i in range(2)]
        psO = [ps(f"psO{i}", [128, 512]) for i in range(2)]
        tA = ps("ssd_tA", [128, 4, 64])
        tB = ps("ssd_tB", [128, 256])
        tC = ps("ssd_tC", [64, 256])
        psTa = ps("psTa", [128, 4, 64], BF16)
        O("dve", lambda e: e.memset(vt[:, :, 64:128], 1.0), writes=["vt1"])
        O("dve", lambda e: e.memset(qt[64:128, :], 0.0), writes=["qt"])
        O("dve", lambda e: e.memset(kt[64:128, :], 0.0), writes=["kt"])
        sbc = [0]

        def job(q_ap, k_ap, v_ap, Tq, nk, pst, e_row0, h):
            nkb = (nk + 127) // 128

            @_nd(2)
            def ld(e, dma):
                dma(qt[0:70, 0:Tq], q_ap)
                dma(kt[0:70, 0:nk], k_ap)
            O("sp", ld, writes=["qt", "kt"], dma=True)
            nfb = nk // 128
            VB = 8
            for b0 in range(0, nfb, VB):
                b1 = min(nfb, b0 + VB)
                O("sp", _nd(1)(lambda e, dma, b0=b0, b1=b1: dma(
                    vt[:, b0:b1, 0:64], v_ap[b0 * 128:b1 * 128, :].rearrange("(b p) c -> p b c", p=128))),
                  reads=["vt"] if b0 else [], writes=["vt"], dma=True)
            if nk % 128:
                nf = nk // 128
                O("sp", _nd(1)(lambda e, dma: dma(vt[0:nk - nf * 128, nf, 0:64], v_ap[nf * 128:nk, :])),
                  reads=["vt"], writes=["vt"], dma=True)
            SBQ = min(512, Tq)
            LA = 2
            tiles = []
            sbs = []
            for q0 in range(0, Tq, SBQ):
                nq = SBQ
                sbc[0] += 1
                o2 = sbc[0] % 2
                blocks = []
                for kb in range(nkb):
                    ks = kb * 128
                    kn = min(128, nk - ks)
                    if ks + kn - 1 <= pst + q0:
                        blocks.append((kb, ks, kn, q0, False))
                    elif ks <= pst + q0 + nq - 1:
                        blocks.append((kb, ks, kn, ks - pst, True))
                sbs.append((q0, nq, o2, len(blocks)))
                for bi, blk in enumerate(blocks):
                    tiles.append((len(sbs) - 1, bi, blk))

            def emit_qk(ti):
                si, bi, (kb, ks, kn, qlo, diag) = tiles[ti]
                q0, nq, o2, nb_ = sbs[si]
                n1 = q0 + nq - qlo
                s3 = ti % 2
                p4 = ti % 4
                wd_ = min(128, n1)

                def qkmm(e, s3=s3, ks=ks, kn=kn, qlo=qlo, n1=n1, diag=diag, wd_=wd_):
                    ins = e.matmul(psS[s3][0:kn, 0:n1], lhsT=kt[0:128, ks:ks + kn], rhs=qt[0:128, qlo:qlo + n1],
                                   start=True, stop=not diag)
                    if diag:
                        ins = e.matmul(psS[s3][0:kn, 0:wd_], lhsT=c_idb[0:kn, 0:kn], rhs=c_ngm[0:kn, 0:wd_],
                                       start=False, stop=True)
                    return ins
                O("pe", qkmm, reads=["qt", "kt", "const"], writes=[f"psS{s3}"])
                O("act", lambda e, s3=s3, p4=p4, kn=kn, n1=n1: e.activation(
                    out=pb[p4][0:kn, 0:n1], in_=psS[s3][0:kn, 0:n1], func=AF.Exp),
                  reads=[f"psS{s3}"], writes=[f"pb{p4}"])

            def emit_pv(ti):
                si, bi, (kb, ks, kn, qlo, diag) = tiles[ti]
                q0, nq, o2, nb_ = sbs[si]
                n1 = q0 + nq - qlo
                p4 = ti % 4
                O("pe", lambda e, o2=o2, p4=p4, kb=kb, kn=kn, qlo=qlo, n1=n1, q0=q0, nq=nq, bi=bi, nb_=nb_: e.matmul(
                    psO[o2][:, qlo - q0:nq], lhsT=vt[0:kn, kb, :], rhs=pb[p4][0:kn, 0:n1],
                    start=(bi == 0), stop=(bi == nb_ - 1)),
                  reads=[f"pb{p4}", "vt", "vt1"], writes=[f"psO{o2}"])
                if bi == nb_ - 1:
                    finish(si)

            def finish(si):
                q0, nq, o2, nb_ = sbs[si]
                O("dve", lambda e, o2=o2, nq=nq: e.reciprocal(out=rc[64:128, 0:nq], in_=psO[o2][64:128, 0:nq]),
                  reads=[f"psO{o2}"], writes=["rc"])
                O("dve", lambda e, o2=o2, nq=nq: e.tensor_tensor(out=at[:, 0:nq], in0=psO[o2][0:64, 0:nq], in1=rc[64:128, 0:nq], op=ALU.mult),
                  reads=[f"psO{o2}", "rc"], writes=["at"])
                pending.append((cur_idx[0], lambda si=si: finish2(si)))

            def finish2(si):
                q0, nq, o2, nb_ = sbs[si]
                nj = (nq + 127) // 128
                wj = min(128, nq)
                p2 = 0

                def tra(e, p2=p2, nj=nj, wj=wj):
                    for j in range(nj):
                        ins = e.transpose(out=psTa[0:wj, j, 0:64], in_=at[:, j * 128:j * 128 + wj], identity=c_idb[0:64, 0:64])
                    return ins
                O("pe", tra, reads=["at", "const"], writes=["psTa"])
                a2 = nxt() % 2
                O("act", lambda e, p2=p2, a2=a2, nj=nj, wj=wj: e.copy(out=att[a2][0:wj, 0:nj, :], in_=psTa[0:wj, 0:nj, 0:64]),
                  reads=["psTa"], writes=[f"att{a2}"])
                r0 = e_row0 + q0
                O("sp", _nd(1)(lambda e, dma, a2=a2, r0=r0, nq=nq, nj=nj, wj=wj, h=h: dma(
                    E[r0:r0 + nq, h * 64:(h + 1) * 64].rearrange("(j p) c -> p j c", p=wj), att[a2][0:wj, 0:nj, :])),
                  reads=[f"att{a2}"], writes=[f"Ea_{r0}_{h}"], dma=True)
                if EARLY_AG and Tq == T and h == HG - 1 and (q0 // 512) % 2 == 1 and STAGE >= 5:
                    q = q0 // 1024
                    rd = [f"Ea_{rr}_{hh}" for rr in (q * 1024, q * 1024 + 512) for hh in range(HG)] + \
                         [f"Eu_{rr}" for rr in (q * 1024, q * 1024 + 512)]
                    if all(r_ in tk.res for r_ in rd):
                        O("pool", lambda e, q=q: e.collective_compute("AllGather", ALU.bypass, replica_groups=RG,
                                                                      ins=[E[q * 1024:(q + 1) * 1024, :]], outs=[Gq[q][:, :]]),
                          reads=rd, writes=[f"G{q}"])
                    else:
                        deferred.append(q)

            nt_ = len(tiles)
            pending = []
            cur_idx = [0]
            for idx in range(nt_ + LA):
                cur_idx[0] = idx
                if idx < nt_:
                    emit_qk(idx)
                if idx - LA >= 0:
                    emit_pv(idx - LA)
                while pending and idx >= pending[0][0] + 6:
                    pending.pop(0)[1]()
                yield
            while pending:
                pending.pop(0)[1]()

        bt = [sb(f"bt{i}", [128, 512], BF16) for i in range(2)]
        ct = [sb(f"ct{i}", [128, 512], BF16) for i in range(2)]
        xk = [sb(f"xk{i}", [64, 8, 384], BF16) for i in range(2)]
        fdm = [sb(f"fdm{i}", [64, 8, 8]) for i in range(2)]
        zs = [sb(f"zs{i}", [64, 8, 256]) for i in range(2)]
        u8 = [sb(f"u8{i}", [64, 8, 256], BF16) for i in range(2)]
        Hf = sb("Hf", [128, 256])
        Hb = sb("Hb", [128, 256], BF16)
        hio = sb("hio", [128, 2, 128])
        dA = sb("dA", [64, 4])
        Rr = sb("Rr", [64, 4, 64])
        cum = sb("cum", [64, 4])
        ecum = sb("ecum", [64, 4])
        seg = sb("seg", [64, 4, 64])
        CBm = sb("CBm", [64, 64])
        MT = sb("MT", [64, 4, 64], BF16)
        ysb = sb("ysb", [64, 256])
        wl = sb("wl", [64, 4])
        xw = sb("xw", [64, 256], BF16)
        dec = sb("dec", [128, 4])
        psR = tA
        psH = tB[:, :]
        psY = tB[0:64, :]
        psC = tB[0:64, 0:4]
        psF = tB[:, :].rearrange("p (a n) -> p a n", n=128)
        psYo = tC[:, :]
        psCB = tC[:, 0:64]

        ldk = [0]

        def seq_job(r_base, ntok, init_ap, out_ap, tag):
            if init_ap is None:
                O("dve", lambda e: e.memset(Hf[:, :], 0.0), writes=["Hf"])
            else:
                O("sp", _nd(1)(lambda e, dma: dma(hio[:, :, :], init_ap.rearrange("(a p) n -> p a n", p=128))),
                  writes=["hio"], dma=True)

                def trin(e):
                    for a in range(2):
                        ins = e.transpose(out=psF[:, a, :], in_=hio[:, a, :], identity=c_idf[:, :])
                    return ins
                O("pe", trin, reads=["hio", "const"], writes=["tB"])
                O("act", lambda e: e.copy(out=Hf[:, :], in_=tB[:, :]), reads=["tB"], writes=["Hf"])
            O("act", lambda e: e.copy(out=Hb[:, :], in_=Hf[:, :]), reads=["Hf"], writes=["Hb"])
            SC = min(512, ntok)
            nchk = SC // 64
            r0s = list(range(r_base, r_base + ntok, SC))

            def emit_ld(r0, i2):
                @_nd(5)
                def ld(e, dma, i2=i2, r0=r0):
                    dma(bt[i2][:, 0:SC], UT[0:128, r0:r0 + SC])
                    dma(ct[i2][:, 0:SC], UT[128:256, r0:r0 + SC])
                    dma(xk[i2][:, 0:nchk, :], Utok[r0:r0 + SC, :].rearrange("(c p) f -> p c f", p=64))
                    dma(fdm[i2][:, 0:nchk, :], FD[r0:r0 + SC, :].rearrange("(c p) f -> p c f", p=64))
                    dma(zs[i2][:, 0:nchk, :], ZS[r0:r0 + SC, :].rearrange("(c p) f -> p c f", p=64))
                O("sp", ld, writes=[f"ssdin{i2}"], dma=True)

            ldk[0] += 1
            base_k = ldk[0]
            emit_ld(r0s[0], base_k % 2)
            for k_, r0 in enumerate(r0s):
                i2 = (base_k + k_) % 2
                if k_ + 1 < len(r0s):
                    emit_ld(r0s[k_ + 1], (base_k + k_ + 1) % 2)
                ldk[0] = base_k + k_
                IN = f"ssdin{i2}"
                for c in range(nchk):
                    dt_ = fdm[i2][:, c, 4:8]
                    cs = slice(c * 64, (c + 1) * 64)
                    O("dve", lambda e, dt_=dt_: e.tensor_tensor(out=dA[:, :], in0=dt_, in1=c_r4[0:64, 0:4], op=ALU.mult),
                      reads=[IN, "cr4"], writes=["dA"])
                    for h in range(HG):
                        O("dve", lambda e, h=h: e.tensor_scalar(out=Rr[:, h, :], in0=c_trf[0:64, 0:64], scalar1=dA[:, h:h + 1],
                                                                scalar2=None, op0=ALU.mult),
                          reads=["dA", "const"], writes=["Rr"])
                    yield
                    O("pe", lambda e: e.matmul(psC[:, :], lhsT=c_trf[0:64, 0:64], rhs=dA[:, :], start=True, stop=True),
                      reads=["dA", "const"], writes=["tB"])
                    O("pe", lambda e: e.matmul(psR[:, :, :], lhsT=c_onf[0:64, :], rhs=Rr[:, :, :], start=True, stop=True),
                      reads=["Rr", "const"], writes=["tA"])
                    yield
                    O("act", lambda e: e.copy(out=cum[:, :], in_=psC[:, :]), reads=["tB"], writes=["cum"])
                    O("act", lambda e: e.activation(out=ecum[:, :], in_=psC[:, :], func=AF.Exp), reads=["tB"], writes=["ecum"])
                    for h in range(HG):
                        O("dve", lambda e, h=h: e.tensor_scalar(out=seg[:, h, :], in0=psR[0:64, h, :], scalar1=cum[:, h:h + 1],
                                                                scalar2=0.0, op0=ALU.subtract, op1=ALU.min),
                          reads=["tA", "cum"], writes=["seg"])
                    yield
                    O("act", lambda e: e.activation(out=seg[:, :, :], in_=seg[:, :, :], func=AF.Exp), reads=["seg"], writes=["seg"])
                    O("pe", lambda e, i2=i2, cs=cs: e.matmul(psCB[:, :], lhsT=bt[i2][:, cs], rhs=ct[i2][:, cs], start=True, stop=True),
                      reads=[IN], writes=["tC"])
                    O("dve", lambda e: e.tensor_tensor(out=CBm[:, :], in0=psCB[:, :], in1=c_trf[0:64, 0:64], op=ALU.mult),
                      reads=["tC", "const"], writes=["CBm"])
                    yield
                    for h in range(HG):
                        O("dve", lambda e, h=h, dt_=dt_: e.scalar_tensor_tensor(out=MT[:, h, :], in0=seg[:, h, :], scalar=dt_[:, h:h + 1],
                                                                               in1=CBm[:, :], op0=ALU.mult, op1=ALU.mult),
                          reads=["seg", "CBm", IN], writes=["MT"])

                    yield
                    def ydiag(e, i2=i2, c=c):
                        for h in range(HG):
                            ins = e.matmul(psY[:, h * 64:(h + 1) * 64], lhsT=MT[:, h, :], rhs=xk[i2][:, c, h * 64:(h + 1) * 64],
                                           start=True, stop=True)
                        return ins
                    O("pe", ydiag, reads=["MT", IN], writes=["tB"])
                    O("pe", lambda e, i2=i2, cs=cs: e.matmul(psYo[:, :], lhsT=ct[i2][:, cs], rhs=Hb[:, :], start=True, stop=True),
                      reads=[IN, "Hb"], writes=["tC"])
                    yield
                    O("act", lambda e: e.copy(out=ysb[:, :], in_=psY[:, :]), reads=["tB"], writes=["ysb"])
                    for h in range(HG):
                        hs = slice(h * 64, (h + 1) * 64)
                        O("dve", lambda e, h=h, hs=hs: e.scalar_tensor_tensor(out=ysb[:, hs], in0=psYo[:, hs], scalar=ecum[:, h:h + 1],
                                                                             in1=ysb[:, hs], op0=ALU.mult, op1=ALU.add),
                          reads=["tC", "ecum", "ysb"], writes=["ysb"])
                        O("dve", lambda e, h=h, hs=hs, i2=i2, c=c: e.scalar_tensor_tensor(
                            out=ysb[:, hs], in0=xk[i2][:, c, hs], scalar=c_r4[0:64, 4 + h:5 + h], in1=ysb[:, hs],
                            op0=ALU.mult, op1=ALU.add), reads=[IN, "const", "ysb"], writes=["ysb"])
                    O("dve", lambda e, i2=i2, c=c: e.tensor_tensor(out=u8[i2][:, c, :], in0=ysb[:, :], in1=zs[i2][:, c, :], op=ALU.mult),
                      reads=["ysb", IN], writes=[f"u8{i2}"])
                    yield
                    O("dve", lambda e: e.tensor_tensor(out=wl[:, :], in0=psR[0:64, :, 63], in1=cum[:, :], op=ALU.subtract),
                      reads=["tA", "cum"], writes=["wl"])
                    O("act", lambda e: e.activation(out=wl[:, :], in_=wl[:, :], func=AF.Exp), reads=["wl"], writes=["wl"])
                    O("dve", lambda e, dt_=dt_: e.tensor_tensor(out=wl[:, :], in0=wl[:, :], in1=dt_, op=ALU.mult),
                      reads=["wl", IN], writes=["wl"])
                    for h in range(HG):
                        hs = slice(h * 64, (h + 1) * 64)
                        O("dve", lambda e, h=h, hs=hs, i2=i2, c=c: e.tensor_scalar(out=xw[:, hs], in0=xk[i2][:, c, hs], scalar1=wl[:, h:h + 1],
                                                                                  scalar2=None, op0=ALU.mult),
                          reads=[IN, "wl"], writes=["xw"])
                    yield
                    O("pe", lambda e, i2=i2, c=c: e.matmul(psH[:, :], lhsT=xk[i2][:, c, 256:384], rhs=xw[:, :], start=True, stop=True),
                      reads=[IN, "xw"], writes=["tB"])
                    O("act", lambda e: e.activation(out=dec[:, :], in_=psR[:, :, 63], func=AF.Exp), reads=["tA"], writes=["dec"])
                    yield
                    for h in range(HG):
                        hs = slice(h * 64, (h + 1) * 64)
                        O("dve", lambda e, h=h, hs=hs: e.scalar_tensor_tensor(out=Hf[:, hs], in0=Hf[:, hs], scalar=dec[:, h:h + 1],
                                                                             in1=psH[:, hs], op0=ALU.mult, op1=ALU.add),
                          reads=["Hf", "dec", "tB"], writes=["Hf"])
                    O("act", lambda e: e.copy(out=Hb[:, :], in_=Hf[:, :]), reads=["Hf"], writes=["Hb"])
                yield
                O("sp", _nd(1)(lambda e, dma, i2=i2, r0=r0: dma(
                    E[r0:r0 + SC, 256:512].rearrange("(c p) f -> p c f", p=64), u8[i2][:, 0:nchk, :])),
                  reads=[f"u8{i2}"], writes=[f"Eu_{r0}"], dma=True)
            def trout(e):
                for a in range(2):
                    ins = e.transpose(out=psF[:, a, :], in_=Hf[:, a * 128:(a + 1) * 128], identity=c_idf[:, :])
                return ins
            O("pe", trout, reads=["Hf", "const"], writes=["tB"])
            O("act", lambda e: e.copy(out=hio[:, :, :], in_=psF[:, :, :]), reads=["tB"], writes=["hio"])
            O("sp", _nd(1)(lambda e, dma: dma(out_ap.rearrange("(a p) n -> p a n", p=128), hio[:, :, :])),
              reads=["hio"], writes=[f"ssm_{tag}"], dma=True)


        def att_gen():
            for h in range(HG):
                yield from job(QT[:, h, 0:T], KTn[:, h, 0:T], Vb[0:T, h * 64:(h + 1) * 64], T, T, 0, 0, h)
            for s in range(NSQ):
                ts0 = T + s * TS
                for h in range(HG):
                    yield from job(QT[:, h, ts0:ts0 + TS], KTs[s, :, h, 0:past + TS], Vs[s, 0:past + TS, h * 64:(h + 1) * 64],
                                   TS, past + TS, past, ts0, h)

        def ssd_gen():
            yield from seq_job(0, T, None, ssm_p[:, :], "p")
            for s in range(NSQ):
                yield from seq_job(T + s * TS, TS, c_ssm[s, :, :], ssm_s[s, :, :], f"s{s}")

        n_att = HG * sum((i + 1) * 4 + 4 for i in range(T // 512)) + NSQ * HG * (past // 128 + 3)
        n_ssd = (T // 64 + NSQ) * 11
        ga, gs_ = att_gen(), ssd_gen()
        da = ds = 0
        a_alive = s_alive = True
        while a_alive or s_alive:
            if a_alive:
                try:
                    next(ga)
                    da += 1
                except StopIteration:
                    a_alive = False
            while s_alive and (not a_alive or (os.environ.get('INTERLEAVE', '1') == '1' and ds * n_att <= da * n_ssd)):
                try:
                    next(gs_)
                    ds += 1
                except StopIteration:
                    s_alive = False
        tk.barrier()
        tk.flush(sems, dsems)
        es.close()

    if STAGE >= 3:
        phase23()


    if STAGE >= 5:
        for q in deferred:
            O("pool", lambda e, q=q: e.collective_compute("AllGather", ALU.bypass, replica_groups=RG,
                                                          ins=[E[q * 1024:(q + 1) * 1024, :]], outs=[Gq[q][:, :]]),
              writes=[f"G{q}"])
        O("pool", lambda e: e.collective_compute("AllGather", ALU.bypass, replica_groups=RG,
                                                 ins=[E[T:T + 512, :]], outs=[Gs[:, :]]), writes=["Gs"])

    def phase5():
        es = ExitStack()

        def sb(name, shape, dt=F32):
            return es.enter_context(nc.sbuf_tensor(f'ph4_' + name, list(shape), dt))

        def ps(name, shape, dt=F32):
            return es.enter_context(nc.psum_tensor(f'ph4_' + name, list(shape), dt))

        mixT = sb("mixT", [128, 16, 512], BF16)
        x1 = sb("x1", [128, 4, D])
        AT = sb("AT", [128, 44, 512], BF16)
        NWB = 3
        wbuf = [sb(f"wbuf{i}", [128, 8192], BF16) for i in range(NWB)]
        selT = sb("selT", [128, 8, 256], BF16)
        selS = sb("selS", [128, 4, 128], BF16)
        c_gffn = sb("c_gffn", [128, D])
        c_gssd = sb("c_gssd", [128, 8])
        junk = sb("junk", [128, D])
        hb = [sb(f"hb{i}", [128, D], BF16) for i in range(2)]
        ss = [sb(f"ss{i}", [128, 2]) for i in range(2)]
        sq = sb("sq", [128, 512], BF16)
        rstd = sb("rstd", [128, 512])
        gt = [sb(f"gt{i}", [128, 512]) for i in range(2)]
        pz = [ps(f"pz{i}", [128, 512]) for i in range(6)]
        psT = [ps(f"psT{i}", [128, 4, 128], BF16) for i in range(2)]
        zc = [0]

        def nz():
            zc[0] += 1
            return zc[0] % 6
        wc = [0]

        def wload(view_fn, src_ap, src_reads):
            i = wc[0] % NWB
            wc[0] += 1
            O("sp", _nd(1)(lambda e, dma, i=i: dma(view_fn(wbuf[i]), src_ap)), reads=src_reads, writes=[f"wbuf{i}"], dma=True)
            return i

        @_nd(4)
        def ldc(e, dma):
            dma(selT[:, :, :], sel[:, :, :])
            dma(selS[:, :, :], sel_s[:, :, :])
            dma(c_gffn[:, :], gffn[0:1, :].partition_broadcast(128))
            dma(c_gssd[:, :], gssd[:, :])
        O("sp", ldc, writes=["p5c"], dma=True)

        items = [(Gq[q], 8, selT, 2, f"G{q}") for q in range(NCH)]
        groups = [items[i:i + 2] for i in range(0, NCH, 2)] + [[(Gs, 4, selS, 1, "Gs")]]
        def do_group(grp, row):
            ntile = sum(it[3] for it in grp)
            ntok = ntile * 128
            O("sp", _nd(1)(lambda e, dma, row=row, ntile=ntile: dma(
                x1[:, 0:ntile, :], x2[row:row + ntok, :].rearrange("(j p) c -> p j c", p=128))), writes=["x1"], dma=True)
            toff = 0
            for (G, nblk, st, nt, gname) in grp:
                rows = nblk * 128
                for r in range(4):
                    wi = wload(lambda w, nblk=nblk: w[:, 0:nblk * 512].rearrange("p (b c) -> p b c", c=512),
                               G[r * rows:(r + 1) * rows, :].rearrange("(b p) c -> p b c", p=128), [gname])
                    gv = wbuf[wi][:, 0:nblk * 512].rearrange("p (b c) -> p b c", c=512)
                    for i in range(nt):
                        z = nz()
                        pzv = pz[z][:, :].rearrange("p (a t) -> p a t", t=128)

                        def selmm(e, gv=gv, st=st, nblk=nblk, i=i, pzv=pzv):
                            for cch in range(4):
                                for blk in range(nblk):
                                    ins = e.matmul(pzv[:, cch, :], lhsT=gv[:, blk, cch * 128:(cch + 1) * 128],
                                                   rhs=st[:, blk, i * 128:(i + 1) * 128], start=(blk == 0), stop=(blk == nblk - 1))
                            return ins
                        O("pe", selmm, reads=[f"wbuf{wi}", "p5c"], writes=[f"pz{z}"])
                        c0 = toff + i * 128
                        eng = "act" if (r + i) % 2 else "dve"
                        if eng == "act":
                            O("act", lambda e, r=r, c0=c0, pzv=pzv: e.copy(out=mixT[:, r * 4:r * 4 + 4, c0:c0 + 128], in_=pzv),
                              reads=[f"pz{z}"], writes=["mixT"])
                        else:
                            O("dve", lambda e, r=r, c0=c0, pzv=pzv: e.tensor_copy(out=mixT[:, r * 4:r * 4 + 4, c0:c0 + 128], in_=pzv),
                              reads=[f"pz{z}"], writes=["mixT"])
                toff += nt * 128
            for gi in range(2):
                kcs = [(2 * gi) * 4 + 2, (2 * gi) * 4 + 3, (2 * gi + 1) * 4 + 2, (2 * gi + 1) * 4 + 3]
                z = nz()
                for j, kc in enumerate(kcs):
                    O("act", lambda e, kc=kc: e.activation(out=sq[:, 0:ntok], in_=mixT[:, kc, 0:ntok], func=AF.Square),
                      reads=["mixT"], writes=["sq"])
                    O("pe", lambda e, z=z, j=j: e.matmul(pz[z][:, 0:ntok], lhsT=c_onb[:, :], rhs=sq[:, 0:ntok], start=(j == 0), stop=(j == 3)),
                      reads=["sq", "const"], writes=[f"pz{z}"])
                O("act", lambda e, z=z: e.activation(out=rstd[:, 0:ntok], in_=pz[z][:, 0:ntok], func=AF.Sqrt, bias=EPS, scale=1.0 / 512),
                  reads=[f"pz{z}"], writes=["rstd"])
                O("dve", lambda e: e.reciprocal(out=rstd[:, 0:ntok], in_=rstd[:, 0:ntok]), reads=["rstd"], writes=["rstd"])
                for j, kc in enumerate(kcs):
                    O("dve", lambda e, kc=kc, j=j, gi=gi: e.scalar_tensor_tensor(
                        out=mixT[:, kc, 0:ntok], in0=mixT[:, kc, 0:ntok], scalar=c_gssd[:, gi * 4 + j:gi * 4 + j + 1],
                        in1=rstd[:, 0:ntok], op0=ALU.mult, op1=ALU.mult), reads=["mixT", "rstd", "p5c"], writes=["mixT"])
            for cg in range(4):
                wi = wload(lambda w: w[:, :].rearrange("p (k n) -> p k n", n=512),
                           wo_b[:, cg * 512:(cg + 1) * 512].rearrange("(k p) n -> p k n", p=128), [f"wo_b_{r}" for r in range(0, D, 512)])
                wv = wbuf[wi][:, :].rearrange("p (k n) -> p k n", n=512)
                for i in range(ntile):
                    z = nz()

                    def womm(e, wv=wv, i=i, z=z):
                        for kc in range(16):
                            ins = e.matmul(pz[z][:, :], lhsT=mixT[:, kc, i * 128:(i + 1) * 128], rhs=wv[:, kc, :],
                                           start=(kc == 0), stop=(kc == 15))
                        return ins
                    O("pe", womm, reads=[f"wbuf{wi}", "mixT"], writes=[f"pz{z}"])
                    O("dve", lambda e, i=i, cg=cg, z=z: e.tensor_tensor(out=x1[:, i, cg * 512:(cg + 1) * 512], in0=pz[z][:, :],
                                                                       in1=x1[:, i, cg * 512:(cg + 1) * 512], op=ALU.add),
                      reads=[f"pz{z}", "x1"], writes=["x1"])
            for i in range(ntile):
                i2 = nxt() % 2
                O("act", lambda e, i=i, i2=i2: e.activation(out=junk[:, :], in_=x1[:, i, :], func=AF.Square, accum_out=ss[i2][:, 0:1]),
                  reads=["x1"], writes=["junk", f"ss{i2}"])
                O("act", lambda e, i2=i2: e.activation(out=ss[i2][:, 1:2], in_=ss[i2][:, 0:1], func=AF.Sqrt, bias=EPS, scale=1.0 / D),
                  reads=[f"ss{i2}"], writes=[f"ss{i2}"])
                O("dve", lambda e, i2=i2: e.reciprocal(out=ss[i2][:, 1:2], in_=ss[i2][:, 1:2]), reads=[f"ss{i2}"], writes=[f"ss{i2}"])
                O("dve", lambda e, i=i, i2=i2: e.scalar_tensor_tensor(out=hb[i2][:, :], in0=x1[:, i, :], scalar=ss[i2][:, 1:2],
                                                                     in1=c_gffn[:, :], op0=ALU.mult, op1=ALU.mult),
                  reads=["x1", f"ss{i2}", "p5c"], writes=[f"hb{i2}"])
                for k4 in range(4):
                    p2 = nxt() % 2

                    def tr(e, i2=i2, k4=k4, p2=p2):
                        for j in range(4):
                            kc = k4 * 4 + j
                            ins = e.transpose(out=psT[p2][:, j, :], in_=hb[i2][:, kc * 128:(kc + 1) * 128], identity=c_idb[:, :])
                        return ins
                    O("pe", tr, reads=[f"hb{i2}", "const"], writes=[f"psT{p2}"])
                    if k4 % 2:
                        O("act", lambda e, k4=k4, p2=p2, i=i: e.copy(out=mixT[:, k4 * 4:(k4 + 1) * 4, i * 128:(i + 1) * 128], in_=psT[p2][:, :, :]),
                          reads=[f"psT{p2}"], writes=["mixT"])
                    else:
                        O("dve", lambda e, k4=k4, p2=p2, i=i: e.tensor_copy(out=mixT[:, k4 * 4:(k4 + 1) * 4, i * 128:(i + 1) * 128], in_=psT[p2][:, :, :]),
                          reads=[f"psT{p2}"], writes=["mixT"])
            wg_reads = [f"wg_b_{r}" for r in range(0, D, 512)]
            wu_reads = [f"wu_b_{r}" for r in range(0, D, 512)]
            for fb in range(DFF // 512):
                wgi = wload(lambda w: w[:, :].rearrange("p (k n) -> p k n", n=512),
                            wg_b[:, fb * 512:(fb + 1) * 512].rearrange("(k p) n -> p k n", p=128), wg_reads)
                wui = wload(lambda w: w[:, :].rearrange("p (k n) -> p k n", n=512),
                            wu_b[:, fb * 512:(fb + 1) * 512].rearrange("(k p) n -> p k n", p=128), wu_reads)
                wgv = wbuf[wgi][:, :].rearrange("p (k n) -> p k n", n=512)
                wuv = wbuf[wui][:, :].rearrange("p (k n) -> p k n", n=512)
                for c4 in range(4):
                    zg, zu = nz(), nz()

                    def gmm(e, wv=wgv, c4=c4, z=zg):
                        for kc in range(16):
                            ins = e.matmul(pz[z][:, 0:ntok], lhsT=wv[:, kc, c4 * 128:(c4 + 1) * 128], rhs=mixT[:, kc, 0:ntok],
                                           start=(kc == 0), stop=(kc == 15))
                        return ins

                    def umm(e, wv=wuv, c4=c4, z=zu):
                        for kc in range(16):
                            ins = e.matmul(pz[z][:, 0:ntok], lhsT=wv[:, kc, c4 * 128:(c4 + 1) * 128], rhs=mixT[:, kc, 0:ntok],
                                           start=(kc == 0), stop=(kc == 15))
                        return ins
                    O("pe", gmm, reads=[f"wbuf{wgi}", "mixT"], writes=[f"pz{zg}"])
                    O("pe", umm, reads=[f"wbuf{wui}", "mixT"], writes=[f"pz{zu}"])
                    g2 = nxt() % 2
                    O("act", lambda e, z=zg, g2=g2: e.activation(out=gt[g2][:, 0:ntok], in_=pz[z][:, 0:ntok], func=AF.Silu),
                      reads=[f"pz{zg}"], writes=[f"gt{g2}"])
                    O("dve", lambda e, z=zu, g2=g2, fb=fb, c4=c4: e.tensor_tensor(out=AT[:, fb * 4 + c4, 0:ntok], in0=pz[z][:, 0:ntok],
                                                                                 in1=gt[g2][:, 0:ntok], op=ALU.mult),
                      reads=[f"pz{zu}", f"gt{g2}"], writes=["AT"])
            wd_reads = [f"wd_b_{r}" for r in range(0, DFF, 512)]
            for cg in range(4):
                zs_ = [nz() for _ in range(ntile)]
                for blk in range(4):
                    wi = wload(lambda w: w[:, 0:11 * 512].rearrange("p (f n) -> p f n", n=512),
                               wd_b[blk * 1408:(blk + 1) * 1408, cg * 512:(cg + 1) * 512].rearrange("(f p) n -> p f n", p=128), wd_reads)
                    wv = wbuf[wi][:, 0:11 * 512].rearrange("p (f n) -> p f n", n=512)
                    for i in range(ntile):
                        def dmm(e, wv=wv, i=i, blk=blk, z=zs_[i]):
                            for f in range(11):
                                ins = e.matmul(pz[z][:, :], lhsT=AT[:, blk * 11 + f, i * 128:(i + 1) * 128], rhs=wv[:, f, :],
                                               start=(blk == 0 and f == 0), stop=(blk == 3 and f == 10))
                            return ins
                        O("pe", dmm, reads=[f"wbuf{wi}", "AT"], writes=[f"pz{zs_[i]}"])
                for i in range(ntile):
                    O("dve", lambda e, i=i, cg=cg, z=zs_[i]: e.tensor_tensor(out=x1[:, i, cg * 512:(cg + 1) * 512], in0=pz[z][:, :],
                                                                            in1=x1[:, i, cg * 512:(cg + 1) * 512], op=ALU.add),
                      reads=[f"pz{zs_[i]}", "x1"], writes=["x1"])
            O("sp", _nd(1)(lambda e, dma, row=row, ntile=ntile, ntok=ntok: dma(
                y2[row:row + ntok, :].rearrange("(j p) c -> p j c", p=128), x1[:, 0:ntile, :])), reads=["x1"], writes=["y2"], dma=True)
            return ntok

        row = 0
        for grp in groups:
            row += do_group(grp, row)
        tk.barrier()
        tk.flush(sems, dsems)
        es.close()

    if STAGE >= 8:
        phase5()
    if os.environ.get("DBG_E"):
        E_dbg = dout("E_dbg", [TT, 512], BF16)
        O("sp", _nd(1)(lambda e, dma: dma(E_dbg[:, :], E[:, :])), writes=["E_dbg"], dma=True)
    tk.barrier()
    tk.flush(sems, dsems)
    ges.close()
    sstack.close()
    return nc


def make_in_maps(inp, T, NSQ=8, TS=64, past=PAST):
    bf = ml_dtypes.bfloat16
    f32 = np.float32
    NCH = T // 1024
    ident = np.eye(128, dtype=f32)
    bd = np.zeros((128, 128), f32)
    bd[:64, :64] = 1.0
    bd[64:, 64:] = 1.0
    triu = np.triu(np.ones((128, 128), f32))
    negm = ((1.0 - triu) * -30000.0).astype(f32)
    ones = np.ones((128, 128), f32)
    w_in = inp["w_in"][0]
    w_conv = inp["w_conv"][0]
    b_conv = inp["b_conv"][0]
    o_f = 3 * 1024
    o_z = o_f + 16
    o_x = o_z + 1024
    o_B = o_x + 1024
    o_C = o_B + 256
    o_dt = o_C + 256
    perm = []
    for kc in range(16):
        r, cch = kc // 4, kc % 4
        base = 256 * r + 128 * cch if cch < 2 else 1024 + 256 * r + 128 * (cch - 2)
        perm.extend(range(base, base + 128))
    w_out_p = np.ascontiguousarray(inp["w_out"][0][perm])
    maps = []
    for c in range(8):
        b, g = c // 4, c % 4
        j = g
        xs = inp["x_sample"][8 * b:8 * b + 8].reshape(NSQ * TS, D)
        x_all = np.concatenate([inp["x_prompt"][b], xs], axis=0)
        hs = slice(256 * g, 256 * g + 256)
        grp = g // 2
        wsl = np.concatenate([
            w_in[:, 0:1024][:, hs], w_in[:, 1024:2048][:, hs], w_in[:, 2048:3072][:, hs],
            w_in[:, o_z:o_z + 1024][:, hs], w_in[:, o_x:o_x + 1024][:, hs],
            w_in[:, o_B + 128 * grp:o_B + 128 * grp + 128], w_in[:, o_C + 128 * grp:o_C + 128 * grp + 128],
            w_in[:, o_f + 4 * g:o_f + 4 * g + 4], w_in[:, o_dt + 4 * g:o_dt + 4 * g + 4]], axis=1)
        cols = np.zeros((128, 16), f32)
        cols[:, 0] = np.tile(inp["g_q"][0], 2)
        cols[:, 1] = np.tile(inp["g_k"][0], 2)
        cols[0:4, 2] = -1.0
        cols[4:8, 2] = 1.0
        cols[0:4, 3] = inp["f_bias"][0][4 * g:4 * g + 4]
        cols[4:8, 3] = inp["dt_bias"][0][4 * g:4 * g + 4]
        cols[0:4, 4] = -1.0
        cols[4:8, 4] = 1.0
        ccols = np.concatenate([np.arange(256 * g, 256 * g + 256), 1024 + 128 * grp + np.arange(128),
                                1280 + 128 * grp + np.arange(128)])
        wconv = np.zeros((128, 4, 5), f32)
        for ci in range(4):
            cc = ccols[ci * 128:(ci + 1) * 128]
            wconv[:, ci, 0:4] = w_conv[:, cc].T
            wconv[:, ci, 4] = b_conv[cc]
        row4 = np.concatenate([inp["a_log"][0][4 * g:4 * g + 4], inp["d_skip"][0][4 * g:4 * g + 4]])[None, :].astype(f32)
        sq = slice(8 * b, 8 * b + 8)
        c_k = inp["cache_fox_k"][0][sq][:, :, 4 * g:4 * g + 4, :].reshape(NSQ, past, 256)
        c_v = inp["cache_fox_v"][0][sq][:, :, 4 * g:4 * g + 4, :].reshape(NSQ, past, 256)
        c_lf = inp["cache_fox_logf"][0][sq][:, :, 4 * g:4 * g + 4]
        c_convT = np.transpose(inp["cache_conv"][0][sq][:, :, ccols], (2, 0, 1))
        c_ssm = inp["state_ssm"][0][sq][:, 4 * g:4 * g + 4].reshape(NSQ, 256, 128)
        xp = inp["x_prompt"][b].reshape(NCH, 4, 256, D)[:, j].reshape(NCH * 256, D)
        x2 = np.concatenate([xp, inp["x_sample"][8 * b + 2 * j:8 * b + 2 * j + 2].reshape(2 * TS, D)], axis=0)
        sel = np.zeros((128, 8, 256), f32)
        for m in range(256):
            t = 256 * j + m
            sel[t % 128, t // 128, m] = 1.0
        sel_s = np.zeros((128, 4, 128), f32)
        for m in range(128):
            t = 128 * j + m
            sel_s[t % 128, t // 128, m] = 1.0
        gs = inp["g_ssd"][0]
        gssd = np.zeros((128, 8), f32)
        for gi in range(2):
            for jj in range(4):
                base = 256 * (2 * gi + jj // 2) + 128 * (jj % 2)
                gssd[:, gi * 4 + jj] = gs[base:base + 128]
        ca = np.ascontiguousarray
        maps.append({
            "x_all": ca(x_all), "w_in": ca(wsl), "gmix": ca(inp["g_mix"]), "cols": cols, "wconv": wconv, "row4": row4,
            "ident_f": ident, "ident_b": ident.astype(bf), "bd_ones": bd.astype(bf),
            "triu_f": triu, "triu_b": triu.astype(bf), "negm_b": negm.astype(bf), "ones_f": ones, "ones_b": ones.astype(bf),
            "c_k": ca(c_k), "c_v": ca(c_v), "c_lf": ca(c_lf), "c_convT": ca(c_convT), "c_ssm": ca(c_ssm),
            "x2": ca(x2), "sel": sel.astype(bf), "sel_s": sel_s.astype(bf), "gffn": ca(inp["g_ffn"]), "gssd": gssd,
            "w_out": w_out_p, "w_gate": inp["w_gate"][0], "w_up": inp["w_up"][0], "w_down": inp["w_down"][0],
        })
    return maps


def assemble(res, T, B=2, NSQ=8, TS=64):
    NCH = T // 1024
    f32 = np.float32
    DB = 8 * B
    yp = np.zeros((B, T, D), f32)
    ys = np.zeros((DB, TS, D), f32)
    p_conv = np.zeros((1, B, 3, 1536), f32)
    p_ssm = np.zeros((1, B, 16, 64, 128), f32)
    p_k = np.zeros((1, B, T, 16, 64), f32)
    p_v = np.zeros((1, B, T, 16, 64), f32)
    p_f = np.zeros((1, B, T, 16), f32)
    s_conv = np.zeros((1, DB, 3, 1536), f32)
    s_ssm = np.zeros((1, DB, 16, 64, 128), f32)
    s_k = np.zeros((1, DB, TS, 16, 64), f32)
    s_v = np.zeros((1, DB, TS, 16, 64), f32)
    s_f = np.zeros((1, DB, TS, 16), f32)
    for c in range(8):
        b, g = c // 4, c % 4
        j = g
        grp = g // 2
        r = res[c]
        y2 = r["y2"]
        yp[b].reshape(NCH, 4, 256, D)[:, j] = y2[:NCH * 256].reshape(NCH, 256, D)
        ys[8 * b + 2 * j:8 * b + 2 * j + 2] = y2[NCH * 256:].reshape(2, TS, D)
        sq = slice(8 * b, 8 * b + 8)
        p_k[0, b, :, 4 * g:4 * g + 4] = r["k_out"][:T].reshape(T, 4, 64)
        p_v[0, b, :, 4 * g:4 * g + 4] = r["v_out"][:T].reshape(T, 4, 64)
        p_f[0, b, :, 4 * g:4 * g + 4] = r["lf_out"][:T]
        s_k[0, sq, :, 4 * g:4 * g + 4] = r["k_out"][T:].reshape(NSQ, TS, 4, 64)
        s_v[0, sq, :, 4 * g:4 * g + 4] = r["v_out"][T:].reshape(NSQ, TS, 4, 64)
        s_f[0, sq, :, 4 * g:4 * g + 4] = r["lf_out"][T:].reshape(NSQ, TS, 4)
        p_ssm[0, b, 4 * g:4 * g + 4] = r["ssm_p"].reshape(4, 64, 128)
        s_ssm[0, sq, 4 * g:4 * g + 4] = r["ssm_s"].reshape(NSQ, 4, 64, 128)
        cp = r["conv_p"].T
        cs = np.transpose(r["conv_s"], (1, 2, 0))
        p_conv[0, b, :, 256 * g:256 * g + 256] = cp[:, 0:256]
        s_conv[0, sq, :, 256 * g:256 * g + 256] = cs[:, :, 0:256]
        if g % 2 == 0:
            p_conv[0, b, :, 1024 + 128 * grp:1024 + 128 * grp + 128] = cp[:, 256:384]
            p_conv[0, b, :, 1280 + 128 * grp:1280 + 128 * grp + 128] = cp[:, 384:512]
            s_conv[0, sq, :, 1024 + 128 * grp:1024 + 128 * grp + 128] = cs[:, :, 256:384]
            s_conv[0, sq, :, 1280 + 128 * grp:1280 + 128 * grp + 128] = cs[:, :, 384:512]
    return (yp, ys, p_conv, p_ssm, p_k, p_v, p_f, s_conv, s_ssm, s_k, s_v, s_f)


_NC_CACHE = {}


def kernel(**inputs):
    inp = {k: np.asarray(v) for k, v in inputs.items()}
    T = inp["x_prompt"].shape[1]
    if T not in _NC_CACHE:
        _NC_CACHE[T] = build(T)
    nc = _NC_CACHE[T]
    maps = make_in_maps(inp, T)
    res = run_bass_kernel_spmd(nc, maps, core_ids=list(range(8)))
    return assemble(res.results, T)
```
